# Optimizing a Trainium2 kernel written in Bass

```python
import math
import jax, jax.numpy as jnp
from jax import lax
import numpy as np

D_MODEL = 1024
BATCH = 2
SEQ = 8192
DEPTH = 4

ATTN_HEADS_PER_GROUP = 8
ATTN_HEAD_DIM = 128
ATTN_WINDOWS = (128, 512, 2048)
ATTN_DILATIONS = (1, 4, 16)
N_ATTN_GROUPS = 3
ATTN_BLOCK = 128
ROPE_THETA = 500000.0
ROPE_DIM = ATTN_HEAD_DIM // 4
ATTN_QKV_WIDTH = N_ATTN_GROUPS * 3 * ATTN_HEADS_PER_GROUP * ATTN_HEAD_DIM
ATTN_OUT_WIDTH = ATTN_HEADS_PER_GROUP * ATTN_HEAD_DIM

SSM_EXPAND = 2
SSM_D_INNER = SSM_EXPAND * D_MODEL
SSM_HEAD_DIM = 64
SSM_HEADS = SSM_D_INNER // SSM_HEAD_DIM
SSM_STATE = 128
SSM_GROUPS = 8
SSM_CONV = 4
SSM_CHUNK = 128
SSM_CONV_DIM = SSM_D_INNER + 2 * SSM_GROUPS * SSM_STATE
SSM_IN_WIDTH = SSM_D_INNER + SSM_CONV_DIM + SSM_HEADS

D_FF = 2816
FFN_CONV = 3

NORM_EPS = 1e-5

kernel_name = "hybrid_dilated_attn_mamba2_convffn"


def rmsnorm(x, w):
    xf = x.astype(jnp.float32)
    y = xf * lax.rsqrt(jnp.mean(xf * xf, axis=-1, keepdims=True) + NORM_EPS)
    return (y * w.astype(jnp.float32)).astype(x.dtype)


def gated_group_rmsnorm(y, z, w, groups):
    g = y.astype(jnp.float32) * jax.nn.silu(z.astype(jnp.float32))
    shp = g.shape
    g = g.reshape(shp[:-1] + (groups, shp[-1] // groups))
    g = g * lax.rsqrt(jnp.mean(g * g, axis=-1, keepdims=True) + NORM_EPS)
    return g.reshape(shp) * w.astype(jnp.float32)


def causal_depthwise_conv(x, w, b):
    k = w.shape[0]
    y = lax.conv_general_dilated(
        x, w[:, None, :].astype(x.dtype), window_strides=(1,), padding=[(k - 1, 0)],
        dimension_numbers=("NWC", "WIO", "NWC"), feature_group_count=x.shape[-1])
    return y + b.astype(x.dtype)


def rope_tables(seq):
    pos = jnp.arange(seq, dtype=jnp.float32)
    inv_freq = ROPE_THETA ** (-jnp.arange(0, ROPE_DIM, 2, dtype=jnp.float32) / ROPE_DIM)
    ang = pos[:, None] * inv_freq[None, :]
    return jnp.cos(ang), jnp.sin(ang)


def apply_partial_rope(t, cos, sin):
    t = t.astype(jnp.float32)
    half = ROPE_DIM // 2
    c = cos[:, None, None, :]
    s = sin[:, None, None, :]
    x1 = t[..., :half]
    x2 = t[..., half:ROPE_DIM]
    return jnp.concatenate([x1 * c - x2 * s, x2 * c + x1 * s, t[..., ROPE_DIM:]], axis=-1)


def dilated_window_attention(q, k, v, dilation, steps):
    bsz, s, h, hd = q.shape
    length = s // dilation
    nb = -(-length // ATTN_BLOCK)
    lp = nb * ATTN_BLOCK

    def to_strided(t):
        t = t.reshape(bsz, length, dilation, h, hd).transpose(0, 2, 3, 1, 4)
        t = jnp.pad(t, ((0, 0), (0, 0), (0, 0), (0, lp - length), (0, 0)))
        return t.reshape(bsz, dilation, h, nb, ATTN_BLOCK, hd)

    def with_prev(t):
        prev = jnp.pad(t, ((0, 0), (0, 0), (0, 0), (1, 0), (0, 0), (0, 0)))[:, :, :, :-1]
        return jnp.concatenate([prev, t], axis=-2)

    qb = to_strided(q)
    kk = with_prev(to_strided(k))
    vv = with_prev(to_strided(v))
    scores = jnp.einsum("brhnqe,brhnke->brhnqk", qb, kk) * (hd ** -0.5)
    n_idx = jnp.arange(nb)[:, None, None]
    i_idx = jnp.arange(ATTN_BLOCK)[None, :, None]
    j_idx = jnp.arange(2 * ATTN_BLOCK)[None, None, :]
    delta = ATTN_BLOCK + i_idx - j_idx
    key_pos = (n_idx - 1) * ATTN_BLOCK + j_idx
    allowed = (delta >= 0) & (delta <= steps) & (key_pos >= 0)
    scores = jnp.where(allowed, scores, -jnp.inf)
    m = jnp.max(scores, axis=-1, keepdims=True)
    p = jnp.exp(scores - m)
    den = jnp.sum(p, axis=-1, keepdims=True)
    o = jnp.einsum("brhnqk,brhnke->brhnqe", p, vv) / den
    lse = (m + jnp.log(den))[..., 0]
    o = o.reshape(bsz, dilation, h, lp, hd)[:, :, :, :length]
    o = o.transpose(0, 3, 1, 2, 4).reshape(bsz, s, h, hd)
    lse = lse.reshape(bsz, dilation, h, lp)[:, :, :, :length]
    lse = lse.transpose(0, 3, 1, 2).reshape(bsz, s, h)
    return o, lse


def dilated_attention_mixer(h, w_qkv, w_o, cos, sin):
    bsz, s, _ = h.shape
    qkv = (h @ w_qkv).reshape(bsz, s, N_ATTN_GROUPS, 3, ATTN_HEADS_PER_GROUP, ATTN_HEAD_DIM)
    q = apply_partial_rope(qkv[:, :, :, 0], cos, sin)
    k = apply_partial_rope(qkv[:, :, :, 1], cos, sin)
    v = qkv[:, :, :, 2].astype(jnp.float32)
    outs, lses = [], []
    for g in range(N_ATTN_GROUPS):
        dil = ATTN_DILATIONS[g]
        o_g, lse_g = dilated_window_attention(q[:, :, g], k[:, :, g], v[:, :, g], dil,
                                              ATTN_WINDOWS[g] // dil)
        outs.append(o_g)
        lses.append(lse_g)
    wts = jax.nn.softmax(jnp.stack(lses, axis=2), axis=2)
    o = jnp.einsum("bsgh,bsghe->bshe", wts, jnp.stack(outs, axis=2))
    return o.reshape(bsz, s, ATTN_OUT_WIDTH).astype(h.dtype) @ w_o


def ssd_chunked_scan(x, dt, a, bm, cm):
    b, s, h, p = x.shape
    g, n = bm.shape[2], bm.shape[3]
    r = h // g
    c = s // SSM_CHUNK
    q = SSM_CHUNK
    x = x.reshape(b, c, q, g, r, p)
    dt = dt.reshape(b, c, q, g, r)
    bm = bm.reshape(b, c, q, g, n)
    cm = cm.reshape(b, c, q, g, n)
    a_dt = dt * a.reshape(g, r)
    a_cs = jnp.cumsum(a_dt, axis=2)
    xdt = x * dt[..., None]
    seg = a_cs[:, :, :, None] - a_cs[:, :, None, :]
    causal = jnp.tril(jnp.ones((q, q), dtype=bool))[:, :, None, None]
    lmat = jnp.exp(jnp.where(causal, seg, -jnp.inf))
    cb = jnp.einsum("bcign,bcjgn->bcijg", cm, bm)
    y_diag = jnp.einsum("bcijgr,bcjgrp->bcigrp", cb[..., None] * lmat, xdt)
    decay = jnp.exp(a_cs[:, :, -1:] - a_cs)
    states = jnp.einsum("bcjgn,bcjgr,bcjgrp->bcgrpn", bm, decay, xdt)
    chunk_decay = jnp.exp(a_cs[:, :, -1])

    def step(state, inp):
        st_c, dec_c = inp
        return state * dec_c[..., None, None] + st_c, state

    init = jnp.zeros((b, g, r, p, n), dtype=x.dtype)
    _, prev = lax.scan(step, init, (jnp.moveaxis(states, 1, 0), jnp.moveaxis(chunk_decay, 1, 0)))
    prev = jnp.moveaxis(prev, 0, 1)
    y_off = jnp.einsum("bcign,bcgrpn->bcigrp", cm, prev) * jnp.exp(a_cs)[..., None]
    return (y_diag + y_off).reshape(b, s, h, p)


def ssd_mixer(h, w_in, conv_w, conv_b, dt_bias, a_log, d_skip, norm_w, w_out):
    bsz, s, _ = h.shape
    gn = SSM_GROUPS * SSM_STATE
    zxbcdt = h @ w_in
    z = zxbcdt[..., :SSM_D_INNER]
    xbc = zxbcdt[..., SSM_D_INNER:SSM_D_INNER + SSM_CONV_DIM]
    dt_raw = zxbcdt[..., SSM_D_INNER + SSM_CONV_DIM:]
    xbc = jax.nn.silu(causal_depthwise_conv(xbc, conv_w, conv_b))
    xs = xbc[..., :SSM_D_INNER].reshape(bsz, s, SSM_HEADS, SSM_HEAD_DIM).astype(jnp.float32)
    bm = xbc[..., SSM_D_INNER:SSM_D_INNER + gn].reshape(bsz, s, SSM_GROUPS, SSM_STATE)
    cm = xbc[..., SSM_D_INNER + gn:].reshape(bsz, s, SSM_GROUPS, SSM_STATE)
    dt = jax.nn.softplus(dt_raw.astype(jnp.float32) + dt_bias.astype(jnp.float32))
    a = -jnp.exp(a_log.astype(jnp.float32))
    y = ssd_chunked_scan(xs, dt, a, bm.astype(jnp.float32), cm.astype(jnp.float32))
    y = y + d_skip.astype(jnp.float32)[:, None] * xs
    y = gated_group_rmsnorm(y.reshape(bsz, s, SSM_D_INNER), z, norm_w, SSM_GROUPS)
    return y.astype(h.dtype) @ w_out


def conv_ffn(h, w_up, conv_w, conv_b, w_down):
    u = causal_depthwise_conv(h @ w_up, conv_w, conv_b)
    gate, up = u[..., :D_FF], u[..., D_FF:]
    return (jax.nn.silu(gate) * up) @ w_down


def setup_inputs(seed: int = 0) -> dict:
    key = jax.random.key(seed)
    ks = jax.random.split(key, 20)
    n_attn = (DEPTH + 1) // 2
    n_ssm = DEPTH // 2
    f32 = jnp.float32

    def nrm(k, shape, scale):
        return jax.random.normal(k, shape, dtype=f32) * scale

    u = jax.random.uniform(ks[7], (n_ssm, SSM_HEADS), dtype=f32)
    dt0 = jnp.exp(u * (math.log(0.1) - math.log(0.001)) + math.log(0.001))
    dt0 = jnp.maximum(dt0, 1e-4)
    dt_bias = dt0 + jnp.log(-jnp.expm1(-dt0))
    a_log = jnp.log(jax.random.uniform(ks[8], (n_ssm, SSM_HEADS), dtype=f32, minval=1.0, maxval=16.0))
    return {
        "x": nrm(ks[0], (BATCH, SEQ, D_MODEL), 1.0),
        "mix_norm_w": 1.0 + nrm(ks[1], (DEPTH, D_MODEL), 0.02),
        "attn_w_qkv": nrm(ks[2], (n_attn, D_MODEL, ATTN_QKV_WIDTH), D_MODEL ** -0.5),
        "attn_w_o": nrm(ks[3], (n_attn, ATTN_OUT_WIDTH, D_MODEL), ATTN_OUT_WIDTH ** -0.5),
        "ssm_w_in": nrm(ks[4], (n_ssm, D_MODEL, SSM_IN_WIDTH), D_MODEL ** -0.5),
        "ssm_conv_w": nrm(ks[5], (n_ssm, SSM_CONV, SSM_CONV_DIM), SSM_CONV ** -0.5),
        "ssm_conv_b": nrm(ks[6], (n_ssm, SSM_CONV_DIM), 0.01),
        "ssm_dt_bias": dt_bias,
        "ssm_a_log": a_log,
        "ssm_d": 1.0 + nrm(ks[9], (n_ssm, SSM_HEADS), 0.1),
        "ssm_norm_w": 1.0 + nrm(ks[10], (n_ssm, SSM_D_INNER), 0.02),
        "ssm_w_out": nrm(ks[11], (n_ssm, SSM_D_INNER, D_MODEL), SSM_D_INNER ** -0.5),
        "ffn_norm_w": 1.0 + nrm(ks[12], (DEPTH, D_MODEL), 0.02),
        "ffn_w_up": nrm(ks[13], (DEPTH, D_MODEL, 2 * D_FF), D_MODEL ** -0.5),
        "ffn_conv_w": nrm(ks[14], (DEPTH, FFN_CONV, 2 * D_FF), FFN_CONV ** -0.5),
        "ffn_conv_b": nrm(ks[15], (DEPTH, 2 * D_FF), 0.01),
        "ffn_w_down": nrm(ks[16], (DEPTH, D_FF, D_MODEL), D_FF ** -0.5),
        "final_norm_w": 1.0 + nrm(ks[17], (D_MODEL,), 0.02),
    }


def reference(x, mix_norm_w, attn_w_qkv, attn_w_o, ssm_w_in, ssm_conv_w, ssm_conv_b,
              ssm_dt_bias, ssm_a_log, ssm_d, ssm_norm_w, ssm_w_out, ffn_norm_w, ffn_w_up,
              ffn_conv_w, ffn_conv_b, ffn_w_down, final_norm_w):
    cos, sin = rope_tables(x.shape[1])
    for i in range(DEPTH):
        h = rmsnorm(x, mix_norm_w[i])
        j = i // 2
        if i % 2 == 0:
            x = x + dilated_attention_mixer(h, attn_w_qkv[j], attn_w_o[j], cos, sin)
        else:
            x = x + ssd_mixer(h, ssm_w_in[j], ssm_conv_w[j], ssm_conv_b[j], ssm_dt_bias[j],
                              ssm_a_log[j], ssm_d[j], ssm_norm_w[j], ssm_w_out[j])
        h = rmsnorm(x, ffn_norm_w[i])
        x = x + conv_ffn(h, ffn_w_up[i], ffn_conv_w[i], ffn_conv_b[i], ffn_w_down[i])
    return rmsnorm(x, final_norm_w)
```

```python
import contextlib
import numpy as np
import concourse.bass as bass
import concourse.mybir as mybir
from concourse.bass_utils import run_bass_kernel_spmd

F32 = mybir.dt.float32
BF16 = mybir.dt.bfloat16
ALU = mybir.AluOpType
AF = mybir.ActivationFunctionType
AX = mybir.AxisListType

NCORES = 8
T = 2048
D = 1024
KC = 8
DFF = 2816
NJ = 22
EPS = 1e-5
HALO = 3


class Dep:
    __slots__ = ("w", "rs", "ps")

    def __init__(self, ps=False):
        self.w = None
        self.rs = []
        self.ps = ps


class Sched:
    ENGS = ("pe", "act", "dve", "pool", "sp")
    NDMA = 6

    def __init__(self, nc, es):
        self.nc = nc
        self.h = {"pe": nc.tensor, "act": nc.scalar, "dve": nc.vector,
                  "pool": nc.gpsimd, "sp": nc.sync}
        self.sems = {}
        self.cnt = {}
        self.ops = {e: [] for e in self.ENGS}
        self.seen = {e: {} for e in self.ENGS}
        for e in self.ENGS + ("cc",):
            self.sems[e] = es.enter_context(nc.semaphore("s_" + e))
            self.cnt[e] = 0
        self.dsem = {}
        self.dcnt = {}
        self.drr = {}
        for e in ("sp", "pool", "act"):
            for i in range(self.NDMA):
                k = "d_%s%d" % (e, i)
                self.sems[k] = es.enter_context(nc.semaphore(k))
                self.cnt[k] = 0
            self.drr[e] = 0

    def _need(self, eng, waits, tok, skip_same_pe=True):
        if tok is None:
            return
        k, v = tok
        if eng == "pe" and k == "pe":
            return
        if self.seen[eng].get(k, 0) >= v:
            return
        if waits.get(k, 0) < v:
            waits[k] = v

    def op(self, eng, fns, reads=(), writes=(), dma=False, cc=False):
        if not isinstance(fns, (list, tuple)):
            fns = [fns]
        waits = {}
        ps_reads = [d for d in reads if d.ps]
        if ps_reads:
            reads = [d for d in reads if not d.ps]
            writes = list(writes) + ps_reads
        for d in reads:
            self._need(eng, waits, d.w)
        for d in writes:
            self._need(eng, waits, d.w)
            for r in d.rs:
                self._need(eng, waits, r)
        if dma:
            i = self.drr[eng]
            self.drr[eng] = (i + 1) % self.NDMA
            k = "d_%s%d" % (eng, i)
            if self.cnt[k] > 0:
                self._need(eng, waits, (k, self.cnt[k]))
            self.cnt[k] += 16
            inc = 16
        elif cc:
            k = "cc"
            self.cnt[k] += 1
            inc = 1
        else:
            k = eng
            self.cnt[k] += 1
            inc = 1
        tok = (k, self.cnt[k])
        for kk, v in waits.items():
            self.seen[eng][kk] = v
        self.ops[eng].append((list(waits.items()), list(fns), k, inc))
        for d in reads:
            d.rs.append(tok)
        for d in writes:
            d.w = tok
            d.rs = []
        return tok

    def wait_all(self, eng, toks):
        waits = {}
        for t in toks:
            self._need(eng, waits, t)
        for kk, v in waits.items():
            self.seen[eng][kk] = v
        self.ops[eng].append((list(waits.items()), [], None, 0))

    def barrier(self, exclude_cc=False):
        toks = [(k, v) for k, v in self.cnt.items() if v > 0 and not (exclude_cc and k == "cc")]
        for e in self.ENGS:
            self.wait_all(e, toks)

    def replay(self, eng, h):
        for waits, fns, k, inc in self.ops[eng]:
            for kk, v in waits:
                h.wait_ge(self.sems[kk], v)
            n = len(fns)
            for i, fn in enumerate(fns):
                ins = fn(h)
                if i == n - 1:
                    ins.then_inc(self.sems[k], inc)

    pre_sp = None

    def emit(self):
        nc = self.nc
        with nc.Block() as block:
            @block.tensor
            def _(e):
                self.replay("pe", e)

            @block.scalar
            def _(e):
                self.replay("act", e)

            @block.vector
            def _(e):
                self.replay("dve", e)

            @block.gpsimd
            def _(e):
                self.replay("pool", e)

            @block.sync
            def _(e):
                if self.pre_sp is not None:
                    self.pre_sp(e)
                self.replay("sp", e)
        self.ops = {e: [] for e in self.ENGS}


class Ring:
    def __init__(self, aps, ps=False):
        self.aps = aps
        self.deps = [Dep(ps) for _ in aps]
        self.i = 0

    def next(self):
        i = self.i
        self.i = (i + 1) % len(self.aps)
        return self.aps[i], self.deps[i]


class Ctx:
    def __init__(self, nc, es, sched=None):
        self.nc = nc
        self.es = es
        self.s = sched if sched is not None else Sched(nc, es)
        self.n = Ctx.N
        Ctx.N += 1000

    N = 0

    def sb(self, shape, dt, name=None):
        self.n += 1
        return self.es.enter_context(self.nc.sbuf_tensor("%s_%d" % (name or "t", self.n), list(shape), dt))

    def ps(self, shape, dt, name=None):
        self.n += 1
        return self.es.enter_context(self.nc.psum_tensor("%s_%d" % (name or "p", self.n), list(shape), dt))


def emit_norm(cx, xT, xdeps, hT, hdep, nw, nwdep, col0, ncols, ones, onesdep, psring, sqring, rsring):
    s = cx.s
    c0 = col0
    while c0 < col0 + ncols:
        n = min(512, col0 + ncols - c0)
        ps, psd = psring.next()
        sqs = []
        for kc in range(KC):
            sq, sqd = sqring.next()
            s.op("act", lambda e, sq=sq, kc=kc, c0=c0, n=n: e.activation(out=sq[:, 0:n], in_=xT[:, kc, c0:c0 + n], func=AF.Square),
                 reads=(xdeps(kc, c0, n) if callable(xdeps) else [xdeps[kc]]), writes=[sqd])
            s.op("pe", lambda e, ps=ps, sq=sq, kc=kc, n=n: e.matmul(ps[:, 0:n], lhsT=ones[:, 0:128], rhs=sq[:, 0:n], start=(kc == 0), stop=(kc == KC - 1)),
                 reads=[sqd, onesdep], writes=[psd])
        rs, rsd = rsring.next()
        s.op("act", lambda e, rs=rs, ps=ps, n=n: e.activation(out=rs[:, 0:n], in_=ps[:, 0:n], func=AF.Sqrt, bias=ones[:, 128:129], scale=1.0 / D),
             reads=[psd, onesdep], writes=[rsd])
        s.op("dve", lambda e, rs=rs, n=n: e.reciprocal(out=rs[:, 0:n], in_=rs[:, 0:n]),
             reads=[rsd], writes=[rsd])
        for kc in range(KC):
            s.op("dve", lambda e, rs=rs, kc=kc, c0=c0, n=n: e.scalar_tensor_tensor(
                out=hT[:, kc, c0:c0 + n], in0=xT[:, kc, c0:c0 + n], scalar=nw[:, kc:kc + 1], in1=rs[:, 0:n],
                op0=ALU.mult, op1=ALU.mult),
                reads=(xdeps(kc, c0, n) if callable(xdeps) else [xdeps[kc]]) + [rsd, nwdep], writes=[hdep[c0 // 512] if isinstance(hdep, list) else hdep])
        c0 += n


def emit_halo(cx, P, dst3, dstdeps):
    s = cx.s
    hs = cx.sb([128, 4, KC * HALO], F32, "hs")
    fl = cx.sb([128, 8], F32, "fl")
    tmp = cx.sb([128, KC * HALO], F32, "htmp")
    hsd, fld, tmpd = Dep(), Dep(), Dep()
    s.op("sp", lambda e: e.dma_start(out=hs[:, :, :], in_=P.hall.rearrange("(r p) f -> p r f", p=128)), writes=[hsd], dma=True)
    s.op("sp", lambda e: e.dma_start(out=fl[:, :], in_=P.flags[:, :]), writes=[fld], dma=True)
    s.op("dve", lambda e: e.tensor_scalar(out=tmp[:, :], in0=hs[:, 0, :], scalar1=fl[:, 0:1], scalar2=0.0, op0=ALU.mult, op1=ALU.add),
         reads=[hsd, fld], writes=[tmpd])
    for r in range(1, 4):
        s.op("dve", lambda e, r=r: e.scalar_tensor_tensor(out=tmp[:, :], in0=hs[:, r, :], scalar=fl[:, r:r + 1], in1=tmp[:, :], op0=ALU.mult, op1=ALU.add),
             reads=[hsd, fld, tmpd], writes=[tmpd])
    s.op("dve", lambda e: e.tensor_copy(out=dst3, in_=tmp[:, :].rearrange("p (kc t) -> p kc t", t=HALO)), reads=[tmpd], writes=dstdeps)


def emit_ffn(P, io, final_norm=False, GJ=4):
    nc = P.nc
    x_own, nw_d, wup_d, cw_d, wdn_d, x_out = io["x_in"], io["nw"], io["wup"], io["cw"], io["wdn"], io["x_out"]
    if final_norm:
        fw_d = io["fw"]

    W = HALO + T
    with contextlib.ExitStack() as es:
        cx = Ctx(nc, es, P.S)
        s = cx.s
        xT = cx.sb([128, KC, W], F32, "xT")
        hT = cx.sb([128, KC, W], BF16, "hT")
        nw = cx.sb([128, KC], F32, "nw")
        cw = cx.sb([128, NJ, 2, 4], F32, "cw")
        ones = cx.sb([128, 129], F32, "ones")
        xd = [[Dep() for _ in range(4)] for _ in range(KC)]
        xhd = Dep()

        def xdf(kc, c0, n):
            b = c0 // 512
            deps = []
            if b == 0:
                deps.append(xhd)
            if b >= 1:
                deps.append(xd[kc][b - 1])
            if b <= 3:
                deps.append(xd[kc][b])
            return deps
        hdep = [Dep() for _ in range(5)]
        nwdep, cwdep, onesdep = Dep(), Dep(), Dep()
        psring = Ring([cx.ps([128, 512], F32, "ps") for _ in range(8)], ps=True)
        sqring = Ring([cx.sb([128, 512], F32, "sq") for _ in range(3)])
        rsring = Ring([cx.sb([128, 512], F32, "rs") for _ in range(2)])
        wupring = Ring([cx.sb([128, KC, 256], BF16, "wup") for _ in range(3)])
        ubuf = [cx.sb([128, W], F32, "u%d" % i) for i in range(2)]
        udep = [Dep(), Dep()]
        cbuf = [cx.sb([128, T], F32, "c%d" % i) for i in range(2)]
        cdep = [Dep(), Dep()]
        gring = Ring([cx.sb([128, GJ, T], BF16, "g") for _ in range(2)])
        wdring = Ring([cx.sb([128, GJ, D], BF16, "wd") for _ in range(2)])

        s.op("sp", lambda e: e.dma_start(out=nw[:, :], in_=nw_d[:, :]), writes=[nwdep], dma=True)
        s.op("sp", lambda e: e.dma_start(out=cw[:, :, :, :], in_=cw_d[:, :, :, :]), writes=[cwdep], dma=True)
        s.op("dve", lambda e: e.memset(ones[:, 0:128], 1.0), writes=[onesdep])
        s.op("dve", lambda e: e.memset(ones[:, 128:129], EPS), writes=[onesdep])
        emit_halo(cx, P, xT[:, :, 0:HALO], [xhd])
        for tb in range(4):
            for kc in range(KC):
                s.op("sp", lambda e, kc=kc, tb=tb: e.dma_start(out=xT[:, kc, HALO + tb * 512:HALO + (tb + 1) * 512],
                                                               in_=x_own[kc * 128:(kc + 1) * 128, tb * 512:(tb + 1) * 512]),
                     writes=[xd[kc][tb]], dma=True)
        if final_norm:
            fw = cx.sb([128, KC], F32, "fw")
            fwdep = Dep()
            s.op("sp", lambda e: e.dma_start(out=fw[:, :], in_=fw_d[:, :]), writes=[fwdep], dma=True)

        emit_norm(cx, xT, xdf, hT, hdep, nw, nwdep, 0, W, ones, onesdep, psring, sqring, rsring)

        wdn_v = wdn_d.rearrange("(j p) d -> p j d", p=128)
        j = 0
        groups = []
        while j < NJ:
            groups.append(list(range(j, min(NJ, j + GJ))))
            j += GJ
        def emit_down(grp, g, gdep, wd, wddep):
            for dc in range(KC):
                for tb in range(4):
                    ps, psd = psring.next()
                    s.op("pe", [lambda e, ps=ps, wd=wd, g=g, jj=jj, dc=dc, tb=tb, n=len(grp): e.matmul(
                        ps[:, :], lhsT=wd[:, jj, dc * 128:(dc + 1) * 128], rhs=g[:, jj, tb * 512:(tb + 1) * 512],
                        start=(jj == 0), stop=(jj == n - 1)) for jj in range(len(grp))],
                        reads=[wddep, gdep], writes=[psd])
                    c0 = HALO + tb * 512
                    s.op("dve", lambda e, ps=ps, dc=dc, c0=c0: e.tensor_tensor(
                        out=xT[:, dc, c0:c0 + 512], in0=ps[:, :], in1=xT[:, dc, c0:c0 + 512], op=ALU.add),
                        reads=[psd, xd[dc][tb]], writes=[xd[dc][tb]])

        pending = None
        for grp in groups:
            g, gdep = gring.next()
            wd, wddep = wdring.next()
            s.op("pool", lambda e, wd=wd, grp=grp: e.dma_start(out=wd[:, 0:len(grp), :], in_=wdn_v[:, grp[0]:grp[0] + len(grp), :]),
                 writes=[wddep], dma=True)
            for jj, j in enumerate(grp):
                wup, wupdep = wupring.next()
                s.op("pool", lambda e, wup=wup, j=j: e.dma_start(out=wup[:, :, :], in_=wup_d[j, :, :, :]), writes=[wupdep], dma=True)
                for half in range(2):
                    u = ubuf[half]
                    ps, psd = psring.next()
                    s.op("pe", [lambda e, ps=ps, wup=wup, kc=kc, half=half: e.matmul(
                        ps[:, 0:HALO], lhsT=wup[:, kc, half * 128:(half + 1) * 128], rhs=hT[:, kc, 0:HALO],
                        start=(kc == 0), stop=(kc == KC - 1)) for kc in range(KC)],
                        reads=[wupdep, hdep[0]], writes=[psd])
                    s.op("act", lambda e, ps=ps, u=u: e.activation(out=u[:, 0:HALO], in_=ps[:, 0:HALO], func=AF.Copy),
                         reads=[psd], writes=[udep[half]])
                    for tb in range(4):
                        ps, psd = psring.next()
                        c0 = HALO + tb * 512
                        s.op("pe", [lambda e, ps=ps, wup=wup, kc=kc, half=half, c0=c0: e.matmul(
                            ps[:, :], lhsT=wup[:, kc, half * 128:(half + 1) * 128], rhs=hT[:, kc, c0:c0 + 512],
                            start=(kc == 0), stop=(kc == KC - 1)) for kc in range(KC)],
                            reads=[wupdep, hdep[c0 // 512], hdep[(c0 + 511) // 512]], writes=[psd])
                        s.op("act", lambda e, ps=ps, u=u, c0=c0: e.activation(out=u[:, c0:c0 + 512], in_=ps[:, :], func=AF.Copy),
                             reads=[psd], writes=[udep[half]])
                    c = cbuf[half]
                    ceng = "dve" if half == 0 else CONV_ENG2
                    s.op(ceng, lambda e, u=u, c=c, j=j, half=half: e.tensor_scalar(
                        out=c[:, :], in0=u[:, 3:3 + T], scalar1=cw[:, j, half, 2:3], scalar2=cw[:, j, half, 3:4],
                        op0=ALU.mult, op1=ALU.add), reads=[udep[half], cwdep], writes=[cdep[half]])
                    s.op(ceng, lambda e, u=u, c=c, j=j, half=half: e.scalar_tensor_tensor(
                        out=c[:, :], in0=u[:, 2:2 + T], scalar=cw[:, j, half, 1:2], in1=c[:, :],
                        op0=ALU.mult, op1=ALU.add), reads=[udep[half], cwdep, cdep[half]], writes=[cdep[half]])
                    s.op(ceng, lambda e, u=u, c=c, j=j, half=half: e.scalar_tensor_tensor(
                        out=c[:, :], in0=u[:, 1:1 + T], scalar=cw[:, j, half, 0:1], in1=c[:, :],
                        op0=ALU.mult, op1=ALU.add), reads=[udep[half], cwdep, cdep[half]], writes=[cdep[half]])
                    if half == 0:
                        s.op("act", lambda e, c=c: e.activation(out=c[:, :], in_=c[:, :], func=AF.Silu),
                             reads=[cdep[0]], writes=[cdep[0]])
                if jj == 0 and pending is not None:
                    emit_down(*pending)
                    pending = None
                s.op("dve", lambda e, g=g, jj=jj: e.tensor_tensor(out=g[:, jj, :], in0=cbuf[0][:, :], in1=cbuf[1][:, :], op=ALU.mult),
                     reads=[cdep[0], cdep[1]], writes=[gdep])
            pending = (grp, g, gdep, wd, wddep)
        emit_down(*pending)
        outtoks = []
        if final_norm:
            for tb in range(4):
                c0 = HALO + tb * 512
                ps, psd = psring.next()
                for kc in range(KC):
                    sq, sqd = sqring.next()
                    s.op("act", lambda e, sq=sq, kc=kc, c0=c0: e.activation(out=sq[:, :], in_=xT[:, kc, c0:c0 + 512], func=AF.Square),
                         reads=[xd[kc][tb]], writes=[sqd])
                    s.op("pe", lambda e, ps=ps, sq=sq, kc=kc: e.matmul(ps[:, :], lhsT=ones[:, 0:128], rhs=sq[:, :], start=(kc == 0), stop=(kc == KC - 1)),
                         reads=[sqd, onesdep], writes=[psd])
                rs, rsd = rsring.next()
                s.op("act", lambda e, rs=rs, ps=ps: e.activation(out=rs[:, :], in_=ps[:, :], func=AF.Sqrt, bias=ones[:, 128:129], scale=1.0 / D),
                     reads=[psd, onesdep], writes=[rsd])
                s.op("dve", lambda e, rs=rs: e.reciprocal(out=rs[:, :], in_=rs[:, :]),
                     reads=[rsd], writes=[rsd])
                for kc in range(KC):
                    s.op("dve", lambda e, rs=rs, kc=kc, c0=c0: e.scalar_tensor_tensor(
                        out=xT[:, kc, c0:c0 + 512], in0=xT[:, kc, c0:c0 + 512], scalar=fw[:, kc:kc + 1], in1=rs[:, :],
                        op0=ALU.mult, op1=ALU.mult), reads=[xd[kc][tb], rsd, fwdep], writes=[xd[kc][tb]])
        for kc in range(KC):
            outtoks.append(s.op("sp", lambda e, kc=kc: e.dma_start(out=x_out[kc * 128:(kc + 1) * 128, :], in_=xT[:, kc, HALO:W]),
                                reads=xd[kc], dma=True))
        if io.get("hmsg") is not None:
            s.op("sp", lambda e: e.dma_start(out=io["hmsg"], in_=xT[:, :, W - HALO:W]), reads=[xd[kc][3] for kc in range(KC)], dma=True)
        s.barrier()
        s.emit()


def _cols128(v):
    return np.ascontiguousarray(v.reshape(-1, 128).T)


def _shards_T(x):
    out = []
    for c in range(NCORES):
        b, q = divmod(c, 4)
        out.append(np.ascontiguousarray(x[b, q * T:(q + 1) * T, :].T))
    return out


def _halo_cols(xTs, n=HALO):
    out = []
    for c in range(NCORES):
        if c % 4 == 0:
            h = np.zeros((D, n), np.float32)
        else:
            h = xTs[c - 1][:, T - n:]
        out.append(np.ascontiguousarray(h.reshape(KC, 128, n).transpose(1, 0, 2)))
    return out


_PROGS = {}


def _prog(key, fn):
    if key not in _PROGS:
        _PROGS[key] = fn()
    return _PROGS[key]


def run_ffn(xTs, nw, w_up, conv_w, conv_b, w_down, final_w=None):
    nc = _prog(("ffn", final_w is not None), lambda: build_ffn(final_norm=final_w is not None))
    wup = np.empty((NJ, 128, KC, 256), np.float32)
    wr = w_up.reshape(KC, 128, 2, NJ, 128)
    wup[:] = wr.transpose(3, 1, 0, 2, 4).reshape(NJ, 128, KC, 256)
    cw = np.empty((128, NJ, 2, 4), np.float32)
    cwr = conv_w.reshape(3, 2, NJ, 128)
    cw[:, :, :, 0:3] = cwr.transpose(3, 2, 1, 0)
    cw[:, :, :, 3] = conv_b.reshape(2, NJ, 128).transpose(2, 1, 0)
    halos = _halo_cols(xTs)
    maps = []
    for c in range(NCORES):
        m = {"x_own": xTs[c], "x_halo": halos[c], "nw": _cols128(nw), "wup": wup, "cw": cw,
             "wdn": np.ascontiguousarray(w_down)}
        if final_w is not None:
            m["fw"] = _cols128(final_w)
        maps.append(m)
    res = run_bass_kernel_spmd(nc, maps, core_ids=list(range(NCORES)))
    return [np.asarray(r["x_out"]) for r in res.results]


NG = 3
NH = 8
DIL = (1, 4, 16)
SCALE = 128.0 ** -0.5


def colview(t2d, d, c0, n):
    if d == 1:
        return t2d[:, c0:c0 + n], 1
    L = T // d
    v = t2d.rearrange("p (l r) -> p r l", r=d)
    r0, l0 = divmod(c0, L)
    if n <= L:
        return v[:, r0, l0:l0 + n], 1
    return v[:, r0:r0 + n // L, :], n // L


def v3(ap, A):
    if A == 1:
        return ap
    return ap.rearrange("p (a b) -> p a b", a=A)


def emit_rope(cx, ps, psd, full, dstdep, d, t0, cosT, sinT, tabdep, pm, pmdep, pkring, t1ring, tbring):
    s = cx.s
    if d == 1:
        tmp, tmpd = full[:, t0:t0 + 512], dstdep
    else:
        tmp, tmpd = tbring.next()
        tmp = tmp[:, :]
    s.op("act", lambda e: e.activation(out=tmp, in_=ps[:, :], func=AF.Copy), reads=[psd], writes=[tmpd])
    pk, pkd = pkring.next()
    s.op("pe", lambda e: e.matmul(pk[0:32, :], lhsT=pm[0:32, 0:32], rhs=tmp[0:32, :], start=True, stop=True),
         reads=[tmpd, pmdep], writes=[pkd])
    t1, t1d = t1ring.next()
    t2, t2d = t1ring.next()
    s.op("dve", lambda e: e.tensor_tensor(out=t1[0:32, :], in0=ps[0:32, :], in1=cosT[0:32, t0:t0 + 512], op=ALU.mult),
         reads=[psd, tabdep], writes=[t1d])
    s.op("dve", lambda e: e.tensor_tensor(out=t2[0:32, :], in0=pk[0:32, :], in1=sinT[0:32, t0:t0 + 512], op=ALU.mult),
         reads=[pkd, tabdep], writes=[t2d])
    s.op("dve", lambda e: e.tensor_tensor(out=tmp[0:32, :], in0=t1[0:32, :], in1=t2[0:32, :], op=ALU.add),
         reads=[t1d, t2d, tmpd], writes=[tmpd])
    if d != 1:
        n = 512 // d
        l0 = t0 // d
        dv = full.rearrange("p (r l) -> p l r", r=d)[:, l0:l0 + n, :]
        s.op("act", lambda e: e.activation(out=dv, in_=tmp.rearrange("p (l r) -> p l r", r=d), func=AF.Copy), reads=[tmpd], writes=[dstdep])


def load_norm_h(cx, x_own, nw, nwdep, hT, hdep, ones, onesdep, psring, sqring, rsring, xbring):
    s = cx.s
    for tb in range(4):
        xb, xbd = xbring.next()
        for kc in range(KC):
            s.op("sp", lambda e, xb=xb, kc=kc, tb=tb: e.dma_start(out=xb[:, kc, :], in_=x_own[kc * 128:(kc + 1) * 128, tb * 512:(tb + 1) * 512]),
                 writes=[xbd], dma=True)
        ps, psd = psring.next()
        for kc in range(KC):
            sq, sqd = sqring.next()
            s.op("act", lambda e, sq=sq, xb=xb, kc=kc: e.activation(out=sq[:, :], in_=xb[:, kc, :], func=AF.Square),
                 reads=[xbd], writes=[sqd])
            s.op("pe", lambda e, ps=ps, sq=sq, kc=kc: e.matmul(ps[:, :], lhsT=ones[:, 0:128], rhs=sq[:, :], start=(kc == 0), stop=(kc == KC - 1)),
                 reads=[sqd, onesdep], writes=[psd])
        rs, rsd = rsring.next()
        s.op("act", lambda e, rs=rs, ps=ps: e.activation(out=rs[:, :], in_=ps[:, :], func=AF.Sqrt, bias=ones[:, 128:129], scale=1.0 / D),
             reads=[psd, onesdep], writes=[rsd])
        s.op("dve", lambda e, rs=rs: e.reciprocal(out=rs[:, :], in_=rs[:, :]), reads=[rsd], writes=[rsd])
        for kc in range(KC):
            s.op("dve", lambda e, rs=rs, xb=xb, kc=kc, tb=tb: e.scalar_tensor_tensor(
                out=hT[:, kc, tb * 512:(tb + 1) * 512], in0=xb[:, kc, :], scalar=nw[:, kc:kc + 1], in1=rs[:, :],
                op0=ALU.mult, op1=ALU.mult), reads=[xbd, rsd, nwdep], writes=[hdep])


KOFF = (0, 128, 640)
BOFF = (0, 1, 5)
KMSG = 2688


def emit_attn_kv(P, io):
    nc = P.nc
    x_own, nw_d, wk_d, wv_d, cos_d, sin_d, pm_d = io["x_in"], io["nw"], io["wk"], io["wv"], io["cosT"], io["sinT"], io["pm"]
    k_out, v_out, kmsg, vmsg = io["k_own"], io["v_own"], io["kmsg"], io["vmsg"]
    kmd = [Dep() for _ in range(NH)]
    vmd = [Dep() for _ in range(NH)]
    with contextlib.ExitStack() as es:
        cx = Ctx(nc, es, P.S)
        s = cx.s
        hT = cx.sb([128, KC, T], BF16, "hT")
        hdep = Dep()
        nw = cx.sb([128, KC], F32, "nw")
        ones = cx.sb([128, 129], F32, "ones")
        cosT = cx.sb([128, T], F32, "cosT")
        sinT = cx.sb([128, T], F32, "sinT")
        pm = cx.sb([128, 128], BF16, "pm")
        nwdep, onesdep, tabdep, pmdep = Dep(), Dep(), Dep(), Dep()
        psring = Ring([cx.ps([128, 512], F32, "ps") for _ in range(6)], ps=True)
        pkring = Ring([cx.ps([128, 512], F32, "pk") for _ in range(2)], ps=True)
        sqring = Ring([cx.sb([128, 512], F32, "sq") for _ in range(3)])
        rsring = Ring([cx.sb([128, 512], F32, "rs") for _ in range(2)])
        xbring = Ring([cx.sb([128, KC, 512], F32, "xb") for _ in range(2)])
        t1ring = Ring([cx.sb([128, 512], F32, "t1") for _ in range(4)])
        tbring = Ring([cx.sb([128, 512], BF16, "tb16") for _ in range(4)])
        wkring = Ring([cx.sb([128, KC, 128], BF16, "wk") for _ in range(3)])
        wvring = Ring([cx.sb([128, KC, 1024], BF16, "wv") for _ in range(2)])
        kring = Ring([cx.sb([128, T], BF16, "kt") for _ in range(2)])
        vstage = cx.sb([128, NH, 16, 128], BF16, "vst")
        vsdep = Dep()

        s.op("sp", lambda e: e.dma_start(out=nw[:, :], in_=nw_d[:, :]), writes=[nwdep], dma=True)
        s.op("sp", lambda e: e.dma_start(out=cosT[:, :], in_=cos_d[:, :]), writes=[tabdep], dma=True)
        s.op("sp", lambda e: e.dma_start(out=sinT[:, :], in_=sin_d[:, :]), writes=[tabdep], dma=True)
        s.op("pool", lambda e: e.dma_start(out=pm[:, :], in_=pm_d[:, :]), writes=[pmdep], dma=True)
        s.op("dve", lambda e: e.memset(ones[:, 0:128], 1.0), writes=[onesdep])
        s.op("dve", lambda e: e.memset(ones[:, 128:129], EPS), writes=[onesdep])
        load_norm_h(cx, x_own, nw, nwdep, hT, hdep, ones, onesdep, psring, sqring, rsring, xbring)
        for kc in range(KC):
            s.op("sp", lambda e, kc=kc: e.dma_start(out=P.hsave[:, kc, 0:T], in_=hT[:, kc, :]), reads=[hdep], dma=True)

        outtoks = []
        for g in range(NG):
            d = DIL[g]
            wv, wvdep = wvring.next()
            s.op("pool", lambda e, wv=wv, g=g: e.dma_start(out=wv[:, :, :], in_=wv_d[g, :, :, :]), writes=[wvdep], dma=True)
            for blk in range(16):
                for half in range(2):
                    ps, psd = psring.next()
                    fns = []
                    for kc in range(KC):
                        tv, _ = colview(hT[:, kc, :], d, blk * 128, 128)
                        fns.append(lambda e, ps=ps, tv=tv, wv=wv, kc=kc, half=half: e.matmul(
                            ps[:, :], lhsT=tv, rhs=wv[:, kc, half * 512:(half + 1) * 512], start=(kc == 0), stop=(kc == KC - 1)))
                    s.op("pe", fns, reads=[hdep, wvdep], writes=[psd])
                    eng = "act" if half == 0 else "dve"
                    if eng == "act":
                        s.op("act", lambda e, ps=ps, blk=blk, half=half: e.activation(
                            out=vstage[:, half * 4:(half + 1) * 4, blk, :], in_=ps[:, :].rearrange("p (h e) -> p h e", h=4), func=AF.Copy),
                            reads=[psd], writes=[vsdep])
                    else:
                        s.op("dve", lambda e, ps=ps, blk=blk, half=half: e.tensor_copy(
                            out=vstage[:, half * 4:(half + 1) * 4, blk, :], in_=ps[:, :].rearrange("p (h e) -> p h e", h=4)),
                            reads=[psd], writes=[vsdep])
            nbg = 16 // d
            for h in range(NH):
                outtoks.append(s.op("sp", lambda e, g=g, h=h: e.dma_start(out=v_out[g, h, :, :, :], in_=vstage[:, h, :, :]),
                                    reads=[vsdep], dma=True))
                s.op("sp", lambda e, g=g, h=h, d=d, nbg=nbg: e.dma_start(
                    out=vmsg[h, :, BOFF[g]:BOFF[g] + d, :],
                    in_=vstage[:, h, :, :].rearrange("p (r n) e -> p r n e", r=d)[:, :, nbg - 1, :]), reads=[vsdep], writes=[vmd[h]], dma=True)
        quads = [(h, g, qd) for h in range(NH) for g in range(NG) for qd in range(4)]
        qst = {}
        tiles = {}

        def kP(i):
            h, g, qd = quads[i]
            d = DIL[g]
            if qd == 0:
                wk, wkdep = wkring.next()
                s.op("pool", lambda e: e.dma_start(out=wk[:, :, :], in_=wk_d[g, h, :, :, :]), writes=[wkdep], dma=True)
                kt, ktdep = kring.next()
                tiles[(h, g)] = (wk, wkdep, kt, ktdep)
                if g == 0 and h == 0:
                    for hh in range(NH):
                        io["coll_v"](hh, vmd[hh])
                if g == 1 and h >= 1:
                    io["coll_k"](h - 1, kmd[h - 1])
            wk, wkdep, kt, ktdep = tiles[(h, g)]
            ps, psd = psring.next()
            s.op("pe", [lambda e, kc=kc: e.matmul(ps[:, :], lhsT=wk[:, kc, :], rhs=hT[:, kc, qd * 512:(qd + 1) * 512], start=(kc == 0), stop=(kc == KC - 1))
                        for kc in range(KC)], reads=[hdep, wkdep], writes=[psd])
            t0 = qd * 512
            if d == 1:
                tmp, tmpd = kt[:, t0:t0 + 512], ktdep
            else:
                tmp, tmpd = tbring.next()
                tmp = tmp[:, :]
            s.op("act", lambda e: e.activation(out=tmp, in_=ps[:, :], func=AF.Copy), reads=[psd], writes=[tmpd])
            qst[i] = dict(ps=ps, psd=psd, tmp=tmp, tmpd=tmpd, kt=kt, ktdep=ktdep, d=d, t0=t0, h=h, g=g, qd=qd)

        def kR23(i):
            q = qst[i]
            ps, psd, tmp, tmpd, t0 = q["ps"], q["psd"], q["tmp"], q["tmpd"], q["t0"]
            pk, pkd = pkring.next()
            s.op("pe", lambda e: e.matmul(pk[0:32, :], lhsT=pm[0:32, 0:32], rhs=tmp[0:32, :], start=True, stop=True), reads=[tmpd, pmdep], writes=[pkd])
            t1, t1d = t1ring.next()
            t2, t2d = t1ring.next()
            s.op("dve", lambda e: e.tensor_tensor(out=t1[0:32, :], in0=ps[0:32, :], in1=cosT[0:32, t0:t0 + 512], op=ALU.mult), reads=[psd, tabdep], writes=[t1d])
            s.op("dve", lambda e: e.tensor_tensor(out=t2[0:32, :], in0=pk[0:32, :], in1=sinT[0:32, t0:t0 + 512], op=ALU.mult), reads=[pkd, tabdep], writes=[t2d])
            s.op("dve", lambda e: e.tensor_tensor(out=tmp[0:32, :], in0=t1[0:32, :], in1=t2[0:32, :], op=ALU.add), reads=[t1d, t2d, tmpd], writes=[tmpd])

        def kR4(i):
            q = qst.pop(i)
            tmp, tmpd, kt, ktdep, d, t0, h, g, qd = q["tmp"], q["tmpd"], q["kt"], q["ktdep"], q["d"], q["t0"], q["h"], q["g"], q["qd"]
            if d != 1:
                n = 512 // d
                l0 = t0 // d
                dv = kt[:, :].rearrange("p (r l) -> p l r", r=d)[:, l0:l0 + n, :]
                s.op("act", lambda e: e.activation(out=dv, in_=tmp.rearrange("p (l r) -> p l r", r=d), func=AF.Copy), reads=[tmpd], writes=[ktdep])
            if qd == 3:
                outtoks.append(s.op("sp", lambda e: e.dma_start(out=k_out[g, h, :, :], in_=kt[:, :]), reads=[ktdep], dma=True))
                Lg = T // d
                s.op("sp", lambda e: e.dma_start(out=kmsg[h, :, KOFF[g]:KOFF[g] + d * 128].rearrange("p (r l) -> p r l", r=d),
                                                 in_=kt[:, :].rearrange("p (r l) -> p r l", r=d)[:, :, Lg - 128:Lg]), reads=[ktdep], writes=[kmd[h]], dma=True)

        NQ = len(quads)
        for it in range(NQ + 2):
            if 0 <= it - 2 < NQ:
                kR4(it - 2)
            if 0 <= it - 1 < NQ:
                kR23(it - 1)
            if it < NQ:
                kP(it)
        io["coll_k"](NH - 1, kmd[NH - 1])
        s.barrier(exclude_cc=True)
        s.emit()


def rope_tables_np(pos0):
    pos = np.arange(pos0, pos0 + T, dtype=np.float32)
    inv = (np.float32(500000.0) ** (-np.arange(0, 32, 2, dtype=np.float32) / np.float32(32))).astype(np.float32)
    ang = (pos[None, :] * inv[:, None]).astype(np.float32)
    c = np.ones((128, T), np.float32)
    sn = np.zeros((128, T), np.float32)
    c[0:16] = np.cos(ang)
    c[16:32] = np.cos(ang)
    sn[0:16] = -np.sin(ang)
    sn[16:32] = np.sin(ang)
    return c, sn


def perm_matrix():
    pm = np.zeros((128, 128), np.float32)
    for e in range(16):
        pm[e + 16, e] = 1.0
        pm[e, e + 16] = 1.0
    return pm


def run_attn_kv(xTs, nw, w_qkv):
    nc = _prog("attn_kv", build_attn_kv)
    wr = w_qkv.reshape(KC, 128, NG, 3, NH, 128)
    wk = np.ascontiguousarray(wr[:, :, :, 1].transpose(2, 3, 1, 0, 4))
    wv = np.ascontiguousarray(wr[:, :, :, 2].transpose(2, 1, 0, 3, 4).reshape(NG, 128, KC, 1024))
    pm = perm_matrix()
    maps = []
    for c in range(NCORES):
        cs, sn = rope_tables_np((c % 4) * T)
        maps.append({"x_own": xTs[c], "nw": _cols128(nw), "wk": wk, "wv": wv, "cosT": cs, "sinT": sn, "pm": pm})
    res = run_bass_kernel_spmd(nc, maps, core_ids=list(range(NCORES)))
    return [(np.asarray(r["k_out"]), np.asarray(r["v_out"])) for r in res.results]


def emit_attn_main(P, io):
    nc = P.nc
    x_own, nw_d, wq_d, wo_d, cos_d, sin_d, pm_d, mk_d = io["x_in"], io["nw"], io["wq"], io["wo"], io["cosT"], io["sinT"], io["pm"], io["masks"]
    k_own, v_own, x_out = io["k_own"], io["v_own"], io["x_out"]
    with contextlib.ExitStack() as es:
        cx = Ctx(nc, es, P.S)
        s = cx.s
        hT = cx.sb([128, KC, T], BF16, "hT")
        hdep = Dep()
        aT = cx.sb([128, NH, T], BF16, "aT")
        adeps = [Dep() for _ in range(NH)]
        nw = cx.sb([128, KC], F32, "nw")
        ones = cx.sb([128, 129], F32, "ones")
        onesb = cx.sb([128, 128], BF16, "onesb")
        cosT = cx.sb([128, T], F32, "cosT")
        sinT = cx.sb([128, T], F32, "sinT")
        pm = cx.sb([128, 128], BF16, "pm")
        mk = cx.sb([128, 3, 512], BF16, "mk")
        idm = cx.sb([128, 128], BF16, "idm")
        idmdep = Dep()
        nwdep, onesdep, tabdep, pmdep, mkdep, onesbdep = Dep(), Dep(), Dep(), Dep(), Dep(), Dep()
        psring = Ring([cx.ps([128, 512], F32, "ps") for _ in range(2)], ps=True)
        pkring = Ring([cx.ps([128, 512], F32, "pk") for _ in range(1)], ps=True)
        sring = Ring([cx.ps([128, 512], F32, "pss") for _ in range(3)], ps=True)
        odring = Ring([cx.ps([128, 512], F32, "pod") for _ in range(2)], ps=True)
        sqring = Ring([cx.sb([128, 512], F32, "sq") for _ in range(2)])
        rsring = Ring([cx.sb([128, 512], F32, "rs") for _ in range(2)])
        t1ring = Ring([cx.sb([128, 512], F32, "t1") for _ in range(4)])
        tbring = Ring([cx.sb([128, 512], BF16, "tb16") for _ in range(3)])
        wqring = Ring([cx.sb([128, KC, 128], BF16, "wq") for _ in range(3)])
        woring = Ring([cx.sb([128, NH, 128], BF16, "wo") for _ in range(2)])
        kring = Ring([cx.sb([128, 4096], BF16, "ks") for _ in range(2)])
        vring = Ring([cx.sb([128, 4096], BF16, "vs") for _ in range(2)])
        qring = Ring([cx.sb([128, T], BF16, "qs") for _ in range(2)])
        pring = Ring([cx.sb([128, 512], BF16, "pT") for _ in range(3)])
        acc = cx.sb([128, 2, T], F32, "acc")
        accdep = Dep()
        rden = cx.sb([128, T], F32, "rden")
        rdendep = Dep()
        oring = Ring([cx.sb([128, 512], F32, "ob") for _ in range(6)])

        s.op("sp", lambda e: e.dma_start(out=nw[:, :], in_=nw_d[:, :]), writes=[nwdep], dma=True)
        s.op("sp", lambda e: e.dma_start(out=cosT[:, :], in_=cos_d[:, :]), writes=[tabdep], dma=True)
        s.op("sp", lambda e: e.dma_start(out=sinT[:, :], in_=sin_d[:, :]), writes=[tabdep], dma=True)
        s.op("pool", lambda e: e.dma_start(out=pm[:, :], in_=pm_d[:, :]), writes=[pmdep], dma=True)
        s.op("pool", lambda e: e.dma_start(out=mk[:, :, :], in_=mk_d[:, :, :]), writes=[mkdep], dma=True)
        s.op("dve", lambda e: e.memset(ones[:, 0:128], 1.0), writes=[onesdep])
        s.op("dve", lambda e: e.memset(ones[:, 128:129], EPS), writes=[onesdep])
        s.op("dve", lambda e: e.memset(onesb[:, :], 1.0), writes=[onesbdep])
        s.op("pool", lambda e: e.memset(idm[:, :], 1.0), writes=[idmdep])
        s.op("pool", lambda e: e.affine_select(out=idm[:, :], in_=idm[:, :], pattern=[[1, 128]], compare_op=ALU.is_equal, fill=0.0, base=0, channel_multiplier=-1),
             reads=[idmdep], writes=[idmdep])
        hds = [Dep() for _ in range(4)]
        for tb in range(4):
            for kc in range(KC):
                s.op("sp", lambda e, kc=kc, tb=tb: e.dma_start(out=hT[:, kc, tb * 512:(tb + 1) * 512], in_=P.hsave[:, kc, tb * 512:(tb + 1) * 512]),
                     writes=[hds[tb]], dma=True)

        def QP(h, g):
            d = DIL[g]
            L = T // d
            nb = L // 128
            LK = 128 + L
            ks, ksdep = kring.next()
            vs, vsdep = vring.next()
            ksv = ks[:, 0:d * LK].rearrange("p (r l) -> p r l", r=d)
            vsv = vs[:, 0:d * (nb + 1) * 128].rearrange("p (r n e) -> p r n e", r=d, n=nb + 1)
            s.op("sp", lambda e, ksv=ksv, g=g, h=h, d=d: e.dma_start(
                out=ksv[:, :, 0:128], in_=io["khalo"](g, h)),
                reads=[io["klocd"][h]], writes=[ksdep], dma=True)
            s.op("sp", lambda e, ksv=ksv, g=g, h=h, d=d, LK=LK: e.dma_start(
                out=ksv[:, :, 128:LK], in_=k_own[g, h, :, :].rearrange("p (r l) -> p r l", r=d)),
                writes=[ksdep], dma=True)
            s.op("sp", lambda e, vsv=vsv, g=g, h=h, d=d: e.dma_start(
                out=vsv[:, :, 0, :], in_=io["vhalo"](g, h)),
                reads=[io["vlocd"][h]], writes=[vsdep], dma=True)
            s.op("sp", lambda e, vsv=vsv, g=g, h=h, d=d, nb=nb: e.dma_start(
                out=vsv[:, :, 1:nb + 1, :], in_=v_own[g, h, :, :, :].rearrange("p (r n) e -> p r n e", r=d)),
                writes=[vsdep], dma=True)
            wq, wqdep = wqring.next()
            s.op("pool", lambda e, wq=wq, g=g, h=h: e.dma_start(out=wq[:, :, :], in_=wq_d[g, h, :, :, :]), writes=[wqdep], dma=True)
            qs, qsdep = qring.next()
            for qd in range(4):
                ps, psd = psring.next()
                s.op("pe", [lambda e, ps=ps, wq=wq, kc=kc, qd=qd: e.matmul(
                    ps[:, :], lhsT=wq[:, kc, :], rhs=hT[:, kc, qd * 512:(qd + 1) * 512], start=(kc == 0), stop=(kc == KC - 1)) for kc in range(KC)],
                    reads=[hds[qd], wqdep], writes=[psd])
                emit_rope(cx, ps, psd, qs[:, :], qsdep, d, qd * 512, cosT, sinT, tabdep, pm, pmdep, pkring, t1ring, tbring)

            return dict(ksv=ksv, vsv=vsv, qs=qs, ksdep=ksdep, vsdep=vsdep, qsdep=qsdep, d=d, nb=nb)

        def UN(h, g, st):
            ksv, vsv, qs, ksdep, vsdep, qsdep, d, nb = st['ksv'], st['vsv'], st['qs'], st['ksdep'], st['vsdep'], st['qsdep'], st['d'], st['nb']
            def qk(pr, ksv=ksv, qs=qs, nb=nb, ksdep=ksdep, qsdep=qsdep, d=d):
                pss, pssd = sring.next()
                fns = []
                for uu in range(2):
                    u = pr * 2 + uu
                    r, n = divmod(u, nb)
                    for half in range(2):
                        fns.append(lambda e, pss=pss, r=r, n=n, u=u, uu=uu, half=half: e.matmul(
                            pss[:, uu * 256 + half * 128: uu * 256 + half * 128 + 128],
                            lhsT=ksv[:, r, (n + half) * 128:(n + half + 1) * 128],
                            rhs=qs[:, u * 128:(u + 1) * 128], start=(uu == 0 and half == 0), stop=False, skip_group_check=True))
                if d == 1:
                    var = 0 if pr == 0 else 1
                elif d == 4:
                    var = 0 if pr % 2 == 0 else 1
                else:
                    var = 2
                fns.append(lambda e, pss=pss, var=var: e.matmul(pss[:, :], lhsT=idm[:, :], rhs=mk[:, var, :], start=False, stop=True, skip_group_check=True))
                s.op("pe", fns, reads=[ksdep, qsdep, mkdep, idmdep], writes=[pssd])
                return pss, pssd

            def rest(pr, pss, pssd, vsv=vsv, d=d, g=g, nb=nb, vsdep=vsdep):
                pT, pTd = pring.next()
                s.op("act", lambda e: e.activation(out=pT[:, :], in_=pss[:, :], func=AF.Exp, scale=SCALE),
                     reads=[pssd], writes=[pTd])
                pod, podd = odring.next()
                fns = []
                for uu in range(2):
                    u = pr * 2 + uu
                    r, n = divmod(u, nb)
                    for half in range(2):
                        fns.append(lambda e, r=r, n=n, uu=uu, half=half: e.matmul(
                            pod[:, uu * 128:(uu + 1) * 128], lhsT=vsv[:, r, n + half, :],
                            rhs=pT[:, uu * 256 + half * 128: uu * 256 + half * 128 + 128],
                            start=(half == 0), stop=(half == 1)))
                    for half in range(2):
                        fns.append(lambda e, uu=uu, half=half: e.matmul(
                            pod[:, 256 + uu * 128: 256 + (uu + 1) * 128], lhsT=onesb[:, :],
                            rhs=pT[:, uu * 256 + half * 128: uu * 256 + half * 128 + 128],
                            start=(half == 0), stop=(half == 1)))
                s.op("pe", fns, reads=[vsdep, pTd, onesbdep], writes=[podd])
                return pod, podd

            def rest2(pr, pod, podd, d=d, g=g):
                if d == 16:
                    for w in range(2):
                        av, A = colview(acc[:, w, :], d, pr * 256, 256)
                        src = v3(pod[:, w * 256:(w + 1) * 256], A)
                        s.op("dve", lambda e, av=av, src=src: e.tensor_tensor(out=av, in0=src, in1=av, op=ALU.add),
                             reads=[podd, accdep], writes=[accdep])
                else:
                    if d == 1:
                        av = acc[:, :, pr * 256:(pr + 1) * 256]
                    else:
                        L4 = T // 4
                        r0, l0 = divmod(pr * 256, L4)
                        av = acc[:, :, :].rearrange("p w (l r) -> p w r l", r=4)[:, :, r0, l0:l0 + 256]
                    src = pod[:, :].rearrange("p (w c) -> p w c", w=2)
                    if g == 0:
                        s.op("dve", lambda e, av=av, src=src: e.tensor_copy(out=av, in_=src), reads=[podd], writes=[accdep])
                    else:
                        s.op("dve", lambda e, av=av, src=src: e.tensor_tensor(out=av, in0=src, in1=av, op=ALU.add),
                             reads=[podd, accdep], writes=[accdep])

            prev = qk(0)
            pend = None
            for pr in range(8):
                nxt = qk(pr + 1) if pr + 1 < 8 else None
                pod, podd = rest(pr, *prev)
                if pend is not None:
                    rest2(*pend)
                pend = (pr, pod, podd)
                prev = nxt
            rest2(*pend)
            if g == NG - 1:
                s.op("dve", lambda e: e.reciprocal(out=rden[:, :], in_=acc[:, 1, :]), reads=[accdep], writes=[rdendep])
                s.op("dve", lambda e, h=h: e.tensor_tensor(out=aT[:, h, :], in0=acc[:, 0, :], in1=rden[:, :], op=ALU.mult),
                     reads=[accdep, rdendep], writes=[adeps[h]])

        units = [(h, g) for h in range(NH) for g in range(NG)]
        stq = QP(*units[0])
        for ui, (h, g) in enumerate(units):
            nstq = QP(*units[ui + 1]) if ui + 1 < len(units) else None
            UN(h, g, stq)
            stq = nstq

        wo_v = wo_d.rearrange("(h e) d -> e h d", e=128)
        wops = Ring(psring.aps + sring.aps + odring.aps)
        wops.deps = psring.deps + sring.deps + odring.deps
        outtoks = []
        for dc in range(KC):
            wo, wodep = woring.next()
            s.op("pool", lambda e, wo=wo, dc=dc: e.dma_start(out=wo[:, :, :], in_=wo_v[:, :, dc * 128:(dc + 1) * 128]), writes=[wodep], dma=True)
            for tb in range(4):
                ps, psd = wops.next()
                s.op("pe", [lambda e, ps=ps, wo=wo, h=h, tb=tb: e.matmul(
                    ps[:, :], lhsT=wo[:, h, :], rhs=aT[:, h, tb * 512:(tb + 1) * 512], start=(h == 0), stop=(h == NH - 1))
                    for h in range(NH)], reads=[wodep] + adeps, writes=[psd])
                ob, obd = oring.next()
                s.op("sp", lambda e, ob=ob, dc=dc, tb=tb: e.dma_start(out=ob[:, :], in_=x_own[dc * 128:(dc + 1) * 128, tb * 512:(tb + 1) * 512]),
                     writes=[obd], dma=True)
                s.op("dve", lambda e, ob=ob, ps=ps: e.tensor_tensor(out=ob[:, :], in0=ps[:, :], in1=ob[:, :], op=ALU.add),
                     reads=[psd, obd], writes=[obd])
                outtoks.append(s.op("sp", lambda e, ob=ob, dc=dc, tb=tb: e.dma_start(
                    out=x_out[dc * 128:(dc + 1) * 128, tb * 512:(tb + 1) * 512], in_=ob[:, :]), reads=[obd], dma=True))
                if tb == 3 and io.get("hmsg") is not None:
                    s.op("sp", lambda e, ob=ob, dc=dc: e.dma_start(out=io["hmsg"][:, dc, :], in_=ob[:, 512 - HALO:512]), reads=[obd], dma=True)
        s.barrier()
        s.emit()


def attn_masks(valid):
    j = np.arange(128)[:, None]
    i = np.arange(128)[None, :]
    prevN = (j >= i).astype(np.float32)
    cur = (j <= i).astype(np.float32)
    prevH = prevN * np.float32(1.0 if valid else 0.0)
    m = np.zeros((128, 3, 512), np.float32)
    for v, (a, b) in enumerate(((prevH, prevN), (prevN, prevN), (prevH, prevH))):
        m[:, v, 0:128] = a
        m[:, v, 128:256] = cur
        m[:, v, 256:384] = b
        m[:, v, 384:512] = cur
    return np.where(m > 0.5, np.float32(0.0), np.float32(-30000.0)).astype(np.float32)


def run_attn(xTs, nw, w_qkv, w_o):
    kv = run_attn_kv(xTs, nw, w_qkv)
    nc = _prog("attn_main", build_attn_main)
    wr = w_qkv.reshape(KC, 128, NG, 3, NH, 128)
    wq = np.ascontiguousarray(wr[:, :, :, 0].transpose(2, 3, 1, 0, 4))
    pm = perm_matrix()
    maps = []
    for c in range(NCORES):
        cs, sn = rope_tables_np((c % 4) * T)
        k_own, v_own = kv[c]
        k_halo = np.zeros_like(k_own)
        v_halo = np.zeros_like(v_own)
        if c % 4 != 0:
            kp, vp = kv[c - 1]
            for g in range(NG):
                d = DIL[g]
                L = T // d
                nb = L // 128
                k_halo[g, :, :, 0:d * 128] = kp[g].reshape(NH, 128, d, L)[:, :, :, L - 128:].reshape(NH, 128, d * 128)
                v_halo[g, :, :, 0:d, :] = vp[g].reshape(NH, 128, d, nb, 128)[:, :, :, nb - 1, :]
        maps.append({"x_own": xTs[c], "nw": _cols128(nw), "wq": wq, "wo": np.ascontiguousarray(w_o),
                     "cosT": cs, "sinT": sn, "pm": pm, "masks": attn_masks(c % 4 != 0),
                     "k_own": k_own, "k_halo": k_halo, "v_own": v_own, "v_halo": v_halo})
    res = run_bass_kernel_spmd(nc, maps, core_ids=list(range(NCORES)))
    return [np.asarray(r["x_out"]) for r in res.results]


DIN = 2048
NHS = 32
CONV_ENG2 = "dve"
OFF_ENG = "pool"


def emit_ssm(P, io, phase):
    debug = False
    B_ = phase == "B"
    nc = P.nc
    dbg = {}
    x_own, nw_d, win_d, wdt_d, cw_d, vec_d = io["x_in"], io["nw"], io["win"], io["wdt"], io["cw"], io["vecs"]
    if B_:
        wz_d, gw_d, wout_d, x_out = io["wz"], io["gw"], io["wout"], io["x_out"]
    else:
        smsg = io["smsg"]
    W = HALO + T
    with contextlib.ExitStack() as es:
        cx = Ctx(nc, es, P.S)
        s = cx.s
        hT = cx.sb([128, KC, W], BF16, "hT")
        hdep = Dep()
        nw = cx.sb([128, KC], F32, "nw")
        ones = cx.sb([128, 129], F32, "ones")
        U = cx.sb([128, 128], F32, "U")
        idb = cx.sb([128, 128], BF16, "idb")
        cw = cx.sb([128, 32, 5], F32, "cw")
        wdt = cx.sb([128, KC, 32], BF16, "wdt")
        vecs = cx.sb([128, 3, 32], F32, "vecs")
        nwdep, onesdep, Udep, iddep, cwdep, wdtdep, vecdep = [Dep() for _ in range(7)]
        psring = Ring([cx.ps([128, 512], F32, "ps") for _ in range(3)], ps=True)
        ptr = cx.ps([128, 8, 128], BF16, "ptr")
        ptrd = Dep(True)
        misc = cx.ps([128, 512], F32, "misc")
        miscd = Dep(True)
        cbk = cx.ps([128, 512], F32, "cbk")
        cbkd = Dep(True)
        ybk = cx.ps([128, 512], F32, "ybk")
        ybkd = Dep(True)
        stbk = cx.ps([128, 512], F32, "stbk")
        stbkd = Dep(True)
        ybring = Ring([ybk, stbk])
        ybring.deps = [ybkd, stbkd]
        sqring = Ring([cx.sb([128, 256], F32, "sq") for _ in range(2)])
        rsring = Ring([cx.sb([128, 256], F32, "rs") for _ in range(2)])
        if not B_:
            xbr = Ring([cx.sb([128, KC, 512], F32, "xb") for _ in range(2)])
            sq5 = Ring([cx.sb([128, 512], F32, "sq5") for _ in range(3)])
            rs5 = Ring([cx.sb([128, 512], F32, "rs5") for _ in range(2)])
        wring = Ring([cx.sb([128, KC, 128], BF16, "w") for _ in range(3)])
        rawring = Ring([cx.sb([128, 3 + 512], F32, "raw") for _ in range(2)])
        cvring = Ring([cx.sb([128, 512], F32, "cv") for _ in range(3)])
        xcring = Ring([cx.sb([128, 512], BF16, "xc") for _ in range(2)])
        tail = cx.sb([128, 32, 3], F32, "tail")
        taild = [Dep() for _ in range(32)]
        xtok = cx.sb([128, 4, DIN], BF16, "xtok")
        xtokd = Dep()
        btok = cx.sb([128, 4, 1024], BF16, "btok")
        btokd = Dep()
        BT = cx.sb([128, 8, 512], BF16, "BT")
        BTd = Dep()
        dt = cx.sb([128, 4, 32], F32, "dt")
        adt = cx.sb([128, 4, 32], F32, "adt")
        dtd, adtd = Dep(), Dep()
        sm = cx.sb([128, 8, 4, 32], F32, "sm")
        smd = [Dep() for _ in range(8)]
        tsum = cx.sb([128, 32], F32, "tsum")
        tsumd = Dep()
        xdtd = cx.sb([128, DIN], BF16, "xdtd")
        xdtdd = Dep()
        S = cx.sb([128, DIN], F32, "S")
        Sd = Dep()
        if B_:
            CT = cx.sb([128, 8, 512], BF16, "CT")
            CTd = Dep()
            wzring = Ring([cx.sb([128, KC, 512], BF16, "wz") for _ in range(1)])
            sz = cx.sb([128, 4, DIN], BF16, "sz")
            szd = Dep()
            gw = cx.sb([128, 16], F32, "gw")
            gwd = Dep()
            xdt = cx.sb([128, DIN], BF16, "xdt")
            xsD = cx.sb([128, DIN], BF16, "xsD")
            xdt_d, xsD_d = Dep(), Dep()
            Sb = cx.sb([128, DIN], BF16, "Sb")
            Sbd = Dep()
            cbmq = [cx.sb([128, 4, 128], F32, "cbmq%d" % i) for i in range(2)]
            cbmqd = [Dep(), Dep()]
            ttring = Ring([cx.sb([128, 512], F32, "tt") for _ in range(2)])
            lxring = Ring([cx.sb([128, 512], F32, "lx") for _ in range(2)])
            mtring = Ring([cx.sb([128, 512], BF16, "mt") for _ in range(2)])
            ytring = Ring([cx.sb([128, 256], F32, "yt") for _ in range(2)])
            y = cx.sb([128, DIN], F32, "y")
            yd = Dep()
            xb = y[:, :].rearrange("p (kc c) -> p kc c", kc=KC)
            xbd = yd
            junk = cx.sb([128, 256], BF16, "junk")
            junkd = Dep()
            ssq = cx.sb([128, 16], F32, "ssq")
            ssqd = Dep()
            gn = cx.sb([128, DIN], BF16, "gn")
            gnd = Dep()
            gnT = cx.sb([128, 16, 512], BF16, "gnT")
            gnTd = Dep()
            woring = Ring([cx.sb([128, 16, 128], BF16, "wo") for _ in range(2)])
            oring = Ring([cx.sb([128, 512], F32, "ob") for _ in range(2)])
            spl = cx.sb([128, 3, 32], F32, "spl")
            spld = Dep()

        s.op("sp", lambda e: e.dma_start(out=nw[:, :], in_=nw_d[:, :]), writes=[nwdep], dma=True)
        s.op("sp", lambda e: e.dma_start(out=cw[:, :, :], in_=cw_d[:, :, :]), writes=[cwdep], dma=True)
        s.op("pool", lambda e: e.dma_start(out=wdt[:, :, :], in_=wdt_d[:, :, :]), writes=[wdtdep], dma=True)
        for i in range(3):
            s.op("sp", lambda e, i=i: e.dma_start(out=vecs[:, i, :], in_=vec_d[i, :].partition_broadcast(128)), writes=[vecdep], dma=True)
        s.op("dve", lambda e: e.memset(ones[:, 0:128], 1.0), writes=[onesdep])
        s.op("dve", lambda e: e.memset(ones[:, 128:129], EPS), writes=[onesdep])
        s.op("pool", lambda e: e.memset(U[:, :], 1.0), writes=[Udep])
        s.op("pool", lambda e: e.affine_select(out=U[:, :], in_=U[:, :], pattern=[[1, 128]], compare_op=ALU.is_ge, fill=0.0, base=0, channel_multiplier=-1),
             reads=[Udep], writes=[Udep])
        s.op("pool", lambda e: e.memset(idb[:, :], 1.0), writes=[iddep])
        s.op("pool", lambda e: e.affine_select(out=idb[:, :], in_=idb[:, :], pattern=[[1, 128]], compare_op=ALU.is_equal, fill=0.0, base=0, channel_multiplier=-1),
             reads=[iddep], writes=[iddep])
        s.op("act", lambda e: e.activation(out=vecs[:, 1, :], in_=vecs[:, 1, :], func=AF.Exp), reads=[vecdep], writes=[vecdep])
        s.op("dve", lambda e: e.tensor_scalar(out=vecs[:, 1, :], in0=vecs[:, 1, :], scalar1=-1.0, scalar2=0.0, op0=ALU.mult, op1=ALU.add), reads=[vecdep], writes=[vecdep])
        s.op("dve", lambda e: e.memset(tsum[:, :], 0.0), writes=[tsumd])
        def s_init():
            HS = SMSG // 2
            sallA = P.sall[0].rearrange("(r p) f -> r p f", p=128)
            sallB = P.sall[1].rearrange("(r p) f -> r p f", p=128)
            fl2 = cx.sb([128, 8], F32, "fl2")
            fl2d = Dep()
            s.op("sp", lambda e: e.dma_start(out=fl2[:, :], in_=P.flags[:, :]), writes=[fl2d], dma=True)
            s.op("dve", lambda e: e.memset(S[:, :], 0.0), writes=[Sd])
            for r in range(4):
                s.op("sp", lambda e, r=r: e.dma_start(out=spl[:, 0, :], in_=sallB[r, :, DIN - HS:DIN - HS + NHS]), writes=[spld], dma=True)
                s.op("sp", lambda e, r=r: e.dma_start(out=y[:, 0:HS], in_=sallA[r, :, :]), writes=[yd], dma=True)
                s.op("sp", lambda e, r=r: e.dma_start(out=y[:, HS:DIN], in_=sallB[r, :, 0:DIN - HS]), writes=[yd], dma=True)
                s.op("act", lambda e, r=r: e.activation(out=spl[:, 1, :], in_=spl[:, 0, :], func=AF.Exp, scale=fl2[:, 4 + r:5 + r]), reads=[spld, fl2d], writes=[spld])
                s.op("dve", lambda e: e.tensor_tensor(out=S[:, :].rearrange("p (h q) -> p h q", h=NHS), in0=S[:, :].rearrange("p (h q) -> p h q", h=NHS),
                                                      in1=spl[:, 1, :].unsqueeze(2).to_broadcast([128, NHS, 64]), op=ALU.mult), reads=[Sd, spld], writes=[Sd])
                s.op("dve", lambda e, r=r: e.scalar_tensor_tensor(out=S[:, :], in0=y[:, :], scalar=fl2[:, 4 + r:5 + r], in1=S[:, :], op0=ALU.mult, op1=ALU.add),
                     reads=[Sd, yd, fl2d], writes=[Sd])
            s.op("act", lambda e: e.activation(out=Sb[:, :], in_=S[:, :], func=AF.Copy), reads=[Sd], writes=[Sbd])

        if B_:
            s.op("sp", lambda e: e.dma_start(out=gw[:, :], in_=gw_d[:, :]), writes=[gwd], dma=True)
            pass
        else:
            s.op("dve", lambda e: e.memset(S[:, :], 0.0), writes=[Sd])

        def norm_cols(load_fn, n, c0):
            if not B_:
                xb, xbd = xbr.next()
                sqring_, rsring_ = sq5, rs5
            ps, psd = psring.next()
            load_fn(xb, xbd)
            for kc in range(KC):
                sq, sqd = sqring_.next()
                s.op("act", lambda e, sq=sq, kc=kc: e.activation(out=sq[:, 0:n], in_=xb[:, kc, 0:n], func=AF.Square), reads=[xbd], writes=[sqd])
                s.op("pe", lambda e, ps=ps, sq=sq, kc=kc: e.matmul(ps[:, 0:n], lhsT=ones[:, 0:128], rhs=sq[:, 0:n], start=(kc == 0), stop=(kc == KC - 1)),
                     reads=[sqd, onesdep], writes=[psd])
            rs, rsd = rsring_.next()
            s.op("act", lambda e, rs=rs, ps=ps: e.activation(out=rs[:, 0:n], in_=ps[:, 0:n], func=AF.Sqrt, bias=ones[:, 128:129], scale=1.0 / D),
                 reads=[psd, onesdep], writes=[rsd])
            s.op("dve", lambda e, rs=rs: e.reciprocal(out=rs[:, 0:n], in_=rs[:, 0:n]), reads=[rsd], writes=[rsd])
            for kc in range(KC):
                s.op("dve", lambda e, rs=rs, kc=kc: e.scalar_tensor_tensor(out=hT[:, kc, c0:c0 + n], in0=xb[:, kc, 0:n], scalar=nw[:, kc:kc + 1], in1=rs[:, 0:n],
                                                                         op0=ALU.mult, op1=ALU.mult), reads=[xbd, rsd, nwdep], writes=[hdep])

        if B_:
            for kc in range(KC):
                s.op("sp", lambda e, kc=kc: e.dma_start(out=hT[:, kc, :], in_=P.hsave[:, kc, :]), writes=[hdep], dma=True)
        else:
            norm_cols(lambda xb, xbd: emit_halo(cx, P, xb[:, :, 0:HALO], [xbd]), HALO, 0)
            for blk in range(T // 512):
                def ld(xb, xbd, blk=blk):
                    for kc in range(KC):
                        s.op("sp", lambda e, kc=kc: e.dma_start(out=xb[:, kc, :], in_=x_own[kc * 128:(kc + 1) * 128, blk * 512:(blk + 1) * 512]), writes=[xbd], dma=True)
                norm_cols(ld, 512, HALO + blk * 512)
            for kc in range(KC):
                s.op("sp", lambda e, kc=kc: e.dma_start(out=P.hsave[:, kc, :], in_=hT[:, kc, :]), reads=[hdep], dma=True)

        def bc64(ap32):
            return ap32.unsqueeze(2).to_broadcast([128, ap32.shape[1], 64])

        def h64(ap):
            return ap.rearrange("p (h q) -> p h q", q=64)

        outtoks = []
        deferred = []
        nchunks = 32 if B_ else 24
        for tb in range(4):
            c0 = HALO + tb * 512
            pst = {}

            def s1(cc, tb=tb, c0=c0):
                w, wdep = wring.next()
                s.op("pool", lambda e: e.dma_start(out=w[:, :, :], in_=win_d[cc, :, :, :]), writes=[wdep], dma=True)
                raw, rawd = rawring.next()
                if tb == 0:
                    s.op("pe", [lambda e, kc=kc: e.matmul(misc[:, 0:HALO], lhsT=w[:, kc, :], rhs=hT[:, kc, 0:HALO], start=(kc == 0), stop=(kc == KC - 1))
                                for kc in range(KC)], reads=[wdep, hdep], writes=[miscd])
                    s.op("act", lambda e: e.activation(out=raw[:, 0:HALO], in_=misc[:, 0:HALO], func=AF.Copy), reads=[miscd], writes=[rawd])
                else:
                    s.op("act", lambda e: e.activation(out=raw[:, 0:HALO], in_=tail[:, cc, :], func=AF.Copy), reads=[taild[cc]], writes=[rawd])
                ps, psd = psring.next()
                s.op("pe", [lambda e, kc=kc: e.matmul(ps[:, :], lhsT=w[:, kc, :], rhs=hT[:, kc, c0:c0 + 512], start=(kc == 0), stop=(kc == KC - 1))
                            for kc in range(KC)], reads=[wdep, hdep], writes=[psd])
                s.op("act", lambda e: e.activation(out=raw[:, HALO:HALO + 512], in_=ps[:, :], func=AF.Copy), reads=[psd], writes=[rawd])
                if tb < 3:
                    s.op("act", lambda e: e.activation(out=tail[:, cc, :], in_=raw[:, 512:515], func=AF.Copy), reads=[rawd], writes=[taild[cc]])
                pst[cc] = {"raw": raw, "rawd": rawd}

            def s2(cc):
                raw, rawd = pst[cc]["raw"], pst[cc]["rawd"]
                cv, cvd = cvring.next()
                eng = "dve" if cc % 2 == 0 else CONV_ENG2
                s.op(eng, lambda e: e.tensor_scalar(out=cv[:, :], in0=raw[:, 3:515], scalar1=cw[:, cc, 3:4], scalar2=cw[:, cc, 4:5],
                                                    op0=ALU.mult, op1=ALU.add), reads=[rawd, cwdep], writes=[cvd])
                for k in range(3):
                    s.op(eng, lambda e, k=k: e.scalar_tensor_tensor(out=cv[:, :], in0=raw[:, k:k + 512], scalar=cw[:, cc, k:k + 1], in1=cv[:, :],
                                                                    op0=ALU.mult, op1=ALU.add), reads=[rawd, cwdep, cvd], writes=[cvd])
                pst[cc].update({"cv": cv, "cvd": cvd})

            def s3a(cc):
                cv, cvd = pst[cc]["cv"], pst[cc]["cvd"]
                if cc < 16:
                    xc, xcd = xcring.next()
                    s.op("act", lambda e: e.activation(out=xc[:, :], in_=cv[:, :], func=AF.Silu), reads=[cvd], writes=[xcd])
                    pst[cc].update({"src": xc, "srcd": xcd})
                elif cc < 24:
                    gi = cc - 16
                    s.op("act", lambda e: e.activation(out=BT[:, gi, :], in_=cv[:, :], func=AF.Silu), reads=[cvd], writes=[BTd])
                    pst[cc].update({"src": BT[:, gi, :], "srcd": BTd})
                else:
                    gi = cc - 24
                    s.op("act", lambda e: e.activation(out=CT[:, gi, :], in_=cv[:, :], func=AF.Silu), reads=[cvd], writes=[CTd])

            def s3b(cc):
                if cc >= 24:
                    return
                src, srcd = pst[cc]["src"], pst[cc]["srcd"]
                s.op("pe", [lambda e, q=q: e.transpose(out=ptr[:, q, :], in_=src[:, q * 128:(q + 1) * 128], identity=idb[:, :]) for q in range(4)],
                     reads=[srcd, iddep], writes=[ptrd])

            def s3c(cc):
                if cc >= 24:
                    return
                if cc < 16:
                    s.op("act", lambda e: e.activation(out=xtok[:, :, cc * 128:(cc + 1) * 128], in_=ptr[:, 0:4, :], func=AF.Copy), reads=[ptrd], writes=[xtokd])
                else:
                    gi = cc - 16
                    s.op("act", lambda e: e.activation(out=btok[:, :, gi * 128:(gi + 1) * 128], in_=ptr[:, 0:4, :], func=AF.Copy), reads=[ptrd], writes=[btokd])

            clist = list(range(24, 32)) if B_ else list(range(24))
            ncl = len(clist)
            if B_:
                s.op("sp", lambda e, tb=tb: e.dma_start(out=xtok[:, :, :], in_=P.xs_save[tb].rearrange("p (c f) -> p c f", c=4)), writes=[xtokd], dma=True)
                s.op("sp", lambda e, tb=tb: e.dma_start(out=btok[:, :, :], in_=P.bs_save[tb].rearrange("p (c f) -> p c f", c=4)), writes=[btokd], dma=True)
                s.op("sp", lambda e, tb=tb: e.dma_start(out=BT[:, :, :], in_=P.bt_save[tb].rearrange("p (c f) -> p c f", c=8)), writes=[BTd], dma=True)
                s.op("sp", lambda e, tb=tb: e.dma_start(out=dt[:, :, :], in_=P.dt_save[tb].rearrange("p (c f) -> p c f", c=4)), writes=[dtd], dma=True)
            for it in range(ncl + 3):
                if 0 <= it - 3 < ncl:
                    s3c(clist[it - 3])
                if it < ncl:
                    s1(clist[it])
                if 0 <= it - 2 < ncl:
                    s3b(clist[it - 2])
                if it < ncl:
                    s2(clist[it])
                if 0 <= it - 1 < ncl:
                    s3a(clist[it - 1])
            if not B_:
                for ck in range(4):
                    t0 = c0 + ck * 128
                    s.op("pe", [lambda e, kc=kc, t0=t0: e.matmul(misc[:, 0:32], lhsT=hT[:, kc, t0:t0 + 128], rhs=wdt[:, kc, :], start=(kc == 0), stop=(kc == KC - 1))
                                for kc in range(KC)], reads=[hdep, wdtdep], writes=[miscd])
                    s.op("dve", lambda e, ck=ck: e.tensor_tensor(out=dt[:, ck, :], in0=misc[:, 0:32], in1=vecs[:, 0, :], op=ALU.add), reads=[miscd, vecdep], writes=[dtd])
                s.op("act", lambda e: e.activation(out=dt[:, :, :], in_=dt[:, :, :], func=AF.Exp), reads=[dtd], writes=[dtd])
                s.op("act", lambda e: e.activation(out=dt[:, :, :], in_=dt[:, :, :], func=AF.Ln, bias=1.0), reads=[dtd], writes=[dtd])
                s.op("sp", lambda e, tb=tb: e.dma_start(out=P.xs_save[tb].rearrange("p (c f) -> p c f", c=4), in_=xtok[:, :, :]), reads=[xtokd], dma=True)
                s.op("sp", lambda e, tb=tb: e.dma_start(out=P.bs_save[tb].rearrange("p (c f) -> p c f", c=4), in_=btok[:, :, :]), reads=[btokd], dma=True)
                s.op("sp", lambda e, tb=tb: e.dma_start(out=P.bt_save[tb].rearrange("p (c f) -> p c f", c=8), in_=BT[:, :, :]), reads=[BTd], dma=True)
                s.op("sp", lambda e, tb=tb: e.dma_start(out=P.dt_save[tb].rearrange("p (c f) -> p c f", c=4), in_=dt[:, :, :]), reads=[dtd], dma=True)
            s.op("dve", lambda e: e.tensor_tensor(out=adt[:, :, :], in0=dt[:, :, :], in1=vecs[:, 1:2, :].to_broadcast([128, 4, 32]), op=ALU.mult),
                 reads=[dtd, vecdep], writes=[adtd])
            if B_:
                for cb in range(4):
                    wz, wzdep = wzring.next()
                    s.op("pool", lambda e, wz=wz, cb=cb: e.dma_start(out=wz[:, :, :], in_=wz_d[cb, :, :, :]), writes=[wzdep], dma=True)
                    for ck in range(4):
                        t0 = c0 + ck * 128
                        ps, psd = psring.next()
                        s.op("pe", [lambda e, ps=ps, wz=wz, kc=kc, t0=t0: e.matmul(ps[:, :], lhsT=hT[:, kc, t0:t0 + 128], rhs=wz[:, kc, :], start=(kc == 0), stop=(kc == KC - 1))
                                    for kc in range(KC)], reads=[hdep, wzdep], writes=[psd])
                        s.op("act", lambda e, ps=ps, ck=ck, cb=cb: e.activation(out=sz[:, ck, cb * 512:(cb + 1) * 512], in_=ps[:, :], func=AF.Silu), reads=[psd], writes=[szd])
            if B_ and tb == 0:
                s_init()
            mv = misc[:, 0:256].rearrange("p (c w h) -> p c w h", c=4, w=2)
            fns = []
            for ck in range(4):
                fns.append(lambda e, ck=ck: e.matmul(misc[:, ck * 64:ck * 64 + 32], lhsT=U[:, :], rhs=adt[:, ck, :], start=True, stop=True))
                fns.append(lambda e, ck=ck: e.matmul(misc[:, ck * 64 + 32:ck * 64 + 64], lhsT=ones[:, 0:128], rhs=adt[:, ck, :], start=True, stop=True))
            s.op("pe", fns, reads=[Udep, onesdep, adtd], writes=[miscd])
            s.op("act", lambda e: e.activation(out=sm[:, 0, :, :], in_=mv[:, :, 0, :], func=AF.Copy), reads=[miscd], writes=[smd[0]])
            s.op("act", lambda e: e.activation(out=sm[:, 7, :, :], in_=mv[:, :, 1, :], func=AF.Copy), reads=[miscd], writes=[smd[7]])
            s.op("dve", lambda e: e.tensor_tensor(out=sm[:, 1, :, :], in0=sm[:, 7, :, :], in1=sm[:, 0, :, :], op=ALU.subtract), reads=[smd[7], smd[0]], writes=[smd[1]])
            s.op("act", lambda e: e.activation(out=sm[:, 2, :, :], in_=sm[:, 1, :, :], func=AF.Exp), reads=[smd[1]], writes=[smd[2]])
            s.op("act", lambda e: e.activation(out=sm[:, 4, :, :], in_=sm[:, 7, :, :], func=AF.Exp), reads=[smd[7]], writes=[smd[4]])
            for ck in range(4):
                s.op("dve", lambda e, ck=ck: e.tensor_tensor(out=tsum[:, :], in0=sm[:, 7, ck, :], in1=tsum[:, :], op=ALU.add), reads=[smd[7], tsumd], writes=[tsumd])
            s.op("dve", lambda e: e.tensor_tensor(out=sm[:, 6, :, :], in0=dt[:, :, :], in1=sm[:, 2, :, :], op=ALU.mult), reads=[dtd, smd[2]], writes=[smd[6]])
            if B_:
                s.op("act", lambda e: e.activation(out=sm[:, 3, :, :], in_=sm[:, 0, :, :], func=AF.Exp), reads=[smd[0]], writes=[smd[3]])
                s.op("dve", lambda e: e.tensor_scalar(out=sm[:, 5, :, :], in0=sm[:, 0, :, :], scalar1=-1.0, scalar2=0.0, op0=ALU.mult, op1=ALU.add), reads=[smd[0]], writes=[smd[5]])
            for ck in range(4):
                k0 = ck * 128
                s.op(OFF_ENG, lambda e, ck=ck: e.tensor_tensor(out=h64(xdtd[:, :]), in0=h64(xtok[:, ck, :]), in1=bc64(sm[:, 6, ck, :]), op=ALU.mult),
                     reads=[xtokd, smd[6]], writes=[xdtdd])
                if B_:
                    s.op(OFF_ENG, lambda e, ck=ck: e.tensor_tensor(out=h64(xdt[:, :]), in0=h64(xtok[:, ck, :]), in1=bc64(dt[:, ck, :]), op=ALU.mult),
                         reads=[xtokd, dtd], writes=[xdt_d])
                    s.op(OFF_ENG, lambda e, ck=ck: e.tensor_tensor(out=h64(xsD[:, :]), in0=h64(xtok[:, ck, :]), in1=bc64(vecs[:, 2, :]), op=ALU.mult),
                         reads=[xtokd, vecdep], writes=[xsD_d])
                    for q in range(2):
                        s.op("pe", [lambda e, q=q, gg=gg, k0=k0: e.matmul(cbk[:, gg * 128:(gg + 1) * 128], lhsT=BT[:, 4 * q + gg, k0:k0 + 128], rhs=CT[:, 4 * q + gg, k0:k0 + 128],
                                                                       start=True, stop=True) for gg in range(4)], reads=[BTd, CTd], writes=[cbkd])
                        s.op("dve", lambda e, q=q: e.tensor_tensor(out=cbmq[q][:, :, :], in0=cbk[:, :].rearrange("p (a b) -> p a b", a=4),
                                                                    in1=U[:, :].unsqueeze(1).to_broadcast([128, 4, 128]), op=ALU.mult), reads=[cbkd, Udep], writes=[cbmqd[q]])

                    def stA(g, ck=ck):
                        ps, psd = psring.next()
                        s.op("pe", [lambda e, ps=ps, r=r, hh=4 * g + r: e.matmul(ps[:, r * 128:(r + 1) * 128], lhsT=adt[:, ck, hh:hh + 1].to_broadcast([128, 128]), rhs=U[:, :],
                                                                              start=True, stop=True) for r in range(4)], reads=[adtd, Udep], writes=[psd])
                        return ps, psd

                    def stB1(g, ps, psd, ck=ck):
                        tt, ttd = ttring.next()
                        s.op("dve", lambda e: e.tensor_tensor(out=tt[:, :].rearrange("p (a b) -> p a b", a=4), in0=ps[:, :].rearrange("p (a b) -> p a b", a=4),
                                                              in1=sm[:, 5, ck, 4 * g:4 * g + 4].unsqueeze(2).to_broadcast([128, 4, 128]), op=ALU.add), reads=[psd, smd[5]], writes=[ttd])
                        lx, lxd = lxring.next()
                        s.op("act", lambda e: e.activation(out=lx[:, :], in_=tt[:, :], func=AF.Exp), reads=[ttd], writes=[lxd])
                        return lx, lxd

                    def stB2(g, lx, lxd):
                        mt, mtd = mtring.next()
                        q, gg = divmod(g, 4)
                        s.op("dve", lambda e: e.scalar_tensor_tensor(out=mt[:, :].rearrange("p (a b) -> p a b", a=4), in0=lx[:, :].rearrange("p (a b) -> p a b", a=4), scalar=1.0,
                                                                     in1=cbmq[q][:, gg:gg + 1, :].to_broadcast([128, 4, 128]), op0=ALU.min, op1=ALU.mult),
                             reads=[lxd, cbmqd[q]], writes=[mtd])
                        return mt, mtd

                    def stC1(g, mt, mtd, k0=k0):
                        yb, ybd = ybring.next()
                        fns = [lambda e: e.matmul(yb[:, 0:256], lhsT=idb[:, :], rhs=xsD[:, g * 256:(g + 1) * 256], start=True, stop=False)]
                        for r in range(4):
                            hh = 4 * g + r
                            fns.append(lambda e, r=r, hh=hh: e.matmul(yb[:, r * 64:(r + 1) * 64], lhsT=mt[:, r * 128:(r + 1) * 128], rhs=xdt[:, hh * 64:(hh + 1) * 64], start=False, stop=(r == 3)))
                        fns.append(lambda e: e.matmul(yb[:, 256:512], lhsT=CT[:, g, k0:k0 + 128], rhs=Sb[:, g * 256:(g + 1) * 256], start=True, stop=True))
                        s.op("pe", fns, reads=[iddep, xsD_d, mtd, xdt_d, CTd, Sbd], writes=[ybd])
                        return yb, ybd

                    def stC2(g, yb, ybd, ck=ck):
                        yt, ytd = ytring.next()
                        s.op("dve", lambda e: e.tensor_tensor(out=h64(yt[:, :]), in0=h64(yb[:, 256:512]), in1=bc64(sm[:, 3, ck, 4 * g:4 * g + 4]), op=ALU.mult),
                             reads=[ybd, smd[3]], writes=[ytd])
                        s.op("dve", lambda e: e.tensor_tensor(out=y[:, g * 256:(g + 1) * 256], in0=yb[:, 0:256], in1=yt[:, :], op=ALU.add),
                             reads=[ybd, ytd], writes=[yd])

                    As = {0: stA(0), 1: stA(1)}
                    Ls = {0: stB1(0, *As[0])}
                    Ys = {}
                    for gi_ in range(8):
                        if gi_ + 2 < 8:
                            As[gi_ + 2] = stA(gi_ + 2)
                        if gi_ + 1 < 8:
                            Ls[gi_ + 1] = stB1(gi_ + 1, *As[gi_ + 1])
                        mt, mtd = stB2(gi_, *Ls[gi_])
                        Ys[gi_] = stC1(gi_, mt, mtd)
                        if gi_ >= 2 and deferred:
                            deferred.pop(0)()
                        if gi_ >= 1:
                            stC2(gi_ - 1, *Ys[gi_ - 1])
                    stC2(7, *Ys[7])
                if debug and tb == 0 and ck == 0:
                    outtoks.append(s.op("sp", lambda e: e.dma_start(out=dbg["g_dt"][:, :, :], in_=dt[:, :, :]), reads=[dtd], dma=True))
                    outtoks.append(s.op("sp", lambda e: e.dma_start(out=dbg["g_sm"][:, :, :], in_=sm[:, :, :]), reads=smd, dma=True))
                    outtoks.append(s.op("sp", lambda e: e.dma_start(out=dbg["g_xtok"][:, :, :], in_=xtok[:, :, :]), reads=[xtokd], dma=True))
                    outtoks.append(s.op("sp", lambda e: e.dma_start(out=dbg["g_btok"][:, :, :], in_=btok[:, :, :]), reads=[btokd], dma=True))
                    outtoks.append(s.op("sp", lambda e: e.dma_start(out=dbg["g_hT"][:, :, :], in_=hT[:, :, :]), reads=[hdep], dma=True))
                    if B_:
                        outtoks.append(s.op("sp", lambda e: e.dma_start(out=dbg["g_y"][:, :], in_=y[:, :]), reads=[yd], dma=True))
                        outtoks.append(s.op("sp", lambda e: e.dma_start(out=dbg["g_sz"][:, :, :], in_=sz[:, :, :]), reads=[szd], dma=True))
                s.op("dve", lambda e, ck=ck: e.tensor_tensor(out=h64(S[:, :]), in0=h64(S[:, :]), in1=bc64(sm[:, 4, ck, :]), op=ALU.mult), reads=[Sd, smd[4]] + ([Sbd] if B_ else []), writes=[Sd])
                for gp in range(4):
                    stp, stpd = psring.next()
                    s.op("pe", [lambda e, g=g, ck=ck, stp=stp: e.matmul(stp[:, (g % 2) * 256:(g % 2) * 256 + 256], lhsT=btok[:, ck, g * 128:(g + 1) * 128], rhs=xdtd[:, g * 256:(g + 1) * 256],
                                                                      start=True, stop=True) for g in (2 * gp, 2 * gp + 1)], reads=[btokd, xdtdd], writes=[stpd])
                    s.op("dve", lambda e, gp=gp, stp=stp: e.tensor_tensor(out=S[:, gp * 512:(gp + 1) * 512], in0=stp[:, :], in1=S[:, gp * 512:(gp + 1) * 512], op=ALU.add),
                         reads=[stpd, Sd], writes=[Sd])
                if B_:
                    s.op("act", lambda e: e.activation(out=Sb[:, :], in_=S[:, :], func=AF.Copy), reads=[Sd], writes=[Sbd])
                    s.op("dve", lambda e, ck=ck: e.tensor_tensor(out=y[:, :], in0=y[:, :], in1=sz[:, ck, :], op=ALU.mult), reads=[yd, szd], writes=[yd])
                    s.op("dve", lambda e: e.memset(ssq[:, :], 0.0), writes=[ssqd])
                    for g in range(8):
                        s.op("act", lambda e, g=g: e.activation(out=junk[:, :], in_=y[:, g * 256:(g + 1) * 256], func=AF.Square, accum_out=ssq[:, g:g + 1]),
                             reads=[yd], writes=[junkd, ssqd])
                    s.op("act", lambda e: e.activation(out=ssq[:, 8:16], in_=ssq[:, 0:8], func=AF.Sqrt, bias=ones[:, 128:129], scale=1.0 / 256), reads=[ssqd, onesdep], writes=[ssqd])
                    s.op("dve", lambda e: e.reciprocal(out=ssq[:, 8:16], in_=ssq[:, 8:16]), reads=[ssqd], writes=[ssqd])
                    s.op("dve", lambda e: e.tensor_tensor(out=gn[:, :].rearrange("p (g q) -> p g q", g=8), in0=y[:, :].rearrange("p (g q) -> p g q", g=8),
                                                          in1=ssq[:, 8:16].unsqueeze(2).to_broadcast([128, 8, 256]), op=ALU.mult), reads=[yd, ssqd], writes=[gnd])
                    if debug and tb == 0 and ck == 0:
                        outtoks.append(s.op("sp", lambda e: e.dma_start(out=dbg["g_yg"][:, :], in_=y[:, :]), reads=[yd], dma=True))
                        outtoks.append(s.op("sp", lambda e: e.dma_start(out=dbg["g_gn"][:, :], in_=gn[:, :]), reads=[gnd], dma=True))
                        outtoks.append(s.op("sp", lambda e: e.dma_start(out=dbg["g_S"][:, :], in_=S[:, :]), reads=[Sd], dma=True))
                    def p4b(c4, k0=k0):
                        s.op("pe", [lambda e, q=q: e.transpose(out=ptr[:, q, :], in_=gn[:, (c4 * 4 + q) * 128:(c4 * 4 + q + 1) * 128], identity=idb[:, :]) for q in range(4)],
                             reads=[gnd, iddep], writes=[ptrd])
                        for q in range(4):
                            ccx = c4 * 4 + q
                            s.op("act", lambda e, q=q, ccx=ccx: e.activation(out=gnT[:, ccx, k0:k0 + 128], in_=ptr[:, q, :], func=AF.Copy, scale=gw[:, ccx:ccx + 1]),
                                 reads=[ptrd, gwd], writes=[gnTd])
                    deferred.extend([(lambda c4=c4, f=p4b: f(c4)) for c4 in range(4)])
            while deferred:
                deferred.pop(0)()
            if debug and B_ and tb == 0:
                outtoks.append(s.op("sp", lambda e: e.dma_start(out=dbg["g_gnT"][:, :, :], in_=gnT[:, :, :]), reads=[gnTd], dma=True))
            if B_:
                for dc in range(KC):
                    wo, wodep = woring.next()
                    s.op("pool", lambda e, wo=wo, dc=dc: e.dma_start(out=wo[:, :, :], in_=wout_d[dc, :, :, :]), writes=[wodep], dma=True)
                    ps, psd = psring.next()
                    s.op("pe", [lambda e, ps=ps, wo=wo, kc=kc: e.matmul(ps[:, :], lhsT=wo[:, kc, :], rhs=gnT[:, kc, :], start=(kc == 0), stop=(kc == 15)) for kc in range(16)],
                         reads=[wodep, gnTd], writes=[psd])
                    ob, obd = oring.next()
                    s.op("sp", lambda e, ob=ob, dc=dc, tb=tb: e.dma_start(out=ob[:, :], in_=x_own[dc * 128:(dc + 1) * 128, tb * 512:(tb + 1) * 512]), writes=[obd], dma=True)
                    s.op("dve", lambda e, ob=ob, ps=ps: e.tensor_tensor(out=ob[:, :], in0=ps[:, :], in1=ob[:, :], op=ALU.add), reads=[psd, obd], writes=[obd])
                    outtoks.append(s.op("sp", lambda e, ob=ob, dc=dc, tb=tb: e.dma_start(out=x_out[dc * 128:(dc + 1) * 128, tb * 512:(tb + 1) * 512], in_=ob[:, :]),
                                        reads=[obd], dma=True))
                    if tb == 3 and io.get("hmsg") is not None:
                        s.op("sp", lambda e, ob=ob, dc=dc: e.dma_start(out=io["hmsg"][:, dc, :], in_=ob[:, 512 - HALO:512]), reads=[obd], dma=True)
        if not B_:
            HS = SMSG // 2
            outtoks.append(s.op("sp", lambda e: e.dma_start(out=smsg[0][:, :], in_=S[:, 0:HS]), reads=[Sd], dma=True))
            outtoks.append(s.op("sp", lambda e: e.dma_start(out=smsg[1][:, 0:DIN - HS], in_=S[:, HS:DIN]), reads=[Sd], dma=True))
            outtoks.append(s.op("sp", lambda e: e.dma_start(out=smsg[1][:, DIN - HS:DIN - HS + NHS], in_=tsum[:, :]), reads=[tsumd], dma=True))
        s.barrier()
        s.emit()


def _ssm_common_maps(xTs, nw, w_in, conv_w, conv_b, dt_bias, a_log, d_skip):
    wr = w_in[:, DIN:DIN + 4096].reshape(KC, 128, 32, 128)
    win = np.ascontiguousarray(wr.transpose(2, 1, 0, 3))
    wdt = np.ascontiguousarray(w_in[:, DIN + 4096:].reshape(KC, 128, 32).transpose(1, 0, 2))
    cw = np.empty((128, 32, 5), np.float32)
    cw[:, :, 0:4] = conv_w.reshape(4, 32, 128).transpose(2, 1, 0)
    cw[:, :, 4] = conv_b.reshape(32, 128).T
    vecs = np.ascontiguousarray(np.stack([dt_bias, a_log, d_skip]).astype(np.float32))
    halos = _halo_cols(xTs)
    return [{"x_own": xTs[c], "x_halo": halos[c], "nw": _cols128(nw), "win": win, "wdt": wdt, "cw": cw, "vecs": vecs}
            for c in range(NCORES)]


def run_ssm(xTs, nw, w_in, conv_w, conv_b, dt_bias, a_log, d_skip, norm_w, w_out):
    maps = _ssm_common_maps(xTs, nw, w_in, conv_w, conv_b, dt_bias, a_log, d_skip)
    ncA = _prog("ssmA", lambda: build_ssm("A"))
    resA = run_bass_kernel_spmd(ncA, maps, core_ids=list(range(NCORES)))
    sl = [np.asarray(r["s_out"]) for r in resA.results]
    dl = [np.asarray(r["d_out"]) for r in resA.results]
    ncB = _prog("ssmB", lambda: build_ssm("B"))
    wz = np.ascontiguousarray(w_in[:, 0:DIN].reshape(KC, 128, 4, 512).transpose(2, 1, 0, 3))
    gw = np.ascontiguousarray(norm_w.reshape(16, 128).T)
    wout = np.ascontiguousarray(w_out.reshape(16, 128, KC, 128).transpose(2, 1, 0, 3))
    for c in range(NCORES):
        q = c % 4
        sp = np.zeros((3, 128, DIN), np.float32)
        dp = np.zeros((3, 128, NHS), np.float32)
        for i, src in enumerate((c - 3, c - 2, c - 1)):
            if src >= c - q:
                sp[i] = sl[src]
                dp[i] = dl[src]
        maps[c].update({"wz": wz, "gw": gw, "wout": wout, "sprev": sp, "dprev": dp})
    resB = run_bass_kernel_spmd(ncB, maps, core_ids=list(range(NCORES)))
    return [np.asarray(r["x_out"]) for r in resB.results]


I32 = mybir.dt.int32
GROUPS = [[0, 1, 2, 3], [4, 5, 6, 7]]
SMSG = DIN + NHS
VMSG = 21 * 128


class Prog:
    pass


def build_fused(stop=None):
    nc = bass.Bass("TRN2", target_bir_lowering=False)
    P = Prog()
    P.nc = nc
    P.st = {}

    def din(name, shape, dt=F32):
        return nc.dram_tensor(name, list(shape), dt, kind="ExternalInput").ap()

    SMSG_ = SMSG
    x_d = din("x", [D, T])
    out_d = nc.dram_tensor("out", [D, T], F32, kind="ExternalOutput").ap()
    pidx_d = din("pidx", [1, 4], I32)
    cos_d, sin_d, pm_d, mk_d = din("cosT", [128, T]), din("sinT", [128, T]), din("pm", [128, 128]), din("masks", [128, 3, 512])
    fw_d = din("fw", [128, KC])
    L = []
    for i in range(4):
        d = {"mnw": din("mnw%d" % i, [128, KC]), "fnw": din("fnw%d" % i, [128, KC]), "wup": din("wup%d" % i, [NJ, 128, KC, 256]),
             "fcw": din("fcw%d" % i, [128, NJ, 2, 4]), "wdn": din("wdn%d" % i, [DFF, D])}
        if i % 2 == 0:
            d.update({"wq": din("wq%d" % i, [NG, NH, 128, KC, 128]), "wk": din("wk%d" % i, [NG, NH, 128, KC, 128]),
                      "wv": din("wv%d" % i, [NG, 128, KC, 1024]), "wo": din("wo%d" % i, [D, D])})
        else:
            d.update({"win": din("win%d" % i, [32, 128, KC, 128]), "wdt": din("wdt%d" % i, [128, KC, 32]), "scw": din("scw%d" % i, [128, 32, 5]),
                      "vecs": din("vecs%d" % i, [3, 32]), "wz": din("wz%d" % i, [4, 128, KC, 512]), "gw": din("gw%d" % i, [128, 16]),
                      "wout": din("wout%d" % i, [KC, 128, 16, 128])})
        L.append(d)
    xb = [nc.dram_tensor("xb%d" % i, [D, T], F32).ap() for i in range(2)]
    k_own = nc.dram_tensor("k_own", [NG, NH, 128, T], BF16).ap()
    v_own = nc.dram_tensor("v_own", [NG, NH, 128, 16, 128], BF16).ap()
    kmsg = nc.dram_tensor("kmsg", [NH, 128, KMSG], BF16).ap()
    kall = nc.dram_tensor("kall", [NH, 5 * 128, KMSG], BF16).ap()
    vmsg = nc.dram_tensor("vmsg", [NH, 128, VMSG], BF16).ap()
    vall = nc.dram_tensor("vall", [NH, 5 * 128, VMSG], BF16).ap()
    hmsg = nc.dram_tensor("hmsg", [128, KC * HALO], F32).ap()
    hall = nc.dram_tensor("hall", [4 * 128, KC * HALO], F32).ap()
    kloc = nc.dram_tensor("kloc", [NH, 128, KMSG], BF16).ap()
    vloc = nc.dram_tensor("vloc", [NH, 128, VMSG], BF16).ap()
    flags_d = din("flags", [128, 8])
    P.hall = hall
    P.hsave = nc.dram_tensor("hsave", [128, KC, HALO + T], BF16).ap()
    P.xs_save = [nc.dram_tensor("xs_save%d" % i, [128, 4 * DIN], BF16).ap() for i in range(4)]
    P.bs_save = [nc.dram_tensor("bs_save%d" % i, [128, 4 * 1024], BF16).ap() for i in range(4)]
    P.bt_save = [nc.dram_tensor("bt_save%d" % i, [128, 8 * 512], BF16).ap() for i in range(4)]
    P.dt_save = [nc.dram_tensor("dt_save%d" % i, [128, 4 * 32], F32).ap() for i in range(4)]
    P.flags = flags_d
    smsg = [nc.dram_tensor("smsg%d" % i, [128, SMSG // 2], F32).ap() for i in range(2)]
    sall = [nc.dram_tensor("sall%d" % i, [4 * 128, SMSG // 2], F32).ap() for i in range(2)]
    P.sall = sall

    with contextlib.ExitStack() as ges:
        S = Sched(nc, ges)
        P.S = S

        def setup(e):
            ins = None
            P.st["regs"] = []
            for k in range(1):
                reg = e.alloc_register("pidx%d" % k)
                ins = e.reg_load(reg, pidx_d[0:1, k:k + 1])
                P.st["regs"].append(reg)
                P.st["c%d" % (k + 1)] = e.snap(reg, min_val=0, max_val=4)
            return ins
        S.op("sp", setup)

        def pre_sp(e):
            for k, reg in enumerate(P.st.get("regs", [])):
                P.st["c%d" % (k + 1)] = e.snap(reg, min_val=0, max_val=4)
        S.pre_sp = pre_sp
        with contextlib.ExitStack() as es:
            cx = Ctx(nc, es, S)
            zb = cx.sb([128, KMSG], BF16, "zb")
            zf = cx.sb([128, SMSG], F32, "zf")
            zd = Dep()
            S.op("dve", lambda e: e.memset(zb[:, :], 0.0), writes=[zd])
            S.op("dve", lambda e: e.memset(zf[:, :], 0.0), writes=[zd])
            for h in range(NH):
                S.op("sp", lambda e, h=h: e.dma_start(out=kall[h, 512:640, :], in_=zb[:, :]), reads=[zd], dma=True)
                S.op("sp", lambda e, h=h: e.dma_start(out=vall[h, 512:640, :], in_=zb[:, 0:VMSG]), reads=[zd], dma=True)
            S.barrier()
            S.emit()

        def coll(msg2d, all2d, nrows):
            S.op("pool", lambda e: e.collective_compute("AllGather", ALU.bypass, replica_groups=GROUPS,
                                                        ins=[msg2d.opt()], outs=[all2d[0:4 * nrows, :].opt()]), cc=True)
            S.barrier()

        kalld = [Dep() for _ in range(NH)]
        valld = [Dep() for _ in range(NH)]
        klocd = [Dep() for _ in range(NH)]
        vlocd = [Dep() for _ in range(NH)]

        def coll_k(h, dep):
            S.op("pool", lambda e: e.collective_compute("AllGather", ALU.bypass, replica_groups=GROUPS,
                                                        ins=[kmsg[h, :, :].opt()], outs=[kall[h, 0:512, :].opt()]), reads=[dep], writes=[kalld[h]], cc=True)

        def coll_v(h, dep):
            S.op("pool", lambda e: e.collective_compute("AllGather", ALU.bypass, replica_groups=GROUPS,
                                                        ins=[vmsg[h, :, :].opt()], outs=[vall[h, 0:512, :].opt()]), reads=[dep], writes=[valld[h]], cc=True)

        def coll_kv():
            kv = kall.rearrange("h (r p) c -> h r p c", p=128)
            vv = vall.rearrange("h (r p) c -> h r p c", p=128)
            for h in range(NH):
                S.op("sp", lambda e, h=h: e.dma_start(out=vloc[h, :, :], in_=vv[h][P.st["c1"]]), reads=[valld[h]], writes=[vlocd[h]], dma=True)
                S.op("sp", lambda e, h=h: e.dma_start(out=kloc[h, :, :], in_=kv[h][P.st["c1"]]), reads=[kalld[h]], writes=[klocd[h]], dma=True)

        hmsg3 = hmsg.rearrange("p (kc t) -> p kc t", t=HALO)
        kmsg3 = kmsg
        vmsg4 = vmsg.rearrange("h p (b e) -> h p b e", e=128)
        vloc4 = vloc.rearrange("h p (b e) -> h p b e", e=128)
        khalo = lambda g, h: kloc[h, :, KOFF[g]:KOFF[g] + DIL[g] * 128].rearrange("p (q l) -> p q l", q=DIL[g])
        vhalo = lambda g, h: vloc4[h, :, BOFF[g]:BOFF[g] + DIL[g], :]

        step = [0]

        def go():
            step[0] += 1
            return stop is None or step[0] <= stop

        _coll = coll

        def coll(a, b, n):
            if go():
                _coll(a, b, n)

        cur = x_d
        nxt = 0
        for i in range(4):
            d = L[i]
            last = i == 3
            if i % 2 == 0:
                if go():
                  emit_attn_kv(P, {"x_in": cur, "nw": d["mnw"], "wk": d["wk"], "wv": d["wv"], "cosT": cos_d, "sinT": sin_d, "pm": pm_d,
                                 "k_own": k_own, "v_own": v_own, "kmsg": kmsg3, "vmsg": vmsg4, "coll_k": coll_k, "coll_v": coll_v})
                if go():
                    coll_kv()
                if go():
                  emit_attn_main(P, {"x_in": cur, "nw": d["mnw"], "wq": d["wq"], "wo": d["wo"], "cosT": cos_d, "sinT": sin_d, "pm": pm_d,
                                   "masks": mk_d, "k_own": k_own, "v_own": v_own, "khalo": khalo, "vhalo": vhalo, "klocd": klocd, "vlocd": vlocd,
                                   "x_out": xb[nxt], "hmsg": hmsg3})
            else:
                common = {"x_in": cur, "nw": d["mnw"], "win": d["win"], "wdt": d["wdt"], "cw": d["scw"], "vecs": d["vecs"]}
                if go():
                    emit_ssm(P, dict(common, smsg=smsg), "A")
                coll(smsg[0], sall[0], 128)
                coll(smsg[1], sall[1], 128)
                if go():
                  emit_ssm(P, dict(common, wz=d["wz"], gw=d["gw"], wout=d["wout"], x_out=xb[nxt], hmsg=hmsg3), "B")
            cur = xb[nxt]
            nxt = 1 - nxt
            coll(hmsg, hall, 128)
            io = {"x_in": cur, "nw": d["fnw"], "wup": d["wup"], "cw": d["fcw"], "wdn": d["wdn"],
                  "x_out": out_d if last else xb[nxt], "hmsg": None if (last or i % 2 == 1) else hmsg3}
            if last:
                io["fw"] = fw_d
            if go():
                emit_ffn(P, io, final_norm=last)
            if not last:
                cur = xb[nxt]
                nxt = 1 - nxt
                if i % 2 == 0:
                    coll(hmsg, hall, 128)
        if stop is not None:
            S.emit()
    return nc


def _prep_maps(inp):
    f = lambda a: np.ascontiguousarray(np.asarray(a, dtype=np.float32))
    x = f(inp["x"])
    xTs = _shards_T(x)
    shared = {"pm": perm_matrix(), "fw": _cols128(f(inp["final_norm_w"]))}
    for i in range(4):
        j = i // 2
        shared["mnw%d" % i] = _cols128(f(inp["mix_norm_w"])[i])
        shared["fnw%d" % i] = _cols128(f(inp["ffn_norm_w"])[i])
        w_up = f(inp["ffn_w_up"])[i]
        shared["wup%d" % i] = np.ascontiguousarray(w_up.reshape(KC, 128, 2, NJ, 128).transpose(3, 1, 0, 2, 4).reshape(NJ, 128, KC, 256))
        cw = np.empty((128, NJ, 2, 4), np.float32)
        cw[:, :, :, 0:3] = f(inp["ffn_conv_w"])[i].reshape(3, 2, NJ, 128).transpose(3, 2, 1, 0)
        cw[:, :, :, 3] = f(inp["ffn_conv_b"])[i].reshape(2, NJ, 128).transpose(2, 1, 0)
        shared["fcw%d" % i] = cw
        shared["wdn%d" % i] = f(inp["ffn_w_down"])[i]
        if i % 2 == 0:
            wr = f(inp["attn_w_qkv"])[j].reshape(KC, 128, NG, 3, NH, 128)
            shared["wq%d" % i] = np.ascontiguousarray(wr[:, :, :, 0].transpose(2, 3, 1, 0, 4))
            shared["wk%d" % i] = np.ascontiguousarray(wr[:, :, :, 1].transpose(2, 3, 1, 0, 4))
            shared["wv%d" % i] = np.ascontiguousarray(wr[:, :, :, 2].transpose(2, 1, 0, 3, 4).reshape(NG, 128, KC, 1024))
            shared["wo%d" % i] = f(inp["attn_w_o"])[j]
        else:
            w_in = f(inp["ssm_w_in"])[j]
            shared["win%d" % i] = np.ascontiguousarray(w_in[:, DIN:DIN + 4096].reshape(KC, 128, 32, 128).transpose(2, 1, 0, 3))
            shared["wdt%d" % i] = np.ascontiguousarray(w_in[:, DIN + 4096:].reshape(KC, 128, 32).transpose(1, 0, 2))
            scw = np.empty((128, 32, 5), np.float32)
            scw[:, :, 0:4] = f(inp["ssm_conv_w"])[j].reshape(4, 32, 128).transpose(2, 1, 0)
            scw[:, :, 4] = f(inp["ssm_conv_b"])[j].reshape(32, 128).T
            shared["scw%d" % i] = scw
            shared["vecs%d" % i] = np.ascontiguousarray(np.stack([f(inp["ssm_dt_bias"])[j], f(inp["ssm_a_log"])[j], f(inp["ssm_d"])[j]]))
            shared["wz%d" % i] = np.ascontiguousarray(w_in[:, 0:DIN].reshape(KC, 128, 4, 512).transpose(2, 1, 0, 3))
            shared["gw%d" % i] = np.ascontiguousarray(f(inp["ssm_norm_w"])[j].reshape(16, 128).T)
            shared["wout%d" % i] = np.ascontiguousarray(f(inp["ssm_w_out"])[j].reshape(16, 128, KC, 128).transpose(2, 1, 0, 3))
    maps = []
    for c in range(NCORES):
        q = c % 4
        cs, sn = rope_tables_np(q * T)
        m = dict(shared)
        fl = np.zeros((128, 8), np.float32)
        if q >= 1:
            fl[:, q - 1] = 1.0
        for r in range(4):
            if r < q:
                fl[:, 4 + r] = 1.0
        m.update({"x": xTs[c], "cosT": cs, "sinT": sn, "masks": attn_masks(q != 0), "flags": fl,
                  "pidx": np.array([[q - 1 if q >= 1 else 4, q - 2 if q >= 2 else 4, q - 3 if q >= 3 else 4, 0]], np.int32)})
        maps.append(m)
    return maps


def kernel(**inputs):
    maps = _prep_maps(inputs)
    nc = _prog("fused", build_fused)
    res = run_bass_kernel_spmd(nc, maps, core_ids=list(range(NCORES)))
    out = np.empty((2, 4 * T, D), np.float32)
    for c in range(NCORES):
        b, q = divmod(c, 4)
        out[b, q * T:(q + 1) * T, :] = np.asarray(res.results[c]["out"]).T
    return out
```

```python
import contextlib
import numpy as np
import concourse.bass as bass
import concourse.mybir as mybir
from concourse.bass_utils import run_bass_kernel_spmd

F32 = mybir.dt.float32
BF16 = mybir.dt.bfloat16
ALU = mybir.AluOpType
AF = mybir.ActivationFunctionType
AX = mybir.AxisListType

NCORES = 8
T = 2048
D = 1024
KC = 8
DFF = 2816
NJ = 22
EPS = 1e-5
HALO = 3


class Dep:
    __slots__ = ("w", "rs", "ps")

    def __init__(self, ps=False):
        self.w = None
        self.rs = []
        self.ps = ps


class Sched:
    ENGS = ("pe", "act", "dve", "pool", "sp")
    NDMA = 12

    def __init__(self, nc, es):
        self.nc = nc
        self.h = {"pe": nc.tensor, "act": nc.scalar, "dve": nc.vector,
                  "pool": nc.gpsimd, "sp": nc.sync}
        self.sems = {}
        self.cnt = {}
        self.ops = {e: [] for e in self.ENGS}
        self.seen = {e: {} for e in self.ENGS}
        for e in self.ENGS + ("cc",):
            self.sems[e] = es.enter_context(nc.semaphore("s_" + e))
            self.cnt[e] = 0
        self.dsem = {}
        self.dcnt = {}
        self.drr = {}
        for e in ("sp", "pool", "act"):
            for i in range(self.NDMA):
                k = "d_%s%d" % (e, i)
                self.sems[k] = es.enter_context(nc.semaphore(k))
                self.cnt[k] = 0
            self.drr[e] = 0

    def _need(self, eng, waits, tok, skip_same_pe=True):
        if tok is None:
            return
        k, v = tok
        if eng == "pe" and k == "pe":
            return
        if self.seen[eng].get(k, 0) >= v:
            return
        if waits.get(k, 0) < v:
            waits[k] = v

    def op(self, eng, fns, reads=(), writes=(), dma=False, cc=False):
        if not isinstance(fns, (list, tuple)):
            fns = [fns]
        waits = {}
        ps_reads = [d for d in reads if d.ps]
        if ps_reads:
            reads = [d for d in reads if not d.ps]
            writes = list(writes) + ps_reads
        for d in reads:
            self._need(eng, waits, d.w)
        for d in writes:
            self._need(eng, waits, d.w)
            for r in d.rs:
                self._need(eng, waits, r)
        if dma:
            i = self.drr[eng]
            self.drr[eng] = (i + 1) % self.NDMA
            k = "d_%s%d" % (eng, i)
            if self.cnt[k] > 0:
                self._need(eng, waits, (k, self.cnt[k]))
            self.cnt[k] += 16
            inc = 16
        elif cc:
            k = "cc"
            self.cnt[k] += 1
            inc = 1
        else:
            k = eng
            self.cnt[k] += 1
            inc = 1
        tok = (k, self.cnt[k])
        for kk, v in waits.items():
            self.seen[eng][kk] = v
        self.ops[eng].append((list(waits.items()), list(fns), k, inc))
        for d in reads:
            d.rs.append(tok)
        for d in writes:
            d.w = tok
            d.rs = []
        return tok

    def wait_all(self, eng, toks):
        waits = {}
        for t in toks:
            self._need(eng, waits, t)
        for kk, v in waits.items():
            self.seen[eng][kk] = v
        self.ops[eng].append((list(waits.items()), [], None, 0))

    def barrier(self, exclude_cc=False):
        toks = [(k, v) for k, v in self.cnt.items() if v > 0 and not (exclude_cc and k == "cc")]
        for e in self.ENGS:
            self.wait_all(e, toks)

    def replay(self, eng, h):
        for waits, fns, k, inc in self.ops[eng]:
            for kk, v in waits:
                h.wait_ge(self.sems[kk], v)
            n = len(fns)
            for i, fn in enumerate(fns):
                ins = fn(h)
                if i == n - 1:
                    ins.then_inc(self.sems[k], inc)

    pre_sp = None

    def emit(self):
        nc = self.nc
        with nc.Block() as block:
            @block.tensor
            def _(e):
                self.replay("pe", e)

            @block.scalar
            def _(e):
                self.replay("act", e)

            @block.vector
            def _(e):
                self.replay("dve", e)

            @block.gpsimd
            def _(e):
                self.replay("pool", e)

            @block.sync
            def _(e):
                if self.pre_sp is not None:
                    self.pre_sp(e)
                self.replay("sp", e)
        self.ops = {e: [] for e in self.ENGS}


class Ring:
    def __init__(self, aps, ps=False):
        self.aps = aps
        self.deps = [Dep(ps) for _ in aps]
        self.i = 0

    def next(self):
        i = self.i
        self.i = (i + 1) % len(self.aps)
        return self.aps[i], self.deps[i]


class Ctx:
    def __init__(self, nc, es, sched=None):
        self.nc = nc
        self.es = es
        self.s = sched if sched is not None else Sched(nc, es)
        self.n = Ctx.N
        Ctx.N += 1000

    N = 0

    def sb(self, shape, dt, name=None):
        self.n += 1
        return self.es.enter_context(self.nc.sbuf_tensor("%s_%d" % (name or "t", self.n), list(shape), dt))

    def ps(self, shape, dt, name=None):
        self.n += 1
        return self.es.enter_context(self.nc.psum_tensor("%s_%d" % (name or "p", self.n), list(shape), dt))


def emit_norm(cx, xT, xdeps, hT, hdep, nw, nwdep, col0, ncols, ones, onesdep, psring, sqring, rsring):
    s = cx.s
    c0 = col0
    while c0 < col0 + ncols:
        n = min(512, col0 + ncols - c0)
        ps, psd = psring.next()
        sqs = []
        for kc in range(KC):
            sq, sqd = sqring.next()
            s.op("act", lambda e, sq=sq, kc=kc, c0=c0, n=n: e.activation(out=sq[:, 0:n], in_=xT[:, kc, c0:c0 + n], func=AF.Square),
                 reads=(xdeps(kc, c0, n) if callable(xdeps) else [xdeps[kc]]), writes=[sqd])
            s.op("pe", lambda e, ps=ps, sq=sq, kc=kc, n=n: e.matmul(ps[:, 0:n], lhsT=ones[:, 0:128], rhs=sq[:, 0:n], start=(kc == 0), stop=(kc == KC - 1)),
                 reads=[sqd, onesdep], writes=[psd])
        rs, rsd = rsring.next()
        s.op("act", lambda e, rs=rs, ps=ps, n=n: e.activation(out=rs[:, 0:n], in_=ps[:, 0:n], func=AF.Sqrt, bias=ones[:, 128:129], scale=1.0 / D),
             reads=[psd, onesdep], writes=[rsd])
        s.op("dve", lambda e, rs=rs, n=n: e.reciprocal(out=rs[:, 0:n], in_=rs[:, 0:n]),
             reads=[rsd], writes=[rsd])
        for kc in range(KC):
            s.op("dve", lambda e, rs=rs, kc=kc, c0=c0, n=n: e.scalar_tensor_tensor(
                out=hT[:, kc, c0:c0 + n], in0=xT[:, kc, c0:c0 + n], scalar=nw[:, kc:kc + 1], in1=rs[:, 0:n],
                op0=ALU.mult, op1=ALU.mult),
                reads=(xdeps(kc, c0, n) if callable(xdeps) else [xdeps[kc]]) + [rsd, nwdep], writes=[hdep[c0 // 512] if isinstance(hdep, list) else hdep])
        c0 += n


def emit_halo(cx, P, dst3, dstdeps):
    s = cx.s
    hs = cx.sb([128, 4, KC * HALO], F32, "hs")
    fl = cx.sb([128, 8], F32, "fl")
    tmp = cx.sb([128, KC * HALO], F32, "htmp")
    hsd, fld, tmpd = Dep(), Dep(), Dep()
    s.op("sp", lambda e: e.dma_start(out=hs[:, :, :], in_=P.hall.rearrange("(r p) f -> p r f", p=128)), writes=[hsd], dma=True)
    s.op("sp", lambda e: e.dma_start(out=fl[:, :], in_=P.flags[:, :]), writes=[fld], dma=True)
    s.op("dve", lambda e: e.tensor_scalar(out=tmp[:, :], in0=hs[:, 0, :], scalar1=fl[:, 0:1], scalar2=0.0, op0=ALU.mult, op1=ALU.add),
         reads=[hsd, fld], writes=[tmpd])
    for r in range(1, 4):
        s.op("dve", lambda e, r=r: e.scalar_tensor_tensor(out=tmp[:, :], in0=hs[:, r, :], scalar=fl[:, r:r + 1], in1=tmp[:, :], op0=ALU.mult, op1=ALU.add),
             reads=[hsd, fld, tmpd], writes=[tmpd])
    s.op("dve", lambda e: e.tensor_copy(out=dst3, in_=tmp[:, :].rearrange("p (kc t) -> p kc t", t=HALO)), reads=[tmpd], writes=dstdeps)


def emit_ffn(P, io, final_norm=False, GJ=4):
    nc = P.nc
    x_own, nw_d, wup_d, cw_d, wdn_d, x_out = io["x_in"], io["nw"], io["wup"], io["cw"], io["wdn"], io["x_out"]
    if final_norm:
        fw_d = io["fw"]

    W = HALO + T
    with contextlib.ExitStack() as es:
        cx = Ctx(nc, es, P.S)
        s = cx.s
        xT = cx.sb([128, KC, W], F32, "xT")
        hT = cx.sb([128, KC, W], BF16, "hT")
        nw = cx.sb([128, KC], F32, "nw")
        cw = cx.sb([128, NJ, 2, 4], F32, "cw")
        ones = cx.sb([128, 129], F32, "ones")
        xd = [[Dep() for _ in range(4)] for _ in range(KC)]
        xhd = Dep()

        def xdf(kc, c0, n):
            b = c0 // 512
            deps = []
            if b == 0:
                deps.append(xhd)
            if b >= 1:
                deps.append(xd[kc][b - 1])
            if b <= 3:
                deps.append(xd[kc][b])
            return deps
        hdep = [Dep() for _ in range(5)]
        nwdep, cwdep, onesdep = Dep(), Dep(), Dep()
        psring = Ring([cx.ps([128, 512], F32, "ps") for _ in range(8)], ps=True)
        sqring = Ring([cx.sb([128, 512], F32, "sq") for _ in range(3)])
        rsring = Ring([cx.sb([128, 512], F32, "rs") for _ in range(2)])
        wupring = Ring([cx.sb([128, KC, 256], BF16, "wup") for _ in range(3)])
        ubuf = [cx.sb([128, W], F32, "u%d" % i) for i in range(2)]
        udep = [Dep(), Dep()]
        cbuf = [cx.sb([128, T], F32, "c%d" % i) for i in range(2)]
        cdep = [Dep(), Dep()]
        gring = Ring([cx.sb([128, GJ, T], BF16, "g") for _ in range(2)])
        wdring = Ring([cx.sb([128, GJ, D], BF16, "wd") for _ in range(2)])

        s.op("sp", lambda e: e.dma_start(out=nw[:, :], in_=nw_d[:, :]), writes=[nwdep], dma=True)
        s.op("sp", lambda e: e.dma_start(out=cw[:, :, :, :], in_=cw_d[:, :, :, :]), writes=[cwdep], dma=True)
        s.op("dve", lambda e: e.memset(ones[:, 0:128], 1.0), writes=[onesdep])
        s.op("dve", lambda e: e.memset(ones[:, 128:129], EPS), writes=[onesdep])
        emit_halo(cx, P, xT[:, :, 0:HALO], [xhd])
        for tb in range(4):
            for kc in range(KC):
                s.op("sp", lambda e, kc=kc, tb=tb: e.dma_start(out=xT[:, kc, HALO + tb * 512:HALO + (tb + 1) * 512],
                                                               in_=x_own[kc * 128:(kc + 1) * 128, tb * 512:(tb + 1) * 512]),
                     writes=[xd[kc][tb]], dma=True)
        if final_norm:
            fw = cx.sb([128, KC], F32, "fw")
            fwdep = Dep()
            s.op("sp", lambda e: e.dma_start(out=fw[:, :], in_=fw_d[:, :]), writes=[fwdep], dma=True)

        emit_norm(cx, xT, xdf, hT, hdep, nw, nwdep, 0, W, ones, onesdep, psring, sqring, rsring)

        wdn_v = wdn_d.rearrange("(j p) d -> p j d", p=128)
        j = 0
        groups = []
        while j < NJ:
            groups.append(list(range(j, min(NJ, j + GJ))))
            j += GJ
        def emit_down(grp, g, gdep, wd, wddep):
            for dc in range(KC):
                for tb in range(4):
                    ps, psd = psring.next()
                    s.op("pe", [lambda e, ps=ps, wd=wd, g=g, jj=jj, dc=dc, tb=tb, n=len(grp): e.matmul(
                        ps[:, :], lhsT=wd[:, jj, dc * 128:(dc + 1) * 128], rhs=g[:, jj, tb * 512:(tb + 1) * 512],
                        start=(jj == 0), stop=(jj == n - 1)) for jj in range(len(grp))],
                        reads=[wddep, gdep], writes=[psd])
                    c0 = HALO + tb * 512
                    s.op("dve", lambda e, ps=ps, dc=dc, c0=c0: e.tensor_tensor(
                        out=xT[:, dc, c0:c0 + 512], in0=ps[:, :], in1=xT[:, dc, c0:c0 + 512], op=ALU.add),
                        reads=[psd, xd[dc][tb]], writes=[xd[dc][tb]])

        pending = None
        for grp in groups:
            g, gdep = gring.next()
            wd, wddep = wdring.next()
            s.op("pool", lambda e, wd=wd, grp=grp: e.dma_start(out=wd[:, 0:len(grp), :], in_=wdn_v[:, grp[0]:grp[0] + len(grp), :]),
                 writes=[wddep], dma=True)
            for jj, j in enumerate(grp):
                wup, wupdep = wupring.next()
                s.op("pool", lambda e, wup=wup, j=j: e.dma_start(out=wup[:, :, :], in_=wup_d[j, :, :, :]), writes=[wupdep], dma=True)
                for half in range(2):
                    u = ubuf[half]
                    ps, psd = psring.next()
                    s.op("pe", [lambda e, ps=ps, wup=wup, kc=kc, half=half: e.matmul(
                        ps[:, 0:HALO], lhsT=wup[:, kc, half * 128:(half + 1) * 128], rhs=hT[:, kc, 0:HALO],
                        start=(kc == 0), stop=(kc == KC - 1)) for kc in range(KC)],
                        reads=[wupdep, hdep[0]], writes=[psd])
                    s.op("act", lambda e, ps=ps, u=u: e.activation(out=u[:, 0:HALO], in_=ps[:, 0:HALO], func=AF.Copy),
                         reads=[psd], writes=[udep[half]])
                    for tb in range(4):
                        ps, psd = psring.next()
                        c0 = HALO + tb * 512
                        s.op("pe", [lambda e, ps=ps, wup=wup, kc=kc, half=half, c0=c0: e.matmul(
                            ps[:, :], lhsT=wup[:, kc, half * 128:(half + 1) * 128], rhs=hT[:, kc, c0:c0 + 512],
                            start=(kc == 0), stop=(kc == KC - 1)) for kc in range(KC)],
                            reads=[wupdep, hdep[c0 // 512], hdep[(c0 + 511) // 512]], writes=[psd])
                        s.op("act", lambda e, ps=ps, u=u, c0=c0: e.activation(out=u[:, c0:c0 + 512], in_=ps[:, :], func=AF.Copy),
                             reads=[psd], writes=[udep[half]])
                    c = cbuf[half]
                    ceng = "dve" if half == 0 else CONV_ENG2
                    s.op(ceng, lambda e, u=u, c=c, j=j, half=half: e.tensor_scalar(
                        out=c[:, :], in0=u[:, 3:3 + T], scalar1=cw[:, j, half, 2:3], scalar2=cw[:, j, half, 3:4],
                        op0=ALU.mult, op1=ALU.add), reads=[udep[half], cwdep], writes=[cdep[half]])
                    s.op(ceng, lambda e, u=u, c=c, j=j, half=half: e.scalar_tensor_tensor(
                        out=c[:, :], in0=u[:, 2:2 + T], scalar=cw[:, j, half, 1:2], in1=c[:, :],
                        op0=ALU.mult, op1=ALU.add), reads=[udep[half], cwdep, cdep[half]], writes=[cdep[half]])
                    s.op(ceng, lambda e, u=u, c=c, j=j, half=half: e.scalar_tensor_tensor(
                        out=c[:, :], in0=u[:, 1:1 + T], scalar=cw[:, j, half, 0:1], in1=c[:, :],
                        op0=ALU.mult, op1=ALU.add), reads=[udep[half], cwdep, cdep[half]], writes=[cdep[half]])
                    if half == 0:
                        s.op("act", lambda e, c=c: e.activation(out=c[:, :], in_=c[:, :], func=AF.Silu),
                             reads=[cdep[0]], writes=[cdep[0]])
                if jj == 0 and pending is not None:
                    emit_down(*pending)
                    pending = None
                s.op("dve", lambda e, g=g, jj=jj: e.tensor_tensor(out=g[:, jj, :], in0=cbuf[0][:, :], in1=cbuf[1][:, :], op=ALU.mult),
                     reads=[cdep[0], cdep[1]], writes=[gdep])
            pending = (grp, g, gdep, wd, wddep)
        emit_down(*pending)
        outtoks = []
        if final_norm:
            for tb in range(4):
                c0 = HALO + tb * 512
                ps, psd = psring.next()
                for kc in range(KC):
                    sq, sqd = sqring.next()
                    s.op("act", lambda e, sq=sq, kc=kc, c0=c0: e.activation(out=sq[:, :], in_=xT[:, kc, c0:c0 + 512], func=AF.Square),
                         reads=[xd[kc][tb]], writes=[sqd])
                    s.op("pe", lambda e, ps=ps, sq=sq, kc=kc: e.matmul(ps[:, :], lhsT=ones[:, 0:128], rhs=sq[:, :], start=(kc == 0), stop=(kc == KC - 1)),
                         reads=[sqd, onesdep], writes=[psd])
                rs, rsd = rsring.next()
                s.op("act", lambda e, rs=rs, ps=ps: e.activation(out=rs[:, :], in_=ps[:, :], func=AF.Sqrt, bias=ones[:, 128:129], scale=1.0 / D),
                     reads=[psd, onesdep], writes=[rsd])
                s.op("dve", lambda e, rs=rs: e.reciprocal(out=rs[:, :], in_=rs[:, :]),
                     reads=[rsd], writes=[rsd])
                for kc in range(KC):
                    s.op("dve", lambda e, rs=rs, kc=kc, c0=c0: e.scalar_tensor_tensor(
                        out=xT[:, kc, c0:c0 + 512], in0=xT[:, kc, c0:c0 + 512], scalar=fw[:, kc:kc + 1], in1=rs[:, :],
                        op0=ALU.mult, op1=ALU.mult), reads=[xd[kc][tb], rsd, fwdep], writes=[xd[kc][tb]])
        for kc in range(KC):
            outtoks.append(s.op("sp", lambda e, kc=kc: e.dma_start(out=x_out[kc * 128:(kc + 1) * 128, :], in_=xT[:, kc, HALO:W]),
                                reads=xd[kc], dma=True))
        if io.get("hmsg") is not None:
            s.op("sp", lambda e: e.dma_start(out=io["hmsg"], in_=xT[:, :, W - HALO:W]), reads=[xd[kc][3] for kc in range(KC)], dma=True)
        s.barrier()
        s.emit()


def _cols128(v):
    return np.ascontiguousarray(v.reshape(-1, 128).T)


def _shards_T(x):
    out = []
    for c in range(NCORES):
        b, q = divmod(c, 4)
        out.append(np.ascontiguousarray(x[b, q * T:(q + 1) * T, :].T))
    return out


def _halo_cols(xTs, n=HALO):
    out = []
    for c in range(NCORES):
        if c % 4 == 0:
            h = np.zeros((D, n), np.float32)
        else:
            h = xTs[c - 1][:, T - n:]
        out.append(np.ascontiguousarray(h.reshape(KC, 128, n).transpose(1, 0, 2)))
    return out


_PROGS = {}


def _prog(key, fn):
    if key not in _PROGS:
        _PROGS[key] = fn()
    return _PROGS[key]


def run_ffn(xTs, nw, w_up, conv_w, conv_b, w_down, final_w=None):
    nc = _prog(("ffn", final_w is not None), lambda: build_ffn(final_norm=final_w is not None))
    wup = np.empty((NJ, 128, KC, 256), np.float32)
    wr = w_up.reshape(KC, 128, 2, NJ, 128)
    wup[:] = wr.transpose(3, 1, 0, 2, 4).reshape(NJ, 128, KC, 256)
    cw = np.empty((128, NJ, 2, 4), np.float32)
    cwr = conv_w.reshape(3, 2, NJ, 128)
    cw[:, :, :, 0:3] = cwr.transpose(3, 2, 1, 0)
    cw[:, :, :, 3] = conv_b.reshape(2, NJ, 128).transpose(2, 1, 0)
    halos = _halo_cols(xTs)
    maps = []
    for c in range(NCORES):
        m = {"x_own": xTs[c], "x_halo": halos[c], "nw": _cols128(nw), "wup": wup, "cw": cw,
             "wdn": np.ascontiguousarray(w_down)}
        if final_w is not None:
            m["fw"] = _cols128(final_w)
        maps.append(m)
    res = run_bass_kernel_spmd(nc, maps, core_ids=list(range(NCORES)))
    return [np.asarray(r["x_out"]) for r in res.results]


NG = 3
NH = 8
DIL = (1, 4, 16)
SCALE = 128.0 ** -0.5


def colview(t2d, d, c0, n):
    if d == 1:
        return t2d[:, c0:c0 + n], 1
    L = T // d
    v = t2d.rearrange("p (l r) -> p r l", r=d)
    r0, l0 = divmod(c0, L)
    if n <= L:
        return v[:, r0, l0:l0 + n], 1
    return v[:, r0:r0 + n // L, :], n // L


def v3(ap, A):
    if A == 1:
        return ap
    return ap.rearrange("p (a b) -> p a b", a=A)


def emit_rope(cx, ps, psd, full, dstdep, d, t0, cosT, sinT, tabdep, pm, pmdep, pkring, t1ring, tbring):
    s = cx.s
    if d == 1:
        tmp, tmpd = full[:, t0:t0 + 512], dstdep
    else:
        tmp, tmpd = tbring.next()
        tmp = tmp[:, :]
    s.op("act", lambda e: e.activation(out=tmp, in_=ps[:, :], func=AF.Copy), reads=[psd], writes=[tmpd])
    pk, pkd = pkring.next()
    s.op("pe", lambda e: e.matmul(pk[0:32, :], lhsT=pm[0:32, 0:32], rhs=tmp[0:32, :], start=True, stop=True),
         reads=[tmpd, pmdep], writes=[pkd])
    t1, t1d = t1ring.next()
    t2, t2d = t1ring.next()
    s.op("dve", lambda e: e.tensor_tensor(out=t1[0:32, :], in0=ps[0:32, :], in1=cosT[0:32, t0:t0 + 512], op=ALU.mult),
         reads=[psd, tabdep], writes=[t1d])
    s.op("dve", lambda e: e.tensor_tensor(out=t2[0:32, :], in0=pk[0:32, :], in1=sinT[0:32, t0:t0 + 512], op=ALU.mult),
         reads=[pkd, tabdep], writes=[t2d])
    s.op("dve", lambda e: e.tensor_tensor(out=tmp[0:32, :], in0=t1[0:32, :], in1=t2[0:32, :], op=ALU.add),
         reads=[t1d, t2d, tmpd], writes=[tmpd])
    if d != 1:
        n = 512 // d
        l0 = t0 // d
        dv = full.rearrange("p (r l) -> p l r", r=d)[:, l0:l0 + n, :]
        s.op("act", lambda e: e.activation(out=dv, in_=tmp.rearrange("p (l r) -> p l r", r=d), func=AF.Copy), reads=[tmpd], writes=[dstdep])


def load_norm_h(cx, x_own, nw, nwdep, hT, hdep, ones, onesdep, psring, sqring, rsring, xbring):
    s = cx.s
    for tb in range(4):
        xb, xbd = xbring.next()
        for kc in range(KC):
            s.op("sp", lambda e, xb=xb, kc=kc, tb=tb: e.dma_start(out=xb[:, kc, :], in_=x_own[kc * 128:(kc + 1) * 128, tb * 512:(tb + 1) * 512]),
                 writes=[xbd], dma=True)
        ps, psd = psring.next()
        for kc in range(KC):
            sq, sqd = sqring.next()
            s.op("act", lambda e, sq=sq, xb=xb, kc=kc: e.activation(out=sq[:, :], in_=xb[:, kc, :], func=AF.Square),
                 reads=[xbd], writes=[sqd])
            s.op("pe", lambda e, ps=ps, sq=sq, kc=kc: e.matmul(ps[:, :], lhsT=ones[:, 0:128], rhs=sq[:, :], start=(kc == 0), stop=(kc == KC - 1)),
                 reads=[sqd, onesdep], writes=[psd])
        rs, rsd = rsring.next()
        s.op("act", lambda e, rs=rs, ps=ps: e.activation(out=rs[:, :], in_=ps[:, :], func=AF.Sqrt, bias=ones[:, 128:129], scale=1.0 / D),
             reads=[psd, onesdep], writes=[rsd])
        s.op("dve", lambda e, rs=rs: e.reciprocal(out=rs[:, :], in_=rs[:, :]), reads=[rsd], writes=[rsd])
        for kc in range(KC):
            s.op("dve", lambda e, rs=rs, xb=xb, kc=kc, tb=tb: e.scalar_tensor_tensor(
                out=hT[:, kc, tb * 512:(tb + 1) * 512], in0=xb[:, kc, :], scalar=nw[:, kc:kc + 1], in1=rs[:, :],
                op0=ALU.mult, op1=ALU.mult), reads=[xbd, rsd, nwdep], writes=[hdep])


KOFF = (0, 128, 640)
BOFF = (0, 1, 5)
KMSG = 2688


def emit_attn_kv(P, io):
    nc = P.nc
    x_own, nw_d, wk_d, wv_d, cos_d, sin_d, pm_d = io["x_in"], io["nw"], io["wk"], io["wv"], io["cosT"], io["sinT"], io["pm"]
    k_out, v_out, kmsg, vmsg = io["k_own"], io["v_own"], io["kmsg"], io["vmsg"]
    kmd = [Dep() for _ in range(NH)]
    vmd = [Dep() for _ in range(NH)]
    with contextlib.ExitStack() as es:
        cx = Ctx(nc, es, P.S)
        s = cx.s
        hT = cx.sb([128, KC, T], BF16, "hT")
        hdep = Dep()
        nw = cx.sb([128, KC], F32, "nw")
        ones = cx.sb([128, 129], F32, "ones")
        cosT = cx.sb([128, T], F32, "cosT")
        sinT = cx.sb([128, T], F32, "sinT")
        pm = cx.sb([128, 128], BF16, "pm")
        nwdep, onesdep, tabdep, pmdep = Dep(), Dep(), Dep(), Dep()
        psring = Ring([cx.ps([128, 512], F32, "ps") for _ in range(6)], ps=True)
        pkring = Ring([cx.ps([128, 512], F32, "pk") for _ in range(2)], ps=True)
        sqring = Ring([cx.sb([128, 512], F32, "sq") for _ in range(3)])
        rsring = Ring([cx.sb([128, 512], F32, "rs") for _ in range(2)])
        xbring = Ring([cx.sb([128, KC, 512], F32, "xb") for _ in range(2)])
        t1ring = Ring([cx.sb([128, 512], F32, "t1") for _ in range(4)])
        tbring = Ring([cx.sb([128, 512], BF16, "tb16") for _ in range(4)])
        wkring = Ring([cx.sb([128, KC, 128], BF16, "wk") for _ in range(3)])
        wvring = Ring([cx.sb([128, KC, 1024], BF16, "wv") for _ in range(2)])
        kring = Ring([cx.sb([128, T], BF16, "kt") for _ in range(2)])
        vstage = cx.sb([128, NH, 16, 128], BF16, "vst")
        vsdep = Dep()

        s.op("sp", lambda e: e.dma_start(out=nw[:, :], in_=nw_d[:, :]), writes=[nwdep], dma=True)
        s.op("sp", lambda e: e.dma_start(out=cosT[:, :], in_=cos_d[:, :]), writes=[tabdep], dma=True)
        s.op("sp", lambda e: e.dma_start(out=sinT[:, :], in_=sin_d[:, :]), writes=[tabdep], dma=True)
        s.op("pool", lambda e: e.dma_start(out=pm[:, :], in_=pm_d[:, :]), writes=[pmdep], dma=True)
        s.op("dve", lambda e: e.memset(ones[:, 0:128], 1.0), writes=[onesdep])
        s.op("dve", lambda e: e.memset(ones[:, 128:129], EPS), writes=[onesdep])
        load_norm_h(cx, x_own, nw, nwdep, hT, hdep, ones, onesdep, psring, sqring, rsring, xbring)
        for kc in range(KC):
            s.op("sp", lambda e, kc=kc: e.dma_start(out=P.hsave[:, kc, 0:T], in_=hT[:, kc, :]), reads=[hdep], dma=True)

        outtoks = []
        for g in range(NG):
            d = DIL[g]
            wv, wvdep = wvring.next()
            s.op("pool", lambda e, wv=wv, g=g: e.dma_start(out=wv[:, :, :], in_=wv_d[g, :, :, :]), writes=[wvdep], dma=True)
            for blk in range(16):
                for half in range(2):
                    ps, psd = psring.next()
                    fns = []
                    for kc in range(KC):
                        tv, _ = colview(hT[:, kc, :], d, blk * 128, 128)
                        fns.append(lambda e, ps=ps, tv=tv, wv=wv, kc=kc, half=half: e.matmul(
                            ps[:, :], lhsT=tv, rhs=wv[:, kc, half * 512:(half + 1) * 512], start=(kc == 0), stop=(kc == KC - 1)))
                    s.op("pe", fns, reads=[hdep, wvdep], writes=[psd])
                    eng = "act" if half == 0 else "dve"
                    if eng == "act":
                        s.op("act", lambda e, ps=ps, blk=blk, half=half: e.activation(
                            out=vstage[:, half * 4:(half + 1) * 4, blk, :], in_=ps[:, :].rearrange("p (h e) -> p h e", h=4), func=AF.Copy),
                            reads=[psd], writes=[vsdep])
                    else:
                        s.op("dve", lambda e, ps=ps, blk=blk, half=half: e.tensor_copy(
                            out=vstage[:, half * 4:(half + 1) * 4, blk, :], in_=ps[:, :].rearrange("p (h e) -> p h e", h=4)),
                            reads=[psd], writes=[vsdep])
            nbg = 16 // d
            for h in range(NH):
                outtoks.append(s.op("sp", lambda e, g=g, h=h: e.dma_start(out=v_out[g, h, :, :, :], in_=vstage[:, h, :, :]),
                                    reads=[vsdep], dma=True))
                s.op("sp", lambda e, g=g, h=h, d=d, nbg=nbg: e.dma_start(
                    out=vmsg[h, :, BOFF[g]:BOFF[g] + d, :],
                    in_=vstage[:, h, :, :].rearrange("p (r n) e -> p r n e", r=d)[:, :, nbg - 1, :]), reads=[vsdep], writes=[vmd[h]], dma=True)
        quads = [(h, g, qd) for h in range(NH) for g in range(NG) for qd in range(4)]
        qst = {}
        tiles = {}

        def kP(i):
            h, g, qd = quads[i]
            d = DIL[g]
            if qd == 0:
                wk, wkdep = wkring.next()
                s.op("pool", lambda e: e.dma_start(out=wk[:, :, :], in_=wk_d[g, h, :, :, :]), writes=[wkdep], dma=True)
                kt, ktdep = kring.next()
                tiles[(h, g)] = (wk, wkdep, kt, ktdep)
                if g == 0 and h == 0:
                    for hh in range(NH):
                        io["coll_v"](hh, vmd[hh])
                if g == 1 and h >= 1:
                    io["coll_k"](h - 1, kmd[h - 1])
            wk, wkdep, kt, ktdep = tiles[(h, g)]
            ps, psd = psring.next()
            s.op("pe", [lambda e, kc=kc: e.matmul(ps[:, :], lhsT=wk[:, kc, :], rhs=hT[:, kc, qd * 512:(qd + 1) * 512], start=(kc == 0), stop=(kc == KC - 1))
                        for kc in range(KC)], reads=[hdep, wkdep], writes=[psd])
            t0 = qd * 512
            if d == 1:
                tmp, tmpd = kt[:, t0:t0 + 512], ktdep
            else:
                tmp, tmpd = tbring.next()
                tmp = tmp[:, :]
            s.op("act", lambda e: e.activation(out=tmp, in_=ps[:, :], func=AF.Copy), reads=[psd], writes=[tmpd])
            qst[i] = dict(ps=ps, psd=psd, tmp=tmp, tmpd=tmpd, kt=kt, ktdep=ktdep, d=d, t0=t0, h=h, g=g, qd=qd)

        def kR23(i):
            q = qst[i]
            ps, psd, tmp, tmpd, t0 = q["ps"], q["psd"], q["tmp"], q["tmpd"], q["t0"]
            pk, pkd = pkring.next()
            s.op("pe", lambda e: e.matmul(pk[0:32, :], lhsT=pm[0:32, 0:32], rhs=tmp[0:32, :], start=True, stop=True), reads=[tmpd, pmdep], writes=[pkd])
            t1, t1d = t1ring.next()
            t2, t2d = t1ring.next()
            s.op("dve", lambda e: e.tensor_tensor(out=t1[0:32, :], in0=ps[0:32, :], in1=cosT[0:32, t0:t0 + 512], op=ALU.mult), reads=[psd, tabdep], writes=[t1d])
            s.op("dve", lambda e: e.tensor_tensor(out=t2[0:32, :], in0=pk[0:32, :], in1=sinT[0:32, t0:t0 + 512], op=ALU.mult), reads=[pkd, tabdep], writes=[t2d])
            s.op("dve", lambda e: e.tensor_tensor(out=tmp[0:32, :], in0=t1[0:32, :], in1=t2[0:32, :], op=ALU.add), reads=[t1d, t2d, tmpd], writes=[tmpd])

        def kR4(i):
            q = qst.pop(i)
            tmp, tmpd, kt, ktdep, d, t0, h, g, qd = q["tmp"], q["tmpd"], q["kt"], q["ktdep"], q["d"], q["t0"], q["h"], q["g"], q["qd"]
            if d != 1:
                n = 512 // d
                l0 = t0 // d
                dv = kt[:, :].rearrange("p (r l) -> p l r", r=d)[:, l0:l0 + n, :]
                s.op("act", lambda e: e.activation(out=dv, in_=tmp.rearrange("p (l r) -> p l r", r=d), func=AF.Copy), reads=[tmpd], writes=[ktdep])
            if qd == 3:
                outtoks.append(s.op("sp", lambda e: e.dma_start(out=k_out[g, h, :, :], in_=kt[:, :]), reads=[ktdep], dma=True))
                Lg = T // d
                s.op("sp", lambda e: e.dma_start(out=kmsg[h, :, KOFF[g]:KOFF[g] + d * 128].rearrange("p (r l) -> p r l", r=d),
                                                 in_=kt[:, :].rearrange("p (r l) -> p r l", r=d)[:, :, Lg - 128:Lg]), reads=[ktdep], writes=[kmd[h]], dma=True)

        NQ = len(quads)
        for it in range(NQ + 2):
            if 0 <= it - 2 < NQ:
                kR4(it - 2)
            if 0 <= it - 1 < NQ:
                kR23(it - 1)
            if it < NQ:
                kP(it)
        io["coll_k"](NH - 1, kmd[NH - 1])
        s.barrier(exclude_cc=True)
        s.emit()


def rope_tables_np(pos0):
    pos = np.arange(pos0, pos0 + T, dtype=np.float32)
    inv = (np.float32(500000.0) ** (-np.arange(0, 32, 2, dtype=np.float32) / np.float32(32))).astype(np.float32)
    ang = (pos[None, :] * inv[:, None]).astype(np.float32)
    c = np.ones((128, T), np.float32)
    sn = np.zeros((128, T), np.float32)
    c[0:16] = np.cos(ang)
    c[16:32] = np.cos(ang)
    sn[0:16] = -np.sin(ang)
    sn[16:32] = np.sin(ang)
    return c, sn


def perm_matrix():
    pm = np.zeros((128, 128), np.float32)
    for e in range(16):
        pm[e + 16, e] = 1.0
        pm[e, e + 16] = 1.0
    return pm


def run_attn_kv(xTs, nw, w_qkv):
    nc = _prog("attn_kv", build_attn_kv)
    wr = w_qkv.reshape(KC, 128, NG, 3, NH, 128)
    wk = np.ascontiguousarray(wr[:, :, :, 1].transpose(2, 3, 1, 0, 4))
    wv = np.ascontiguousarray(wr[:, :, :, 2].transpose(2, 1, 0, 3, 4).reshape(NG, 128, KC, 1024))
    pm = perm_matrix()
    maps = []
    for c in range(NCORES):
        cs, sn = rope_tables_np((c % 4) * T)
        maps.append({"x_own": xTs[c], "nw": _cols128(nw), "wk": wk, "wv": wv, "cosT": cs, "sinT": sn, "pm": pm})
    res = run_bass_kernel_spmd(nc, maps, core_ids=list(range(NCORES)))
    return [(np.asarray(r["k_out"]), np.asarray(r["v_out"])) for r in res.results]


def emit_attn_main(P, io):
    nc = P.nc
    x_own, nw_d, wq_d, wo_d, cos_d, sin_d, pm_d, mk_d = io["x_in"], io["nw"], io["wq"], io["wo"], io["cosT"], io["sinT"], io["pm"], io["masks"]
    k_own, v_own, x_out = io["k_own"], io["v_own"], io["x_out"]
    with contextlib.ExitStack() as es:
        cx = Ctx(nc, es, P.S)
        s = cx.s
        hT = cx.sb([128, KC, T], BF16, "hT")
        hdep = Dep()
        aT = cx.sb([128, NH, T], BF16, "aT")
        adeps = [Dep() for _ in range(NH)]
        nw = cx.sb([128, KC], F32, "nw")
        ones = cx.sb([128, 129], F32, "ones")
        onesb = cx.sb([128, 128], BF16, "onesb")
        cosT = cx.sb([128, T], F32, "cosT")
        sinT = cx.sb([128, T], F32, "sinT")
        pm = cx.sb([128, 128], BF16, "pm")
        mk = cx.sb([128, 3, 512], BF16, "mk")
        idm = cx.sb([128, 128], BF16, "idm")
        idmdep = Dep()
        nwdep, onesdep, tabdep, pmdep, mkdep, onesbdep = Dep(), Dep(), Dep(), Dep(), Dep(), Dep()
        psring = Ring([cx.ps([128, 512], F32, "ps") for _ in range(2)], ps=True)
        pkring = Ring([cx.ps([128, 512], F32, "pk") for _ in range(1)], ps=True)
        sring = Ring([cx.ps([128, 512], F32, "pss") for _ in range(3)], ps=True)
        odring = Ring([cx.ps([128, 512], F32, "pod") for _ in range(2)], ps=True)
        sqring = Ring([cx.sb([128, 512], F32, "sq") for _ in range(2)])
        rsring = Ring([cx.sb([128, 512], F32, "rs") for _ in range(2)])
        t1ring = Ring([cx.sb([128, 512], F32, "t1") for _ in range(4)])
        tbring = Ring([cx.sb([128, 512], BF16, "tb16") for _ in range(3)])
        wqring = Ring([cx.sb([128, KC, 128], BF16, "wq") for _ in range(3)])
        woring = Ring([cx.sb([128, NH, 128], BF16, "wo") for _ in range(2)])
        kring = Ring([cx.sb([128, 4096], BF16, "ks") for _ in range(2)])
        vring = Ring([cx.sb([128, 4096], BF16, "vs") for _ in range(2)])
        qring = Ring([cx.sb([128, T], BF16, "qs") for _ in range(2)])
        pring = Ring([cx.sb([128, 512], BF16, "pT") for _ in range(3)])
        acc = cx.sb([128, 2, T], F32, "acc")
        accdep = Dep()
        rden = cx.sb([128, T], F32, "rden")
        rdendep = Dep()
        oring = Ring([cx.sb([128, 512], F32, "ob") for _ in range(6)])

        s.op("sp", lambda e: e.dma_start(out=nw[:, :], in_=nw_d[:, :]), writes=[nwdep], dma=True)
        s.op("sp", lambda e: e.dma_start(out=cosT[:, :], in_=cos_d[:, :]), writes=[tabdep], dma=True)
        s.op("sp", lambda e: e.dma_start(out=sinT[:, :], in_=sin_d[:, :]), writes=[tabdep], dma=True)
        s.op("pool", lambda e: e.dma_start(out=pm[:, :], in_=pm_d[:, :]), writes=[pmdep], dma=True)
        s.op("pool", lambda e: e.dma_start(out=mk[:, :, :], in_=mk_d[:, :, :]), writes=[mkdep], dma=True)
        s.op("dve", lambda e: e.memset(ones[:, 0:128], 1.0), writes=[onesdep])
        s.op("dve", lambda e: e.memset(ones[:, 128:129], EPS), writes=[onesdep])
        s.op("dve", lambda e: e.memset(onesb[:, :], 1.0), writes=[onesbdep])
        s.op("pool", lambda e: e.memset(idm[:, :], 1.0), writes=[idmdep])
        s.op("pool", lambda e: e.affine_select(out=idm[:, :], in_=idm[:, :], pattern=[[1, 128]], compare_op=ALU.is_equal, fill=0.0, base=0, channel_multiplier=-1),
             reads=[idmdep], writes=[idmdep])
        for kc in range(KC):
            s.op("sp", lambda e, kc=kc: e.dma_start(out=hT[:, kc, :], in_=P.hsave[:, kc, 0:T]), writes=[hdep], dma=True)

        def QP(h, g):
            d = DIL[g]
            L = T // d
            nb = L // 128
            LK = 128 + L
            ks, ksdep = kring.next()
            vs, vsdep = vring.next()
            ksv = ks[:, 0:d * LK].rearrange("p (r l) -> p r l", r=d)
            vsv = vs[:, 0:d * (nb + 1) * 128].rearrange("p (r n e) -> p r n e", r=d, n=nb + 1)
            s.op("sp", lambda e, ksv=ksv, g=g, h=h, d=d: e.dma_start(
                out=ksv[:, :, 0:128], in_=io["khalo"](g, h)),
                reads=[io["klocd"][h]], writes=[ksdep], dma=True)
            s.op("sp", lambda e, ksv=ksv, g=g, h=h, d=d, LK=LK: e.dma_start(
                out=ksv[:, :, 128:LK], in_=k_own[g, h, :, :].rearrange("p (r l) -> p r l", r=d)),
                writes=[ksdep], dma=True)
            s.op("sp", lambda e, vsv=vsv, g=g, h=h, d=d: e.dma_start(
                out=vsv[:, :, 0, :], in_=io["vhalo"](g, h)),
                reads=[io["vlocd"][h]], writes=[vsdep], dma=True)
            s.op("sp", lambda e, vsv=vsv, g=g, h=h, d=d, nb=nb: e.dma_start(
                out=vsv[:, :, 1:nb + 1, :], in_=v_own[g, h, :, :, :].rearrange("p (r n) e -> p r n e", r=d)),
                writes=[vsdep], dma=True)
            wq, wqdep = wqring.next()
            s.op("pool", lambda e, wq=wq, g=g, h=h: e.dma_start(out=wq[:, :, :], in_=wq_d[g, h, :, :, :]), writes=[wqdep], dma=True)
            qs, qsdep = qring.next()
            for qd in range(4):
                ps, psd = psring.next()
                s.op("pe", [lambda e, ps=ps, wq=wq, kc=kc, qd=qd: e.matmul(
                    ps[:, :], lhsT=wq[:, kc, :], rhs=hT[:, kc, qd * 512:(qd + 1) * 512], start=(kc == 0), stop=(kc == KC - 1)) for kc in range(KC)],
                    reads=[hdep, wqdep], writes=[psd])
                emit_rope(cx, ps, psd, qs[:, :], qsdep, d, qd * 512, cosT, sinT, tabdep, pm, pmdep, pkring, t1ring, tbring)

            return dict(ksv=ksv, vsv=vsv, qs=qs, ksdep=ksdep, vsdep=vsdep, qsdep=qsdep, d=d, nb=nb)

        def UN(h, g, st):
            ksv, vsv, qs, ksdep, vsdep, qsdep, d, nb = st['ksv'], st['vsv'], st['qs'], st['ksdep'], st['vsdep'], st['qsdep'], st['d'], st['nb']
            def qk(pr, ksv=ksv, qs=qs, nb=nb, ksdep=ksdep, qsdep=qsdep, d=d):
                pss, pssd = sring.next()
                fns = []
                for uu in range(2):
                    u = pr * 2 + uu
                    r, n = divmod(u, nb)
                    for half in range(2):
                        fns.append(lambda e, pss=pss, r=r, n=n, u=u, uu=uu, half=half: e.matmul(
                            pss[:, uu * 256 + half * 128: uu * 256 + half * 128 + 128],
                            lhsT=ksv[:, r, (n + half) * 128:(n + half + 1) * 128],
                            rhs=qs[:, u * 128:(u + 1) * 128], start=(uu == 0 and half == 0), stop=False, skip_group_check=True))
                if d == 1:
                    var = 0 if pr == 0 else 1
                elif d == 4:
                    var = 0 if pr % 2 == 0 else 1
                else:
                    var = 2
                fns.append(lambda e, pss=pss, var=var: e.matmul(pss[:, :], lhsT=idm[:, :], rhs=mk[:, var, :], start=False, stop=True, skip_group_check=True))
                s.op("pe", fns, reads=[ksdep, qsdep, mkdep, idmdep], writes=[pssd])
                return pss, pssd

            def rest(pr, pss, pssd, vsv=vsv, d=d, g=g, nb=nb, vsdep=vsdep):
                pT, pTd = pring.next()
                s.op("act", lambda e: e.activation(out=pT[:, :], in_=pss[:, :], func=AF.Exp, scale=SCALE),
                     reads=[pssd], writes=[pTd])
                pod, podd = odring.next()
                fns = []
                for uu in range(2):
                    u = pr * 2 + uu
                    r, n = divmod(u, nb)
                    for half in range(2):
                        fns.append(lambda e, r=r, n=n, uu=uu, half=half: e.matmul(
                            pod[:, uu * 128:(uu + 1) * 128], lhsT=vsv[:, r, n + half, :],
                            rhs=pT[:, uu * 256 + half * 128: uu * 256 + half * 128 + 128],
                            start=(half == 0), stop=(half == 1)))
                    for half in range(2):
                        fns.append(lambda e, uu=uu, half=half: e.matmul(
                            pod[:, 256 + uu * 128: 256 + (uu + 1) * 128], lhsT=onesb[:, :],
                            rhs=pT[:, uu * 256 + half * 128: uu * 256 + half * 128 + 128],
                            start=(half == 0), stop=(half == 1)))
                s.op("pe", fns, reads=[vsdep, pTd, onesbdep], writes=[podd])
                return pod, podd

            def rest2(pr, pod, podd, d=d, g=g):
                if d == 16:
                    for w in range(2):
                        av, A = colview(acc[:, w, :], d, pr * 256, 256)
                        src = v3(pod[:, w * 256:(w + 1) * 256], A)
                        s.op("dve", lambda e, av=av, src=src: e.tensor_tensor(out=av, in0=src, in1=av, op=ALU.add),
                             reads=[podd, accdep], writes=[accdep])
                else:
                    if d == 1:
                        av = acc[:, :, pr * 256:(pr + 1) * 256]
                    else:
                        L4 = T // 4
                        r0, l0 = divmod(pr * 256, L4)
                        av = acc[:, :, :].rearrange("p w (l r) -> p w r l", r=4)[:, :, r0, l0:l0 + 256]
                    src = pod[:, :].rearrange("p (w c) -> p w c", w=2)
                    if g == 0:
                        s.op("dve", lambda e, av=av, src=src: e.tensor_copy(out=av, in_=src), reads=[podd], writes=[accdep])
                    else:
                        s.op("dve", lambda e, av=av, src=src: e.tensor_tensor(out=av, in0=src, in1=av, op=ALU.add),
                             reads=[podd, accdep], writes=[accdep])

            prev = qk(0)
            pend = None
            for pr in range(8):
                nxt = qk(pr + 1) if pr + 1 < 8 else None
                pod, podd = rest(pr, *prev)
                if pend is not None:
                    rest2(*pend)
                pend = (pr, pod, podd)
                prev = nxt
            rest2(*pend)
            if g == NG - 1:
                s.op("dve", lambda e: e.reciprocal(out=rden[:, :], in_=acc[:, 1, :]), reads=[accdep], writes=[rdendep])
                s.op("dve", lambda e, h=h: e.tensor_tensor(out=aT[:, h, :], in0=acc[:, 0, :], in1=rden[:, :], op=ALU.mult),
                     reads=[accdep, rdendep], writes=[adeps[h]])

        units = [(h, g) for h in range(NH) for g in range(NG)]
        stq = QP(*units[0])
        for ui, (h, g) in enumerate(units):
            nstq = QP(*units[ui + 1]) if ui + 1 < len(units) else None
            UN(h, g, stq)
            stq = nstq

        wo_v = wo_d.rearrange("(h e) d -> e h d", e=128)
        wops = Ring(psring.aps + sring.aps + odring.aps)
        wops.deps = psring.deps + sring.deps + odring.deps
        outtoks = []
        for dc in range(KC):
            wo, wodep = woring.next()
            s.op("pool", lambda e, wo=wo, dc=dc: e.dma_start(out=wo[:, :, :], in_=wo_v[:, :, dc * 128:(dc + 1) * 128]), writes=[wodep], dma=True)
            for tb in range(4):
                ps, psd = wops.next()
                s.op("pe", [lambda e, ps=ps, wo=wo, h=h, tb=tb: e.matmul(
                    ps[:, :], lhsT=wo[:, h, :], rhs=aT[:, h, tb * 512:(tb + 1) * 512], start=(h == 0), stop=(h == NH - 1))
                    for h in range(NH)], reads=[wodep] + adeps, writes=[psd])
                ob, obd = oring.next()
                s.op("sp", lambda e, ob=ob, dc=dc, tb=tb: e.dma_start(out=ob[:, :], in_=x_own[dc * 128:(dc + 1) * 128, tb * 512:(tb + 1) * 512]),
                     writes=[obd], dma=True)
                s.op("dve", lambda e, ob=ob, ps=ps: e.tensor_tensor(out=ob[:, :], in0=ps[:, :], in1=ob[:, :], op=ALU.add),
                     reads=[psd, obd], writes=[obd])
                outtoks.append(s.op("sp", lambda e, ob=ob, dc=dc, tb=tb: e.dma_start(
                    out=x_out[dc * 128:(dc + 1) * 128, tb * 512:(tb + 1) * 512], in_=ob[:, :]), reads=[obd], dma=True))
                if tb == 3 and io.get("hmsg") is not None:
                    s.op("sp", lambda e, ob=ob, dc=dc: e.dma_start(out=io["hmsg"][:, dc, :], in_=ob[:, 512 - HALO:512]), reads=[obd], dma=True)
        s.barrier()
        s.emit()


def attn_masks(valid):
    j = np.arange(128)[:, None]
    i = np.arange(128)[None, :]
    prevN = (j >= i).astype(np.float32)
    cur = (j <= i).astype(np.float32)
    prevH = prevN * np.float32(1.0 if valid else 0.0)
    m = np.zeros((128, 3, 512), np.float32)
    for v, (a, b) in enumerate(((prevH, prevN), (prevN, prevN), (prevH, prevH))):
        m[:, v, 0:128] = a
        m[:, v, 128:256] = cur
        m[:, v, 256:384] = b
        m[:, v, 384:512] = cur
    return np.where(m > 0.5, np.float32(0.0), np.float32(-30000.0)).astype(np.float32)


def run_attn(xTs, nw, w_qkv, w_o):
    kv = run_attn_kv(xTs, nw, w_qkv)
    nc = _prog("attn_main", build_attn_main)
    wr = w_qkv.reshape(KC, 128, NG, 3, NH, 128)
    wq = np.ascontiguousarray(wr[:, :, :, 0].transpose(2, 3, 1, 0, 4))
    pm = perm_matrix()
    maps = []
    for c in range(NCORES):
        cs, sn = rope_tables_np((c % 4) * T)
        k_own, v_own = kv[c]
        k_halo = np.zeros_like(k_own)
        v_halo = np.zeros_like(v_own)
        if c % 4 != 0:
            kp, vp = kv[c - 1]
            for g in range(NG):
                d = DIL[g]
                L = T // d
                nb = L // 128
                k_halo[g, :, :, 0:d * 128] = kp[g].reshape(NH, 128, d, L)[:, :, :, L - 128:].reshape(NH, 128, d * 128)
                v_halo[g, :, :, 0:d, :] = vp[g].reshape(NH, 128, d, nb, 128)[:, :, :, nb - 1, :]
        maps.append({"x_own": xTs[c], "nw": _cols128(nw), "wq": wq, "wo": np.ascontiguousarray(w_o),
                     "cosT": cs, "sinT": sn, "pm": pm, "masks": attn_masks(c % 4 != 0),
                     "k_own": k_own, "k_halo": k_halo, "v_own": v_own, "v_halo": v_halo})
    res = run_bass_kernel_spmd(nc, maps, core_ids=list(range(NCORES)))
    return [np.asarray(r["x_out"]) for r in res.results]


DIN = 2048
NHS = 32
CONV_ENG2 = "dve"
OFF_ENG = "pool"


def emit_ssm(P, io, phase):
    debug = False
    B_ = phase == "B"
    nc = P.nc
    dbg = {}
    x_own, nw_d, win_d, wdt_d, cw_d, vec_d = io["x_in"], io["nw"], io["win"], io["wdt"], io["cw"], io["vecs"]
    if B_:
        wz_d, gw_d, wout_d, x_out = io["wz"], io["gw"], io["wout"], io["x_out"]
    else:
        smsg = io["smsg"]
    W = HALO + T
    with contextlib.ExitStack() as es:
        cx = Ctx(nc, es, P.S)
        s = cx.s
        hT = cx.sb([128, KC, W], BF16, "hT")
        hdep = Dep()
        nw = cx.sb([128, KC], F32, "nw")
        ones = cx.sb([128, 129], F32, "ones")
        U = cx.sb([128, 128], F32, "U")
        idb = cx.sb([128, 128], BF16, "idb")
        cw = cx.sb([128, 32, 5], F32, "cw")
        wdt = cx.sb([128, KC, 32], BF16, "wdt")
        vecs = cx.sb([128, 3, 32], F32, "vecs")
        nwdep, onesdep, Udep, iddep, cwdep, wdtdep, vecdep = [Dep() for _ in range(7)]
        psring = Ring([cx.ps([128, 512], F32, "ps") for _ in range(3)], ps=True)
        ptr = cx.ps([128, 8, 128], BF16, "ptr")
        ptrd = Dep(True)
        misc = cx.ps([128, 512], F32, "misc")
        miscd = Dep(True)
        cbk = cx.ps([128, 512], F32, "cbk")
        cbkd = Dep(True)
        ybk = cx.ps([128, 512], F32, "ybk")
        ybkd = Dep(True)
        stbk = cx.ps([128, 512], F32, "stbk")
        stbkd = Dep(True)
        ybring = Ring([ybk, stbk])
        ybring.deps = [ybkd, stbkd]
        sqring = Ring([cx.sb([128, 256], F32, "sq") for _ in range(2)])
        rsring = Ring([cx.sb([128, 256], F32, "rs") for _ in range(2)])
        if not B_:
            xbr = Ring([cx.sb([128, KC, 512], F32, "xb") for _ in range(2)])
            sq5 = Ring([cx.sb([128, 512], F32, "sq5") for _ in range(3)])
            rs5 = Ring([cx.sb([128, 512], F32, "rs5") for _ in range(2)])
        wring = Ring([cx.sb([128, KC, 128], BF16, "w") for _ in range(3)])
        rawring = Ring([cx.sb([128, 3 + 512], F32, "raw") for _ in range(2)])
        cvring = Ring([cx.sb([128, 512], F32, "cv") for _ in range(3)])
        xcring = Ring([cx.sb([128, 512], BF16, "xc") for _ in range(2)])
        tail = cx.sb([128, 32, 3], F32, "tail")
        taild = [Dep() for _ in range(32)]
        xtok = cx.sb([128, 4, DIN], BF16, "xtok")
        xtokd = Dep()
        btok = cx.sb([128, 4, 1024], BF16, "btok")
        btokd = Dep()
        BT = cx.sb([128, 8, 512], BF16, "BT")
        BTd = Dep()
        dt = cx.sb([128, 4, 32], F32, "dt")
        adt = cx.sb([128, 4, 32], F32, "adt")
        dtd, adtd = Dep(), Dep()
        sm = cx.sb([128, 8, 4, 32], F32, "sm")
        smd = [Dep() for _ in range(8)]
        tsum = cx.sb([128, 32], F32, "tsum")
        tsumd = Dep()
        xdtd = cx.sb([128, DIN], BF16, "xdtd")
        xdtdd = Dep()
        S = cx.sb([128, DIN], F32, "S")
        Sd = Dep()
        if B_:
            CT = cx.sb([128, 8, 512], BF16, "CT")
            CTd = Dep()
            wzring = Ring([cx.sb([128, KC, 512], BF16, "wz") for _ in range(1)])
            sz = cx.sb([128, 4, DIN], BF16, "sz")
            szd = Dep()
            gw = cx.sb([128, 16], F32, "gw")
            gwd = Dep()
            xdt = cx.sb([128, DIN], BF16, "xdt")
            xsD = cx.sb([128, DIN], BF16, "xsD")
            xdt_d, xsD_d = Dep(), Dep()
            Sb = cx.sb([128, DIN], BF16, "Sb")
            Sbd = Dep()
            cbmq = [cx.sb([128, 4, 128], F32, "cbmq%d" % i) for i in range(2)]
            cbmqd = [Dep(), Dep()]
            ttring = Ring([cx.sb([128, 512], F32, "tt") for _ in range(2)])
            lxring = Ring([cx.sb([128, 512], F32, "lx") for _ in range(2)])
            mtring = Ring([cx.sb([128, 512], BF16, "mt") for _ in range(2)])
            ytring = Ring([cx.sb([128, 256], F32, "yt") for _ in range(2)])
            y = cx.sb([128, DIN], F32, "y")
            yd = Dep()
            xb = y[:, :].rearrange("p (kc c) -> p kc c", kc=KC)
            xbd = yd
            junk = cx.sb([128, 256], BF16, "junk")
            junkd = Dep()
            ssq = cx.sb([128, 16], F32, "ssq")
            ssqd = Dep()
            gn = cx.sb([128, DIN], BF16, "gn")
            gnd = Dep()
            gnT = cx.sb([128, 16, 512], BF16, "gnT")
            gnTd = Dep()
            woring = Ring([cx.sb([128, 16, 128], BF16, "wo") for _ in range(2)])
            oring = Ring([cx.sb([128, 512], F32, "ob") for _ in range(2)])
            spl = cx.sb([128, 3, 32], F32, "spl")
            spld = Dep()

        s.op("sp", lambda e: e.dma_start(out=nw[:, :], in_=nw_d[:, :]), writes=[nwdep], dma=True)
        s.op("sp", lambda e: e.dma_start(out=cw[:, :, :], in_=cw_d[:, :, :]), writes=[cwdep], dma=True)
        s.op("pool", lambda e: e.dma_start(out=wdt[:, :, :], in_=wdt_d[:, :, :]), writes=[wdtdep], dma=True)
        for i in range(3):
            s.op("sp", lambda e, i=i: e.dma_start(out=vecs[:, i, :], in_=vec_d[i, :].partition_broadcast(128)), writes=[vecdep], dma=True)
        s.op("dve", lambda e: e.memset(ones[:, 0:128], 1.0), writes=[onesdep])
        s.op("dve", lambda e: e.memset(ones[:, 128:129], EPS), writes=[onesdep])
        s.op("pool", lambda e: e.memset(U[:, :], 1.0), writes=[Udep])
        s.op("pool", lambda e: e.affine_select(out=U[:, :], in_=U[:, :], pattern=[[1, 128]], compare_op=ALU.is_ge, fill=0.0, base=0, channel_multiplier=-1),
             reads=[Udep], writes=[Udep])
        s.op("pool", lambda e: e.memset(idb[:, :], 1.0), writes=[iddep])
        s.op("pool", lambda e: e.affine_select(out=idb[:, :], in_=idb[:, :], pattern=[[1, 128]], compare_op=ALU.is_equal, fill=0.0, base=0, channel_multiplier=-1),
             reads=[iddep], writes=[iddep])
        s.op("act", lambda e: e.activation(out=vecs[:, 1, :], in_=vecs[:, 1, :], func=AF.Exp), reads=[vecdep], writes=[vecdep])
        s.op("dve", lambda e: e.tensor_scalar(out=vecs[:, 1, :], in0=vecs[:, 1, :], scalar1=-1.0, scalar2=0.0, op0=ALU.mult, op1=ALU.add), reads=[vecdep], writes=[vecdep])
        s.op("dve", lambda e: e.memset(tsum[:, :], 0.0), writes=[tsumd])
        def s_init():
            HS = SMSG // 2
            sallA = P.sall[0].rearrange("(r p) f -> r p f", p=128)
            sallB = P.sall[1].rearrange("(r p) f -> r p f", p=128)
            fl2 = cx.sb([128, 8], F32, "fl2")
            fl2d = Dep()
            s.op("sp", lambda e: e.dma_start(out=fl2[:, :], in_=P.flags[:, :]), writes=[fl2d], dma=True)
            s.op("dve", lambda e: e.memset(S[:, :], 0.0), writes=[Sd])
            for r in range(4):
                s.op("sp", lambda e, r=r: e.dma_start(out=spl[:, 0, :], in_=sallB[r, :, DIN - HS:DIN - HS + NHS]), writes=[spld], dma=True)
                s.op("sp", lambda e, r=r: e.dma_start(out=y[:, 0:HS], in_=sallA[r, :, :]), writes=[yd], dma=True)
                s.op("sp", lambda e, r=r: e.dma_start(out=y[:, HS:DIN], in_=sallB[r, :, 0:DIN - HS]), writes=[yd], dma=True)
                s.op("act", lambda e, r=r: e.activation(out=spl[:, 1, :], in_=spl[:, 0, :], func=AF.Exp, scale=fl2[:, 4 + r:5 + r]), reads=[spld, fl2d], writes=[spld])
                s.op("dve", lambda e: e.tensor_tensor(out=S[:, :].rearrange("p (h q) -> p h q", h=NHS), in0=S[:, :].rearrange("p (h q) -> p h q", h=NHS),
                                                      in1=spl[:, 1, :].unsqueeze(2).to_broadcast([128, NHS, 64]), op=ALU.mult), reads=[Sd, spld], writes=[Sd])
                s.op("dve", lambda e, r=r: e.scalar_tensor_tensor(out=S[:, :], in0=y[:, :], scalar=fl2[:, 4 + r:5 + r], in1=S[:, :], op0=ALU.mult, op1=ALU.add),
                     reads=[Sd, yd, fl2d], writes=[Sd])
            s.op("act", lambda e: e.activation(out=Sb[:, :], in_=S[:, :], func=AF.Copy), reads=[Sd], writes=[Sbd])

        if B_:
            s.op("sp", lambda e: e.dma_start(out=gw[:, :], in_=gw_d[:, :]), writes=[gwd], dma=True)
            pass
        else:
            s.op("dve", lambda e: e.memset(S[:, :], 0.0), writes=[Sd])

        def norm_cols(load_fn, n, c0):
            if not B_:
                xb, xbd = xbr.next()
                sqring_, rsring_ = sq5, rs5
            ps, psd = psring.next()
            load_fn(xb, xbd)
            for kc in range(KC):
                sq, sqd = sqring_.next()
                s.op("act", lambda e, sq=sq, kc=kc: e.activation(out=sq[:, 0:n], in_=xb[:, kc, 0:n], func=AF.Square), reads=[xbd], writes=[sqd])
                s.op("pe", lambda e, ps=ps, sq=sq, kc=kc: e.matmul(ps[:, 0:n], lhsT=ones[:, 0:128], rhs=sq[:, 0:n], start=(kc == 0), stop=(kc == KC - 1)),
                     reads=[sqd, onesdep], writes=[psd])
            rs, rsd = rsring_.next()
            s.op("act", lambda e, rs=rs, ps=ps: e.activation(out=rs[:, 0:n], in_=ps[:, 0:n], func=AF.Sqrt, bias=ones[:, 128:129], scale=1.0 / D),
                 reads=[psd, onesdep], writes=[rsd])
            s.op("dve", lambda e, rs=rs: e.reciprocal(out=rs[:, 0:n], in_=rs[:, 0:n]), reads=[rsd], writes=[rsd])
            for kc in range(KC):
                s.op("dve", lambda e, rs=rs, kc=kc: e.scalar_tensor_tensor(out=hT[:, kc, c0:c0 + n], in0=xb[:, kc, 0:n], scalar=nw[:, kc:kc + 1], in1=rs[:, 0:n],
                                                                         op0=ALU.mult, op1=ALU.mult), reads=[xbd, rsd, nwdep], writes=[hdep])

        if B_:
            for kc in range(KC):
                s.op("sp", lambda e, kc=kc: e.dma_start(out=hT[:, kc, :], in_=P.hsave[:, kc, :]), writes=[hdep], dma=True)
        else:
            norm_cols(lambda xb, xbd: emit_halo(cx, P, xb[:, :, 0:HALO], [xbd]), HALO, 0)
            for blk in range(T // 512):
                def ld(xb, xbd, blk=blk):
                    for kc in range(KC):
                        s.op("sp", lambda e, kc=kc: e.dma_start(out=xb[:, kc, :], in_=x_own[kc * 128:(kc + 1) * 128, blk * 512:(blk + 1) * 512]), writes=[xbd], dma=True)
                norm_cols(ld, 512, HALO + blk * 512)
            for kc in range(KC):
                s.op("sp", lambda e, kc=kc: e.dma_start(out=P.hsave[:, kc, :], in_=hT[:, kc, :]), reads=[hdep], dma=True)

        def bc64(ap32):
            return ap32.unsqueeze(2).to_broadcast([128, ap32.shape[1], 64])

        def h64(ap):
            return ap.rearrange("p (h q) -> p h q", q=64)

        outtoks = []
        deferred = []
        nchunks = 32 if B_ else 24
        for tb in range(4):
            c0 = HALO + tb * 512
            pst = {}

            def s1(cc, tb=tb, c0=c0):
                w, wdep = wring.next()
                s.op("pool", lambda e: e.dma_start(out=w[:, :, :], in_=win_d[cc, :, :, :]), writes=[wdep], dma=True)
                raw, rawd = rawring.next()
                if tb == 0:
                    s.op("pe", [lambda e, kc=kc: e.matmul(misc[:, 0:HALO], lhsT=w[:, kc, :], rhs=hT[:, kc, 0:HALO], start=(kc == 0), stop=(kc == KC - 1))
                                for kc in range(KC)], reads=[wdep, hdep], writes=[miscd])
                    s.op("act", lambda e: e.activation(out=raw[:, 0:HALO], in_=misc[:, 0:HALO], func=AF.Copy), reads=[miscd], writes=[rawd])
                else:
                    s.op("act", lambda e: e.activation(out=raw[:, 0:HALO], in_=tail[:, cc, :], func=AF.Copy), reads=[taild[cc]], writes=[rawd])
                ps, psd = psring.next()
                s.op("pe", [lambda e, kc=kc: e.matmul(ps[:, :], lhsT=w[:, kc, :], rhs=hT[:, kc, c0:c0 + 512], start=(kc == 0), stop=(kc == KC - 1))
                            for kc in range(KC)], reads=[wdep, hdep], writes=[psd])
                s.op("act", lambda e: e.activation(out=raw[:, HALO:HALO + 512], in_=ps[:, :], func=AF.Copy), reads=[psd], writes=[rawd])
                if tb < 3:
                    s.op("act", lambda e: e.activation(out=tail[:, cc, :], in_=raw[:, 512:515], func=AF.Copy), reads=[rawd], writes=[taild[cc]])
                pst[cc] = {"raw": raw, "rawd": rawd}

            def s2(cc):
                raw, rawd = pst[cc]["raw"], pst[cc]["rawd"]
                cv, cvd = cvring.next()
                eng = "dve" if cc % 2 == 0 else CONV_ENG2
                s.op(eng, lambda e: e.tensor_scalar(out=cv[:, :], in0=raw[:, 3:515], scalar1=cw[:, cc, 3:4], scalar2=cw[:, cc, 4:5],
                                                    op0=ALU.mult, op1=ALU.add), reads=[rawd, cwdep], writes=[cvd])
                for k in range(3):
                    s.op(eng, lambda e, k=k: e.scalar_tensor_tensor(out=cv[:, :], in0=raw[:, k:k + 512], scalar=cw[:, cc, k:k + 1], in1=cv[:, :],
                                                                    op0=ALU.mult, op1=ALU.add), reads=[rawd, cwdep, cvd], writes=[cvd])
                pst[cc].update({"cv": cv, "cvd": cvd})

            def s3a(cc):
                cv, cvd = pst[cc]["cv"], pst[cc]["cvd"]
                if cc < 16:
                    xc, xcd = xcring.next()
                    s.op("act", lambda e: e.activation(out=xc[:, :], in_=cv[:, :], func=AF.Silu), reads=[cvd], writes=[xcd])
                    pst[cc].update({"src": xc, "srcd": xcd})
                elif cc < 24:
                    gi = cc - 16
                    s.op("act", lambda e: e.activation(out=BT[:, gi, :], in_=cv[:, :], func=AF.Silu), reads=[cvd], writes=[BTd])
                    pst[cc].update({"src": BT[:, gi, :], "srcd": BTd})
                else:
                    gi = cc - 24
                    s.op("act", lambda e: e.activation(out=CT[:, gi, :], in_=cv[:, :], func=AF.Silu), reads=[cvd], writes=[CTd])

            def s3b(cc):
                if cc >= 24:
                    return
                src, srcd = pst[cc]["src"], pst[cc]["srcd"]
                s.op("pe", [lambda e, q=q: e.transpose(out=ptr[:, q, :], in_=src[:, q * 128:(q + 1) * 128], identity=idb[:, :]) for q in range(4)],
                     reads=[srcd, iddep], writes=[ptrd])

            def s3c(cc):
                if cc >= 24:
                    return
                if cc < 16:
                    s.op("act", lambda e: e.activation(out=xtok[:, :, cc * 128:(cc + 1) * 128], in_=ptr[:, 0:4, :], func=AF.Copy), reads=[ptrd], writes=[xtokd])
                else:
                    gi = cc - 16
                    s.op("act", lambda e: e.activation(out=btok[:, :, gi * 128:(gi + 1) * 128], in_=ptr[:, 0:4, :], func=AF.Copy), reads=[ptrd], writes=[btokd])

            clist = list(range(24, 32)) if B_ else list(range(24))
            ncl = len(clist)
            if B_:
                s.op("sp", lambda e, tb=tb: e.dma_start(out=xtok[:, :, :], in_=P.xs_save[tb].rearrange("p (c f) -> p c f", c=4)), writes=[xtokd], dma=True)
                s.op("sp", lambda e, tb=tb: e.dma_start(out=btok[:, :, :], in_=P.bs_save[tb].rearrange("p (c f) -> p c f", c=4)), writes=[btokd], dma=True)
                s.op("sp", lambda e, tb=tb: e.dma_start(out=BT[:, :, :], in_=P.bt_save[tb].rearrange("p (c f) -> p c f", c=8)), writes=[BTd], dma=True)
                s.op("sp", lambda e, tb=tb: e.dma_start(out=dt[:, :, :], in_=P.dt_save[tb].rearrange("p (c f) -> p c f", c=4)), writes=[dtd], dma=True)
            for it in range(ncl + 3):
                if 0 <= it - 3 < ncl:
                    s3c(clist[it - 3])
                if it < ncl:
                    s1(clist[it])
                if 0 <= it - 2 < ncl:
                    s3b(clist[it - 2])
                if it < ncl:
                    s2(clist[it])
                if 0 <= it - 1 < ncl:
                    s3a(clist[it - 1])
            if not B_:
                for ck in range(4):
                    t0 = c0 + ck * 128
                    s.op("pe", [lambda e, kc=kc, t0=t0: e.matmul(misc[:, 0:32], lhsT=hT[:, kc, t0:t0 + 128], rhs=wdt[:, kc, :], start=(kc == 0), stop=(kc == KC - 1))
                                for kc in range(KC)], reads=[hdep, wdtdep], writes=[miscd])
                    s.op("dve", lambda e, ck=ck: e.tensor_tensor(out=dt[:, ck, :], in0=misc[:, 0:32], in1=vecs[:, 0, :], op=ALU.add), reads=[miscd, vecdep], writes=[dtd])
                s.op("act", lambda e: e.activation(out=dt[:, :, :], in_=dt[:, :, :], func=AF.Exp), reads=[dtd], writes=[dtd])
                s.op("act", lambda e: e.activation(out=dt[:, :, :], in_=dt[:, :, :], func=AF.Ln, bias=1.0), reads=[dtd], writes=[dtd])
                s.op("sp", lambda e, tb=tb: e.dma_start(out=P.xs_save[tb].rearrange("p (c f) -> p c f", c=4), in_=xtok[:, :, :]), reads=[xtokd], dma=True)
                s.op("sp", lambda e, tb=tb: e.dma_start(out=P.bs_save[tb].rearrange("p (c f) -> p c f", c=4), in_=btok[:, :, :]), reads=[btokd], dma=True)
                s.op("sp", lambda e, tb=tb: e.dma_start(out=P.bt_save[tb].rearrange("p (c f) -> p c f", c=8), in_=BT[:, :, :]), reads=[BTd], dma=True)
                s.op("sp", lambda e, tb=tb: e.dma_start(out=P.dt_save[tb].rearrange("p (c f) -> p c f", c=4), in_=dt[:, :, :]), reads=[dtd], dma=True)
            s.op("dve", lambda e: e.tensor_tensor(out=adt[:, :, :], in0=dt[:, :, :], in1=vecs[:, 1:2, :].to_broadcast([128, 4, 32]), op=ALU.mult),
                 reads=[dtd, vecdep], writes=[adtd])
            if B_:
                for cb in range(4):
                    wz, wzdep = wzring.next()
                    s.op("pool", lambda e, wz=wz, cb=cb: e.dma_start(out=wz[:, :, :], in_=wz_d[cb, :, :, :]), writes=[wzdep], dma=True)
                    for ck in range(4):
                        t0 = c0 + ck * 128
                        ps, psd = psring.next()
                        s.op("pe", [lambda e, ps=ps, wz=wz, kc=kc, t0=t0: e.matmul(ps[:, :], lhsT=hT[:, kc, t0:t0 + 128], rhs=wz[:, kc, :], start=(kc == 0), stop=(kc == KC - 1))
                                    for kc in range(KC)], reads=[hdep, wzdep], writes=[psd])
                        s.op("act", lambda e, ps=ps, ck=ck, cb=cb: e.activation(out=sz[:, ck, cb * 512:(cb + 1) * 512], in_=ps[:, :], func=AF.Silu), reads=[psd], writes=[szd])
            if B_ and tb == 0:
                s_init()
            mv = misc[:, 0:256].rearrange("p (c w h) -> p c w h", c=4, w=2)
            fns = []
            for ck in range(4):
                fns.append(lambda e, ck=ck: e.matmul(misc[:, ck * 64:ck * 64 + 32], lhsT=U[:, :], rhs=adt[:, ck, :], start=True, stop=True))
                fns.append(lambda e, ck=ck: e.matmul(misc[:, ck * 64 + 32:ck * 64 + 64], lhsT=ones[:, 0:128], rhs=adt[:, ck, :], start=True, stop=True))
            s.op("pe", fns, reads=[Udep, onesdep, adtd], writes=[miscd])
            s.op("act", lambda e: e.activation(out=sm[:, 0, :, :], in_=mv[:, :, 0, :], func=AF.Copy), reads=[miscd], writes=[smd[0]])
            s.op("act", lambda e: e.activation(out=sm[:, 7, :, :], in_=mv[:, :, 1, :], func=AF.Copy), reads=[miscd], writes=[smd[7]])
            s.op("dve", lambda e: e.tensor_tensor(out=sm[:, 1, :, :], in0=sm[:, 7, :, :], in1=sm[:, 0, :, :], op=ALU.subtract), reads=[smd[7], smd[0]], writes=[smd[1]])
            s.op("act", lambda e: e.activation(out=sm[:, 2, :, :], in_=sm[:, 1, :, :], func=AF.Exp), reads=[smd[1]], writes=[smd[2]])
            s.op("act", lambda e: e.activation(out=sm[:, 4, :, :], in_=sm[:, 7, :, :], func=AF.Exp), reads=[smd[7]], writes=[smd[4]])
            for ck in range(4):
                s.op("dve", lambda e, ck=ck: e.tensor_tensor(out=tsum[:, :], in0=sm[:, 7, ck, :], in1=tsum[:, :], op=ALU.add), reads=[smd[7], tsumd], writes=[tsumd])
            s.op("dve", lambda e: e.tensor_tensor(out=sm[:, 6, :, :], in0=dt[:, :, :], in1=sm[:, 2, :, :], op=ALU.mult), reads=[dtd, smd[2]], writes=[smd[6]])
            if B_:
                s.op("act", lambda e: e.activation(out=sm[:, 3, :, :], in_=sm[:, 0, :, :], func=AF.Exp), reads=[smd[0]], writes=[smd[3]])
                s.op("dve", lambda e: e.tensor_scalar(out=sm[:, 5, :, :], in0=sm[:, 0, :, :], scalar1=-1.0, scalar2=0.0, op0=ALU.mult, op1=ALU.add), reads=[smd[0]], writes=[smd[5]])
            for ck in range(4):
                k0 = ck * 128
                s.op(OFF_ENG, lambda e, ck=ck: e.tensor_tensor(out=h64(xdtd[:, :]), in0=h64(xtok[:, ck, :]), in1=bc64(sm[:, 6, ck, :]), op=ALU.mult),
                     reads=[xtokd, smd[6]], writes=[xdtdd])
                if B_:
                    s.op(OFF_ENG, lambda e, ck=ck: e.tensor_tensor(out=h64(xdt[:, :]), in0=h64(xtok[:, ck, :]), in1=bc64(dt[:, ck, :]), op=ALU.mult),
                         reads=[xtokd, dtd], writes=[xdt_d])
                    s.op(OFF_ENG, lambda e, ck=ck: e.tensor_tensor(out=h64(xsD[:, :]), in0=h64(xtok[:, ck, :]), in1=bc64(vecs[:, 2, :]), op=ALU.mult),
                         reads=[xtokd, vecdep], writes=[xsD_d])
                    for q in range(2):
                        s.op("pe", [lambda e, q=q, gg=gg, k0=k0: e.matmul(cbk[:, gg * 128:(gg + 1) * 128], lhsT=BT[:, 4 * q + gg, k0:k0 + 128], rhs=CT[:, 4 * q + gg, k0:k0 + 128],
                                                                       start=True, stop=True) for gg in range(4)], reads=[BTd, CTd], writes=[cbkd])
                        s.op("dve", lambda e, q=q: e.tensor_tensor(out=cbmq[q][:, :, :], in0=cbk[:, :].rearrange("p (a b) -> p a b", a=4),
                                                                    in1=U[:, :].unsqueeze(1).to_broadcast([128, 4, 128]), op=ALU.mult), reads=[cbkd, Udep], writes=[cbmqd[q]])

                    def stA(g, ck=ck):
                        ps, psd = psring.next()
                        s.op("pe", [lambda e, ps=ps, r=r, hh=4 * g + r: e.matmul(ps[:, r * 128:(r + 1) * 128], lhsT=adt[:, ck, hh:hh + 1].to_broadcast([128, 128]), rhs=U[:, :],
                                                                              start=True, stop=True) for r in range(4)], reads=[adtd, Udep], writes=[psd])
                        return ps, psd

                    def stB1(g, ps, psd, ck=ck):
                        tt, ttd = ttring.next()
                        s.op("dve", lambda e: e.tensor_tensor(out=tt[:, :].rearrange("p (a b) -> p a b", a=4), in0=ps[:, :].rearrange("p (a b) -> p a b", a=4),
                                                              in1=sm[:, 5, ck, 4 * g:4 * g + 4].unsqueeze(2).to_broadcast([128, 4, 128]), op=ALU.add), reads=[psd, smd[5]], writes=[ttd])
                        lx, lxd = lxring.next()
                        s.op("act", lambda e: e.activation(out=lx[:, :], in_=tt[:, :], func=AF.Exp), reads=[ttd], writes=[lxd])
                        return lx, lxd

                    def stB2(g, lx, lxd):
                        mt, mtd = mtring.next()
                        q, gg = divmod(g, 4)
                        s.op("dve", lambda e: e.scalar_tensor_tensor(out=mt[:, :].rearrange("p (a b) -> p a b", a=4), in0=lx[:, :].rearrange("p (a b) -> p a b", a=4), scalar=1.0,
                                                                     in1=cbmq[q][:, gg:gg + 1, :].to_broadcast([128, 4, 128]), op0=ALU.min, op1=ALU.mult),
                             reads=[lxd, cbmqd[q]], writes=[mtd])
                        return mt, mtd

                    def stC1(g, mt, mtd, k0=k0):
                        yb, ybd = ybring.next()
                        fns = [lambda e: e.matmul(yb[:, 0:256], lhsT=idb[:, :], rhs=xsD[:, g * 256:(g + 1) * 256], start=True, stop=False)]
                        for r in range(4):
                            hh = 4 * g + r
                            fns.append(lambda e, r=r, hh=hh: e.matmul(yb[:, r * 64:(r + 1) * 64], lhsT=mt[:, r * 128:(r + 1) * 128], rhs=xdt[:, hh * 64:(hh + 1) * 64], start=False, stop=(r == 3)))
                        fns.append(lambda e: e.matmul(yb[:, 256:512], lhsT=CT[:, g, k0:k0 + 128], rhs=Sb[:, g * 256:(g + 1) * 256], start=True, stop=True))
                        s.op("pe", fns, reads=[iddep, xsD_d, mtd, xdt_d, CTd, Sbd], writes=[ybd])
                        return yb, ybd

                    def stC2(g, yb, ybd, ck=ck):
                        yt, ytd = ytring.next()
                        s.op("dve", lambda e: e.tensor_tensor(out=h64(yt[:, :]), in0=h64(yb[:, 256:512]), in1=bc64(sm[:, 3, ck, 4 * g:4 * g + 4]), op=ALU.mult),
                             reads=[ybd, smd[3]], writes=[ytd])
                        s.op("dve", lambda e: e.tensor_tensor(out=y[:, g * 256:(g + 1) * 256], in0=yb[:, 0:256], in1=yt[:, :], op=ALU.add),
                             reads=[ybd, ytd], writes=[yd])

                    As = {0: stA(0), 1: stA(1)}
                    Ls = {0: stB1(0, *As[0])}
                    Ys = {}
                    for gi_ in range(8):
                        if gi_ + 2 < 8:
                            As[gi_ + 2] = stA(gi_ + 2)
                        if gi_ + 1 < 8:
                            Ls[gi_ + 1] = stB1(gi_ + 1, *As[gi_ + 1])
                        mt, mtd = stB2(gi_, *Ls[gi_])
                        Ys[gi_] = stC1(gi_, mt, mtd)
                        if gi_ >= 2 and deferred:
                            deferred.pop(0)()
                        if gi_ >= 1:
                            stC2(gi_ - 1, *Ys[gi_ - 1])
                    stC2(7, *Ys[7])
                if debug and tb == 0 and ck == 0:
                    outtoks.append(s.op("sp", lambda e: e.dma_start(out=dbg["g_dt"][:, :, :], in_=dt[:, :, :]), reads=[dtd], dma=True))
                    outtoks.append(s.op("sp", lambda e: e.dma_start(out=dbg["g_sm"][:, :, :], in_=sm[:, :, :]), reads=smd, dma=True))
                    outtoks.append(s.op("sp", lambda e: e.dma_start(out=dbg["g_xtok"][:, :, :], in_=xtok[:, :, :]), reads=[xtokd], dma=True))
                    outtoks.append(s.op("sp", lambda e: e.dma_start(out=dbg["g_btok"][:, :, :], in_=btok[:, :, :]), reads=[btokd], dma=True))
                    outtoks.append(s.op("sp", lambda e: e.dma_start(out=dbg["g_hT"][:, :, :], in_=hT[:, :, :]), reads=[hdep], dma=True))
                    if B_:
                        outtoks.append(s.op("sp", lambda e: e.dma_start(out=dbg["g_y"][:, :], in_=y[:, :]), reads=[yd], dma=True))
                        outtoks.append(s.op("sp", lambda e: e.dma_start(out=dbg["g_sz"][:, :, :], in_=sz[:, :, :]), reads=[szd], dma=True))
                s.op("dve", lambda e, ck=ck: e.tensor_tensor(out=h64(S[:, :]), in0=h64(S[:, :]), in1=bc64(sm[:, 4, ck, :]), op=ALU.mult), reads=[Sd, smd[4]] + ([Sbd] if B_ else []), writes=[Sd])
                for gp in range(4):
                    stp, stpd = psring.next()
                    s.op("pe", [lambda e, g=g, ck=ck, stp=stp: e.matmul(stp[:, (g % 2) * 256:(g % 2) * 256 + 256], lhsT=btok[:, ck, g * 128:(g + 1) * 128], rhs=xdtd[:, g * 256:(g + 1) * 256],
                                                                      start=True, stop=True) for g in (2 * gp, 2 * gp + 1)], reads=[btokd, xdtdd], writes=[stpd])
                    s.op("dve", lambda e, gp=gp, stp=stp: e.tensor_tensor(out=S[:, gp * 512:(gp + 1) * 512], in0=stp[:, :], in1=S[:, gp * 512:(gp + 1) * 512], op=ALU.add),
                         reads=[stpd, Sd], writes=[Sd])
                if B_:
                    s.op("act", lambda e: e.activation(out=Sb[:, :], in_=S[:, :], func=AF.Copy), reads=[Sd], writes=[Sbd])
                    s.op("dve", lambda e, ck=ck: e.tensor_tensor(out=y[:, :], in0=y[:, :], in1=sz[:, ck, :], op=ALU.mult), reads=[yd, szd], writes=[yd])
                    s.op("dve", lambda e: e.memset(ssq[:, :], 0.0), writes=[ssqd])
                    for g in range(8):
                        s.op("act", lambda e, g=g: e.activation(out=junk[:, :], in_=y[:, g * 256:(g + 1) * 256], func=AF.Square, accum_out=ssq[:, g:g + 1]),
                             reads=[yd], writes=[junkd, ssqd])
                    s.op("act", lambda e: e.activation(out=ssq[:, 8:16], in_=ssq[:, 0:8], func=AF.Sqrt, bias=ones[:, 128:129], scale=1.0 / 256), reads=[ssqd, onesdep], writes=[ssqd])
                    s.op("dve", lambda e: e.reciprocal(out=ssq[:, 8:16], in_=ssq[:, 8:16]), reads=[ssqd], writes=[ssqd])
                    s.op("dve", lambda e: e.tensor_tensor(out=gn[:, :].rearrange("p (g q) -> p g q", g=8), in0=y[:, :].rearrange("p (g q) -> p g q", g=8),
                                                          in1=ssq[:, 8:16].unsqueeze(2).to_broadcast([128, 8, 256]), op=ALU.mult), reads=[yd, ssqd], writes=[gnd])
                    if debug and tb == 0 and ck == 0:
                        outtoks.append(s.op("sp", lambda e: e.dma_start(out=dbg["g_yg"][:, :], in_=y[:, :]), reads=[yd], dma=True))
                        outtoks.append(s.op("sp", lambda e: e.dma_start(out=dbg["g_gn"][:, :], in_=gn[:, :]), reads=[gnd], dma=True))
                        outtoks.append(s.op("sp", lambda e: e.dma_start(out=dbg["g_S"][:, :], in_=S[:, :]), reads=[Sd], dma=True))
                    def p4b(c4, k0=k0):
                        s.op("pe", [lambda e, q=q: e.transpose(out=ptr[:, q, :], in_=gn[:, (c4 * 4 + q) * 128:(c4 * 4 + q + 1) * 128], identity=idb[:, :]) for q in range(4)],
                             reads=[gnd, iddep], writes=[ptrd])
                        for q in range(4):
                            ccx = c4 * 4 + q
                            s.op("act", lambda e, q=q, ccx=ccx: e.activation(out=gnT[:, ccx, k0:k0 + 128], in_=ptr[:, q, :], func=AF.Copy, scale=gw[:, ccx:ccx + 1]),
                                 reads=[ptrd, gwd], writes=[gnTd])
                    deferred.extend([(lambda c4=c4, f=p4b: f(c4)) for c4 in range(4)])
            while deferred:
                deferred.pop(0)()
            if debug and B_ and tb == 0:
                outtoks.append(s.op("sp", lambda e: e.dma_start(out=dbg["g_gnT"][:, :, :], in_=gnT[:, :, :]), reads=[gnTd], dma=True))
            if B_:
                for dc in range(KC):
                    wo, wodep = woring.next()
                    s.op("pool", lambda e, wo=wo, dc=dc: e.dma_start(out=wo[:, :, :], in_=wout_d[dc, :, :, :]), writes=[wodep], dma=True)
                    ps, psd = psring.next()
                    s.op("pe", [lambda e, ps=ps, wo=wo, kc=kc: e.matmul(ps[:, :], lhsT=wo[:, kc, :], rhs=gnT[:, kc, :], start=(kc == 0), stop=(kc == 15)) for kc in range(16)],
                         reads=[wodep, gnTd], writes=[psd])
                    ob, obd = oring.next()
                    s.op("sp", lambda e, ob=ob, dc=dc, tb=tb: e.dma_start(out=ob[:, :], in_=x_own[dc * 128:(dc + 1) * 128, tb * 512:(tb + 1) * 512]), writes=[obd], dma=True)
                    s.op("dve", lambda e, ob=ob, ps=ps: e.tensor_tensor(out=ob[:, :], in0=ps[:, :], in1=ob[:, :], op=ALU.add), reads=[psd, obd], writes=[obd])
                    outtoks.append(s.op("sp", lambda e, ob=ob, dc=dc, tb=tb: e.dma_start(out=x_out[dc * 128:(dc + 1) * 128, tb * 512:(tb + 1) * 512], in_=ob[:, :]),
                                        reads=[obd], dma=True))
                    if tb == 3 and io.get("hmsg") is not None:
                        s.op("sp", lambda e, ob=ob, dc=dc: e.dma_start(out=io["hmsg"][:, dc, :], in_=ob[:, 512 - HALO:512]), reads=[obd], dma=True)
        if not B_:
            HS = SMSG // 2
            outtoks.append(s.op("sp", lambda e: e.dma_start(out=smsg[0][:, :], in_=S[:, 0:HS]), reads=[Sd], dma=True))
            outtoks.append(s.op("sp", lambda e: e.dma_start(out=smsg[1][:, 0:DIN - HS], in_=S[:, HS:DIN]), reads=[Sd], dma=True))
            outtoks.append(s.op("sp", lambda e: e.dma_start(out=smsg[1][:, DIN - HS:DIN - HS + NHS], in_=tsum[:, :]), reads=[tsumd], dma=True))
        s.barrier()
        s.emit()


def _ssm_common_maps(xTs, nw, w_in, conv_w, conv_b, dt_bias, a_log, d_skip):
    wr = w_in[:, DIN:DIN + 4096].reshape(KC, 128, 32, 128)
    win = np.ascontiguousarray(wr.transpose(2, 1, 0, 3))
    wdt = np.ascontiguousarray(w_in[:, DIN + 4096:].reshape(KC, 128, 32).transpose(1, 0, 2))
    cw = np.empty((128, 32, 5), np.float32)
    cw[:, :, 0:4] = conv_w.reshape(4, 32, 128).transpose(2, 1, 0)
    cw[:, :, 4] = conv_b.reshape(32, 128).T
    vecs = np.ascontiguousarray(np.stack([dt_bias, a_log, d_skip]).astype(np.float32))
    halos = _halo_cols(xTs)
    return [{"x_own": xTs[c], "x_halo": halos[c], "nw": _cols128(nw), "win": win, "wdt": wdt, "cw": cw, "vecs": vecs}
            for c in range(NCORES)]


def run_ssm(xTs, nw, w_in, conv_w, conv_b, dt_bias, a_log, d_skip, norm_w, w_out):
    maps = _ssm_common_maps(xTs, nw, w_in, conv_w, conv_b, dt_bias, a_log, d_skip)
    ncA = _prog("ssmA", lambda: build_ssm("A"))
    resA = run_bass_kernel_spmd(ncA, maps, core_ids=list(range(NCORES)))
    sl = [np.asarray(r["s_out"]) for r in resA.results]
    dl = [np.asarray(r["d_out"]) for r in resA.results]
    ncB = _prog("ssmB", lambda: build_ssm("B"))
    wz = np.ascontiguousarray(w_in[:, 0:DIN].reshape(KC, 128, 4, 512).transpose(2, 1, 0, 3))
    gw = np.ascontiguousarray(norm_w.reshape(16, 128).T)
    wout = np.ascontiguousarray(w_out.reshape(16, 128, KC, 128).transpose(2, 1, 0, 3))
    for c in range(NCORES):
        q = c % 4
        sp = np.zeros((3, 128, DIN), np.float32)
        dp = np.zeros((3, 128, NHS), np.float32)
        for i, src in enumerate((c - 3, c - 2, c - 1)):
            if src >= c - q:
                sp[i] = sl[src]
                dp[i] = dl[src]
        maps[c].update({"wz": wz, "gw": gw, "wout": wout, "sprev": sp, "dprev": dp})
    resB = run_bass_kernel_spmd(ncB, maps, core_ids=list(range(NCORES)))
    return [np.asarray(r["x_out"]) for r in resB.results]


I32 = mybir.dt.int32
GROUPS = [[0, 1, 2, 3], [4, 5, 6, 7]]
SMSG = DIN + NHS
VMSG = 21 * 128


class Prog:
    pass


def build_fused(stop=None):
    nc = bass.Bass("TRN2", target_bir_lowering=False)
    P = Prog()
    P.nc = nc
    P.st = {}

    def din(name, shape, dt=F32):
        return nc.dram_tensor(name, list(shape), dt, kind="ExternalInput").ap()

    SMSG_ = SMSG
    x_d = din("x", [D, T])
    out_d = nc.dram_tensor("out", [D, T], F32, kind="ExternalOutput").ap()
    pidx_d = din("pidx", [1, 4], I32)
    cos_d, sin_d, pm_d, mk_d = din("cosT", [128, T]), din("sinT", [128, T]), din("pm", [128, 128]), din("masks", [128, 3, 512])
    fw_d = din("fw", [128, KC])
    L = []
    for i in range(4):
        d = {"mnw": din("mnw%d" % i, [128, KC]), "fnw": din("fnw%d" % i, [128, KC]), "wup": din("wup%d" % i, [NJ, 128, KC, 256]),
             "fcw": din("fcw%d" % i, [128, NJ, 2, 4]), "wdn": din("wdn%d" % i, [DFF, D])}
        if i % 2 == 0:
            d.update({"wq": din("wq%d" % i, [NG, NH, 128, KC, 128]), "wk": din("wk%d" % i, [NG, NH, 128, KC, 128]),
                      "wv": din("wv%d" % i, [NG, 128, KC, 1024]), "wo": din("wo%d" % i, [D, D])})
        else:
            d.update({"win": din("win%d" % i, [32, 128, KC, 128]), "wdt": din("wdt%d" % i, [128, KC, 32]), "scw": din("scw%d" % i, [128, 32, 5]),
                      "vecs": din("vecs%d" % i, [3, 32]), "wz": din("wz%d" % i, [4, 128, KC, 512]), "gw": din("gw%d" % i, [128, 16]),
                      "wout": din("wout%d" % i, [KC, 128, 16, 128])})
        L.append(d)
    xb = [nc.dram_tensor("xb%d" % i, [D, T], F32).ap() for i in range(2)]
    k_own = nc.dram_tensor("k_own", [NG, NH, 128, T], BF16).ap()
    v_own = nc.dram_tensor("v_own", [NG, NH, 128, 16, 128], BF16).ap()
    kmsg = nc.dram_tensor("kmsg", [NH, 128, KMSG], BF16).ap()
    kall = nc.dram_tensor("kall", [NH, 5 * 128, KMSG], BF16).ap()
    vmsg = nc.dram_tensor("vmsg", [NH, 128, VMSG], BF16).ap()
    vall = nc.dram_tensor("vall", [NH, 5 * 128, VMSG], BF16).ap()
    hmsg = nc.dram_tensor("hmsg", [128, KC * HALO], F32).ap()
    hall = nc.dram_tensor("hall", [4 * 128, KC * HALO], F32).ap()
    kloc = nc.dram_tensor("kloc", [NH, 128, KMSG], BF16).ap()
    vloc = nc.dram_tensor("vloc", [NH, 128, VMSG], BF16).ap()
    flags_d = din("flags", [128, 8])
    P.hall = hall
    P.hsave = nc.dram_tensor("hsave", [128, KC, HALO + T], BF16).ap()
    P.xs_save = [nc.dram_tensor("xs_save%d" % i, [128, 4 * DIN], BF16).ap() for i in range(4)]
    P.bs_save = [nc.dram_tensor("bs_save%d" % i, [128, 4 * 1024], BF16).ap() for i in range(4)]
    P.bt_save = [nc.dram_tensor("bt_save%d" % i, [128, 8 * 512], BF16).ap() for i in range(4)]
    P.dt_save = [nc.dram_tensor("dt_save%d" % i, [128, 4 * 32], F32).ap() for i in range(4)]
    P.flags = flags_d
    smsg = [nc.dram_tensor("smsg%d" % i, [128, SMSG // 2], F32).ap() for i in range(2)]
    sall = [nc.dram_tensor("sall%d" % i, [4 * 128, SMSG // 2], F32).ap() for i in range(2)]
    P.sall = sall

    with contextlib.ExitStack() as ges:
        S = Sched(nc, ges)
        P.S = S

        def setup(e):
            ins = None
            P.st["regs"] = []
            for k in range(1):
                reg = e.alloc_register("pidx%d" % k)
                ins = e.reg_load(reg, pidx_d[0:1, k:k + 1])
                P.st["regs"].append(reg)
                P.st["c%d" % (k + 1)] = e.snap(reg, min_val=0, max_val=4)
            return ins
        S.op("sp", setup)

        def pre_sp(e):
            for k, reg in enumerate(P.st.get("regs", [])):
                P.st["c%d" % (k + 1)] = e.snap(reg, min_val=0, max_val=4)
        S.pre_sp = pre_sp
        with contextlib.ExitStack() as es:
            cx = Ctx(nc, es, S)
            zb = cx.sb([128, KMSG], BF16, "zb")
            zf = cx.sb([128, SMSG], F32, "zf")
            zd = Dep()
            S.op("dve", lambda e: e.memset(zb[:, :], 0.0), writes=[zd])
            S.op("dve", lambda e: e.memset(zf[:, :], 0.0), writes=[zd])
            for h in range(NH):
                S.op("sp", lambda e, h=h: e.dma_start(out=kall[h, 512:640, :], in_=zb[:, :]), reads=[zd], dma=True)
                S.op("sp", lambda e, h=h: e.dma_start(out=vall[h, 512:640, :], in_=zb[:, 0:VMSG]), reads=[zd], dma=True)
            S.barrier()
            S.emit()

        def coll(msg2d, all2d, nrows):
            S.op("pool", lambda e: e.collective_compute("AllGather", ALU.bypass, replica_groups=GROUPS,
                                                        ins=[msg2d.opt()], outs=[all2d[0:4 * nrows, :].opt()]), cc=True)
            S.barrier()

        kalld = [Dep() for _ in range(NH)]
        valld = [Dep() for _ in range(NH)]
        klocd = [Dep() for _ in range(NH)]
        vlocd = [Dep() for _ in range(NH)]

        def coll_k(h, dep):
            S.op("pool", lambda e: e.collective_compute("AllGather", ALU.bypass, replica_groups=GROUPS,
                                                        ins=[kmsg[h, :, :].opt()], outs=[kall[h, 0:512, :].opt()]), reads=[dep], writes=[kalld[h]], cc=True)

        def coll_v(h, dep):
            S.op("pool", lambda e: e.collective_compute("AllGather", ALU.bypass, replica_groups=GROUPS,
                                                        ins=[vmsg[h, :, :].opt()], outs=[vall[h, 0:512, :].opt()]), reads=[dep], writes=[valld[h]], cc=True)

        def coll_kv():
            kv = kall.rearrange("h (r p) c -> h r p c", p=128)
            vv = vall.rearrange("h (r p) c -> h r p c", p=128)
            for h in range(NH):
                S.op("sp", lambda e, h=h: e.dma_start(out=vloc[h, :, :], in_=vv[h][P.st["c1"]]), reads=[valld[h]], writes=[vlocd[h]], dma=True)
                S.op("sp", lambda e, h=h: e.dma_start(out=kloc[h, :, :], in_=kv[h][P.st["c1"]]), reads=[kalld[h]], writes=[klocd[h]], dma=True)

        hmsg3 = hmsg.rearrange("p (kc t) -> p kc t", t=HALO)
        kmsg3 = kmsg
        vmsg4 = vmsg.rearrange("h p (b e) -> h p b e", e=128)
        vloc4 = vloc.rearrange("h p (b e) -> h p b e", e=128)
        khalo = lambda g, h: kloc[h, :, KOFF[g]:KOFF[g] + DIL[g] * 128].rearrange("p (q l) -> p q l", q=DIL[g])
        vhalo = lambda g, h: vloc4[h, :, BOFF[g]:BOFF[g] + DIL[g], :]

        step = [0]

        def go():
            step[0] += 1
            return stop is None or step[0] <= stop

        _coll = coll

        def coll(a, b, n):
            if go():
                _coll(a, b, n)

        cur = x_d
        nxt = 0
        for i in range(4):
            d = L[i]
            last = i == 3
            if i % 2 == 0:
                if go():
                  emit_attn_kv(P, {"x_in": cur, "nw": d["mnw"], "wk": d["wk"], "wv": d["wv"], "cosT": cos_d, "sinT": sin_d, "pm": pm_d,
                                 "k_own": k_own, "v_own": v_own, "kmsg": kmsg3, "vmsg": vmsg4, "coll_k": coll_k, "coll_v": coll_v})
                if go():
                    coll_kv()
                if go():
                  emit_attn_main(P, {"x_in": cur, "nw": d["mnw"], "wq": d["wq"], "wo": d["wo"], "cosT": cos_d, "sinT": sin_d, "pm": pm_d,
                                   "masks": mk_d, "k_own": k_own, "v_own": v_own, "khalo": khalo, "vhalo": vhalo, "klocd": klocd, "vlocd": vlocd,
                                   "x_out": xb[nxt], "hmsg": hmsg3})
            else:
                common = {"x_in": cur, "nw": d["mnw"], "win": d["win"], "wdt": d["wdt"], "cw": d["scw"], "vecs": d["vecs"]}
                if go():
                    emit_ssm(P, dict(common, smsg=smsg), "A")
                coll(smsg[0], sall[0], 128)
                coll(smsg[1], sall[1], 128)
                if go():
                  emit_ssm(P, dict(common, wz=d["wz"], gw=d["gw"], wout=d["wout"], x_out=xb[nxt], hmsg=hmsg3), "B")
            cur = xb[nxt]
            nxt = 1 - nxt
            coll(hmsg, hall, 128)
            io = {"x_in": cur, "nw": d["fnw"], "wup": d["wup"], "cw": d["fcw"], "wdn": d["wdn"],
                  "x_out": out_d if last else xb[nxt], "hmsg": None if (last or i % 2 == 1) else hmsg3}
            if last:
                io["fw"] = fw_d
            if go():
                emit_ffn(P, io, final_norm=last)
            if not last:
                cur = xb[nxt]
                nxt = 1 - nxt
                if i % 2 == 0:
                    coll(hmsg, hall, 128)
        if stop is not None:
            S.emit()
    return nc


def _prep_maps(inp):
    f = lambda a: np.ascontiguousarray(np.asarray(a, dtype=np.float32))
    x = f(inp["x"])
    xTs = _shards_T(x)
    shared = {"pm": perm_matrix(), "fw": _cols128(f(inp["final_norm_w"]))}
    for i in range(4):
        j = i // 2
        shared["mnw%d" % i] = _cols128(f(inp["mix_norm_w"])[i])
        shared["fnw%d" % i] = _cols128(f(inp["ffn_norm_w"])[i])
        w_up = f(inp["ffn_w_up"])[i]
        shared["wup%d" % i] = np.ascontiguousarray(w_up.reshape(KC, 128, 2, NJ, 128).transpose(3, 1, 0, 2, 4).reshape(NJ, 128, KC, 256))
        cw = np.empty((128, NJ, 2, 4), np.float32)
        cw[:, :, :, 0:3] = f(inp["ffn_conv_w"])[i].reshape(3, 2, NJ, 128).transpose(3, 2, 1, 0)
        cw[:, :, :, 3] = f(inp["ffn_conv_b"])[i].reshape(2, NJ, 128).transpose(2, 1, 0)
        shared["fcw%d" % i] = cw
        shared["wdn%d" % i] = f(inp["ffn_w_down"])[i]
        if i % 2 == 0:
            wr = f(inp["attn_w_qkv"])[j].reshape(KC, 128, NG, 3, NH, 128)
            shared["wq%d" % i] = np.ascontiguousarray(wr[:, :, :, 0].transpose(2, 3, 1, 0, 4))
            shared["wk%d" % i] = np.ascontiguousarray(wr[:, :, :, 1].transpose(2, 3, 1, 0, 4))
            shared["wv%d" % i] = np.ascontiguousarray(wr[:, :, :, 2].transpose(2, 1, 0, 3, 4).reshape(NG, 128, KC, 1024))
            shared["wo%d" % i] = f(inp["attn_w_o"])[j]
        else:
            w_in = f(inp["ssm_w_in"])[j]
            shared["win%d" % i] = np.ascontiguousarray(w_in[:, DIN:DIN + 4096].reshape(KC, 128, 32, 128).transpose(2, 1, 0, 3))
            shared["wdt%d" % i] = np.ascontiguousarray(w_in[:, DIN + 4096:].reshape(KC, 128, 32).transpose(1, 0, 2))
            scw = np.empty((128, 32, 5), np.float32)
            scw[:, :, 0:4] = f(inp["ssm_conv_w"])[j].reshape(4, 32, 128).transpose(2, 1, 0)
            scw[:, :, 4] = f(inp["ssm_conv_b"])[j].reshape(32, 128).T
            shared["scw%d" % i] = scw
            shared["vecs%d" % i] = np.ascontiguousarray(np.stack([f(inp["ssm_dt_bias"])[j], f(inp["ssm_a_log"])[j], f(inp["ssm_d"])[j]]))
            shared["wz%d" % i] = np.ascontiguousarray(w_in[:, 0:DIN].reshape(KC, 128, 4, 512).transpose(2, 1, 0, 3))
            shared["gw%d" % i] = np.ascontiguousarray(f(inp["ssm_norm_w"])[j].reshape(16, 128).T)
            shared["wout%d" % i] = np.ascontiguousarray(f(inp["ssm_w_out"])[j].reshape(16, 128, KC, 128).transpose(2, 1, 0, 3))
    maps = []
    for c in range(NCORES):
        q = c % 4
        cs, sn = rope_tables_np(q * T)
        m = dict(shared)
        fl = np.zeros((128, 8), np.float32)
        if q >= 1:
            fl[:, q - 1] = 1.0
        for r in range(4):
            if r < q:
                fl[:, 4 + r] = 1.0
        m.update({"x": xTs[c], "cosT": cs, "sinT": sn, "masks": attn_masks(q != 0), "flags": fl,
                  "pidx": np.array([[q - 1 if q >= 1 else 4, q - 2 if q >= 2 else 4, q - 3 if q >= 3 else 4, 0]], np.int32)})
        maps.append(m)
    return maps


def kernel(**inputs):
    maps = _prep_maps(inputs)
    nc = _prog("fused", build_fused)
    res = run_bass_kernel_spmd(nc, maps, core_ids=list(range(NCORES)))
    out = np.empty((2, 4 * T, D), np.float32)
    for c in range(NCORES):
        b, q = divmod(c, 4)
        out[b, q * T:(q + 1) * T, :] = np.asarray(res.results[c]["out"]).T
    return out
```

```python
import contextlib
import numpy as np
import concourse.bass as bass
import concourse.mybir as mybir
from concourse.bass_utils import run_bass_kernel_spmd

F32 = mybir.dt.float32
BF16 = mybir.dt.bfloat16
ALU = mybir.AluOpType
AF = mybir.ActivationFunctionType
AX = mybir.AxisListType

NCORES = 8
T = 2048
D = 1024
KC = 8
DFF = 2816
NJ = 22
EPS = 1e-5
HALO = 3


class Dep:
    __slots__ = ("w", "rs", "ps")

    def __init__(self, ps=False):
        self.w = None
        self.rs = []
        self.ps = ps


class Sched:
    ENGS = ("pe", "act", "dve", "pool", "sp")
    NDMA = 6

    def __init__(self, nc, es):
        self.nc = nc
        self.h = {"pe": nc.tensor, "act": nc.scalar, "dve": nc.vector,
                  "pool": nc.gpsimd, "sp": nc.sync}
        self.sems = {}
        self.cnt = {}
        self.ops = {e: [] for e in self.ENGS}
        self.seen = {e: {} for e in self.ENGS}
        for e in self.ENGS + ("cc",):
            self.sems[e] = es.enter_context(nc.semaphore("s_" + e))
            self.cnt[e] = 0
        self.dsem = {}
        self.dcnt = {}
        self.drr = {}
        for e in ("sp", "pool", "act"):
            for i in range(self.NDMA):
                k = "d_%s%d" % (e, i)
                self.sems[k] = es.enter_context(nc.semaphore(k))
                self.cnt[k] = 0
            self.drr[e] = 0

    def _need(self, eng, waits, tok, skip_same_pe=True):
        if tok is None:
            return
        k, v = tok
        if eng == "pe" and k == "pe":
            return
        if self.seen[eng].get(k, 0) >= v:
            return
        if waits.get(k, 0) < v:
            waits[k] = v

    def op(self, eng, fns, reads=(), writes=(), dma=False, cc=False):
        if not isinstance(fns, (list, tuple)):
            fns = [fns]
        waits = {}
        ps_reads = [d for d in reads if d.ps]
        if ps_reads:
            reads = [d for d in reads if not d.ps]
            writes = list(writes) + ps_reads
        for d in reads:
            self._need(eng, waits, d.w)
        for d in writes:
            self._need(eng, waits, d.w)
            for r in d.rs:
                self._need(eng, waits, r)
        if dma:
            i = self.drr[eng]
            self.drr[eng] = (i + 1) % self.NDMA
            k = "d_%s%d" % (eng, i)
            if self.cnt[k] > 0:
                self._need(eng, waits, (k, self.cnt[k]))
            self.cnt[k] += 16
            inc = 16
        elif cc:
            k = "cc"
            self.cnt[k] += 1
            inc = 1
        else:
            k = eng
            self.cnt[k] += 1
            inc = 1
        tok = (k, self.cnt[k])
        for kk, v in waits.items():
            self.seen[eng][kk] = v
        self.ops[eng].append((list(waits.items()), list(fns), k, inc))
        for d in reads:
            d.rs.append(tok)
        for d in writes:
            d.w = tok
            d.rs = []
        return tok

    def wait_all(self, eng, toks):
        waits = {}
        for t in toks:
            self._need(eng, waits, t)
        for kk, v in waits.items():
            self.seen[eng][kk] = v
        self.ops[eng].append((list(waits.items()), [], None, 0))

    def barrier(self, exclude_cc=False):
        toks = [(k, v) for k, v in self.cnt.items() if v > 0 and not (exclude_cc and k == "cc")]
        for e in self.ENGS:
            self.wait_all(e, toks)

    def replay(self, eng, h):
        for waits, fns, k, inc in self.ops[eng]:
            for kk, v in waits:
                h.wait_ge(self.sems[kk], v)
            n = len(fns)
            for i, fn in enumerate(fns):
                ins = fn(h)
                if i == n - 1:
                    ins.then_inc(self.sems[k], inc)

    pre_sp = None

    def emit(self):
        nc = self.nc
        with nc.Block() as block:
            @block.tensor
            def _(e):
                self.replay("pe", e)

            @block.scalar
            def _(e):
                self.replay("act", e)

            @block.vector
            def _(e):
                self.replay("dve", e)

            @block.gpsimd
            def _(e):
                self.replay("pool", e)

            @block.sync
            def _(e):
                if self.pre_sp is not None:
                    self.pre_sp(e)
                self.replay("sp", e)
        self.ops = {e: [] for e in self.ENGS}


class Ring:
    def __init__(self, aps, ps=False):
        self.aps = aps
        self.deps = [Dep(ps) for _ in aps]
        self.i = 0

    def next(self):
        i = self.i
        self.i = (i + 1) % len(self.aps)
        return self.aps[i], self.deps[i]


class Ctx:
    def __init__(self, nc, es, sched=None):
        self.nc = nc
        self.es = es
        self.s = sched if sched is not None else Sched(nc, es)
        self.n = Ctx.N
        Ctx.N += 1000

    N = 0

    def sb(self, shape, dt, name=None):
        self.n += 1
        return self.es.enter_context(self.nc.sbuf_tensor("%s_%d" % (name or "t", self.n), list(shape), dt))

    def ps(self, shape, dt, name=None):
        self.n += 1
        return self.es.enter_context(self.nc.psum_tensor("%s_%d" % (name or "p", self.n), list(shape), dt))


def emit_norm(cx, xT, xdeps, hT, hdep, nw, nwdep, col0, ncols, ones, onesdep, psring, sqring, rsring):
    s = cx.s
    c0 = col0
    while c0 < col0 + ncols:
        n = min(512, col0 + ncols - c0)
        ps, psd = psring.next()
        sqs = []
        for kc in range(KC):
            sq, sqd = sqring.next()
            s.op("act", lambda e, sq=sq, kc=kc, c0=c0, n=n: e.activation(out=sq[:, 0:n], in_=xT[:, kc, c0:c0 + n], func=AF.Square),
                 reads=(xdeps(kc, c0, n) if callable(xdeps) else [xdeps[kc]]), writes=[sqd])
            s.op("pe", lambda e, ps=ps, sq=sq, kc=kc, n=n: e.matmul(ps[:, 0:n], lhsT=ones[:, 0:128], rhs=sq[:, 0:n], start=(kc == 0), stop=(kc == KC - 1)),
                 reads=[sqd, onesdep], writes=[psd])
        rs, rsd = rsring.next()
        s.op("act", lambda e, rs=rs, ps=ps, n=n: e.activation(out=rs[:, 0:n], in_=ps[:, 0:n], func=AF.Sqrt, bias=ones[:, 128:129], scale=1.0 / D),
             reads=[psd, onesdep], writes=[rsd])
        s.op("dve", lambda e, rs=rs, n=n: e.reciprocal(out=rs[:, 0:n], in_=rs[:, 0:n]),
             reads=[rsd], writes=[rsd])
        for kc in range(KC):
            s.op("dve", lambda e, rs=rs, kc=kc, c0=c0, n=n: e.scalar_tensor_tensor(
                out=hT[:, kc, c0:c0 + n], in0=xT[:, kc, c0:c0 + n], scalar=nw[:, kc:kc + 1], in1=rs[:, 0:n],
                op0=ALU.mult, op1=ALU.mult),
                reads=(xdeps(kc, c0, n) if callable(xdeps) else [xdeps[kc]]) + [rsd, nwdep], writes=[hdep[c0 // 512] if isinstance(hdep, list) else hdep])
        c0 += n


def emit_halo(cx, P, dst3, dstdeps):
    s = cx.s
    hs = cx.sb([128, 4, KC * HALO], F32, "hs")
    fl = cx.sb([128, 8], F32, "fl")
    tmp = cx.sb([128, KC * HALO], F32, "htmp")
    hsd, fld, tmpd = Dep(), Dep(), Dep()
    s.op("sp", lambda e: e.dma_start(out=hs[:, :, :], in_=P.hall.rearrange("(r p) f -> p r f", p=128)), writes=[hsd], dma=True)
    s.op("sp", lambda e: e.dma_start(out=fl[:, :], in_=P.flags[:, :]), writes=[fld], dma=True)
    s.op("dve", lambda e: e.tensor_scalar(out=tmp[:, :], in0=hs[:, 0, :], scalar1=fl[:, 0:1], scalar2=0.0, op0=ALU.mult, op1=ALU.add),
         reads=[hsd, fld], writes=[tmpd])
    for r in range(1, 4):
        s.op("dve", lambda e, r=r: e.scalar_tensor_tensor(out=tmp[:, :], in0=hs[:, r, :], scalar=fl[:, r:r + 1], in1=tmp[:, :], op0=ALU.mult, op1=ALU.add),
             reads=[hsd, fld, tmpd], writes=[tmpd])
    s.op("dve", lambda e: e.tensor_copy(out=dst3, in_=tmp[:, :].rearrange("p (kc t) -> p kc t", t=HALO)), reads=[tmpd], writes=dstdeps)


def emit_ffn(P, io, final_norm=False, GJ=4):
    nc = P.nc
    x_own, nw_d, wup_d, cw_d, wdn_d, x_out = io["x_in"], io["nw"], io["wup"], io["cw"], io["wdn"], io["x_out"]
    if final_norm:
        fw_d = io["fw"]

    W = HALO + T
    with contextlib.ExitStack() as es:
        cx = Ctx(nc, es, P.S)
        s = cx.s
        xT = cx.sb([128, KC, W], F32, "xT")
        hT = cx.sb([128, KC, W], BF16, "hT")
        nw = cx.sb([128, KC], F32, "nw")
        cw = cx.sb([128, NJ, 2, 4], F32, "cw")
        ones = cx.sb([128, 129], F32, "ones")
        xd = [[Dep() for _ in range(4)] for _ in range(KC)]
        xhd = Dep()

        def xdf(kc, c0, n):
            b = c0 // 512
            deps = []
            if b == 0:
                deps.append(xhd)
            if b >= 1:
                deps.append(xd[kc][b - 1])
            if b <= 3:
                deps.append(xd[kc][b])
            return deps
        hdep = [Dep() for _ in range(5)]
        nwdep, cwdep, onesdep = Dep(), Dep(), Dep()
        psring = Ring([cx.ps([128, 512], F32, "ps") for _ in range(8)], ps=True)
        sqring = Ring([cx.sb([128, 512], F32, "sq") for _ in range(3)])
        rsring = Ring([cx.sb([128, 512], F32, "rs") for _ in range(2)])
        wupring = Ring([cx.sb([128, KC, 256], BF16, "wup") for _ in range(3)])
        ubuf = [cx.sb([128, W], F32, "u%d" % i) for i in range(2)]
        udep = [Dep(), Dep()]
        cbuf = [cx.sb([128, T], F32, "c%d" % i) for i in range(2)]
        cdep = [Dep(), Dep()]
        gring = Ring([cx.sb([128, GJ, T], BF16, "g") for _ in range(2)])
        wdring = Ring([cx.sb([128, GJ, D], BF16, "wd") for _ in range(2)])

        s.op("sp", lambda e: e.dma_start(out=nw[:, :], in_=nw_d[:, :]), writes=[nwdep], dma=True)
        s.op("sp", lambda e: e.dma_start(out=cw[:, :, :, :], in_=cw_d[:, :, :, :]), writes=[cwdep], dma=True)
        s.op("dve", lambda e: e.memset(ones[:, 0:128], 1.0), writes=[onesdep])
        s.op("dve", lambda e: e.memset(ones[:, 128:129], EPS), writes=[onesdep])
        emit_halo(cx, P, xT[:, :, 0:HALO], [xhd])
        for tb in range(4):
            for kc in range(KC):
                s.op("sp", lambda e, kc=kc, tb=tb: e.dma_start(out=xT[:, kc, HALO + tb * 512:HALO + (tb + 1) * 512],
                                                               in_=x_own[kc * 128:(kc + 1) * 128, tb * 512:(tb + 1) * 512]),
                     writes=[xd[kc][tb]], dma=True)
        if final_norm:
            fw = cx.sb([128, KC], F32, "fw")
            fwdep = Dep()
            s.op("sp", lambda e: e.dma_start(out=fw[:, :], in_=fw_d[:, :]), writes=[fwdep], dma=True)

        emit_norm(cx, xT, xdf, hT, hdep, nw, nwdep, 0, W, ones, onesdep, psring, sqring, rsring)

        wdn_v = wdn_d.rearrange("(j p) d -> p j d", p=128)
        j = 0
        groups = []
        while j < NJ:
            groups.append(list(range(j, min(NJ, j + GJ))))
            j += GJ
        def emit_down(grp, g, gdep, wd, wddep):
            for dc in range(KC):
                for tb in range(4):
                    ps, psd = psring.next()
                    s.op("pe", [lambda e, ps=ps, wd=wd, g=g, jj=jj, dc=dc, tb=tb, n=len(grp): e.matmul(
                        ps[:, :], lhsT=wd[:, jj, dc * 128:(dc + 1) * 128], rhs=g[:, jj, tb * 512:(tb + 1) * 512],
                        start=(jj == 0), stop=(jj == n - 1)) for jj in range(len(grp))],
                        reads=[wddep, gdep], writes=[psd])
                    c0 = HALO + tb * 512
                    s.op("dve", lambda e, ps=ps, dc=dc, c0=c0: e.tensor_tensor(
                        out=xT[:, dc, c0:c0 + 512], in0=ps[:, :], in1=xT[:, dc, c0:c0 + 512], op=ALU.add),
                        reads=[psd, xd[dc][tb]], writes=[xd[dc][tb]])

        pending = None
        for grp in groups:
            g, gdep = gring.next()
            wd, wddep = wdring.next()
            s.op("pool", lambda e, wd=wd, grp=grp: e.dma_start(out=wd[:, 0:len(grp), :], in_=wdn_v[:, grp[0]:grp[0] + len(grp), :]),
                 writes=[wddep], dma=True)
            for jj, j in enumerate(grp):
                wup, wupdep = wupring.next()
                s.op("pool", lambda e, wup=wup, j=j: e.dma_start(out=wup[:, :, :], in_=wup_d[j, :, :, :]), writes=[wupdep], dma=True)
                for half in range(2):
                    u = ubuf[half]
                    ps, psd = psring.next()
                    s.op("pe", [lambda e, ps=ps, wup=wup, kc=kc, half=half: e.matmul(
                        ps[:, 0:HALO], lhsT=wup[:, kc, half * 128:(half + 1) * 128], rhs=hT[:, kc, 0:HALO],
                        start=(kc == 0), stop=(kc == KC - 1)) for kc in range(KC)],
                        reads=[wupdep, hdep[0]], writes=[psd])
                    s.op("act", lambda e, ps=ps, u=u: e.activation(out=u[:, 0:HALO], in_=ps[:, 0:HALO], func=AF.Copy),
                         reads=[psd], writes=[udep[half]])
                    for tb in range(4):
                        ps, psd = psring.next()
                        c0 = HALO + tb * 512
                        s.op("pe", [lambda e, ps=ps, wup=wup, kc=kc, half=half, c0=c0: e.matmul(
                            ps[:, :], lhsT=wup[:, kc, half * 128:(half + 1) * 128], rhs=hT[:, kc, c0:c0 + 512],
                            start=(kc == 0), stop=(kc == KC - 1)) for kc in range(KC)],
                            reads=[wupdep, hdep[c0 // 512], hdep[(c0 + 511) // 512]], writes=[psd])
                        s.op("act", lambda e, ps=ps, u=u, c0=c0: e.activation(out=u[:, c0:c0 + 512], in_=ps[:, :], func=AF.Copy),
                             reads=[psd], writes=[udep[half]])
                    c = cbuf[half]
                    ceng = "dve" if half == 0 else CONV_ENG2
                    s.op(ceng, lambda e, u=u, c=c, j=j, half=half: e.tensor_scalar(
                        out=c[:, :], in0=u[:, 3:3 + T], scalar1=cw[:, j, half, 2:3], scalar2=cw[:, j, half, 3:4],
                        op0=ALU.mult, op1=ALU.add), reads=[udep[half], cwdep], writes=[cdep[half]])
                    s.op(ceng, lambda e, u=u, c=c, j=j, half=half: e.scalar_tensor_tensor(
                        out=c[:, :], in0=u[:, 2:2 + T], scalar=cw[:, j, half, 1:2], in1=c[:, :],
                        op0=ALU.mult, op1=ALU.add), reads=[udep[half], cwdep, cdep[half]], writes=[cdep[half]])
                    s.op(ceng, lambda e, u=u, c=c, j=j, half=half: e.scalar_tensor_tensor(
                        out=c[:, :], in0=u[:, 1:1 + T], scalar=cw[:, j, half, 0:1], in1=c[:, :],
                        op0=ALU.mult, op1=ALU.add), reads=[udep[half], cwdep, cdep[half]], writes=[cdep[half]])
                    if half == 0:
                        s.op("act", lambda e, c=c: e.activation(out=c[:, :], in_=c[:, :], func=AF.Silu),
                             reads=[cdep[0]], writes=[cdep[0]])
                if jj == 0 and pending is not None:
                    emit_down(*pending)
                    pending = None
                s.op("dve", lambda e, g=g, jj=jj: e.tensor_tensor(out=g[:, jj, :], in0=cbuf[0][:, :], in1=cbuf[1][:, :], op=ALU.mult),
                     reads=[cdep[0], cdep[1]], writes=[gdep])
            pending = (grp, g, gdep, wd, wddep)
        emit_down(*pending)
        outtoks = []
        if final_norm:
            for tb in range(4):
                c0 = HALO + tb * 512
                ps, psd = psring.next()
                for kc in range(KC):
                    sq, sqd = sqring.next()
                    s.op("act", lambda e, sq=sq, kc=kc, c0=c0: e.activation(out=sq[:, :], in_=xT[:, kc, c0:c0 + 512], func=AF.Square),
                         reads=[xd[kc][tb]], writes=[sqd])
                    s.op("pe", lambda e, ps=ps, sq=sq, kc=kc: e.matmul(ps[:, :], lhsT=ones[:, 0:128], rhs=sq[:, :], start=(kc == 0), stop=(kc == KC - 1)),
                         reads=[sqd, onesdep], writes=[psd])
                rs, rsd = rsring.next()
                s.op("act", lambda e, rs=rs, ps=ps: e.activation(out=rs[:, :], in_=ps[:, :], func=AF.Sqrt, bias=ones[:, 128:129], scale=1.0 / D),
                     reads=[psd, onesdep], writes=[rsd])
                s.op("dve", lambda e, rs=rs: e.reciprocal(out=rs[:, :], in_=rs[:, :]),
                     reads=[rsd], writes=[rsd])
                for kc in range(KC):
                    s.op("dve", lambda e, rs=rs, kc=kc, c0=c0: e.scalar_tensor_tensor(
                        out=xT[:, kc, c0:c0 + 512], in0=xT[:, kc, c0:c0 + 512], scalar=fw[:, kc:kc + 1], in1=rs[:, :],
                        op0=ALU.mult, op1=ALU.mult), reads=[xd[kc][tb], rsd, fwdep], writes=[xd[kc][tb]])
        for kc in range(KC):
            outtoks.append(s.op("sp", lambda e, kc=kc: e.dma_start(out=x_out[kc * 128:(kc + 1) * 128, :], in_=xT[:, kc, HALO:W]),
                                reads=xd[kc], dma=True))
        if io.get("hmsg") is not None:
            s.op("sp", lambda e: e.dma_start(out=io["hmsg"], in_=xT[:, :, W - HALO:W]), reads=[xd[kc][3] for kc in range(KC)], dma=True)
        s.barrier()
        s.emit()


def _cols128(v):
    return np.ascontiguousarray(v.reshape(-1, 128).T)


def _shards_T(x):
    out = []
    for c in range(NCORES):
        b, q = divmod(c, 4)
        out.append(np.ascontiguousarray(x[b, q * T:(q + 1) * T, :].T))
    return out


def _halo_cols(xTs, n=HALO):
    out = []
    for c in range(NCORES):
        if c % 4 == 0:
            h = np.zeros((D, n), np.float32)
        else:
            h = xTs[c - 1][:, T - n:]
        out.append(np.ascontiguousarray(h.reshape(KC, 128, n).transpose(1, 0, 2)))
    return out


_PROGS = {}


def _prog(key, fn):
    if key not in _PROGS:
        _PROGS[key] = fn()
    return _PROGS[key]


def run_ffn(xTs, nw, w_up, conv_w, conv_b, w_down, final_w=None):
    nc = _prog(("ffn", final_w is not None), lambda: build_ffn(final_norm=final_w is not None))
    wup = np.empty((NJ, 128, KC, 256), np.float32)
    wr = w_up.reshape(KC, 128, 2, NJ, 128)
    wup[:] = wr.transpose(3, 1, 0, 2, 4).reshape(NJ, 128, KC, 256)
    cw = np.empty((128, NJ, 2, 4), np.float32)
    cwr = conv_w.reshape(3, 2, NJ, 128)
    cw[:, :, :, 0:3] = cwr.transpose(3, 2, 1, 0)
    cw[:, :, :, 3] = conv_b.reshape(2, NJ, 128).transpose(2, 1, 0)
    halos = _halo_cols(xTs)
    maps = []
    for c in range(NCORES):
        m = {"x_own": xTs[c], "x_halo": halos[c], "nw": _cols128(nw), "wup": wup, "cw": cw,
             "wdn": np.ascontiguousarray(w_down)}
        if final_w is not None:
            m["fw"] = _cols128(final_w)
        maps.append(m)
    res = run_bass_kernel_spmd(nc, maps, core_ids=list(range(NCORES)))
    return [np.asarray(r["x_out"]) for r in res.results]


NG = 3
NH = 8
DIL = (1, 4, 16)
SCALE = 128.0 ** -0.5


def colview(t2d, d, c0, n):
    if d == 1:
        return t2d[:, c0:c0 + n], 1
    L = T // d
    v = t2d.rearrange("p (l r) -> p r l", r=d)
    r0, l0 = divmod(c0, L)
    if n <= L:
        return v[:, r0, l0:l0 + n], 1
    return v[:, r0:r0 + n // L, :], n // L


def v3(ap, A):
    if A == 1:
        return ap
    return ap.rearrange("p (a b) -> p a b", a=A)


def emit_rope(cx, ps, psd, full, dstdep, d, t0, cosT, sinT, tabdep, pm, pmdep, pkring, t1ring, tbring):
    s = cx.s
    if d == 1:
        tmp, tmpd = full[:, t0:t0 + 512], dstdep
    else:
        tmp, tmpd = tbring.next()
        tmp = tmp[:, :]
    s.op("act", lambda e: e.activation(out=tmp, in_=ps[:, :], func=AF.Copy), reads=[psd], writes=[tmpd])
    pk, pkd = pkring.next()
    s.op("pe", lambda e: e.matmul(pk[0:32, :], lhsT=pm[0:32, 0:32], rhs=tmp[0:32, :], start=True, stop=True),
         reads=[tmpd, pmdep], writes=[pkd])
    t1, t1d = t1ring.next()
    t2, t2d = t1ring.next()
    s.op("dve", lambda e: e.tensor_tensor(out=t1[0:32, :], in0=ps[0:32, :], in1=cosT[0:32, t0:t0 + 512], op=ALU.mult),
         reads=[psd, tabdep], writes=[t1d])
    s.op("dve", lambda e: e.tensor_tensor(out=t2[0:32, :], in0=pk[0:32, :], in1=sinT[0:32, t0:t0 + 512], op=ALU.mult),
         reads=[pkd, tabdep], writes=[t2d])
    s.op("dve", lambda e: e.tensor_tensor(out=tmp[0:32, :], in0=t1[0:32, :], in1=t2[0:32, :], op=ALU.add),
         reads=[t1d, t2d, tmpd], writes=[tmpd])
    if d != 1:
        n = 512 // d
        l0 = t0 // d
        dv = full.rearrange("p (r l) -> p l r", r=d)[:, l0:l0 + n, :]
        s.op("act", lambda e: e.activation(out=dv, in_=tmp.rearrange("p (l r) -> p l r", r=d), func=AF.Copy), reads=[tmpd], writes=[dstdep])


def load_norm_h(cx, x_own, nw, nwdep, hT, hdep, ones, onesdep, psring, sqring, rsring, xbring):
    s = cx.s
    for tb in range(4):
        xb, xbd = xbring.next()
        for kc in range(KC):
            s.op("sp", lambda e, xb=xb, kc=kc, tb=tb: e.dma_start(out=xb[:, kc, :], in_=x_own[kc * 128:(kc + 1) * 128, tb * 512:(tb + 1) * 512]),
                 writes=[xbd], dma=True)
        ps, psd = psring.next()
        for kc in range(KC):
            sq, sqd = sqring.next()
            s.op("act", lambda e, sq=sq, xb=xb, kc=kc: e.activation(out=sq[:, :], in_=xb[:, kc, :], func=AF.Square),
                 reads=[xbd], writes=[sqd])
            s.op("pe", lambda e, ps=ps, sq=sq, kc=kc: e.matmul(ps[:, :], lhsT=ones[:, 0:128], rhs=sq[:, :], start=(kc == 0), stop=(kc == KC - 1)),
                 reads=[sqd, onesdep], writes=[psd])
        rs, rsd = rsring.next()
        s.op("act", lambda e, rs=rs, ps=ps: e.activation(out=rs[:, :], in_=ps[:, :], func=AF.Sqrt, bias=ones[:, 128:129], scale=1.0 / D),
             reads=[psd, onesdep], writes=[rsd])
        s.op("dve", lambda e, rs=rs: e.reciprocal(out=rs[:, :], in_=rs[:, :]), reads=[rsd], writes=[rsd])
        for kc in range(KC):
            s.op("dve", lambda e, rs=rs, xb=xb, kc=kc, tb=tb: e.scalar_tensor_tensor(
                out=hT[:, kc, tb * 512:(tb + 1) * 512], in0=xb[:, kc, :], scalar=nw[:, kc:kc + 1], in1=rs[:, :],
                op0=ALU.mult, op1=ALU.mult), reads=[xbd, rsd, nwdep], writes=[hdep])


KOFF = (0, 128, 640)
BOFF = (0, 1, 5)
KMSG = 2688


def emit_attn_kv(P, io):
    nc = P.nc
    x_own, nw_d, wk_d, wv_d, cos_d, sin_d, pm_d = io["x_in"], io["nw"], io["wk"], io["wv"], io["cosT"], io["sinT"], io["pm"]
    k_out, v_out, kmsg, vmsg = io["k_own"], io["v_own"], io["kmsg"], io["vmsg"]
    kmd = [Dep() for _ in range(NH)]
    vmd = [Dep() for _ in range(NH)]
    with contextlib.ExitStack() as es:
        cx = Ctx(nc, es, P.S)
        s = cx.s
        hT = cx.sb([128, KC, T], BF16, "hT")
        hdep = Dep()
        nw = cx.sb([128, KC], F32, "nw")
        ones = cx.sb([128, 129], F32, "ones")
        cosT = cx.sb([128, T], F32, "cosT")
        sinT = cx.sb([128, T], F32, "sinT")
        pm = cx.sb([128, 128], BF16, "pm")
        nwdep, onesdep, tabdep, pmdep = Dep(), Dep(), Dep(), Dep()
        psring = Ring([cx.ps([128, 512], F32, "ps") for _ in range(6)], ps=True)
        pkring = Ring([cx.ps([128, 512], F32, "pk") for _ in range(2)], ps=True)
        sqring = Ring([cx.sb([128, 512], F32, "sq") for _ in range(3)])
        rsring = Ring([cx.sb([128, 512], F32, "rs") for _ in range(2)])
        xbring = Ring([cx.sb([128, KC, 512], F32, "xb") for _ in range(2)])
        t1ring = Ring([cx.sb([128, 512], F32, "t1") for _ in range(4)])
        tbring = Ring([cx.sb([128, 512], BF16, "tb16") for _ in range(4)])
        wkring = Ring([cx.sb([128, KC, 128], BF16, "wk") for _ in range(3)])
        wvring = Ring([cx.sb([128, KC, 1024], BF16, "wv") for _ in range(2)])
        kring = Ring([cx.sb([128, T], BF16, "kt") for _ in range(2)])
        vstage = cx.sb([128, NH, 16, 128], BF16, "vst")
        vsdep = Dep()

        s.op("sp", lambda e: e.dma_start(out=nw[:, :], in_=nw_d[:, :]), writes=[nwdep], dma=True)
        s.op("sp", lambda e: e.dma_start(out=cosT[:, :], in_=cos_d[:, :]), writes=[tabdep], dma=True)
        s.op("sp", lambda e: e.dma_start(out=sinT[:, :], in_=sin_d[:, :]), writes=[tabdep], dma=True)
        s.op("pool", lambda e: e.dma_start(out=pm[:, :], in_=pm_d[:, :]), writes=[pmdep], dma=True)
        s.op("dve", lambda e: e.memset(ones[:, 0:128], 1.0), writes=[onesdep])
        s.op("dve", lambda e: e.memset(ones[:, 128:129], EPS), writes=[onesdep])
        load_norm_h(cx, x_own, nw, nwdep, hT, hdep, ones, onesdep, psring, sqring, rsring, xbring)
        for kc in range(KC):
            s.op("sp", lambda e, kc=kc: e.dma_start(out=P.hsave[:, kc, 0:T], in_=hT[:, kc, :]), reads=[hdep], dma=True)

        outtoks = []
        for g in range(NG):
            d = DIL[g]
            wv, wvdep = wvring.next()
            s.op("pool", lambda e, wv=wv, g=g: e.dma_start(out=wv[:, :, :], in_=wv_d[g, :, :, :]), writes=[wvdep], dma=True)
            for blk in range(16):
                for half in range(2):
                    ps, psd = psring.next()
                    fns = []
                    for kc in range(KC):
                        tv, _ = colview(hT[:, kc, :], d, blk * 128, 128)
                        fns.append(lambda e, ps=ps, tv=tv, wv=wv, kc=kc, half=half: e.matmul(
                            ps[:, :], lhsT=tv, rhs=wv[:, kc, half * 512:(half + 1) * 512], start=(kc == 0), stop=(kc == KC - 1)))
                    s.op("pe", fns, reads=[hdep, wvdep], writes=[psd])
                    eng = "act" if half == 0 else "dve"
                    if eng == "act":
                        s.op("act", lambda e, ps=ps, blk=blk, half=half: e.activation(
                            out=vstage[:, half * 4:(half + 1) * 4, blk, :], in_=ps[:, :].rearrange("p (h e) -> p h e", h=4), func=AF.Copy),
                            reads=[psd], writes=[vsdep])
                    else:
                        s.op("dve", lambda e, ps=ps, blk=blk, half=half: e.tensor_copy(
                            out=vstage[:, half * 4:(half + 1) * 4, blk, :], in_=ps[:, :].rearrange("p (h e) -> p h e", h=4)),
                            reads=[psd], writes=[vsdep])
            nbg = 16 // d
            for h in range(NH):
                outtoks.append(s.op("sp", lambda e, g=g, h=h: e.dma_start(out=v_out[g, h, :, :, :], in_=vstage[:, h, :, :]),
                                    reads=[vsdep], dma=True))
                s.op("sp", lambda e, g=g, h=h, d=d, nbg=nbg: e.dma_start(
                    out=vmsg[h, :, BOFF[g]:BOFF[g] + d, :],
                    in_=vstage[:, h, :, :].rearrange("p (r n) e -> p r n e", r=d)[:, :, nbg - 1, :]), reads=[vsdep], writes=[vmd[h]], dma=True)
        quads = [(h, g, qd) for h in range(NH) for g in range(NG) for qd in range(4)]
        qst = {}
        tiles = {}

        def kP(i):
            h, g, qd = quads[i]
            d = DIL[g]
            if qd == 0:
                wk, wkdep = wkring.next()
                s.op("pool", lambda e: e.dma_start(out=wk[:, :, :], in_=wk_d[g, h, :, :, :]), writes=[wkdep], dma=True)
                kt, ktdep = kring.next()
                tiles[(h, g)] = (wk, wkdep, kt, ktdep)
                if g == 0 and h == 0:
                    for hh in range(NH):
                        io["coll_v"](hh, vmd[hh])
                if g == 1 and h >= 1:
                    io["coll_k"](h - 1, kmd[h - 1])
            wk, wkdep, kt, ktdep = tiles[(h, g)]
            ps, psd = psring.next()
            s.op("pe", [lambda e, kc=kc: e.matmul(ps[:, :], lhsT=wk[:, kc, :], rhs=hT[:, kc, qd * 512:(qd + 1) * 512], start=(kc == 0), stop=(kc == KC - 1))
                        for kc in range(KC)], reads=[hdep, wkdep], writes=[psd])
            t0 = qd * 512
            if d == 1:
                tmp, tmpd = kt[:, t0:t0 + 512], ktdep
            else:
                tmp, tmpd = tbring.next()
                tmp = tmp[:, :]
            s.op("act", lambda e: e.activation(out=tmp, in_=ps[:, :], func=AF.Copy), reads=[psd], writes=[tmpd])
            qst[i] = dict(ps=ps, psd=psd, tmp=tmp, tmpd=tmpd, kt=kt, ktdep=ktdep, d=d, t0=t0, h=h, g=g, qd=qd)

        def kR23(i):
            q = qst[i]
            ps, psd, tmp, tmpd, t0 = q["ps"], q["psd"], q["tmp"], q["tmpd"], q["t0"]
            pk, pkd = pkring.next()
            s.op("pe", lambda e: e.matmul(pk[0:32, :], lhsT=pm[0:32, 0:32], rhs=tmp[0:32, :], start=True, stop=True), reads=[tmpd, pmdep], writes=[pkd])
            t1, t1d = t1ring.next()
            t2, t2d = t1ring.next()
            s.op("dve", lambda e: e.tensor_tensor(out=t1[0:32, :], in0=ps[0:32, :], in1=cosT[0:32, t0:t0 + 512], op=ALU.mult), reads=[psd, tabdep], writes=[t1d])
            s.op("dve", lambda e: e.tensor_tensor(out=t2[0:32, :], in0=pk[0:32, :], in1=sinT[0:32, t0:t0 + 512], op=ALU.mult), reads=[pkd, tabdep], writes=[t2d])
            s.op("dve", lambda e: e.tensor_tensor(out=tmp[0:32, :], in0=t1[0:32, :], in1=t2[0:32, :], op=ALU.add), reads=[t1d, t2d, tmpd], writes=[tmpd])

        def kR4(i):
            q = qst.pop(i)
            tmp, tmpd, kt, ktdep, d, t0, h, g, qd = q["tmp"], q["tmpd"], q["kt"], q["ktdep"], q["d"], q["t0"], q["h"], q["g"], q["qd"]
            if d != 1:
                n = 512 // d
                l0 = t0 // d
                dv = kt[:, :].rearrange("p (r l) -> p l r", r=d)[:, l0:l0 + n, :]
                s.op("act", lambda e: e.activation(out=dv, in_=tmp.rearrange("p (l r) -> p l r", r=d), func=AF.Copy), reads=[tmpd], writes=[ktdep])
            if qd == 3:
                outtoks.append(s.op("sp", lambda e: e.dma_start(out=k_out[g, h, :, :], in_=kt[:, :]), reads=[ktdep], dma=True))
                Lg = T // d
                s.op("sp", lambda e: e.dma_start(out=kmsg[h, :, KOFF[g]:KOFF[g] + d * 128].rearrange("p (r l) -> p r l", r=d),
                                                 in_=kt[:, :].rearrange("p (r l) -> p r l", r=d)[:, :, Lg - 128:Lg]), reads=[ktdep], writes=[kmd[h]], dma=True)

        NQ = len(quads)
        for it in range(NQ + 2):
            if 0 <= it - 2 < NQ:
                kR4(it - 2)
            if 0 <= it - 1 < NQ:
                kR23(it - 1)
            if it < NQ:
                kP(it)
        io["coll_k"](NH - 1, kmd[NH - 1])
        s.barrier(exclude_cc=True)
        s.emit()


def rope_tables_np(pos0):
    pos = np.arange(pos0, pos0 + T, dtype=np.float32)
    inv = (np.float32(500000.0) ** (-np.arange(0, 32, 2, dtype=np.float32) / np.float32(32))).astype(np.float32)
    ang = (pos[None, :] * inv[:, None]).astype(np.float32)
    c = np.ones((128, T), np.float32)
    sn = np.zeros((128, T), np.float32)
    c[0:16] = np.cos(ang)
    c[16:32] = np.cos(ang)
    sn[0:16] = -np.sin(ang)
    sn[16:32] = np.sin(ang)
    return c, sn


def perm_matrix():
    pm = np.zeros((128, 128), np.float32)
    for e in range(16):
        pm[e + 16, e] = 1.0
        pm[e, e + 16] = 1.0
    return pm


def run_attn_kv(xTs, nw, w_qkv):
    nc = _prog("attn_kv", build_attn_kv)
    wr = w_qkv.reshape(KC, 128, NG, 3, NH, 128)
    wk = np.ascontiguousarray(wr[:, :, :, 1].transpose(2, 3, 1, 0, 4))
    wv = np.ascontiguousarray(wr[:, :, :, 2].transpose(2, 1, 0, 3, 4).reshape(NG, 128, KC, 1024))
    pm = perm_matrix()
    maps = []
    for c in range(NCORES):
        cs, sn = rope_tables_np((c % 4) * T)
        maps.append({"x_own": xTs[c], "nw": _cols128(nw), "wk": wk, "wv": wv, "cosT": cs, "sinT": sn, "pm": pm})
    res = run_bass_kernel_spmd(nc, maps, core_ids=list(range(NCORES)))
    return [(np.asarray(r["k_out"]), np.asarray(r["v_out"])) for r in res.results]


def emit_attn_main(P, io):
    nc = P.nc
    x_own, nw_d, wq_d, wo_d, cos_d, sin_d, pm_d, mk_d = io["x_in"], io["nw"], io["wq"], io["wo"], io["cosT"], io["sinT"], io["pm"], io["masks"]
    k_own, v_own, x_out = io["k_own"], io["v_own"], io["x_out"]
    with contextlib.ExitStack() as es:
        cx = Ctx(nc, es, P.S)
        s = cx.s
        hT = cx.sb([128, KC, T], BF16, "hT")
        hdep = Dep()
        aT = cx.sb([128, NH, T], BF16, "aT")
        adeps = [Dep() for _ in range(NH)]
        nw = cx.sb([128, KC], F32, "nw")
        ones = cx.sb([128, 129], F32, "ones")
        onesb = cx.sb([128, 128], BF16, "onesb")
        cosT = cx.sb([128, T], F32, "cosT")
        sinT = cx.sb([128, T], F32, "sinT")
        pm = cx.sb([128, 128], BF16, "pm")
        mk = cx.sb([128, 3, 512], BF16, "mk")
        idm = cx.sb([128, 128], BF16, "idm")
        idmdep = Dep()
        nwdep, onesdep, tabdep, pmdep, mkdep, onesbdep = Dep(), Dep(), Dep(), Dep(), Dep(), Dep()
        psring = Ring([cx.ps([128, 512], F32, "ps") for _ in range(2)], ps=True)
        pkring = Ring([cx.ps([128, 512], F32, "pk") for _ in range(1)], ps=True)
        sring = Ring([cx.ps([128, 512], F32, "pss") for _ in range(3)], ps=True)
        odring = Ring([cx.ps([128, 512], F32, "pod") for _ in range(2)], ps=True)
        sqring = Ring([cx.sb([128, 512], F32, "sq") for _ in range(2)])
        rsring = Ring([cx.sb([128, 512], F32, "rs") for _ in range(2)])
        t1ring = Ring([cx.sb([128, 512], F32, "t1") for _ in range(4)])
        tbring = Ring([cx.sb([128, 512], BF16, "tb16") for _ in range(3)])
        wqring = Ring([cx.sb([128, KC, 128], BF16, "wq") for _ in range(3)])
        woring = Ring([cx.sb([128, NH, 128], BF16, "wo") for _ in range(2)])
        kring = Ring([cx.sb([128, 4096], BF16, "ks") for _ in range(2)])
        vring = Ring([cx.sb([128, 4096], BF16, "vs") for _ in range(2)])
        qring = Ring([cx.sb([128, T], BF16, "qs") for _ in range(2)])
        pring = Ring([cx.sb([128, 512], BF16, "pT") for _ in range(3)])
        acc = cx.sb([128, 2, T], F32, "acc")
        accdep = Dep()
        rden = cx.sb([128, T], F32, "rden")
        rdendep = Dep()
        oring = Ring([cx.sb([128, 512], F32, "ob") for _ in range(6)])

        s.op("sp", lambda e: e.dma_start(out=nw[:, :], in_=nw_d[:, :]), writes=[nwdep], dma=True)
        s.op("sp", lambda e: e.dma_start(out=cosT[:, :], in_=cos_d[:, :]), writes=[tabdep], dma=True)
        s.op("sp", lambda e: e.dma_start(out=sinT[:, :], in_=sin_d[:, :]), writes=[tabdep], dma=True)
        s.op("pool", lambda e: e.dma_start(out=pm[:, :], in_=pm_d[:, :]), writes=[pmdep], dma=True)
        s.op("pool", lambda e: e.dma_start(out=mk[:, :, :], in_=mk_d[:, :, :]), writes=[mkdep], dma=True)
        s.op("dve", lambda e: e.memset(ones[:, 0:128], 1.0), writes=[onesdep])
        s.op("dve", lambda e: e.memset(ones[:, 128:129], EPS), writes=[onesdep])
        s.op("dve", lambda e: e.memset(onesb[:, :], 1.0), writes=[onesbdep])
        s.op("pool", lambda e: e.memset(idm[:, :], 1.0), writes=[idmdep])
        s.op("pool", lambda e: e.affine_select(out=idm[:, :], in_=idm[:, :], pattern=[[1, 128]], compare_op=ALU.is_equal, fill=0.0, base=0, channel_multiplier=-1),
             reads=[idmdep], writes=[idmdep])
        for kc in range(KC):
            s.op("sp", lambda e, kc=kc: e.dma_start(out=hT[:, kc, :], in_=P.hsave[:, kc, 0:T]), writes=[hdep], dma=True)

        def QP(h, g):
            d = DIL[g]
            L = T // d
            nb = L // 128
            LK = 128 + L
            ks, ksdep = kring.next()
            vs, vsdep = vring.next()
            ksv = ks[:, 0:d * LK].rearrange("p (r l) -> p r l", r=d)
            vsv = vs[:, 0:d * (nb + 1) * 128].rearrange("p (r n e) -> p r n e", r=d, n=nb + 1)
            s.op("sp", lambda e, ksv=ksv, g=g, h=h, d=d: e.dma_start(
                out=ksv[:, :, 0:128], in_=io["khalo"](g, h)),
                reads=[io["klocd"][h]], writes=[ksdep], dma=True)
            s.op("sp", lambda e, ksv=ksv, g=g, h=h, d=d, LK=LK: e.dma_start(
                out=ksv[:, :, 128:LK], in_=k_own[g, h, :, :].rearrange("p (r l) -> p r l", r=d)),
                writes=[ksdep], dma=True)
            s.op("sp", lambda e, vsv=vsv, g=g, h=h, d=d: e.dma_start(
                out=vsv[:, :, 0, :], in_=io["vhalo"](g, h)),
                reads=[io["vlocd"][h]], writes=[vsdep], dma=True)
            s.op("sp", lambda e, vsv=vsv, g=g, h=h, d=d, nb=nb: e.dma_start(
                out=vsv[:, :, 1:nb + 1, :], in_=v_own[g, h, :, :, :].rearrange("p (r n) e -> p r n e", r=d)),
                writes=[vsdep], dma=True)
            wq, wqdep = wqring.next()
            s.op("pool", lambda e, wq=wq, g=g, h=h: e.dma_start(out=wq[:, :, :], in_=wq_d[g, h, :, :, :]), writes=[wqdep], dma=True)
            qs, qsdep = qring.next()
            for qd in range(4):
                ps, psd = psring.next()
                s.op("pe", [lambda e, ps=ps, wq=wq, kc=kc, qd=qd: e.matmul(
                    ps[:, :], lhsT=wq[:, kc, :], rhs=hT[:, kc, qd * 512:(qd + 1) * 512], start=(kc == 0), stop=(kc == KC - 1)) for kc in range(KC)],
                    reads=[hdep, wqdep], writes=[psd])
                emit_rope(cx, ps, psd, qs[:, :], qsdep, d, qd * 512, cosT, sinT, tabdep, pm, pmdep, pkring, t1ring, tbring)

            return dict(ksv=ksv, vsv=vsv, qs=qs, ksdep=ksdep, vsdep=vsdep, qsdep=qsdep, d=d, nb=nb)

        def UN(h, g, st):
            ksv, vsv, qs, ksdep, vsdep, qsdep, d, nb = st['ksv'], st['vsv'], st['qs'], st['ksdep'], st['vsdep'], st['qsdep'], st['d'], st['nb']
            def qk(pr, ksv=ksv, qs=qs, nb=nb, ksdep=ksdep, qsdep=qsdep, d=d):
                pss, pssd = sring.next()
                fns = []
                for uu in range(2):
                    u = pr * 2 + uu
                    r, n = divmod(u, nb)
                    for half in range(2):
                        fns.append(lambda e, pss=pss, r=r, n=n, u=u, uu=uu, half=half: e.matmul(
                            pss[:, uu * 256 + half * 128: uu * 256 + half * 128 + 128],
                            lhsT=ksv[:, r, (n + half) * 128:(n + half + 1) * 128],
                            rhs=qs[:, u * 128:(u + 1) * 128], start=(uu == 0 and half == 0), stop=False, skip_group_check=True))
                if d == 1:
                    var = 0 if pr == 0 else 1
                elif d == 4:
                    var = 0 if pr % 2 == 0 else 1
                else:
                    var = 2
                fns.append(lambda e, pss=pss, var=var: e.matmul(pss[:, :], lhsT=idm[:, :], rhs=mk[:, var, :], start=False, stop=True, skip_group_check=True))
                s.op("pe", fns, reads=[ksdep, qsdep, mkdep, idmdep], writes=[pssd])
                return pss, pssd

            def rest(pr, pss, pssd, vsv=vsv, d=d, g=g, nb=nb, vsdep=vsdep):
                pT, pTd = pring.next()
                s.op("act", lambda e: e.activation(out=pT[:, :], in_=pss[:, :], func=AF.Exp, scale=SCALE),
                     reads=[pssd], writes=[pTd])
                pod, podd = odring.next()
                fns = []
                for uu in range(2):
                    u = pr * 2 + uu
                    r, n = divmod(u, nb)
                    for half in range(2):
                        fns.append(lambda e, r=r, n=n, uu=uu, half=half: e.matmul(
                            pod[:, uu * 128:(uu + 1) * 128], lhsT=vsv[:, r, n + half, :],
                            rhs=pT[:, uu * 256 + half * 128: uu * 256 + half * 128 + 128],
                            start=(half == 0), stop=(half == 1)))
                    for half in range(2):
                        fns.append(lambda e, uu=uu, half=half: e.matmul(
                            pod[:, 256 + uu * 128: 256 + (uu + 1) * 128], lhsT=onesb[:, :],
                            rhs=pT[:, uu * 256 + half * 128: uu * 256 + half * 128 + 128],
                            start=(half == 0), stop=(half == 1)))
                s.op("pe", fns, reads=[vsdep, pTd, onesbdep], writes=[podd])
                return pod, podd

            def rest2(pr, pod, podd, d=d, g=g):
                if d == 16:
                    for w in range(2):
                        av, A = colview(acc[:, w, :], d, pr * 256, 256)
                        src = v3(pod[:, w * 256:(w + 1) * 256], A)
                        s.op("dve", lambda e, av=av, src=src: e.tensor_tensor(out=av, in0=src, in1=av, op=ALU.add),
                             reads=[podd, accdep], writes=[accdep])
                else:
                    if d == 1:
                        av = acc[:, :, pr * 256:(pr + 1) * 256]
                    else:
                        L4 = T // 4
                        r0, l0 = divmod(pr * 256, L4)
                        av = acc[:, :, :].rearrange("p w (l r) -> p w r l", r=4)[:, :, r0, l0:l0 + 256]
                    src = pod[:, :].rearrange("p (w c) -> p w c", w=2)
                    if g == 0:
                        s.op("dve", lambda e, av=av, src=src: e.tensor_copy(out=av, in_=src), reads=[podd], writes=[accdep])
                    else:
                        s.op("dve", lambda e, av=av, src=src: e.tensor_tensor(out=av, in0=src, in1=av, op=ALU.add),
                             reads=[podd, accdep], writes=[accdep])

            prev = qk(0)
            pend = None
            for pr in range(8):
                nxt = qk(pr + 1) if pr + 1 < 8 else None
                pod, podd = rest(pr, *prev)
                if pend is not None:
                    rest2(*pend)
                pend = (pr, pod, podd)
                prev = nxt
            rest2(*pend)
            if g == NG - 1:
                s.op("dve", lambda e: e.reciprocal(out=rden[:, :], in_=acc[:, 1, :]), reads=[accdep], writes=[rdendep])
                s.op("dve", lambda e, h=h: e.tensor_tensor(out=aT[:, h, :], in0=acc[:, 0, :], in1=rden[:, :], op=ALU.mult),
                     reads=[accdep, rdendep], writes=[adeps[h]])

        units = [(h, g) for h in range(NH) for g in range(NG)]
        stq = QP(*units[0])
        for ui, (h, g) in enumerate(units):
            nstq = QP(*units[ui + 1]) if ui + 1 < len(units) else None
            UN(h, g, stq)
            stq = nstq

        wo_v = wo_d.rearrange("(h e) d -> e h d", e=128)
        wops = Ring(psring.aps + sring.aps + odring.aps)
        wops.deps = psring.deps + sring.deps + odring.deps
        outtoks = []
        for dc in range(KC):
            wo, wodep = woring.next()
            s.op("pool", lambda e, wo=wo, dc=dc: e.dma_start(out=wo[:, :, :], in_=wo_v[:, :, dc * 128:(dc + 1) * 128]), writes=[wodep], dma=True)
            for tb in range(4):
                ps, psd = wops.next()
                s.op("pe", [lambda e, ps=ps, wo=wo, h=h, tb=tb: e.matmul(
                    ps[:, :], lhsT=wo[:, h, :], rhs=aT[:, h, tb * 512:(tb + 1) * 512], start=(h == 0), stop=(h == NH - 1))
                    for h in range(NH)], reads=[wodep] + adeps, writes=[psd])
                ob, obd = oring.next()
                s.op("sp", lambda e, ob=ob, dc=dc, tb=tb: e.dma_start(out=ob[:, :], in_=x_own[dc * 128:(dc + 1) * 128, tb * 512:(tb + 1) * 512]),
                     writes=[obd], dma=True)
                s.op("dve", lambda e, ob=ob, ps=ps: e.tensor_tensor(out=ob[:, :], in0=ps[:, :], in1=ob[:, :], op=ALU.add),
                     reads=[psd, obd], writes=[obd])
                outtoks.append(s.op("sp", lambda e, ob=ob, dc=dc, tb=tb: e.dma_start(
                    out=x_out[dc * 128:(dc + 1) * 128, tb * 512:(tb + 1) * 512], in_=ob[:, :]), reads=[obd], dma=True))
                if tb == 3 and io.get("hmsg") is not None:
                    s.op("sp", lambda e, ob=ob, dc=dc: e.dma_start(out=io["hmsg"][:, dc, :], in_=ob[:, 512 - HALO:512]), reads=[obd], dma=True)
        s.barrier()
        s.emit()


def attn_masks(valid):
    j = np.arange(128)[:, None]
    i = np.arange(128)[None, :]
    prevN = (j >= i).astype(np.float32)
    cur = (j <= i).astype(np.float32)
    prevH = prevN * np.float32(1.0 if valid else 0.0)
    m = np.zeros((128, 3, 512), np.float32)
    for v, (a, b) in enumerate(((prevH, prevN), (prevN, prevN), (prevH, prevH))):
        m[:, v, 0:128] = a
        m[:, v, 128:256] = cur
        m[:, v, 256:384] = b
        m[:, v, 384:512] = cur
    return np.where(m > 0.5, np.float32(0.0), np.float32(-30000.0)).astype(np.float32)


def run_attn(xTs, nw, w_qkv, w_o):
    kv = run_attn_kv(xTs, nw, w_qkv)
    nc = _prog("attn_main", build_attn_main)
    wr = w_qkv.reshape(KC, 128, NG, 3, NH, 128)
    wq = np.ascontiguousarray(wr[:, :, :, 0].transpose(2, 3, 1, 0, 4))
    pm = perm_matrix()
    maps = []
    for c in range(NCORES):
        cs, sn = rope_tables_np((c % 4) * T)
        k_own, v_own = kv[c]
        k_halo = np.zeros_like(k_own)
        v_halo = np.zeros_like(v_own)
        if c % 4 != 0:
            kp, vp = kv[c - 1]
            for g in range(NG):
                d = DIL[g]
                L = T // d
                nb = L // 128
                k_halo[g, :, :, 0:d * 128] = kp[g].reshape(NH, 128, d, L)[:, :, :, L - 128:].reshape(NH, 128, d * 128)
                v_halo[g, :, :, 0:d, :] = vp[g].reshape(NH, 128, d, nb, 128)[:, :, :, nb - 1, :]
        maps.append({"x_own": xTs[c], "nw": _cols128(nw), "wq": wq, "wo": np.ascontiguousarray(w_o),
                     "cosT": cs, "sinT": sn, "pm": pm, "masks": attn_masks(c % 4 != 0),
                     "k_own": k_own, "k_halo": k_halo, "v_own": v_own, "v_halo": v_halo})
    res = run_bass_kernel_spmd(nc, maps, core_ids=list(range(NCORES)))
    return [np.asarray(r["x_out"]) for r in res.results]


DIN = 2048
NHS = 32
CONV_ENG2 = "dve"
OFF_ENG = "pool"


def emit_ssm(P, io, phase):
    debug = False
    B_ = phase == "B"
    nc = P.nc
    dbg = {}
    x_own, nw_d, win_d, wdt_d, cw_d, vec_d = io["x_in"], io["nw"], io["win"], io["wdt"], io["cw"], io["vecs"]
    if B_:
        wz_d, gw_d, wout_d, x_out = io["wz"], io["gw"], io["wout"], io["x_out"]
    else:
        smsg = io["smsg"]
    W = HALO + T
    with contextlib.ExitStack() as es:
        cx = Ctx(nc, es, P.S)
        s = cx.s
        hT = cx.sb([128, KC, W], BF16, "hT")
        hdep = Dep()
        nw = cx.sb([128, KC], F32, "nw")
        ones = cx.sb([128, 129], F32, "ones")
        U = cx.sb([128, 128], F32, "U")
        idb = cx.sb([128, 128], BF16, "idb")
        cw = cx.sb([128, 32, 5], F32, "cw")
        wdt = cx.sb([128, KC, 32], BF16, "wdt")
        vecs = cx.sb([128, 3, 32], F32, "vecs")
        nwdep, onesdep, Udep, iddep, cwdep, wdtdep, vecdep = [Dep() for _ in range(7)]
        psring = Ring([cx.ps([128, 512], F32, "ps") for _ in range(3)], ps=True)
        ptr = cx.ps([128, 8, 128], BF16, "ptr")
        ptrd = Dep(True)
        misc = cx.ps([128, 512], F32, "misc")
        miscd = Dep(True)
        cbk = cx.ps([128, 512], F32, "cbk")
        cbkd = Dep(True)
        ybk = cx.ps([128, 512], F32, "ybk")
        ybkd = Dep(True)
        stbk = cx.ps([128, 512], F32, "stbk")
        stbkd = Dep(True)
        ybring = Ring([ybk, stbk])
        ybring.deps = [ybkd, stbkd]
        if not B_:
            xbr = Ring([cx.sb([128, KC, 512], F32, "xb") for _ in range(2)])
            sq5 = Ring([cx.sb([128, 512], F32, "sq5") for _ in range(3)])
            rs5 = Ring([cx.sb([128, 512], F32, "rs5") for _ in range(2)])
        wring = Ring([cx.sb([128, KC, 128], BF16, "w") for _ in range(3)])
        rawring = Ring([cx.sb([128, 3 + 512], F32, "raw") for _ in range(2)])
        cvring = Ring([cx.sb([128, 512], F32, "cv") for _ in range(2 if B_ else 3)])
        if not B_:
            xcring = Ring([cx.sb([128, 512], BF16, "xc") for _ in range(2)])
        tail = cx.sb([128, 32, 3], F32, "tail")
        taild = [Dep() for _ in range(32)]
        xtok = cx.sb([128, 4, DIN], BF16, "xtok")
        xtokd = Dep()
        btok = cx.sb([128, 4, 1024], BF16, "btok")
        btokd = Dep()
        BT = cx.sb([128, 8, 512], BF16, "BT")
        BTd = Dep()
        dt = cx.sb([128, 4, 32], F32, "dt")
        adt = cx.sb([128, 4, 32], F32, "adt")
        dtd, adtd = Dep(), Dep()
        sm = cx.sb([128, 8, 4, 32], F32, "sm")
        smd = [Dep() for _ in range(8)]
        tsum = cx.sb([128, 32], F32, "tsum")
        tsumd = Dep()
        xdtd = cx.sb([128, DIN], BF16, "xdtd")
        xdtdd = Dep()
        S = cx.sb([128, DIN], F32, "S")
        Sd = Dep()
        if B_:
            CT = cx.sb([128, 8, 512], BF16, "CT")
            CTd = Dep()
            wzring = Ring([cx.sb([128, KC, 512], BF16, "wz") for _ in range(2)])
            sz = cx.sb([128, 4, DIN], BF16, "sz")
            szd = Dep()
            gw = cx.sb([128, 16], F32, "gw")
            gwd = Dep()
            xdt = cx.sb([128, DIN], BF16, "xdt")
            xsD = cx.sb([128, DIN], BF16, "xsD")
            xdt_d, xsD_d = Dep(), Dep()
            Sb = cx.sb([128, DIN], BF16, "Sb")
            Sbd = Dep()
            cbmq = [cx.sb([128, 4, 128], F32, "cbmq%d" % i) for i in range(2)]
            cbmqd = [Dep(), Dep()]
            ttring = Ring([cx.sb([128, 512], F32, "tt") for _ in range(2)])
            lxring = Ring([cx.sb([128, 512], F32, "lx") for _ in range(2)])
            mtring = Ring([cx.sb([128, 512], BF16, "mt") for _ in range(2)])
            ytring = Ring([cx.sb([128, 256], F32, "yt") for _ in range(2)])
            y = cx.sb([128, DIN], F32, "y")
            yd = Dep()
            xb = y[:, :].rearrange("p (kc c) -> p kc c", kc=KC)
            xbd = yd
            junk = cx.sb([128, 256], BF16, "junk")
            junkd = Dep()
            ssq = cx.sb([128, 16], F32, "ssq")
            ssqd = Dep()
            gn = cx.sb([128, DIN], BF16, "gn")
            gnd = Dep()
            gnT = cx.sb([128, 16, 512], BF16, "gnT")
            gnTd = Dep()
            woring = Ring([cx.sb([128, 16, 128], BF16, "wo") for _ in range(2)])
            oring = Ring([cx.sb([128, 512], F32, "ob") for _ in range(2)])
            spl = cx.sb([128, 3, 32], F32, "spl")
            spld = Dep()

        s.op("sp", lambda e: e.dma_start(out=nw[:, :], in_=nw_d[:, :]), writes=[nwdep], dma=True)
        s.op("sp", lambda e: e.dma_start(out=cw[:, :, :], in_=cw_d[:, :, :]), writes=[cwdep], dma=True)
        s.op("pool", lambda e: e.dma_start(out=wdt[:, :, :], in_=wdt_d[:, :, :]), writes=[wdtdep], dma=True)
        for i in range(3):
            s.op("sp", lambda e, i=i: e.dma_start(out=vecs[:, i, :], in_=vec_d[i, :].partition_broadcast(128)), writes=[vecdep], dma=True)
        s.op("dve", lambda e: e.memset(ones[:, 0:128], 1.0), writes=[onesdep])
        s.op("dve", lambda e: e.memset(ones[:, 128:129], EPS), writes=[onesdep])
        s.op("pool", lambda e: e.memset(U[:, :], 1.0), writes=[Udep])
        s.op("pool", lambda e: e.affine_select(out=U[:, :], in_=U[:, :], pattern=[[1, 128]], compare_op=ALU.is_ge, fill=0.0, base=0, channel_multiplier=-1),
             reads=[Udep], writes=[Udep])
        s.op("pool", lambda e: e.memset(idb[:, :], 1.0), writes=[iddep])
        s.op("pool", lambda e: e.affine_select(out=idb[:, :], in_=idb[:, :], pattern=[[1, 128]], compare_op=ALU.is_equal, fill=0.0, base=0, channel_multiplier=-1),
             reads=[iddep], writes=[iddep])
        s.op("act", lambda e: e.activation(out=vecs[:, 1, :], in_=vecs[:, 1, :], func=AF.Exp), reads=[vecdep], writes=[vecdep])
        s.op("dve", lambda e: e.tensor_scalar(out=vecs[:, 1, :], in0=vecs[:, 1, :], scalar1=-1.0, scalar2=0.0, op0=ALU.mult, op1=ALU.add), reads=[vecdep], writes=[vecdep])
        s.op("dve", lambda e: e.memset(tsum[:, :], 0.0), writes=[tsumd])
        def s_init():
            HS = SMSG // 2
            sallA = P.sall[0].rearrange("(r p) f -> r p f", p=128)
            sallB = P.sall[1].rearrange("(r p) f -> r p f", p=128)
            fl2 = cx.sb([128, 8], F32, "fl2")
            fl2d = Dep()
            s.op("sp", lambda e: e.dma_start(out=fl2[:, :], in_=P.flags[:, :]), writes=[fl2d], dma=True)
            s.op("dve", lambda e: e.memset(S[:, :], 0.0), writes=[Sd])
            for r in range(4):
                s.op("sp", lambda e, r=r: e.dma_start(out=spl[:, 0, :], in_=sallB[r, :, DIN - HS:DIN - HS + NHS]), writes=[spld], dma=True)
                s.op("sp", lambda e, r=r: e.dma_start(out=y[:, 0:HS], in_=sallA[r, :, :]), writes=[yd], dma=True)
                s.op("sp", lambda e, r=r: e.dma_start(out=y[:, HS:DIN], in_=sallB[r, :, 0:DIN - HS]), writes=[yd], dma=True)
                s.op("act", lambda e, r=r: e.activation(out=spl[:, 1, :], in_=spl[:, 0, :], func=AF.Exp, scale=fl2[:, 4 + r:5 + r]), reads=[spld, fl2d], writes=[spld])
                s.op("dve", lambda e: e.tensor_tensor(out=S[:, :].rearrange("p (h q) -> p h q", h=NHS), in0=S[:, :].rearrange("p (h q) -> p h q", h=NHS),
                                                      in1=spl[:, 1, :].unsqueeze(2).to_broadcast([128, NHS, 64]), op=ALU.mult), reads=[Sd, spld], writes=[Sd])
                s.op("dve", lambda e, r=r: e.scalar_tensor_tensor(out=S[:, :], in0=y[:, :], scalar=fl2[:, 4 + r:5 + r], in1=S[:, :], op0=ALU.mult, op1=ALU.add),
                     reads=[Sd, yd, fl2d], writes=[Sd])
            s.op("act", lambda e: e.activation(out=Sb[:, :], in_=S[:, :], func=AF.Copy), reads=[Sd], writes=[Sbd])

        if B_:
            s.op("sp", lambda e: e.dma_start(out=gw[:, :], in_=gw_d[:, :]), writes=[gwd], dma=True)
            pass
        else:
            s.op("dve", lambda e: e.memset(S[:, :], 0.0), writes=[Sd])

        def norm_cols(load_fn, n, c0):
            if not B_:
                xb, xbd = xbr.next()
                sqring_, rsring_ = sq5, rs5
            ps, psd = psring.next()
            load_fn(xb, xbd)
            for kc in range(KC):
                sq, sqd = sqring_.next()
                s.op("act", lambda e, sq=sq, kc=kc: e.activation(out=sq[:, 0:n], in_=xb[:, kc, 0:n], func=AF.Square), reads=[xbd], writes=[sqd])
                s.op("pe", lambda e, ps=ps, sq=sq, kc=kc: e.matmul(ps[:, 0:n], lhsT=ones[:, 0:128], rhs=sq[:, 0:n], start=(kc == 0), stop=(kc == KC - 1)),
                     reads=[sqd, onesdep], writes=[psd])
            rs, rsd = rsring_.next()
            s.op("act", lambda e, rs=rs, ps=ps: e.activation(out=rs[:, 0:n], in_=ps[:, 0:n], func=AF.Sqrt, bias=ones[:, 128:129], scale=1.0 / D),
                 reads=[psd, onesdep], writes=[rsd])
            s.op("dve", lambda e, rs=rs: e.reciprocal(out=rs[:, 0:n], in_=rs[:, 0:n]), reads=[rsd], writes=[rsd])
            for kc in range(KC):
                s.op("dve", lambda e, rs=rs, kc=kc: e.scalar_tensor_tensor(out=hT[:, kc, c0:c0 + n], in0=xb[:, kc, 0:n], scalar=nw[:, kc:kc + 1], in1=rs[:, 0:n],
                                                                         op0=ALU.mult, op1=ALU.mult), reads=[xbd, rsd, nwdep], writes=[hdep])

        if B_:
            for kc in range(KC):
                s.op("sp", lambda e, kc=kc: e.dma_start(out=hT[:, kc, :], in_=P.hsave[:, kc, :]), writes=[hdep], dma=True)
        else:
            norm_cols(lambda xb, xbd: emit_halo(cx, P, xb[:, :, 0:HALO], [xbd]), HALO, 0)
            for blk in range(T // 512):
                def ld(xb, xbd, blk=blk):
                    for kc in range(KC):
                        s.op("sp", lambda e, kc=kc: e.dma_start(out=xb[:, kc, :], in_=x_own[kc * 128:(kc + 1) * 128, blk * 512:(blk + 1) * 512]), writes=[xbd], dma=True)
                norm_cols(ld, 512, HALO + blk * 512)
            for kc in range(KC):
                s.op("sp", lambda e, kc=kc: e.dma_start(out=P.hsave[:, kc, :], in_=hT[:, kc, :]), reads=[hdep], dma=True)

        def bc64(ap32):
            return ap32.unsqueeze(2).to_broadcast([128, ap32.shape[1], 64])

        def h64(ap):
            return ap.rearrange("p (h q) -> p h q", q=64)

        outtoks = []
        deferred = []
        nchunks = 32 if B_ else 24
        for tb in range(4):
            c0 = HALO + tb * 512
            pst = {}

            def s1(cc, tb=tb, c0=c0):
                w, wdep = wring.next()
                s.op("pool", lambda e: e.dma_start(out=w[:, :, :], in_=win_d[cc, :, :, :]), writes=[wdep], dma=True)
                raw, rawd = rawring.next()
                if tb == 0:
                    s.op("pe", [lambda e, kc=kc: e.matmul(misc[:, 0:HALO], lhsT=w[:, kc, :], rhs=hT[:, kc, 0:HALO], start=(kc == 0), stop=(kc == KC - 1))
                                for kc in range(KC)], reads=[wdep, hdep], writes=[miscd])
                    s.op("act", lambda e: e.activation(out=raw[:, 0:HALO], in_=misc[:, 0:HALO], func=AF.Copy), reads=[miscd], writes=[rawd])
                else:
                    s.op("act", lambda e: e.activation(out=raw[:, 0:HALO], in_=tail[:, cc, :], func=AF.Copy), reads=[taild[cc]], writes=[rawd])
                ps, psd = psring.next()
                s.op("pe", [lambda e, kc=kc: e.matmul(ps[:, :], lhsT=w[:, kc, :], rhs=hT[:, kc, c0:c0 + 512], start=(kc == 0), stop=(kc == KC - 1))
                            for kc in range(KC)], reads=[wdep, hdep], writes=[psd])
                s.op("act", lambda e: e.activation(out=raw[:, HALO:HALO + 512], in_=ps[:, :], func=AF.Copy), reads=[psd], writes=[rawd])
                if tb < 3:
                    s.op("act", lambda e: e.activation(out=tail[:, cc, :], in_=raw[:, 512:515], func=AF.Copy), reads=[rawd], writes=[taild[cc]])
                pst[cc] = {"raw": raw, "rawd": rawd}

            def s2(cc):
                raw, rawd = pst[cc]["raw"], pst[cc]["rawd"]
                cv, cvd = cvring.next()
                eng = "dve" if cc % 2 == 0 else CONV_ENG2
                s.op(eng, lambda e: e.tensor_scalar(out=cv[:, :], in0=raw[:, 3:515], scalar1=cw[:, cc, 3:4], scalar2=cw[:, cc, 4:5],
                                                    op0=ALU.mult, op1=ALU.add), reads=[rawd, cwdep], writes=[cvd])
                for k in range(3):
                    s.op(eng, lambda e, k=k: e.scalar_tensor_tensor(out=cv[:, :], in0=raw[:, k:k + 512], scalar=cw[:, cc, k:k + 1], in1=cv[:, :],
                                                                    op0=ALU.mult, op1=ALU.add), reads=[rawd, cwdep, cvd], writes=[cvd])
                pst[cc].update({"cv": cv, "cvd": cvd})

            def s3a(cc):
                cv, cvd = pst[cc]["cv"], pst[cc]["cvd"]
                if cc < 16:
                    xc, xcd = xcring.next()
                    s.op("act", lambda e: e.activation(out=xc[:, :], in_=cv[:, :], func=AF.Silu), reads=[cvd], writes=[xcd])
                    pst[cc].update({"src": xc, "srcd": xcd})
                elif cc < 24:
                    gi = cc - 16
                    s.op("act", lambda e: e.activation(out=BT[:, gi, :], in_=cv[:, :], func=AF.Silu), reads=[cvd], writes=[BTd])
                    pst[cc].update({"src": BT[:, gi, :], "srcd": BTd})
                else:
                    gi = cc - 24
                    s.op("act", lambda e: e.activation(out=CT[:, gi, :], in_=cv[:, :], func=AF.Silu), reads=[cvd], writes=[CTd])

            def s3b(cc):
                if cc >= 24:
                    return
                src, srcd = pst[cc]["src"], pst[cc]["srcd"]
                s.op("pe", [lambda e, q=q: e.transpose(out=ptr[:, q, :], in_=src[:, q * 128:(q + 1) * 128], identity=idb[:, :]) for q in range(4)],
                     reads=[srcd, iddep], writes=[ptrd])

            def s3c(cc):
                if cc >= 24:
                    return
                if cc < 16:
                    s.op("act", lambda e: e.activation(out=xtok[:, :, cc * 128:(cc + 1) * 128], in_=ptr[:, 0:4, :], func=AF.Copy), reads=[ptrd], writes=[xtokd])
                else:
                    gi = cc - 16
                    s.op("act", lambda e: e.activation(out=btok[:, :, gi * 128:(gi + 1) * 128], in_=ptr[:, 0:4, :], func=AF.Copy), reads=[ptrd], writes=[btokd])

            clist = list(range(24, 32)) if B_ else list(range(24))
            ncl = len(clist)
            if B_:
                s.op("sp", lambda e, tb=tb: e.dma_start(out=xtok[:, :, :], in_=P.xs_save[tb].rearrange("p (c f) -> p c f", c=4)), writes=[xtokd], dma=True)
                s.op("sp", lambda e, tb=tb: e.dma_start(out=btok[:, :, :], in_=P.bs_save[tb].rearrange("p (c f) -> p c f", c=4)), writes=[btokd], dma=True)
                s.op("sp", lambda e, tb=tb: e.dma_start(out=BT[:, :, :], in_=P.bt_save[tb].rearrange("p (c f) -> p c f", c=8)), writes=[BTd], dma=True)
                s.op("sp", lambda e, tb=tb: e.dma_start(out=dt[:, :, :], in_=P.dt_save[tb].rearrange("p (c f) -> p c f", c=4)), writes=[dtd], dma=True)
            for it in range(ncl + 3):
                if 0 <= it - 3 < ncl:
                    s3c(clist[it - 3])
                if it < ncl:
                    s1(clist[it])
                if 0 <= it - 2 < ncl:
                    s3b(clist[it - 2])
                if it < ncl:
                    s2(clist[it])
                if 0 <= it - 1 < ncl:
                    s3a(clist[it - 1])
            if not B_:
                for ck in range(4):
                    t0 = c0 + ck * 128
                    s.op("pe", [lambda e, kc=kc, t0=t0: e.matmul(misc[:, 0:32], lhsT=hT[:, kc, t0:t0 + 128], rhs=wdt[:, kc, :], start=(kc == 0), stop=(kc == KC - 1))
                                for kc in range(KC)], reads=[hdep, wdtdep], writes=[miscd])
                    s.op("dve", lambda e, ck=ck: e.tensor_tensor(out=dt[:, ck, :], in0=misc[:, 0:32], in1=vecs[:, 0, :], op=ALU.add), reads=[miscd, vecdep], writes=[dtd])
                s.op("act", lambda e: e.activation(out=dt[:, :, :], in_=dt[:, :, :], func=AF.Exp), reads=[dtd], writes=[dtd])
                s.op("act", lambda e: e.activation(out=dt[:, :, :], in_=dt[:, :, :], func=AF.Ln, bias=1.0), reads=[dtd], writes=[dtd])
                s.op("sp", lambda e, tb=tb: e.dma_start(out=P.xs_save[tb].rearrange("p (c f) -> p c f", c=4), in_=xtok[:, :, :]), reads=[xtokd], dma=True)
                s.op("sp", lambda e, tb=tb: e.dma_start(out=P.bs_save[tb].rearrange("p (c f) -> p c f", c=4), in_=btok[:, :, :]), reads=[btokd], dma=True)
                s.op("sp", lambda e, tb=tb: e.dma_start(out=P.bt_save[tb].rearrange("p (c f) -> p c f", c=8), in_=BT[:, :, :]), reads=[BTd], dma=True)
                s.op("sp", lambda e, tb=tb: e.dma_start(out=P.dt_save[tb].rearrange("p (c f) -> p c f", c=4), in_=dt[:, :, :]), reads=[dtd], dma=True)
            s.op("dve", lambda e: e.tensor_tensor(out=adt[:, :, :], in0=dt[:, :, :], in1=vecs[:, 1:2, :].to_broadcast([128, 4, 32]), op=ALU.mult),
                 reads=[dtd, vecdep], writes=[adtd])
            if B_:
                for cb in range(4):
                    wz, wzdep = wzring.next()
                    s.op("pool", lambda e, wz=wz, cb=cb: e.dma_start(out=wz[:, :, :], in_=wz_d[cb, :, :, :]), writes=[wzdep], dma=True)
                    for ck in range(4):
                        t0 = c0 + ck * 128
                        ps, psd = psring.next()
                        s.op("pe", [lambda e, ps=ps, wz=wz, kc=kc, t0=t0: e.matmul(ps[:, :], lhsT=hT[:, kc, t0:t0 + 128], rhs=wz[:, kc, :], start=(kc == 0), stop=(kc == KC - 1))
                                    for kc in range(KC)], reads=[hdep, wzdep], writes=[psd])
                        s.op("act", lambda e, ps=ps, ck=ck, cb=cb: e.activation(out=sz[:, ck, cb * 512:(cb + 1) * 512], in_=ps[:, :], func=AF.Silu), reads=[psd], writes=[szd])
            if B_ and tb == 0:
                s_init()
            mv = misc[:, 0:256].rearrange("p (c w h) -> p c w h", c=4, w=2)
            fns = []
            for ck in range(4):
                fns.append(lambda e, ck=ck: e.matmul(misc[:, ck * 64:ck * 64 + 32], lhsT=U[:, :], rhs=adt[:, ck, :], start=True, stop=True))
                fns.append(lambda e, ck=ck: e.matmul(misc[:, ck * 64 + 32:ck * 64 + 64], lhsT=ones[:, 0:128], rhs=adt[:, ck, :], start=True, stop=True))
            s.op("pe", fns, reads=[Udep, onesdep, adtd], writes=[miscd])
            s.op("act", lambda e: e.activation(out=sm[:, 0, :, :], in_=mv[:, :, 0, :], func=AF.Copy), reads=[miscd], writes=[smd[0]])
            s.op("act", lambda e: e.activation(out=sm[:, 7, :, :], in_=mv[:, :, 1, :], func=AF.Copy), reads=[miscd], writes=[smd[7]])
            s.op("dve", lambda e: e.tensor_tensor(out=sm[:, 1, :, :], in0=sm[:, 7, :, :], in1=sm[:, 0, :, :], op=ALU.subtract), reads=[smd[7], smd[0]], writes=[smd[1]])
            s.op("act", lambda e: e.activation(out=sm[:, 2, :, :], in_=sm[:, 1, :, :], func=AF.Exp), reads=[smd[1]], writes=[smd[2]])
            s.op("act", lambda e: e.activation(out=sm[:, 4, :, :], in_=sm[:, 7, :, :], func=AF.Exp), reads=[smd[7]], writes=[smd[4]])
            for ck in range(4):
                s.op("dve", lambda e, ck=ck: e.tensor_tensor(out=tsum[:, :], in0=sm[:, 7, ck, :], in1=tsum[:, :], op=ALU.add), reads=[smd[7], tsumd], writes=[tsumd])
            s.op("dve", lambda e: e.tensor_tensor(out=sm[:, 6, :, :], in0=dt[:, :, :], in1=sm[:, 2, :, :], op=ALU.mult), reads=[dtd, smd[2]], writes=[smd[6]])
            if B_:
                s.op("act", lambda e: e.activation(out=sm[:, 3, :, :], in_=sm[:, 0, :, :], func=AF.Exp), reads=[smd[0]], writes=[smd[3]])
                s.op("dve", lambda e: e.tensor_scalar(out=sm[:, 5, :, :], in0=sm[:, 0, :, :], scalar1=-1.0, scalar2=0.0, op0=ALU.mult, op1=ALU.add), reads=[smd[0]], writes=[smd[5]])
            for ck in range(4):
                k0 = ck * 128
                s.op(OFF_ENG, lambda e, ck=ck: e.tensor_tensor(out=h64(xdtd[:, :]), in0=h64(xtok[:, ck, :]), in1=bc64(sm[:, 6, ck, :]), op=ALU.mult),
                     reads=[xtokd, smd[6]], writes=[xdtdd])
                if B_:
                    s.op(OFF_ENG, lambda e, ck=ck: e.tensor_tensor(out=h64(xdt[:, :]), in0=h64(xtok[:, ck, :]), in1=bc64(dt[:, ck, :]), op=ALU.mult),
                         reads=[xtokd, dtd], writes=[xdt_d])
                    s.op(OFF_ENG, lambda e, ck=ck: e.tensor_tensor(out=h64(xsD[:, :]), in0=h64(xtok[:, ck, :]), in1=bc64(vecs[:, 2, :]), op=ALU.mult),
                         reads=[xtokd, vecdep], writes=[xsD_d])
                    for q in range(2):
                        s.op("pe", [lambda e, q=q, gg=gg, k0=k0: e.matmul(cbk[:, gg * 128:(gg + 1) * 128], lhsT=BT[:, 4 * q + gg, k0:k0 + 128], rhs=CT[:, 4 * q + gg, k0:k0 + 128],
                                                                       start=True, stop=True) for gg in range(4)], reads=[BTd, CTd], writes=[cbkd])
                        s.op("dve", lambda e, q=q: e.tensor_tensor(out=cbmq[q][:, :, :], in0=cbk[:, :].rearrange("p (a b) -> p a b", a=4),
                                                                    in1=U[:, :].unsqueeze(1).to_broadcast([128, 4, 128]), op=ALU.mult), reads=[cbkd, Udep], writes=[cbmqd[q]])

                    def stA(g, ck=ck):
                        ps, psd = psring.next()
                        s.op("pe", [lambda e, ps=ps, r=r, hh=4 * g + r: e.matmul(ps[:, r * 128:(r + 1) * 128], lhsT=adt[:, ck, hh:hh + 1].to_broadcast([128, 128]), rhs=U[:, :],
                                                                              start=True, stop=True) for r in range(4)], reads=[adtd, Udep], writes=[psd])
                        return ps, psd

                    def stB1(g, ps, psd, ck=ck):
                        tt, ttd = ttring.next()
                        s.op("dve", lambda e: e.tensor_tensor(out=tt[:, :].rearrange("p (a b) -> p a b", a=4), in0=ps[:, :].rearrange("p (a b) -> p a b", a=4),
                                                              in1=sm[:, 5, ck, 4 * g:4 * g + 4].unsqueeze(2).to_broadcast([128, 4, 128]), op=ALU.add), reads=[psd, smd[5]], writes=[ttd])
                        lx, lxd = lxring.next()
                        s.op("act", lambda e: e.activation(out=lx[:, :], in_=tt[:, :], func=AF.Exp), reads=[ttd], writes=[lxd])
                        return lx, lxd

                    def stB2(g, lx, lxd):
                        mt, mtd = mtring.next()
                        q, gg = divmod(g, 4)
                        s.op("dve", lambda e: e.scalar_tensor_tensor(out=mt[:, :].rearrange("p (a b) -> p a b", a=4), in0=lx[:, :].rearrange("p (a b) -> p a b", a=4), scalar=1.0,
                                                                     in1=cbmq[q][:, gg:gg + 1, :].to_broadcast([128, 4, 128]), op0=ALU.min, op1=ALU.mult),
                             reads=[lxd, cbmqd[q]], writes=[mtd])
                        return mt, mtd

                    def stC1(g, mt, mtd, k0=k0):
                        yb, ybd = ybring.next()
                        fns = [lambda e: e.matmul(yb[:, 0:256], lhsT=idb[:, :], rhs=xsD[:, g * 256:(g + 1) * 256], start=True, stop=False)]
                        for r in range(4):
                            hh = 4 * g + r
                            fns.append(lambda e, r=r, hh=hh: e.matmul(yb[:, r * 64:(r + 1) * 64], lhsT=mt[:, r * 128:(r + 1) * 128], rhs=xdt[:, hh * 64:(hh + 1) * 64], start=False, stop=(r == 3)))
                        fns.append(lambda e: e.matmul(yb[:, 256:512], lhsT=CT[:, g, k0:k0 + 128], rhs=Sb[:, g * 256:(g + 1) * 256], start=True, stop=True))
                        s.op("pe", fns, reads=[iddep, xsD_d, mtd, xdt_d, CTd, Sbd], writes=[ybd])
                        return yb, ybd

                    def stC2(g, yb, ybd, ck=ck):
                        yt, ytd = ytring.next()
                        s.op("dve", lambda e: e.tensor_tensor(out=h64(yt[:, :]), in0=h64(yb[:, 256:512]), in1=bc64(sm[:, 3, ck, 4 * g:4 * g + 4]), op=ALU.mult),
                             reads=[ybd, smd[3]], writes=[ytd])
                        s.op("dve", lambda e: e.tensor_tensor(out=y[:, g * 256:(g + 1) * 256], in0=yb[:, 0:256], in1=yt[:, :], op=ALU.add),
                             reads=[ybd, ytd], writes=[yd])

                    As = {0: stA(0), 1: stA(1)}
                    Ls = {0: stB1(0, *As[0])}
                    Ys = {}
                    for gi_ in range(8):
                        if gi_ + 2 < 8:
                            As[gi_ + 2] = stA(gi_ + 2)
                        if gi_ + 1 < 8:
                            Ls[gi_ + 1] = stB1(gi_ + 1, *As[gi_ + 1])
                        mt, mtd = stB2(gi_, *Ls[gi_])
                        Ys[gi_] = stC1(gi_, mt, mtd)
                        if gi_ >= 2 and deferred:
                            deferred.pop(0)()
                        if gi_ >= 1:
                            stC2(gi_ - 1, *Ys[gi_ - 1])
                    stC2(7, *Ys[7])
                if debug and tb == 0 and ck == 0:
                    outtoks.append(s.op("sp", lambda e: e.dma_start(out=dbg["g_dt"][:, :, :], in_=dt[:, :, :]), reads=[dtd], dma=True))
                    outtoks.append(s.op("sp", lambda e: e.dma_start(out=dbg["g_sm"][:, :, :], in_=sm[:, :, :]), reads=smd, dma=True))
                    outtoks.append(s.op("sp", lambda e: e.dma_start(out=dbg["g_xtok"][:, :, :], in_=xtok[:, :, :]), reads=[xtokd], dma=True))
                    outtoks.append(s.op("sp", lambda e: e.dma_start(out=dbg["g_btok"][:, :, :], in_=btok[:, :, :]), reads=[btokd], dma=True))
                    outtoks.append(s.op("sp", lambda e: e.dma_start(out=dbg["g_hT"][:, :, :], in_=hT[:, :, :]), reads=[hdep], dma=True))
                    if B_:
                        outtoks.append(s.op("sp", lambda e: e.dma_start(out=dbg["g_y"][:, :], in_=y[:, :]), reads=[yd], dma=True))
                        outtoks.append(s.op("sp", lambda e: e.dma_start(out=dbg["g_sz"][:, :, :], in_=sz[:, :, :]), reads=[szd], dma=True))
                s.op("dve", lambda e, ck=ck: e.tensor_tensor(out=h64(S[:, :]), in0=h64(S[:, :]), in1=bc64(sm[:, 4, ck, :]), op=ALU.mult), reads=[Sd, smd[4]] + ([Sbd] if B_ else []), writes=[Sd])
                for gp in range(4):
                    stp, stpd = psring.next()
                    s.op("pe", [lambda e, g=g, ck=ck, stp=stp: e.matmul(stp[:, (g % 2) * 256:(g % 2) * 256 + 256], lhsT=btok[:, ck, g * 128:(g + 1) * 128], rhs=xdtd[:, g * 256:(g + 1) * 256],
                                                                      start=True, stop=True) for g in (2 * gp, 2 * gp + 1)], reads=[btokd, xdtdd], writes=[stpd])
                    s.op("dve", lambda e, gp=gp, stp=stp: e.tensor_tensor(out=S[:, gp * 512:(gp + 1) * 512], in0=stp[:, :], in1=S[:, gp * 512:(gp + 1) * 512], op=ALU.add),
                         reads=[stpd, Sd], writes=[Sd])
                if B_:
                    s.op("act", lambda e: e.activation(out=Sb[:, :], in_=S[:, :], func=AF.Copy), reads=[Sd], writes=[Sbd])
                    s.op("dve", lambda e, ck=ck: e.tensor_tensor(out=y[:, :], in0=y[:, :], in1=sz[:, ck, :], op=ALU.mult), reads=[yd, szd], writes=[yd])
                    s.op("dve", lambda e: e.memset(ssq[:, :], 0.0), writes=[ssqd])
                    for g in range(8):
                        s.op("act", lambda e, g=g: e.activation(out=junk[:, :], in_=y[:, g * 256:(g + 1) * 256], func=AF.Square, accum_out=ssq[:, g:g + 1]),
                             reads=[yd], writes=[junkd, ssqd])
                    s.op("act", lambda e: e.activation(out=ssq[:, 8:16], in_=ssq[:, 0:8], func=AF.Sqrt, bias=ones[:, 128:129], scale=1.0 / 256), reads=[ssqd, onesdep], writes=[ssqd])
                    s.op("dve", lambda e: e.reciprocal(out=ssq[:, 8:16], in_=ssq[:, 8:16]), reads=[ssqd], writes=[ssqd])
                    s.op("dve", lambda e: e.tensor_tensor(out=gn[:, :].rearrange("p (g q) -> p g q", g=8), in0=y[:, :].rearrange("p (g q) -> p g q", g=8),
                                                          in1=ssq[:, 8:16].unsqueeze(2).to_broadcast([128, 8, 256]), op=ALU.mult), reads=[yd, ssqd], writes=[gnd])
                    if debug and tb == 0 and ck == 0:
                        outtoks.append(s.op("sp", lambda e: e.dma_start(out=dbg["g_yg"][:, :], in_=y[:, :]), reads=[yd], dma=True))
                        outtoks.append(s.op("sp", lambda e: e.dma_start(out=dbg["g_gn"][:, :], in_=gn[:, :]), reads=[gnd], dma=True))
                        outtoks.append(s.op("sp", lambda e: e.dma_start(out=dbg["g_S"][:, :], in_=S[:, :]), reads=[Sd], dma=True))
                    def p4b(c4, k0=k0):
                        s.op("pe", [lambda e, q=q: e.transpose(out=ptr[:, q, :], in_=gn[:, (c4 * 4 + q) * 128:(c4 * 4 + q + 1) * 128], identity=idb[:, :]) for q in range(4)],
                             reads=[gnd, iddep], writes=[ptrd])
                        for q in range(4):
                            ccx = c4 * 4 + q
                            s.op("act", lambda e, q=q, ccx=ccx: e.activation(out=gnT[:, ccx, k0:k0 + 128], in_=ptr[:, q, :], func=AF.Copy, scale=gw[:, ccx:ccx + 1]),
                                 reads=[ptrd, gwd], writes=[gnTd])
                    deferred.extend([(lambda c4=c4, f=p4b: f(c4)) for c4 in range(4)])
            while deferred:
                deferred.pop(0)()
            if debug and B_ and tb == 0:
                outtoks.append(s.op("sp", lambda e: e.dma_start(out=dbg["g_gnT"][:, :, :], in_=gnT[:, :, :]), reads=[gnTd], dma=True))
            if B_:
                for dc in range(KC):
                    wo, wodep = woring.next()
                    s.op("pool", lambda e, wo=wo, dc=dc: e.dma_start(out=wo[:, :, :], in_=wout_d[dc, :, :, :]), writes=[wodep], dma=True)
                    ps, psd = psring.next()
                    s.op("pe", [lambda e, ps=ps, wo=wo, kc=kc: e.matmul(ps[:, :], lhsT=wo[:, kc, :], rhs=gnT[:, kc, :], start=(kc == 0), stop=(kc == 15)) for kc in range(16)],
                         reads=[wodep, gnTd], writes=[psd])
                    ob, obd = oring.next()
                    s.op("sp", lambda e, ob=ob, dc=dc, tb=tb: e.dma_start(out=ob[:, :], in_=x_own[dc * 128:(dc + 1) * 128, tb * 512:(tb + 1) * 512]), writes=[obd], dma=True)
                    s.op("dve", lambda e, ob=ob, ps=ps: e.tensor_tensor(out=ob[:, :], in0=ps[:, :], in1=ob[:, :], op=ALU.add), reads=[psd, obd], writes=[obd])
                    outtoks.append(s.op("sp", lambda e, ob=ob, dc=dc, tb=tb: e.dma_start(out=x_out[dc * 128:(dc + 1) * 128, tb * 512:(tb + 1) * 512], in_=ob[:, :]),
                                        reads=[obd], dma=True))
                    if tb == 3 and io.get("hmsg") is not None:
                        s.op("sp", lambda e, ob=ob, dc=dc: e.dma_start(out=io["hmsg"][:, dc, :], in_=ob[:, 512 - HALO:512]), reads=[obd], dma=True)
        if not B_:
            HS = SMSG // 2
            outtoks.append(s.op("sp", lambda e: e.dma_start(out=smsg[0][:, :], in_=S[:, 0:HS]), reads=[Sd], dma=True))
            outtoks.append(s.op("sp", lambda e: e.dma_start(out=smsg[1][:, 0:DIN - HS], in_=S[:, HS:DIN]), reads=[Sd], dma=True))
            outtoks.append(s.op("sp", lambda e: e.dma_start(out=smsg[1][:, DIN - HS:DIN - HS + NHS], in_=tsum[:, :]), reads=[tsumd], dma=True))
        s.barrier()
        s.emit()


def _ssm_common_maps(xTs, nw, w_in, conv_w, conv_b, dt_bias, a_log, d_skip):
    wr = w_in[:, DIN:DIN + 4096].reshape(KC, 128, 32, 128)
    win = np.ascontiguousarray(wr.transpose(2, 1, 0, 3))
    wdt = np.ascontiguousarray(w_in[:, DIN + 4096:].reshape(KC, 128, 32).transpose(1, 0, 2))
    cw = np.empty((128, 32, 5), np.float32)
    cw[:, :, 0:4] = conv_w.reshape(4, 32, 128).transpose(2, 1, 0)
    cw[:, :, 4] = conv_b.reshape(32, 128).T
    vecs = np.ascontiguousarray(np.stack([dt_bias, a_log, d_skip]).astype(np.float32))
    halos = _halo_cols(xTs)
    return [{"x_own": xTs[c], "x_halo": halos[c], "nw": _cols128(nw), "win": win, "wdt": wdt, "cw": cw, "vecs": vecs}
            for c in range(NCORES)]


def run_ssm(xTs, nw, w_in, conv_w, conv_b, dt_bias, a_log, d_skip, norm_w, w_out):
    maps = _ssm_common_maps(xTs, nw, w_in, conv_w, conv_b, dt_bias, a_log, d_skip)
    ncA = _prog("ssmA", lambda: build_ssm("A"))
    resA = run_bass_kernel_spmd(ncA, maps, core_ids=list(range(NCORES)))
    sl = [np.asarray(r["s_out"]) for r in resA.results]
    dl = [np.asarray(r["d_out"]) for r in resA.results]
    ncB = _prog("ssmB", lambda: build_ssm("B"))
    wz = np.ascontiguousarray(w_in[:, 0:DIN].reshape(KC, 128, 4, 512).transpose(2, 1, 0, 3))
    gw = np.ascontiguousarray(norm_w.reshape(16, 128).T)
    wout = np.ascontiguousarray(w_out.reshape(16, 128, KC, 128).transpose(2, 1, 0, 3))
    for c in range(NCORES):
        q = c % 4
        sp = np.zeros((3, 128, DIN), np.float32)
        dp = np.zeros((3, 128, NHS), np.float32)
        for i, src in enumerate((c - 3, c - 2, c - 1)):
            if src >= c - q:
                sp[i] = sl[src]
                dp[i] = dl[src]
        maps[c].update({"wz": wz, "gw": gw, "wout": wout, "sprev": sp, "dprev": dp})
    resB = run_bass_kernel_spmd(ncB, maps, core_ids=list(range(NCORES)))
    return [np.asarray(r["x_out"]) for r in resB.results]


I32 = mybir.dt.int32
GROUPS = [[0, 1, 2, 3], [4, 5, 6, 7]]
SMSG = DIN + NHS
VMSG = 21 * 128


class Prog:
    pass


def build_fused(stop=None):
    nc = bass.Bass("TRN2", target_bir_lowering=False)
    P = Prog()
    P.nc = nc
    P.st = {}

    def din(name, shape, dt=F32):
        return nc.dram_tensor(name, list(shape), dt, kind="ExternalInput").ap()

    SMSG_ = SMSG
    x_d = din("x", [D, T])
    out_d = nc.dram_tensor("out", [D, T], F32, kind="ExternalOutput").ap()
    pidx_d = din("pidx", [1, 4], I32)
    cos_d, sin_d, pm_d, mk_d = din("cosT", [128, T]), din("sinT", [128, T]), din("pm", [128, 128]), din("masks", [128, 3, 512])
    fw_d = din("fw", [128, KC])
    L = []
    for i in range(4):
        d = {"mnw": din("mnw%d" % i, [128, KC]), "fnw": din("fnw%d" % i, [128, KC]), "wup": din("wup%d" % i, [NJ, 128, KC, 256]),
             "fcw": din("fcw%d" % i, [128, NJ, 2, 4]), "wdn": din("wdn%d" % i, [DFF, D])}
        if i % 2 == 0:
            d.update({"wq": din("wq%d" % i, [NG, NH, 128, KC, 128]), "wk": din("wk%d" % i, [NG, NH, 128, KC, 128]),
                      "wv": din("wv%d" % i, [NG, 128, KC, 1024]), "wo": din("wo%d" % i, [D, D])})
        else:
            d.update({"win": din("win%d" % i, [32, 128, KC, 128]), "wdt": din("wdt%d" % i, [128, KC, 32]), "scw": din("scw%d" % i, [128, 32, 5]),
                      "vecs": din("vecs%d" % i, [3, 32]), "wz": din("wz%d" % i, [4, 128, KC, 512]), "gw": din("gw%d" % i, [128, 16]),
                      "wout": din("wout%d" % i, [KC, 128, 16, 128])})
        L.append(d)
    xb = [nc.dram_tensor("xb%d" % i, [D, T], F32).ap() for i in range(2)]
    k_own = nc.dram_tensor("k_own", [NG, NH, 128, T], BF16).ap()
    v_own = nc.dram_tensor("v_own", [NG, NH, 128, 16, 128], BF16).ap()
    kmsg = nc.dram_tensor("kmsg", [NH, 128, KMSG], BF16).ap()
    kall = nc.dram_tensor("kall", [NH, 5 * 128, KMSG], BF16).ap()
    vmsg = nc.dram_tensor("vmsg", [NH, 128, VMSG], BF16).ap()
    vall = nc.dram_tensor("vall", [NH, 5 * 128, VMSG], BF16).ap()
    hmsg = nc.dram_tensor("hmsg", [128, KC * HALO], F32).ap()
    hall = nc.dram_tensor("hall", [4 * 128, KC * HALO], F32).ap()
    kloc = nc.dram_tensor("kloc", [NH, 128, KMSG], BF16).ap()
    vloc = nc.dram_tensor("vloc", [NH, 128, VMSG], BF16).ap()
    flags_d = din("flags", [128, 8])
    P.hall = hall
    P.hsave = nc.dram_tensor("hsave", [128, KC, HALO + T], BF16).ap()
    P.xs_save = [nc.dram_tensor("xs_save%d" % i, [128, 4 * DIN], BF16).ap() for i in range(4)]
    P.bs_save = [nc.dram_tensor("bs_save%d" % i, [128, 4 * 1024], BF16).ap() for i in range(4)]
    P.bt_save = [nc.dram_tensor("bt_save%d" % i, [128, 8 * 512], BF16).ap() for i in range(4)]
    P.dt_save = [nc.dram_tensor("dt_save%d" % i, [128, 4 * 32], F32).ap() for i in range(4)]
    P.flags = flags_d
    smsg = [nc.dram_tensor("smsg%d" % i, [128, SMSG // 2], F32).ap() for i in range(2)]
    sall = [nc.dram_tensor("sall%d" % i, [4 * 128, SMSG // 2], F32).ap() for i in range(2)]
    P.sall = sall

    with contextlib.ExitStack() as ges:
        S = Sched(nc, ges)
        P.S = S

        def setup(e):
            ins = None
            P.st["regs"] = []
            for k in range(1):
                reg = e.alloc_register("pidx%d" % k)
                ins = e.reg_load(reg, pidx_d[0:1, k:k + 1])
                P.st["regs"].append(reg)
                P.st["c%d" % (k + 1)] = e.snap(reg, min_val=0, max_val=4)
            return ins
        S.op("sp", setup)

        def pre_sp(e):
            for k, reg in enumerate(P.st.get("regs", [])):
                P.st["c%d" % (k + 1)] = e.snap(reg, min_val=0, max_val=4)
        S.pre_sp = pre_sp
        with contextlib.ExitStack() as es:
            cx = Ctx(nc, es, S)
            zb = cx.sb([128, KMSG], BF16, "zb")
            zf = cx.sb([128, SMSG], F32, "zf")
            zd = Dep()
            S.op("dve", lambda e: e.memset(zb[:, :], 0.0), writes=[zd])
            S.op("dve", lambda e: e.memset(zf[:, :], 0.0), writes=[zd])
            for h in range(NH):
                S.op("sp", lambda e, h=h: e.dma_start(out=kall[h, 512:640, :], in_=zb[:, :]), reads=[zd], dma=True)
                S.op("sp", lambda e, h=h: e.dma_start(out=vall[h, 512:640, :], in_=zb[:, 0:VMSG]), reads=[zd], dma=True)
            S.barrier()
            S.emit()

        def coll(msg2d, all2d, nrows):
            S.op("pool", lambda e: e.collective_compute("AllGather", ALU.bypass, replica_groups=GROUPS,
                                                        ins=[msg2d.opt()], outs=[all2d[0:4 * nrows, :].opt()]), cc=True)
            S.barrier()

        kalld = [Dep() for _ in range(NH)]
        valld = [Dep() for _ in range(NH)]
        klocd = [Dep() for _ in range(NH)]
        vlocd = [Dep() for _ in range(NH)]

        def coll_k(h, dep):
            S.op("pool", lambda e: e.collective_compute("AllGather", ALU.bypass, replica_groups=GROUPS,
                                                        ins=[kmsg[h, :, :].opt()], outs=[kall[h, 0:512, :].opt()]), reads=[dep], writes=[kalld[h]], cc=True)

        def coll_v(h, dep):
            S.op("pool", lambda e: e.collective_compute("AllGather", ALU.bypass, replica_groups=GROUPS,
                                                        ins=[vmsg[h, :, :].opt()], outs=[vall[h, 0:512, :].opt()]), reads=[dep], writes=[valld[h]], cc=True)

        def coll_kv():
            kv = kall.rearrange("h (r p) c -> h r p c", p=128)
            vv = vall.rearrange("h (r p) c -> h r p c", p=128)
            for h in range(NH):
                S.op("sp", lambda e, h=h: e.dma_start(out=vloc[h, :, :], in_=vv[h][P.st["c1"]]), reads=[valld[h]], writes=[vlocd[h]], dma=True)
                S.op("sp", lambda e, h=h: e.dma_start(out=kloc[h, :, :], in_=kv[h][P.st["c1"]]), reads=[kalld[h]], writes=[klocd[h]], dma=True)

        hmsg3 = hmsg.rearrange("p (kc t) -> p kc t", t=HALO)
        kmsg3 = kmsg
        vmsg4 = vmsg.rearrange("h p (b e) -> h p b e", e=128)
        vloc4 = vloc.rearrange("h p (b e) -> h p b e", e=128)
        khalo = lambda g, h: kloc[h, :, KOFF[g]:KOFF[g] + DIL[g] * 128].rearrange("p (q l) -> p q l", q=DIL[g])
        vhalo = lambda g, h: vloc4[h, :, BOFF[g]:BOFF[g] + DIL[g], :]

        step = [0]

        def go():
            step[0] += 1
            return stop is None or step[0] <= stop

        _coll = coll

        def coll(a, b, n):
            if go():
                _coll(a, b, n)

        cur = x_d
        nxt = 0
        for i in range(4):
            d = L[i]
            last = i == 3
            if i % 2 == 0:
                if go():
                  emit_attn_kv(P, {"x_in": cur, "nw": d["mnw"], "wk": d["wk"], "wv": d["wv"], "cosT": cos_d, "sinT": sin_d, "pm": pm_d,
                                 "k_own": k_own, "v_own": v_own, "kmsg": kmsg3, "vmsg": vmsg4, "coll_k": coll_k, "coll_v": coll_v})
                if go():
                    coll_kv()
                if go():
                  emit_attn_main(P, {"x_in": cur, "nw": d["mnw"], "wq": d["wq"], "wo": d["wo"], "cosT": cos_d, "sinT": sin_d, "pm": pm_d,
                                   "masks": mk_d, "k_own": k_own, "v_own": v_own, "khalo": khalo, "vhalo": vhalo, "klocd": klocd, "vlocd": vlocd,
                                   "x_out": xb[nxt], "hmsg": hmsg3})
            else:
                common = {"x_in": cur, "nw": d["mnw"], "win": d["win"], "wdt": d["wdt"], "cw": d["scw"], "vecs": d["vecs"]}
                if go():
                    emit_ssm(P, dict(common, smsg=smsg), "A")
                coll(smsg[0], sall[0], 128)
                coll(smsg[1], sall[1], 128)
                if go():
                  emit_ssm(P, dict(common, wz=d["wz"], gw=d["gw"], wout=d["wout"], x_out=xb[nxt], hmsg=hmsg3), "B")
            cur = xb[nxt]
            nxt = 1 - nxt
            coll(hmsg, hall, 128)
            io = {"x_in": cur, "nw": d["fnw"], "wup": d["wup"], "cw": d["fcw"], "wdn": d["wdn"],
                  "x_out": out_d if last else xb[nxt], "hmsg": None if (last or i % 2 == 1) else hmsg3}
            if last:
                io["fw"] = fw_d
            if go():
                emit_ffn(P, io, final_norm=last)
            if not last:
                cur = xb[nxt]
                nxt = 1 - nxt
                if i % 2 == 0:
                    coll(hmsg, hall, 128)
        if stop is not None:
            S.emit()
    return nc


def _prep_maps(inp):
    f = lambda a: np.ascontiguousarray(np.asarray(a, dtype=np.float32))
    x = f(inp["x"])
    xTs = _shards_T(x)
    shared = {"pm": perm_matrix(), "fw": _cols128(f(inp["final_norm_w"]))}
    for i in range(4):
        j = i // 2
        shared["mnw%d" % i] = _cols128(f(inp["mix_norm_w"])[i])
        shared["fnw%d" % i] = _cols128(f(inp["ffn_norm_w"])[i])
        w_up = f(inp["ffn_w_up"])[i]
        shared["wup%d" % i] = np.ascontiguousarray(w_up.reshape(KC, 128, 2, NJ, 128).transpose(3, 1, 0, 2, 4).reshape(NJ, 128, KC, 256))
        cw = np.empty((128, NJ, 2, 4), np.float32)
        cw[:, :, :, 0:3] = f(inp["ffn_conv_w"])[i].reshape(3, 2, NJ, 128).transpose(3, 2, 1, 0)
        cw[:, :, :, 3] = f(inp["ffn_conv_b"])[i].reshape(2, NJ, 128).transpose(2, 1, 0)
        shared["fcw%d" % i] = cw
        shared["wdn%d" % i] = f(inp["ffn_w_down"])[i]
        if i % 2 == 0:
            wr = f(inp["attn_w_qkv"])[j].reshape(KC, 128, NG, 3, NH, 128)
            shared["wq%d" % i] = np.ascontiguousarray(wr[:, :, :, 0].transpose(2, 3, 1, 0, 4))
            shared["wk%d" % i] = np.ascontiguousarray(wr[:, :, :, 1].transpose(2, 3, 1, 0, 4))
            shared["wv%d" % i] = np.ascontiguousarray(wr[:, :, :, 2].transpose(2, 1, 0, 3, 4).reshape(NG, 128, KC, 1024))
            shared["wo%d" % i] = f(inp["attn_w_o"])[j]
        else:
            w_in = f(inp["ssm_w_in"])[j]
            shared["win%d" % i] = np.ascontiguousarray(w_in[:, DIN:DIN + 4096].reshape(KC, 128, 32, 128).transpose(2, 1, 0, 3))
            shared["wdt%d" % i] = np.ascontiguousarray(w_in[:, DIN + 4096:].reshape(KC, 128, 32).transpose(1, 0, 2))
            scw = np.empty((128, 32, 5), np.float32)
            scw[:, :, 0:4] = f(inp["ssm_conv_w"])[j].reshape(4, 32, 128).transpose(2, 1, 0)
            scw[:, :, 4] = f(inp["ssm_conv_b"])[j].reshape(32, 128).T
            shared["scw%d" % i] = scw
            shared["vecs%d" % i] = np.ascontiguousarray(np.stack([f(inp["ssm_dt_bias"])[j], f(inp["ssm_a_log"])[j], f(inp["ssm_d"])[j]]))
            shared["wz%d" % i] = np.ascontiguousarray(w_in[:, 0:DIN].reshape(KC, 128, 4, 512).transpose(2, 1, 0, 3))
            shared["gw%d" % i] = np.ascontiguousarray(f(inp["ssm_norm_w"])[j].reshape(16, 128).T)
            shared["wout%d" % i] = np.ascontiguousarray(f(inp["ssm_w_out"])[j].reshape(16, 128, KC, 128).transpose(2, 1, 0, 3))
    maps = []
    for c in range(NCORES):
        q = c % 4
        cs, sn = rope_tables_np(q * T)
        m = dict(shared)
        fl = np.zeros((128, 8), np.float32)
        if q >= 1:
            fl[:, q - 1] = 1.0
        for r in range(4):
            if r < q:
                fl[:, 4 + r] = 1.0
        m.update({"x": xTs[c], "cosT": cs, "sinT": sn, "masks": attn_masks(q != 0), "flags": fl,
                  "pidx": np.array([[q - 1 if q >= 1 else 4, q - 2 if q >= 2 else 4, q - 3 if q >= 3 else 4, 0]], np.int32)})
        maps.append(m)
    return maps


def kernel(**inputs):
    maps = _prep_maps(inputs)
    nc = _prog("fused", build_fused)
    res = run_bass_kernel_spmd(nc, maps, core_ids=list(range(NCORES)))
    out = np.empty((2, 4 * T, D), np.float32)
    for c in range(NCORES):
        b, q = divmod(c, 4)
        out[b, q * T:(q + 1) * T, :] = np.asarray(res.results[c]["out"]).T
    return out
```

```python
import contextlib
import numpy as np
import concourse.bass as bass
import concourse.mybir as mybir
from concourse.bass_utils import run_bass_kernel_spmd

F32 = mybir.dt.float32
BF16 = mybir.dt.bfloat16
ALU = mybir.AluOpType
AF = mybir.ActivationFunctionType
AX = mybir.AxisListType

NCORES = 8
T = 2048
D = 1024
KC = 8
DFF = 2816
NJ = 22
EPS = 1e-5
HALO = 3


class Dep:
    __slots__ = ("w", "rs", "ps")

    def __init__(self, ps=False):
        self.w = None
        self.rs = []
        self.ps = ps


class Sched:
    ENGS = ("pe", "act", "dve", "pool", "sp")
    NDMA = 6

    def __init__(self, nc, es):
        self.nc = nc
        self.h = {"pe": nc.tensor, "act": nc.scalar, "dve": nc.vector,
                  "pool": nc.gpsimd, "sp": nc.sync}
        self.sems = {}
        self.cnt = {}
        self.ops = {e: [] for e in self.ENGS}
        self.seen = {e: {} for e in self.ENGS}
        for e in self.ENGS + ("cc",):
            self.sems[e] = es.enter_context(nc.semaphore("s_" + e))
            self.cnt[e] = 0
        self.dsem = {}
        self.dcnt = {}
        self.drr = {}
        for e in ("sp", "pool", "act"):
            for i in range(self.NDMA):
                k = "d_%s%d" % (e, i)
                self.sems[k] = es.enter_context(nc.semaphore(k))
                self.cnt[k] = 0
            self.drr[e] = 0

    def _need(self, eng, waits, tok, skip_same_pe=True):
        if tok is None:
            return
        k, v = tok
        if eng == "pe" and k == "pe":
            return
        if self.seen[eng].get(k, 0) >= v:
            return
        if waits.get(k, 0) < v:
            waits[k] = v

    def op(self, eng, fns, reads=(), writes=(), dma=False, cc=False):
        if not isinstance(fns, (list, tuple)):
            fns = [fns]
        waits = {}
        ps_reads = [d for d in reads if d.ps]
        if ps_reads:
            reads = [d for d in reads if not d.ps]
            writes = list(writes) + ps_reads
        for d in reads:
            self._need(eng, waits, d.w)
        for d in writes:
            self._need(eng, waits, d.w)
            for r in d.rs:
                self._need(eng, waits, r)
        if dma:
            i = self.drr[eng]
            self.drr[eng] = (i + 1) % self.NDMA
            k = "d_%s%d" % (eng, i)
            if self.cnt[k] > 0:
                self._need(eng, waits, (k, self.cnt[k]))
            self.cnt[k] += 16
            inc = 16
        elif cc:
            k = "cc"
            self.cnt[k] += 1
            inc = 1
        else:
            k = eng
            self.cnt[k] += 1
            inc = 1
        tok = (k, self.cnt[k])
        for kk, v in waits.items():
            self.seen[eng][kk] = v
        self.ops[eng].append((list(waits.items()), list(fns), k, inc))
        for d in reads:
            d.rs.append(tok)
        for d in writes:
            d.w = tok
            d.rs = []
        return tok

    def wait_all(self, eng, toks):
        waits = {}
        for t in toks:
            self._need(eng, waits, t)
        for kk, v in waits.items():
            self.seen[eng][kk] = v
        self.ops[eng].append((list(waits.items()), [], None, 0))

    def barrier(self, exclude_cc=False):
        toks = [(k, v) for k, v in self.cnt.items() if v > 0 and not (exclude_cc and k == "cc")]
        for e in self.ENGS:
            self.wait_all(e, toks)

    def replay(self, eng, h):
        for waits, fns, k, inc in self.ops[eng]:
            for kk, v in waits:
                h.wait_ge(self.sems[kk], v)
            n = len(fns)
            for i, fn in enumerate(fns):
                ins = fn(h)
                if i == n - 1:
                    ins.then_inc(self.sems[k], inc)

    pre_sp = None

    def emit(self):
        nc = self.nc
        with nc.Block() as block:
            @block.tensor
            def _(e):
                self.replay("pe", e)

            @block.scalar
            def _(e):
                self.replay("act", e)

            @block.vector
            def _(e):
                self.replay("dve", e)

            @block.gpsimd
            def _(e):
                self.replay("pool", e)

            @block.sync
            def _(e):
                if self.pre_sp is not None:
                    self.pre_sp(e)
                self.replay("sp", e)
        self.ops = {e: [] for e in self.ENGS}


class Ring:
    def __init__(self, aps, ps=False):
        self.aps = aps
        self.deps = [Dep(ps) for _ in aps]
        self.i = 0

    def next(self):
        i = self.i
        self.i = (i + 1) % len(self.aps)
        return self.aps[i], self.deps[i]


class Ctx:
    def __init__(self, nc, es, sched=None):
        self.nc = nc
        self.es = es
        self.s = sched if sched is not None else Sched(nc, es)
        self.n = Ctx.N
        Ctx.N += 1000

    N = 0

    def sb(self, shape, dt, name=None):
        self.n += 1
        return self.es.enter_context(self.nc.sbuf_tensor("%s_%d" % (name or "t", self.n), list(shape), dt))

    def ps(self, shape, dt, name=None):
        self.n += 1
        return self.es.enter_context(self.nc.psum_tensor("%s_%d" % (name or "p", self.n), list(shape), dt))


def emit_norm(cx, xT, xdeps, hT, hdep, nw, nwdep, col0, ncols, ones, onesdep, psring, sqring, rsring):
    s = cx.s
    c0 = col0
    while c0 < col0 + ncols:
        n = min(512, col0 + ncols - c0)
        ps, psd = psring.next()
        sqs = []
        for kc in range(KC):
            sq, sqd = sqring.next()
            s.op("act", lambda e, sq=sq, kc=kc, c0=c0, n=n: e.activation(out=sq[:, 0:n], in_=xT[:, kc, c0:c0 + n], func=AF.Square),
                 reads=(xdeps(kc, c0, n) if callable(xdeps) else [xdeps[kc]]), writes=[sqd])
            s.op("pe", lambda e, ps=ps, sq=sq, kc=kc, n=n: e.matmul(ps[:, 0:n], lhsT=ones[:, 0:128], rhs=sq[:, 0:n], start=(kc == 0), stop=(kc == KC - 1)),
                 reads=[sqd, onesdep], writes=[psd])
        rs, rsd = rsring.next()
        s.op("act", lambda e, rs=rs, ps=ps, n=n: e.activation(out=rs[:, 0:n], in_=ps[:, 0:n], func=AF.Sqrt, bias=ones[:, 128:129], scale=1.0 / D),
             reads=[psd, onesdep], writes=[rsd])
        s.op("dve", lambda e, rs=rs, n=n: e.reciprocal(out=rs[:, 0:n], in_=rs[:, 0:n]),
             reads=[rsd], writes=[rsd])
        for kc in range(KC):
            s.op("dve", lambda e, rs=rs, kc=kc, c0=c0, n=n: e.scalar_tensor_tensor(
                out=hT[:, kc, c0:c0 + n], in0=xT[:, kc, c0:c0 + n], scalar=nw[:, kc:kc + 1], in1=rs[:, 0:n],
                op0=ALU.mult, op1=ALU.mult),
                reads=(xdeps(kc, c0, n) if callable(xdeps) else [xdeps[kc]]) + [rsd, nwdep], writes=[hdep[c0 // 512] if isinstance(hdep, list) else hdep])
        c0 += n


def emit_halo(cx, P, dst3, dstdeps):
    s = cx.s
    hs = cx.sb([128, 4, KC * HALO], F32, "hs")
    fl = cx.sb([128, 8], F32, "fl")
    tmp = cx.sb([128, KC * HALO], F32, "htmp")
    hsd, fld, tmpd = Dep(), Dep(), Dep()
    s.op("sp", lambda e: e.dma_start(out=hs[:, :, :], in_=P.hall.rearrange("(r p) f -> p r f", p=128)), writes=[hsd], dma=True)
    s.op("sp", lambda e: e.dma_start(out=fl[:, :], in_=P.flags[:, :]), writes=[fld], dma=True)
    s.op("dve", lambda e: e.tensor_scalar(out=tmp[:, :], in0=hs[:, 0, :], scalar1=fl[:, 0:1], scalar2=0.0, op0=ALU.mult, op1=ALU.add),
         reads=[hsd, fld], writes=[tmpd])
    for r in range(1, 4):
        s.op("dve", lambda e, r=r: e.scalar_tensor_tensor(out=tmp[:, :], in0=hs[:, r, :], scalar=fl[:, r:r + 1], in1=tmp[:, :], op0=ALU.mult, op1=ALU.add),
             reads=[hsd, fld, tmpd], writes=[tmpd])
    s.op("dve", lambda e: e.tensor_copy(out=dst3, in_=tmp[:, :].rearrange("p (kc t) -> p kc t", t=HALO)), reads=[tmpd], writes=dstdeps)


def emit_ffn(P, io, final_norm=False, GJ=4):
    nc = P.nc
    x_own, nw_d, wup_d, cw_d, wdn_d, x_out = io["x_in"], io["nw"], io["wup"], io["cw"], io["wdn"], io["x_out"]
    if final_norm:
        fw_d = io["fw"]

    W = HALO + T
    with contextlib.ExitStack() as es:
        cx = Ctx(nc, es, P.S)
        s = cx.s
        xT = cx.sb([128, KC, W], F32, "xT")
        hT = cx.sb([128, KC, W], BF16, "hT")
        nw = cx.sb([128, KC], F32, "nw")
        cw = cx.sb([128, NJ, 2, 4], F32, "cw")
        ones = cx.sb([128, 129], F32, "ones")
        xd = [[Dep() for _ in range(4)] for _ in range(KC)]
        xhd = Dep()

        def xdf(kc, c0, n):
            b = c0 // 512
            deps = []
            if b == 0:
                deps.append(xhd)
            if b >= 1:
                deps.append(xd[kc][b - 1])
            if b <= 3:
                deps.append(xd[kc][b])
            return deps
        hdep = [Dep() for _ in range(5)]
        nwdep, cwdep, onesdep = Dep(), Dep(), Dep()
        psring = Ring([cx.ps([128, 512], F32, "ps") for _ in range(8)], ps=True)
        sqring = Ring([cx.sb([128, 512], F32, "sq") for _ in range(3)])
        rsring = Ring([cx.sb([128, 512], F32, "rs") for _ in range(2)])
        wupring = Ring([cx.sb([128, KC, 256], BF16, "wup") for _ in range(4)])
        ubuf = [cx.sb([128, W], F32, "u%d" % i) for i in range(2)]
        udep = [Dep(), Dep()]
        cbuf = [cx.sb([128, T], F32, "c%d" % i) for i in range(2)]
        cdep = [Dep(), Dep()]
        gring = Ring([cx.sb([128, GJ, T], BF16, "g") for _ in range(2)])
        wdring = Ring([cx.sb([128, GJ, D], BF16, "wd") for _ in range(2)])

        s.op("sp", lambda e: e.dma_start(out=nw[:, :], in_=nw_d[:, :]), writes=[nwdep], dma=True)
        s.op("sp", lambda e: e.dma_start(out=cw[:, :, :, :], in_=cw_d[:, :, :, :]), writes=[cwdep], dma=True)
        s.op("dve", lambda e: e.memset(ones[:, 0:128], 1.0), writes=[onesdep])
        s.op("dve", lambda e: e.memset(ones[:, 128:129], EPS), writes=[onesdep])
        emit_halo(cx, P, xT[:, :, 0:HALO], [xhd])
        for tb in range(4):
            for kc in range(KC):
                s.op("sp", lambda e, kc=kc, tb=tb: e.dma_start(out=xT[:, kc, HALO + tb * 512:HALO + (tb + 1) * 512],
                                                               in_=x_own[kc * 128:(kc + 1) * 128, tb * 512:(tb + 1) * 512]),
                     writes=[xd[kc][tb]], dma=True)
        if final_norm:
            fw = cx.sb([128, KC], F32, "fw")
            fwdep = Dep()
            s.op("sp", lambda e: e.dma_start(out=fw[:, :], in_=fw_d[:, :]), writes=[fwdep], dma=True)

        emit_norm(cx, xT, xdf, hT, hdep, nw, nwdep, 0, W, ones, onesdep, psring, sqring, rsring)

        wdn_v = wdn_d.rearrange("(j p) d -> p j d", p=128)
        j = 0
        groups = []
        while j < NJ:
            groups.append(list(range(j, min(NJ, j + GJ))))
            j += GJ
        def emit_down(grp, g, gdep, wd, wddep):
            for dc in range(KC):
                for tb in range(4):
                    ps, psd = psring.next()
                    s.op("pe", [lambda e, ps=ps, wd=wd, g=g, jj=jj, dc=dc, tb=tb, n=len(grp): e.matmul(
                        ps[:, :], lhsT=wd[:, jj, dc * 128:(dc + 1) * 128], rhs=g[:, jj, tb * 512:(tb + 1) * 512],
                        start=(jj == 0), stop=(jj == n - 1)) for jj in range(len(grp))],
                        reads=[wddep, gdep], writes=[psd])
                    c0 = HALO + tb * 512
                    s.op("dve", lambda e, ps=ps, dc=dc, c0=c0: e.tensor_tensor(
                        out=xT[:, dc, c0:c0 + 512], in0=ps[:, :], in1=xT[:, dc, c0:c0 + 512], op=ALU.add),
                        reads=[psd, xd[dc][tb]], writes=[xd[dc][tb]])

        pending = None
        for grp in groups:
            g, gdep = gring.next()
            wd, wddep = wdring.next()
            s.op("pool", lambda e, wd=wd, grp=grp: e.dma_start(out=wd[:, 0:len(grp), :], in_=wdn_v[:, grp[0]:grp[0] + len(grp), :]),
                 writes=[wddep], dma=True)
            for jj, j in enumerate(grp):
                wup, wupdep = wupring.next()
                s.op("pool", lambda e, wup=wup, j=j: e.dma_start(out=wup[:, :, :], in_=wup_d[j, :, :, :]), writes=[wupdep], dma=True)
                for half in range(2):
                    u = ubuf[half]
                    ps, psd = psring.next()
                    s.op("pe", [lambda e, ps=ps, wup=wup, kc=kc, half=half: e.matmul(
                        ps[:, 0:HALO], lhsT=wup[:, kc, half * 128:(half + 1) * 128], rhs=hT[:, kc, 0:HALO],
                        start=(kc == 0), stop=(kc == KC - 1)) for kc in range(KC)],
                        reads=[wupdep, hdep[0]], writes=[psd])
                    s.op("act", lambda e, ps=ps, u=u: e.activation(out=u[:, 0:HALO], in_=ps[:, 0:HALO], func=AF.Copy),
                         reads=[psd], writes=[udep[half]])
                    for tb in range(4):
                        ps, psd = psring.next()
                        c0 = HALO + tb * 512
                        s.op("pe", [lambda e, ps=ps, wup=wup, kc=kc, half=half, c0=c0: e.matmul(
                            ps[:, :], lhsT=wup[:, kc, half * 128:(half + 1) * 128], rhs=hT[:, kc, c0:c0 + 512],
                            start=(kc == 0), stop=(kc == KC - 1)) for kc in range(KC)],
                            reads=[wupdep, hdep[c0 // 512], hdep[(c0 + 511) // 512]], writes=[psd])
                        s.op("act", lambda e, ps=ps, u=u, c0=c0: e.activation(out=u[:, c0:c0 + 512], in_=ps[:, :], func=AF.Copy),
                             reads=[psd], writes=[udep[half]])
                    c = cbuf[half]
                    ceng = "dve" if half == 0 else CONV_ENG2
                    s.op(ceng, lambda e, u=u, c=c, j=j, half=half: e.tensor_scalar(
                        out=c[:, :], in0=u[:, 3:3 + T], scalar1=cw[:, j, half, 2:3], scalar2=cw[:, j, half, 3:4],
                        op0=ALU.mult, op1=ALU.add), reads=[udep[half], cwdep], writes=[cdep[half]])
                    s.op(ceng, lambda e, u=u, c=c, j=j, half=half: e.scalar_tensor_tensor(
                        out=c[:, :], in0=u[:, 2:2 + T], scalar=cw[:, j, half, 1:2], in1=c[:, :],
                        op0=ALU.mult, op1=ALU.add), reads=[udep[half], cwdep, cdep[half]], writes=[cdep[half]])
                    s.op(ceng, lambda e, u=u, c=c, j=j, half=half: e.scalar_tensor_tensor(
                        out=c[:, :], in0=u[:, 1:1 + T], scalar=cw[:, j, half, 0:1], in1=c[:, :],
                        op0=ALU.mult, op1=ALU.add), reads=[udep[half], cwdep, cdep[half]], writes=[cdep[half]])
                    if half == 0:
                        s.op("act", lambda e, c=c: e.activation(out=c[:, :], in_=c[:, :], func=AF.Silu),
                             reads=[cdep[0]], writes=[cdep[0]])
                if jj == 0 and pending is not None:
                    emit_down(*pending)
                    pending = None
                s.op("dve", lambda e, g=g, jj=jj: e.tensor_tensor(out=g[:, jj, :], in0=cbuf[0][:, :], in1=cbuf[1][:, :], op=ALU.mult),
                     reads=[cdep[0], cdep[1]], writes=[gdep])
            pending = (grp, g, gdep, wd, wddep)
        emit_down(*pending)
        outtoks = []
        if final_norm:
            for tb in range(4):
                c0 = HALO + tb * 512
                ps, psd = psring.next()
                for kc in range(KC):
                    sq, sqd = sqring.next()
                    s.op("act", lambda e, sq=sq, kc=kc, c0=c0: e.activation(out=sq[:, :], in_=xT[:, kc, c0:c0 + 512], func=AF.Square),
                         reads=[xd[kc][tb]], writes=[sqd])
                    s.op("pe", lambda e, ps=ps, sq=sq, kc=kc: e.matmul(ps[:, :], lhsT=ones[:, 0:128], rhs=sq[:, :], start=(kc == 0), stop=(kc == KC - 1)),
                         reads=[sqd, onesdep], writes=[psd])
                rs, rsd = rsring.next()
                s.op("act", lambda e, rs=rs, ps=ps: e.activation(out=rs[:, :], in_=ps[:, :], func=AF.Sqrt, bias=ones[:, 128:129], scale=1.0 / D),
                     reads=[psd, onesdep], writes=[rsd])
                s.op("dve", lambda e, rs=rs: e.reciprocal(out=rs[:, :], in_=rs[:, :]),
                     reads=[rsd], writes=[rsd])
                for kc in range(KC):
                    s.op("dve", lambda e, rs=rs, kc=kc, c0=c0: e.scalar_tensor_tensor(
                        out=xT[:, kc, c0:c0 + 512], in0=xT[:, kc, c0:c0 + 512], scalar=fw[:, kc:kc + 1], in1=rs[:, :],
                        op0=ALU.mult, op1=ALU.mult), reads=[xd[kc][tb], rsd, fwdep], writes=[xd[kc][tb]])
        for kc in range(KC):
            outtoks.append(s.op("sp", lambda e, kc=kc: e.dma_start(out=x_out[kc * 128:(kc + 1) * 128, :], in_=xT[:, kc, HALO:W]),
                                reads=xd[kc], dma=True))
        if io.get("hmsg") is not None:
            s.op("sp", lambda e: e.dma_start(out=io["hmsg"], in_=xT[:, :, W - HALO:W]), reads=[xd[kc][3] for kc in range(KC)], dma=True)
        s.barrier()
        s.emit()


def _cols128(v):
    return np.ascontiguousarray(v.reshape(-1, 128).T)


def _shards_T(x):
    out = []
    for c in range(NCORES):
        b, q = divmod(c, 4)
        out.append(np.ascontiguousarray(x[b, q * T:(q + 1) * T, :].T))
    return out


def _halo_cols(xTs, n=HALO):
    out = []
    for c in range(NCORES):
        if c % 4 == 0:
            h = np.zeros((D, n), np.float32)
        else:
            h = xTs[c - 1][:, T - n:]
        out.append(np.ascontiguousarray(h.reshape(KC, 128, n).transpose(1, 0, 2)))
    return out


_PROGS = {}


def _prog(key, fn):
    if key not in _PROGS:
        _PROGS[key] = fn()
    return _PROGS[key]


def run_ffn(xTs, nw, w_up, conv_w, conv_b, w_down, final_w=None):
    nc = _prog(("ffn", final_w is not None), lambda: build_ffn(final_norm=final_w is not None))
    wup = np.empty((NJ, 128, KC, 256), np.float32)
    wr = w_up.reshape(KC, 128, 2, NJ, 128)
    wup[:] = wr.transpose(3, 1, 0, 2, 4).reshape(NJ, 128, KC, 256)
    cw = np.empty((128, NJ, 2, 4), np.float32)
    cwr = conv_w.reshape(3, 2, NJ, 128)
    cw[:, :, :, 0:3] = cwr.transpose(3, 2, 1, 0)
    cw[:, :, :, 3] = conv_b.reshape(2, NJ, 128).transpose(2, 1, 0)
    halos = _halo_cols(xTs)
    maps = []
    for c in range(NCORES):
        m = {"x_own": xTs[c], "x_halo": halos[c], "nw": _cols128(nw), "wup": wup, "cw": cw,
             "wdn": np.ascontiguousarray(w_down)}
        if final_w is not None:
            m["fw"] = _cols128(final_w)
        maps.append(m)
    res = run_bass_kernel_spmd(nc, maps, core_ids=list(range(NCORES)))
    return [np.asarray(r["x_out"]) for r in res.results]


NG = 3
NH = 8
DIL = (1, 4, 16)
SCALE = 128.0 ** -0.5


def colview(t2d, d, c0, n):
    if d == 1:
        return t2d[:, c0:c0 + n], 1
    L = T // d
    v = t2d.rearrange("p (l r) -> p r l", r=d)
    r0, l0 = divmod(c0, L)
    if n <= L:
        return v[:, r0, l0:l0 + n], 1
    return v[:, r0:r0 + n // L, :], n // L


def v3(ap, A):
    if A == 1:
        return ap
    return ap.rearrange("p (a b) -> p a b", a=A)


def emit_rope(cx, ps, psd, full, dstdep, d, t0, cosT, sinT, tabdep, pm, pmdep, pkring, t1ring, tbring):
    s = cx.s
    if d == 1:
        tmp, tmpd = full[:, t0:t0 + 512], dstdep
    else:
        tmp, tmpd = tbring.next()
        tmp = tmp[:, :]
    s.op("act", lambda e: e.activation(out=tmp, in_=ps[:, :], func=AF.Copy), reads=[psd], writes=[tmpd])
    pk, pkd = pkring.next()
    s.op("pe", lambda e: e.matmul(pk[0:32, :], lhsT=pm[0:32, 0:32], rhs=tmp[0:32, :], start=True, stop=True),
         reads=[tmpd, pmdep], writes=[pkd])
    t1, t1d = t1ring.next()
    t2, t2d = t1ring.next()
    s.op("dve", lambda e: e.tensor_tensor(out=t1[0:32, :], in0=ps[0:32, :], in1=cosT[0:32, t0:t0 + 512], op=ALU.mult),
         reads=[psd, tabdep], writes=[t1d])
    s.op("dve", lambda e: e.tensor_tensor(out=t2[0:32, :], in0=pk[0:32, :], in1=sinT[0:32, t0:t0 + 512], op=ALU.mult),
         reads=[pkd, tabdep], writes=[t2d])
    s.op("dve", lambda e: e.tensor_tensor(out=tmp[0:32, :], in0=t1[0:32, :], in1=t2[0:32, :], op=ALU.add),
         reads=[t1d, t2d, tmpd], writes=[tmpd])
    if d != 1:
        n = 512 // d
        l0 = t0 // d
        dv = full.rearrange("p (r l) -> p l r", r=d)[:, l0:l0 + n, :]
        s.op("act", lambda e: e.activation(out=dv, in_=tmp.rearrange("p (l r) -> p l r", r=d), func=AF.Copy), reads=[tmpd], writes=[dstdep])


def load_norm_h(cx, x_own, nw, nwdep, hT, hdep, ones, onesdep, psring, sqring, rsring, xbring):
    s = cx.s
    for tb in range(4):
        xb, xbd = xbring.next()
        for kc in range(KC):
            s.op("sp", lambda e, xb=xb, kc=kc, tb=tb: e.dma_start(out=xb[:, kc, :], in_=x_own[kc * 128:(kc + 1) * 128, tb * 512:(tb + 1) * 512]),
                 writes=[xbd], dma=True)
        ps, psd = psring.next()
        for kc in range(KC):
            sq, sqd = sqring.next()
            s.op("act", lambda e, sq=sq, xb=xb, kc=kc: e.activation(out=sq[:, :], in_=xb[:, kc, :], func=AF.Square),
                 reads=[xbd], writes=[sqd])
            s.op("pe", lambda e, ps=ps, sq=sq, kc=kc: e.matmul(ps[:, :], lhsT=ones[:, 0:128], rhs=sq[:, :], start=(kc == 0), stop=(kc == KC - 1)),
                 reads=[sqd, onesdep], writes=[psd])
        rs, rsd = rsring.next()
        s.op("act", lambda e, rs=rs, ps=ps: e.activation(out=rs[:, :], in_=ps[:, :], func=AF.Sqrt, bias=ones[:, 128:129], scale=1.0 / D),
             reads=[psd, onesdep], writes=[rsd])
        s.op("dve", lambda e, rs=rs: e.reciprocal(out=rs[:, :], in_=rs[:, :]), reads=[rsd], writes=[rsd])
        for kc in range(KC):
            s.op("dve", lambda e, rs=rs, xb=xb, kc=kc, tb=tb: e.scalar_tensor_tensor(
                out=hT[:, kc, tb * 512:(tb + 1) * 512], in0=xb[:, kc, :], scalar=nw[:, kc:kc + 1], in1=rs[:, :],
                op0=ALU.mult, op1=ALU.mult), reads=[xbd, rsd, nwdep], writes=[hdep])


KOFF = (0, 128, 640)
BOFF = (0, 1, 5)
KMSG = 2688


def emit_attn_kv(P, io):
    nc = P.nc
    x_own, nw_d, wk_d, wv_d, cos_d, sin_d, pm_d = io["x_in"], io["nw"], io["wk"], io["wv"], io["cosT"], io["sinT"], io["pm"]
    k_out, v_out, kmsg, vmsg = io["k_own"], io["v_own"], io["kmsg"], io["vmsg"]
    kmd = [Dep() for _ in range(NH)]
    vmd = [Dep() for _ in range(NH)]
    with contextlib.ExitStack() as es:
        cx = Ctx(nc, es, P.S)
        s = cx.s
        hT = cx.sb([128, KC, T], BF16, "hT")
        hdep = Dep()
        nw = cx.sb([128, KC], F32, "nw")
        ones = cx.sb([128, 129], F32, "ones")
        cosT = cx.sb([128, T], F32, "cosT")
        sinT = cx.sb([128, T], F32, "sinT")
        pm = cx.sb([128, 128], BF16, "pm")
        nwdep, onesdep, tabdep, pmdep = Dep(), Dep(), Dep(), Dep()
        psring = Ring([cx.ps([128, 512], F32, "ps") for _ in range(6)], ps=True)
        pkring = Ring([cx.ps([128, 512], F32, "pk") for _ in range(2)], ps=True)
        sqring = Ring([cx.sb([128, 512], F32, "sq") for _ in range(3)])
        rsring = Ring([cx.sb([128, 512], F32, "rs") for _ in range(2)])
        xbring = Ring([cx.sb([128, KC, 512], F32, "xb") for _ in range(2)])
        t1ring = Ring([cx.sb([128, 512], F32, "t1") for _ in range(4)])
        tbring = Ring([cx.sb([128, 512], BF16, "tb16") for _ in range(4)])
        wkring = Ring([cx.sb([128, KC, 128], BF16, "wk") for _ in range(3)])
        wvring = Ring([cx.sb([128, KC, 1024], BF16, "wv") for _ in range(2)])
        kring = Ring([cx.sb([128, T], BF16, "kt") for _ in range(2)])
        vstage = cx.sb([128, NH, 16, 128], BF16, "vst")
        vsdep = Dep()

        s.op("sp", lambda e: e.dma_start(out=nw[:, :], in_=nw_d[:, :]), writes=[nwdep], dma=True)
        s.op("sp", lambda e: e.dma_start(out=cosT[:, :], in_=cos_d[:, :]), writes=[tabdep], dma=True)
        s.op("sp", lambda e: e.dma_start(out=sinT[:, :], in_=sin_d[:, :]), writes=[tabdep], dma=True)
        s.op("pool", lambda e: e.dma_start(out=pm[:, :], in_=pm_d[:, :]), writes=[pmdep], dma=True)
        s.op("dve", lambda e: e.memset(ones[:, 0:128], 1.0), writes=[onesdep])
        s.op("dve", lambda e: e.memset(ones[:, 128:129], EPS), writes=[onesdep])
        load_norm_h(cx, x_own, nw, nwdep, hT, hdep, ones, onesdep, psring, sqring, rsring, xbring)
        for kc in range(KC):
            s.op("sp", lambda e, kc=kc: e.dma_start(out=P.hsave[:, kc, 0:T], in_=hT[:, kc, :]), reads=[hdep], dma=True)

        outtoks = []
        for g in range(NG):
            d = DIL[g]
            wv, wvdep = wvring.next()
            s.op("pool", lambda e, wv=wv, g=g: e.dma_start(out=wv[:, :, :], in_=wv_d[g, :, :, :]), writes=[wvdep], dma=True)
            for blk in range(16):
                for half in range(2):
                    ps, psd = psring.next()
                    fns = []
                    for kc in range(KC):
                        tv, _ = colview(hT[:, kc, :], d, blk * 128, 128)
                        fns.append(lambda e, ps=ps, tv=tv, wv=wv, kc=kc, half=half: e.matmul(
                            ps[:, :], lhsT=tv, rhs=wv[:, kc, half * 512:(half + 1) * 512], start=(kc == 0), stop=(kc == KC - 1)))
                    s.op("pe", fns, reads=[hdep, wvdep], writes=[psd])
                    eng = "act" if half == 0 else "dve"
                    if eng == "act":
                        s.op("act", lambda e, ps=ps, blk=blk, half=half: e.activation(
                            out=vstage[:, half * 4:(half + 1) * 4, blk, :], in_=ps[:, :].rearrange("p (h e) -> p h e", h=4), func=AF.Copy),
                            reads=[psd], writes=[vsdep])
                    else:
                        s.op("dve", lambda e, ps=ps, blk=blk, half=half: e.tensor_copy(
                            out=vstage[:, half * 4:(half + 1) * 4, blk, :], in_=ps[:, :].rearrange("p (h e) -> p h e", h=4)),
                            reads=[psd], writes=[vsdep])
            nbg = 16 // d
            for h in range(NH):
                outtoks.append(s.op("sp", lambda e, g=g, h=h: e.dma_start(out=v_out[g, h, :, :, :], in_=vstage[:, h, :, :]),
                                    reads=[vsdep], dma=True))
                s.op("sp", lambda e, g=g, h=h, d=d, nbg=nbg: e.dma_start(
                    out=vmsg[h, :, BOFF[g]:BOFF[g] + d, :],
                    in_=vstage[:, h, :, :].rearrange("p (r n) e -> p r n e", r=d)[:, :, nbg - 1, :]), reads=[vsdep], writes=[vmd[h]], dma=True)
        quads = [(h, g, qd) for h in range(NH) for g in range(NG) for qd in range(4)]
        qst = {}
        tiles = {}

        def kP(i):
            h, g, qd = quads[i]
            d = DIL[g]
            if qd == 0:
                wk, wkdep = wkring.next()
                s.op("pool", lambda e: e.dma_start(out=wk[:, :, :], in_=wk_d[g, h, :, :, :]), writes=[wkdep], dma=True)
                kt, ktdep = kring.next()
                tiles[(h, g)] = (wk, wkdep, kt, ktdep)
                if g == 0 and h == 0:
                    for hh in range(NH):
                        io["coll_v"](hh, vmd[hh])
                if g == 1 and h >= 1:
                    io["coll_k"](h - 1, kmd[h - 1])
            wk, wkdep, kt, ktdep = tiles[(h, g)]
            ps, psd = psring.next()
            s.op("pe", [lambda e, kc=kc: e.matmul(ps[:, :], lhsT=wk[:, kc, :], rhs=hT[:, kc, qd * 512:(qd + 1) * 512], start=(kc == 0), stop=(kc == KC - 1))
                        for kc in range(KC)], reads=[hdep, wkdep], writes=[psd])
            t0 = qd * 512
            if d == 1:
                tmp, tmpd = kt[:, t0:t0 + 512], ktdep
            else:
                tmp, tmpd = tbring.next()
                tmp = tmp[:, :]
            s.op("act", lambda e: e.activation(out=tmp, in_=ps[:, :], func=AF.Copy), reads=[psd], writes=[tmpd])
            qst[i] = dict(ps=ps, psd=psd, tmp=tmp, tmpd=tmpd, kt=kt, ktdep=ktdep, d=d, t0=t0, h=h, g=g, qd=qd)

        def kR23(i):
            q = qst[i]
            ps, psd, tmp, tmpd, t0 = q["ps"], q["psd"], q["tmp"], q["tmpd"], q["t0"]
            pk, pkd = pkring.next()
            s.op("pe", lambda e: e.matmul(pk[0:32, :], lhsT=pm[0:32, 0:32], rhs=tmp[0:32, :], start=True, stop=True), reads=[tmpd, pmdep], writes=[pkd])
            t1, t1d = t1ring.next()
            t2, t2d = t1ring.next()
            s.op("dve", lambda e: e.tensor_tensor(out=t1[0:32, :], in0=ps[0:32, :], in1=cosT[0:32, t0:t0 + 512], op=ALU.mult), reads=[psd, tabdep], writes=[t1d])
            s.op("dve", lambda e: e.tensor_tensor(out=t2[0:32, :], in0=pk[0:32, :], in1=sinT[0:32, t0:t0 + 512], op=ALU.mult), reads=[pkd, tabdep], writes=[t2d])
            s.op("dve", lambda e: e.tensor_tensor(out=tmp[0:32, :], in0=t1[0:32, :], in1=t2[0:32, :], op=ALU.add), reads=[t1d, t2d, tmpd], writes=[tmpd])

        def kR4(i):
            q = qst.pop(i)
            tmp, tmpd, kt, ktdep, d, t0, h, g, qd = q["tmp"], q["tmpd"], q["kt"], q["ktdep"], q["d"], q["t0"], q["h"], q["g"], q["qd"]
            if d != 1:
                n = 512 // d
                l0 = t0 // d
                dv = kt[:, :].rearrange("p (r l) -> p l r", r=d)[:, l0:l0 + n, :]
                s.op("act", lambda e: e.activation(out=dv, in_=tmp.rearrange("p (l r) -> p l r", r=d), func=AF.Copy), reads=[tmpd], writes=[ktdep])
            if qd == 3:
                outtoks.append(s.op("sp", lambda e: e.dma_start(out=k_out[g, h, :, :], in_=kt[:, :]), reads=[ktdep], dma=True))
                Lg = T // d
                s.op("sp", lambda e: e.dma_start(out=kmsg[h, :, KOFF[g]:KOFF[g] + d * 128].rearrange("p (r l) -> p r l", r=d),
                                                 in_=kt[:, :].rearrange("p (r l) -> p r l", r=d)[:, :, Lg - 128:Lg]), reads=[ktdep], writes=[kmd[h]], dma=True)

        NQ = len(quads)
        for it in range(NQ + 2):
            if 0 <= it - 2 < NQ:
                kR4(it - 2)
            if 0 <= it - 1 < NQ:
                kR23(it - 1)
            if it < NQ:
                kP(it)
        io["coll_k"](NH - 1, kmd[NH - 1])
        s.barrier(exclude_cc=True)
        s.emit()


def rope_tables_np(pos0):
    pos = np.arange(pos0, pos0 + T, dtype=np.float32)
    inv = (np.float32(500000.0) ** (-np.arange(0, 32, 2, dtype=np.float32) / np.float32(32))).astype(np.float32)
    ang = (pos[None, :] * inv[:, None]).astype(np.float32)
    c = np.ones((128, T), np.float32)
    sn = np.zeros((128, T), np.float32)
    c[0:16] = np.cos(ang)
    c[16:32] = np.cos(ang)
    sn[0:16] = -np.sin(ang)
    sn[16:32] = np.sin(ang)
    return c, sn


def perm_matrix():
    pm = np.zeros((128, 128), np.float32)
    for e in range(16):
        pm[e + 16, e] = 1.0
        pm[e, e + 16] = 1.0
    return pm


def run_attn_kv(xTs, nw, w_qkv):
    nc = _prog("attn_kv", build_attn_kv)
    wr = w_qkv.reshape(KC, 128, NG, 3, NH, 128)
    wk = np.ascontiguousarray(wr[:, :, :, 1].transpose(2, 3, 1, 0, 4))
    wv = np.ascontiguousarray(wr[:, :, :, 2].transpose(2, 1, 0, 3, 4).reshape(NG, 128, KC, 1024))
    pm = perm_matrix()
    maps = []
    for c in range(NCORES):
        cs, sn = rope_tables_np((c % 4) * T)
        maps.append({"x_own": xTs[c], "nw": _cols128(nw), "wk": wk, "wv": wv, "cosT": cs, "sinT": sn, "pm": pm})
    res = run_bass_kernel_spmd(nc, maps, core_ids=list(range(NCORES)))
    return [(np.asarray(r["k_out"]), np.asarray(r["v_out"])) for r in res.results]


def emit_attn_main(P, io):
    nc = P.nc
    x_own, nw_d, wq_d, wo_d, cos_d, sin_d, pm_d, mk_d = io["x_in"], io["nw"], io["wq"], io["wo"], io["cosT"], io["sinT"], io["pm"], io["masks"]
    k_own, v_own, x_out = io["k_own"], io["v_own"], io["x_out"]
    with contextlib.ExitStack() as es:
        cx = Ctx(nc, es, P.S)
        s = cx.s
        hT = cx.sb([128, KC, T], BF16, "hT")
        hdep = Dep()
        aT = cx.sb([128, NH, T], BF16, "aT")
        adeps = [Dep() for _ in range(NH)]
        nw = cx.sb([128, KC], F32, "nw")
        ones = cx.sb([128, 129], F32, "ones")
        onesb = cx.sb([128, 128], BF16, "onesb")
        cosT = cx.sb([128, T], F32, "cosT")
        sinT = cx.sb([128, T], F32, "sinT")
        pm = cx.sb([128, 128], BF16, "pm")
        mk = cx.sb([128, 3, 512], BF16, "mk")
        idm = cx.sb([128, 128], BF16, "idm")
        idmdep = Dep()
        nwdep, onesdep, tabdep, pmdep, mkdep, onesbdep = Dep(), Dep(), Dep(), Dep(), Dep(), Dep()
        psring = Ring([cx.ps([128, 512], F32, "ps") for _ in range(2)], ps=True)
        pkring = Ring([cx.ps([128, 512], F32, "pk") for _ in range(1)], ps=True)
        sring = Ring([cx.ps([128, 512], F32, "pss") for _ in range(3)], ps=True)
        odring = Ring([cx.ps([128, 512], F32, "pod") for _ in range(2)], ps=True)
        sqring = Ring([cx.sb([128, 512], F32, "sq") for _ in range(2)])
        rsring = Ring([cx.sb([128, 512], F32, "rs") for _ in range(2)])
        t1ring = Ring([cx.sb([128, 512], F32, "t1") for _ in range(4)])
        tbring = Ring([cx.sb([128, 512], BF16, "tb16") for _ in range(3)])
        wqring = Ring([cx.sb([128, KC, 128], BF16, "wq") for _ in range(3)])
        woring = Ring([cx.sb([128, NH, 128], BF16, "wo") for _ in range(2)])
        kring = Ring([cx.sb([128, 4096], BF16, "ks") for _ in range(2)])
        vring = Ring([cx.sb([128, 4096], BF16, "vs") for _ in range(2)])
        qring = Ring([cx.sb([128, T], BF16, "qs") for _ in range(2)])
        pring = Ring([cx.sb([128, 512], BF16, "pT") for _ in range(3)])
        acc = cx.sb([128, 2, T], F32, "acc")
        accdep = Dep()
        rden = cx.sb([128, T], F32, "rden")
        rdendep = Dep()
        oring = Ring([cx.sb([128, 512], F32, "ob") for _ in range(6)])

        s.op("sp", lambda e: e.dma_start(out=nw[:, :], in_=nw_d[:, :]), writes=[nwdep], dma=True)
        s.op("sp", lambda e: e.dma_start(out=cosT[:, :], in_=cos_d[:, :]), writes=[tabdep], dma=True)
        s.op("sp", lambda e: e.dma_start(out=sinT[:, :], in_=sin_d[:, :]), writes=[tabdep], dma=True)
        s.op("pool", lambda e: e.dma_start(out=pm[:, :], in_=pm_d[:, :]), writes=[pmdep], dma=True)
        s.op("pool", lambda e: e.dma_start(out=mk[:, :, :], in_=mk_d[:, :, :]), writes=[mkdep], dma=True)
        s.op("dve", lambda e: e.memset(ones[:, 0:128], 1.0), writes=[onesdep])
        s.op("dve", lambda e: e.memset(ones[:, 128:129], EPS), writes=[onesdep])
        s.op("dve", lambda e: e.memset(onesb[:, :], 1.0), writes=[onesbdep])
        s.op("pool", lambda e: e.memset(idm[:, :], 1.0), writes=[idmdep])
        s.op("pool", lambda e: e.affine_select(out=idm[:, :], in_=idm[:, :], pattern=[[1, 128]], compare_op=ALU.is_equal, fill=0.0, base=0, channel_multiplier=-1),
             reads=[idmdep], writes=[idmdep])
        for kc in range(KC):
            s.op("sp", lambda e, kc=kc: e.dma_start(out=hT[:, kc, :], in_=P.hsave[:, kc, 0:T]), writes=[hdep], dma=True)

        def QP(h, g):
            d = DIL[g]
            L = T // d
            nb = L // 128
            LK = 128 + L
            ks, ksdep = kring.next()
            vs, vsdep = vring.next()
            ksv = ks[:, 0:d * LK].rearrange("p (r l) -> p r l", r=d)
            vsv = vs[:, 0:d * (nb + 1) * 128].rearrange("p (r n e) -> p r n e", r=d, n=nb + 1)
            s.op("sp", lambda e, ksv=ksv, g=g, h=h, d=d: e.dma_start(
                out=ksv[:, :, 0:128], in_=io["khalo"](g, h)),
                reads=[io["klocd"][h]], writes=[ksdep], dma=True)
            s.op("sp", lambda e, ksv=ksv, g=g, h=h, d=d, LK=LK: e.dma_start(
                out=ksv[:, :, 128:LK], in_=k_own[g, h, :, :].rearrange("p (r l) -> p r l", r=d)),
                writes=[ksdep], dma=True)
            s.op("sp", lambda e, vsv=vsv, g=g, h=h, d=d: e.dma_start(
                out=vsv[:, :, 0, :], in_=io["vhalo"](g, h)),
                reads=[io["vlocd"][h]], writes=[vsdep], dma=True)
            s.op("sp", lambda e, vsv=vsv, g=g, h=h, d=d, nb=nb: e.dma_start(
                out=vsv[:, :, 1:nb + 1, :], in_=v_own[g, h, :, :, :].rearrange("p (r n) e -> p r n e", r=d)),
                writes=[vsdep], dma=True)
            wq, wqdep = wqring.next()
            s.op("pool", lambda e, wq=wq, g=g, h=h: e.dma_start(out=wq[:, :, :], in_=wq_d[g, h, :, :, :]), writes=[wqdep], dma=True)
            qs, qsdep = qring.next()
            for qd in range(4):
                ps, psd = psring.next()
                s.op("pe", [lambda e, ps=ps, wq=wq, kc=kc, qd=qd: e.matmul(
                    ps[:, :], lhsT=wq[:, kc, :], rhs=hT[:, kc, qd * 512:(qd + 1) * 512], start=(kc == 0), stop=(kc == KC - 1)) for kc in range(KC)],
                    reads=[hdep, wqdep], writes=[psd])
                emit_rope(cx, ps, psd, qs[:, :], qsdep, d, qd * 512, cosT, sinT, tabdep, pm, pmdep, pkring, t1ring, tbring)

            return dict(ksv=ksv, vsv=vsv, qs=qs, ksdep=ksdep, vsdep=vsdep, qsdep=qsdep, d=d, nb=nb)

        def UN(h, g, st):
            ksv, vsv, qs, ksdep, vsdep, qsdep, d, nb = st['ksv'], st['vsv'], st['qs'], st['ksdep'], st['vsdep'], st['qsdep'], st['d'], st['nb']
            def qk(pr, ksv=ksv, qs=qs, nb=nb, ksdep=ksdep, qsdep=qsdep, d=d):
                pss, pssd = sring.next()
                fns = []
                for uu in range(2):
                    u = pr * 2 + uu
                    r, n = divmod(u, nb)
                    for half in range(2):
                        fns.append(lambda e, pss=pss, r=r, n=n, u=u, uu=uu, half=half: e.matmul(
                            pss[:, uu * 256 + half * 128: uu * 256 + half * 128 + 128],
                            lhsT=ksv[:, r, (n + half) * 128:(n + half + 1) * 128],
                            rhs=qs[:, u * 128:(u + 1) * 128], start=(uu == 0 and half == 0), stop=False, skip_group_check=True))
                if d == 1:
                    var = 0 if pr == 0 else 1
                elif d == 4:
                    var = 0 if pr % 2 == 0 else 1
                else:
                    var = 2
                fns.append(lambda e, pss=pss, var=var: e.matmul(pss[:, :], lhsT=idm[:, :], rhs=mk[:, var, :], start=False, stop=True, skip_group_check=True))
                s.op("pe", fns, reads=[ksdep, qsdep, mkdep, idmdep], writes=[pssd])
                return pss, pssd

            def rest(pr, pss, pssd, vsv=vsv, d=d, g=g, nb=nb, vsdep=vsdep):
                pT, pTd = pring.next()
                s.op("act", lambda e: e.activation(out=pT[:, :], in_=pss[:, :], func=AF.Exp, scale=SCALE),
                     reads=[pssd], writes=[pTd])
                pod, podd = odring.next()
                fns = []
                for uu in range(2):
                    u = pr * 2 + uu
                    r, n = divmod(u, nb)
                    for half in range(2):
                        fns.append(lambda e, r=r, n=n, uu=uu, half=half: e.matmul(
                            pod[:, uu * 128:(uu + 1) * 128], lhsT=vsv[:, r, n + half, :],
                            rhs=pT[:, uu * 256 + half * 128: uu * 256 + half * 128 + 128],
                            start=(half == 0), stop=(half == 1)))
                    for half in range(2):
                        fns.append(lambda e, uu=uu, half=half: e.matmul(
                            pod[:, 256 + uu * 128: 256 + (uu + 1) * 128], lhsT=onesb[:, :],
                            rhs=pT[:, uu * 256 + half * 128: uu * 256 + half * 128 + 128],
                            start=(half == 0), stop=(half == 1)))
                s.op("pe", fns, reads=[vsdep, pTd, onesbdep], writes=[podd])
                return pod, podd

            def rest2(pr, pod, podd, d=d, g=g):
                if d == 16:
                    for w in range(2):
                        av, A = colview(acc[:, w, :], d, pr * 256, 256)
                        src = v3(pod[:, w * 256:(w + 1) * 256], A)
                        s.op("dve", lambda e, av=av, src=src: e.tensor_tensor(out=av, in0=src, in1=av, op=ALU.add),
                             reads=[podd, accdep], writes=[accdep])
                else:
                    if d == 1:
                        av = acc[:, :, pr * 256:(pr + 1) * 256]
                    else:
                        L4 = T // 4
                        r0, l0 = divmod(pr * 256, L4)
                        av = acc[:, :, :].rearrange("p w (l r) -> p w r l", r=4)[:, :, r0, l0:l0 + 256]
                    src = pod[:, :].rearrange("p (w c) -> p w c", w=2)
                    if g == 0:
                        s.op("dve", lambda e, av=av, src=src: e.tensor_copy(out=av, in_=src), reads=[podd], writes=[accdep])
                    else:
                        s.op("dve", lambda e, av=av, src=src: e.tensor_tensor(out=av, in0=src, in1=av, op=ALU.add),
                             reads=[podd, accdep], writes=[accdep])

            prev = qk(0)
            pend = None
            for pr in range(8):
                nxt = qk(pr + 1) if pr + 1 < 8 else None
                pod, podd = rest(pr, *prev)
                if pend is not None:
                    rest2(*pend)
                pend = (pr, pod, podd)
                prev = nxt
            rest2(*pend)
            if g == NG - 1:
                s.op("dve", lambda e: e.reciprocal(out=rden[:, :], in_=acc[:, 1, :]), reads=[accdep], writes=[rdendep])
                s.op("dve", lambda e, h=h: e.tensor_tensor(out=aT[:, h, :], in0=acc[:, 0, :], in1=rden[:, :], op=ALU.mult),
                     reads=[accdep, rdendep], writes=[adeps[h]])

        units = [(h, g) for h in range(NH) for g in range(NG)]
        stq = QP(*units[0])
        for ui, (h, g) in enumerate(units):
            nstq = QP(*units[ui + 1]) if ui + 1 < len(units) else None
            UN(h, g, stq)
            stq = nstq

        wo_v = wo_d.rearrange("(h e) d -> e h d", e=128)
        wops = Ring(psring.aps + sring.aps + odring.aps)
        wops.deps = psring.deps + sring.deps + odring.deps
        outtoks = []
        for dc in range(KC):
            wo, wodep = woring.next()
            s.op("pool", lambda e, wo=wo, dc=dc: e.dma_start(out=wo[:, :, :], in_=wo_v[:, :, dc * 128:(dc + 1) * 128]), writes=[wodep], dma=True)
            for tb in range(4):
                ps, psd = wops.next()
                s.op("pe", [lambda e, ps=ps, wo=wo, h=h, tb=tb: e.matmul(
                    ps[:, :], lhsT=wo[:, h, :], rhs=aT[:, h, tb * 512:(tb + 1) * 512], start=(h == 0), stop=(h == NH - 1))
                    for h in range(NH)], reads=[wodep] + adeps, writes=[psd])
                ob, obd = oring.next()
                s.op("sp", lambda e, ob=ob, dc=dc, tb=tb: e.dma_start(out=ob[:, :], in_=x_own[dc * 128:(dc + 1) * 128, tb * 512:(tb + 1) * 512]),
                     writes=[obd], dma=True)
                s.op("dve", lambda e, ob=ob, ps=ps: e.tensor_tensor(out=ob[:, :], in0=ps[:, :], in1=ob[:, :], op=ALU.add),
                     reads=[psd, obd], writes=[obd])
                outtoks.append(s.op("sp", lambda e, ob=ob, dc=dc, tb=tb: e.dma_start(
                    out=x_out[dc * 128:(dc + 1) * 128, tb * 512:(tb + 1) * 512], in_=ob[:, :]), reads=[obd], dma=True))
                if tb == 3 and io.get("hmsg") is not None:
                    s.op("sp", lambda e, ob=ob, dc=dc: e.dma_start(out=io["hmsg"][:, dc, :], in_=ob[:, 512 - HALO:512]), reads=[obd], dma=True)
        s.barrier()
        s.emit()


def attn_masks(valid):
    j = np.arange(128)[:, None]
    i = np.arange(128)[None, :]
    prevN = (j >= i).astype(np.float32)
    cur = (j <= i).astype(np.float32)
    prevH = prevN * np.float32(1.0 if valid else 0.0)
    m = np.zeros((128, 3, 512), np.float32)
    for v, (a, b) in enumerate(((prevH, prevN), (prevN, prevN), (prevH, prevH))):
        m[:, v, 0:128] = a
        m[:, v, 128:256] = cur
        m[:, v, 256:384] = b
        m[:, v, 384:512] = cur
    return np.where(m > 0.5, np.float32(0.0), np.float32(-30000.0)).astype(np.float32)


def run_attn(xTs, nw, w_qkv, w_o):
    kv = run_attn_kv(xTs, nw, w_qkv)
    nc = _prog("attn_main", build_attn_main)
    wr = w_qkv.reshape(KC, 128, NG, 3, NH, 128)
    wq = np.ascontiguousarray(wr[:, :, :, 0].transpose(2, 3, 1, 0, 4))
    pm = perm_matrix()
    maps = []
    for c in range(NCORES):
        cs, sn = rope_tables_np((c % 4) * T)
        k_own, v_own = kv[c]
        k_halo = np.zeros_like(k_own)
        v_halo = np.zeros_like(v_own)
        if c % 4 != 0:
            kp, vp = kv[c - 1]
            for g in range(NG):
                d = DIL[g]
                L = T // d
                nb = L // 128
                k_halo[g, :, :, 0:d * 128] = kp[g].reshape(NH, 128, d, L)[:, :, :, L - 128:].reshape(NH, 128, d * 128)
                v_halo[g, :, :, 0:d, :] = vp[g].reshape(NH, 128, d, nb, 128)[:, :, :, nb - 1, :]
        maps.append({"x_own": xTs[c], "nw": _cols128(nw), "wq": wq, "wo": np.ascontiguousarray(w_o),
                     "cosT": cs, "sinT": sn, "pm": pm, "masks": attn_masks(c % 4 != 0),
                     "k_own": k_own, "k_halo": k_halo, "v_own": v_own, "v_halo": v_halo})
    res = run_bass_kernel_spmd(nc, maps, core_ids=list(range(NCORES)))
    return [np.asarray(r["x_out"]) for r in res.results]


DIN = 2048
NHS = 32
CONV_ENG2 = "dve"
OFF_ENG = "pool"


def emit_ssm(P, io, phase):
    debug = False
    B_ = phase == "B"
    nc = P.nc
    dbg = {}
    x_own, nw_d, win_d, wdt_d, cw_d, vec_d = io["x_in"], io["nw"], io["win"], io["wdt"], io["cw"], io["vecs"]
    if B_:
        wz_d, gw_d, wout_d, x_out = io["wz"], io["gw"], io["wout"], io["x_out"]
    else:
        smsg = io["smsg"]
    W = HALO + T
    with contextlib.ExitStack() as es:
        cx = Ctx(nc, es, P.S)
        s = cx.s
        hT = cx.sb([128, KC, W], BF16, "hT")
        hdep = Dep()
        nw = cx.sb([128, KC], F32, "nw")
        ones = cx.sb([128, 129], F32, "ones")
        U = cx.sb([128, 128], F32, "U")
        idb = cx.sb([128, 128], BF16, "idb")
        cw = cx.sb([128, 32, 5], F32, "cw")
        wdt = cx.sb([128, KC, 32], BF16, "wdt")
        vecs = cx.sb([128, 3, 32], F32, "vecs")
        nwdep, onesdep, Udep, iddep, cwdep, wdtdep, vecdep = [Dep() for _ in range(7)]
        psring = Ring([cx.ps([128, 512], F32, "ps") for _ in range(3)], ps=True)
        ptr = cx.ps([128, 8, 128], BF16, "ptr")
        ptrd = Dep(True)
        misc = cx.ps([128, 512], F32, "misc")
        miscd = Dep(True)
        cbk = cx.ps([128, 512], F32, "cbk")
        cbkd = Dep(True)
        ybk = cx.ps([128, 512], F32, "ybk")
        ybkd = Dep(True)
        stbk = cx.ps([128, 512], F32, "stbk")
        stbkd = Dep(True)
        ybring = Ring([ybk, stbk])
        ybring.deps = [ybkd, stbkd]
        if not B_:
            xbr = Ring([cx.sb([128, KC, 512], F32, "xb") for _ in range(2)])
            sq5 = Ring([cx.sb([128, 512], F32, "sq5") for _ in range(3)])
            rs5 = Ring([cx.sb([128, 512], F32, "rs5") for _ in range(2)])
        wring = Ring([cx.sb([128, KC, 128], BF16, "w") for _ in range(3)])
        rawring = Ring([cx.sb([128, 3 + 512], F32, "raw") for _ in range(2)])
        cvring = Ring([cx.sb([128, 512], F32, "cv") for _ in range(2 if B_ else 3)])
        if not B_:
            xcring = Ring([cx.sb([128, 512], BF16, "xc") for _ in range(2)])
        tail = cx.sb([128, 32, 3], F32, "tail")
        taild = [Dep() for _ in range(32)]
        xtok = cx.sb([128, 4, DIN], BF16, "xtok")
        xtokd = Dep()
        btok = cx.sb([128, 4, 1024], BF16, "btok")
        btokd = Dep()
        BT = cx.sb([128, 8, 512], BF16, "BT")
        BTd = Dep()
        dt = cx.sb([128, 4, 32], F32, "dt")
        adt = cx.sb([128, 4, 32], F32, "adt")
        dtd, adtd = Dep(), Dep()
        sm = cx.sb([128, 8, 4, 32], F32, "sm")
        smd = [Dep() for _ in range(8)]
        tsum = cx.sb([128, 32], F32, "tsum")
        tsumd = Dep()
        xdtd = cx.sb([128, DIN], BF16, "xdtd")
        xdtdd = Dep()
        S = cx.sb([128, DIN], F32, "S")
        Sd = Dep()
        if B_:
            CT = cx.sb([128, 8, 512], BF16, "CT")
            CTd = Dep()
            wzring = Ring([cx.sb([128, KC, 512], BF16, "wz") for _ in range(2)])
            sz = cx.sb([128, 4, DIN], BF16, "sz")
            szd = Dep()
            gw = cx.sb([128, 16], F32, "gw")
            gwd = Dep()
            xdt = cx.sb([128, DIN], BF16, "xdt")
            xsD = cx.sb([128, DIN], BF16, "xsD")
            xdt_d, xsD_d = Dep(), Dep()
            Sb = cx.sb([128, DIN], BF16, "Sb")
            Sbd = Dep()
            cbmq = [cx.sb([128, 4, 128], F32, "cbmq%d" % i) for i in range(2)]
            cbmqd = [Dep(), Dep()]
            ttring = Ring([cx.sb([128, 512], F32, "tt") for _ in range(2)])
            lxring = Ring([cx.sb([128, 512], F32, "lx") for _ in range(2)])
            mtring = Ring([cx.sb([128, 512], BF16, "mt") for _ in range(2)])
            ytring = Ring([cx.sb([128, 256], F32, "yt") for _ in range(2)])
            y = cx.sb([128, DIN], F32, "y")
            yd = Dep()
            xb = y[:, :].rearrange("p (kc c) -> p kc c", kc=KC)
            xbd = yd
            junk = cx.sb([128, 256], BF16, "junk")
            junkd = Dep()
            ssq = cx.sb([128, 16], F32, "ssq")
            ssqd = Dep()
            gn = cx.sb([128, DIN], BF16, "gn")
            gnd = Dep()
            gnT = cx.sb([128, 16, 512], BF16, "gnT")
            gnTd = Dep()
            woring = Ring([cx.sb([128, 16, 128], BF16, "wo") for _ in range(2)])
            oring = Ring([cx.sb([128, 512], F32, "ob") for _ in range(2)])
            spl = cx.sb([128, 3, 32], F32, "spl")
            spld = Dep()

        s.op("sp", lambda e: e.dma_start(out=nw[:, :], in_=nw_d[:, :]), writes=[nwdep], dma=True)
        s.op("sp", lambda e: e.dma_start(out=cw[:, :, :], in_=cw_d[:, :, :]), writes=[cwdep], dma=True)
        s.op("pool", lambda e: e.dma_start(out=wdt[:, :, :], in_=wdt_d[:, :, :]), writes=[wdtdep], dma=True)
        for i in range(3):
            s.op("sp", lambda e, i=i: e.dma_start(out=vecs[:, i, :], in_=vec_d[i, :].partition_broadcast(128)), writes=[vecdep], dma=True)
        s.op("dve", lambda e: e.memset(ones[:, 0:128], 1.0), writes=[onesdep])
        s.op("dve", lambda e: e.memset(ones[:, 128:129], EPS), writes=[onesdep])
        s.op("pool", lambda e: e.memset(U[:, :], 1.0), writes=[Udep])
        s.op("pool", lambda e: e.affine_select(out=U[:, :], in_=U[:, :], pattern=[[1, 128]], compare_op=ALU.is_ge, fill=0.0, base=0, channel_multiplier=-1),
             reads=[Udep], writes=[Udep])
        s.op("pool", lambda e: e.memset(idb[:, :], 1.0), writes=[iddep])
        s.op("pool", lambda e: e.affine_select(out=idb[:, :], in_=idb[:, :], pattern=[[1, 128]], compare_op=ALU.is_equal, fill=0.0, base=0, channel_multiplier=-1),
             reads=[iddep], writes=[iddep])
        s.op("act", lambda e: e.activation(out=vecs[:, 1, :], in_=vecs[:, 1, :], func=AF.Exp), reads=[vecdep], writes=[vecdep])
        s.op("dve", lambda e: e.tensor_scalar(out=vecs[:, 1, :], in0=vecs[:, 1, :], scalar1=-1.0, scalar2=0.0, op0=ALU.mult, op1=ALU.add), reads=[vecdep], writes=[vecdep])
        s.op("dve", lambda e: e.memset(tsum[:, :], 0.0), writes=[tsumd])
        def s_init():
            HS = SMSG // 2
            sallA = P.sall[0].rearrange("(r p) f -> r p f", p=128)
            sallB = P.sall[1].rearrange("(r p) f -> r p f", p=128)
            fl2 = cx.sb([128, 8], F32, "fl2")
            fl2d = Dep()
            s.op("sp", lambda e: e.dma_start(out=fl2[:, :], in_=P.flags[:, :]), writes=[fl2d], dma=True)
            s.op("dve", lambda e: e.memset(S[:, :], 0.0), writes=[Sd])
            for r in range(4):
                s.op("sp", lambda e, r=r: e.dma_start(out=spl[:, 0, :], in_=sallB[r, :, DIN - HS:DIN - HS + NHS]), writes=[spld], dma=True)
                s.op("sp", lambda e, r=r: e.dma_start(out=y[:, 0:HS], in_=sallA[r, :, :]), writes=[yd], dma=True)
                s.op("sp", lambda e, r=r: e.dma_start(out=y[:, HS:DIN], in_=sallB[r, :, 0:DIN - HS]), writes=[yd], dma=True)
                s.op("act", lambda e, r=r: e.activation(out=spl[:, 1, :], in_=spl[:, 0, :], func=AF.Exp, scale=fl2[:, 4 + r:5 + r]), reads=[spld, fl2d], writes=[spld])
                s.op("dve", lambda e: e.tensor_tensor(out=S[:, :].rearrange("p (h q) -> p h q", h=NHS), in0=S[:, :].rearrange("p (h q) -> p h q", h=NHS),
                                                      in1=spl[:, 1, :].unsqueeze(2).to_broadcast([128, NHS, 64]), op=ALU.mult), reads=[Sd, spld], writes=[Sd])
                s.op("dve", lambda e, r=r: e.scalar_tensor_tensor(out=S[:, :], in0=y[:, :], scalar=fl2[:, 4 + r:5 + r], in1=S[:, :], op0=ALU.mult, op1=ALU.add),
                     reads=[Sd, yd, fl2d], writes=[Sd])
            s.op("act", lambda e: e.activation(out=Sb[:, :], in_=S[:, :], func=AF.Copy), reads=[Sd], writes=[Sbd])

        if B_:
            s.op("sp", lambda e: e.dma_start(out=gw[:, :], in_=gw_d[:, :]), writes=[gwd], dma=True)
            pass
        else:
            s.op("dve", lambda e: e.memset(S[:, :], 0.0), writes=[Sd])

        def norm_cols(load_fn, n, c0):
            if not B_:
                xb, xbd = xbr.next()
                sqring_, rsring_ = sq5, rs5
            ps, psd = psring.next()
            load_fn(xb, xbd)
            for kc in range(KC):
                sq, sqd = sqring_.next()
                s.op("act", lambda e, sq=sq, kc=kc: e.activation(out=sq[:, 0:n], in_=xb[:, kc, 0:n], func=AF.Square), reads=[xbd], writes=[sqd])
                s.op("pe", lambda e, ps=ps, sq=sq, kc=kc: e.matmul(ps[:, 0:n], lhsT=ones[:, 0:128], rhs=sq[:, 0:n], start=(kc == 0), stop=(kc == KC - 1)),
                     reads=[sqd, onesdep], writes=[psd])
            rs, rsd = rsring_.next()
            s.op("act", lambda e, rs=rs, ps=ps: e.activation(out=rs[:, 0:n], in_=ps[:, 0:n], func=AF.Sqrt, bias=ones[:, 128:129], scale=1.0 / D),
                 reads=[psd, onesdep], writes=[rsd])
            s.op("dve", lambda e, rs=rs: e.reciprocal(out=rs[:, 0:n], in_=rs[:, 0:n]), reads=[rsd], writes=[rsd])
            for kc in range(KC):
                s.op("dve", lambda e, rs=rs, kc=kc: e.scalar_tensor_tensor(out=hT[:, kc, c0:c0 + n], in0=xb[:, kc, 0:n], scalar=nw[:, kc:kc + 1], in1=rs[:, 0:n],
                                                                         op0=ALU.mult, op1=ALU.mult), reads=[xbd, rsd, nwdep], writes=[hdep])

        if B_:
            for kc in range(KC):
                s.op("sp", lambda e, kc=kc: e.dma_start(out=hT[:, kc, :], in_=P.hsave[:, kc, :]), writes=[hdep], dma=True)
        else:
            norm_cols(lambda xb, xbd: emit_halo(cx, P, xb[:, :, 0:HALO], [xbd]), HALO, 0)
            for blk in range(T // 512):
                def ld(xb, xbd, blk=blk):
                    for kc in range(KC):
                        s.op("sp", lambda e, kc=kc: e.dma_start(out=xb[:, kc, :], in_=x_own[kc * 128:(kc + 1) * 128, blk * 512:(blk + 1) * 512]), writes=[xbd], dma=True)
                norm_cols(ld, 512, HALO + blk * 512)
            for kc in range(KC):
                s.op("sp", lambda e, kc=kc: e.dma_start(out=P.hsave[:, kc, :], in_=hT[:, kc, :]), reads=[hdep], dma=True)

        def bc64(ap32):
            return ap32.unsqueeze(2).to_broadcast([128, ap32.shape[1], 64])

        def h64(ap):
            return ap.rearrange("p (h q) -> p h q", q=64)

        outtoks = []
        deferred = []
        nchunks = 32 if B_ else 24
        for tb in range(4):
            c0 = HALO + tb * 512
            pst = {}

            def s1(cc, tb=tb, c0=c0):
                w, wdep = wring.next()
                s.op("pool", lambda e: e.dma_start(out=w[:, :, :], in_=win_d[cc, :, :, :]), writes=[wdep], dma=True)
                raw, rawd = rawring.next()
                if tb == 0:
                    s.op("pe", [lambda e, kc=kc: e.matmul(misc[:, 0:HALO], lhsT=w[:, kc, :], rhs=hT[:, kc, 0:HALO], start=(kc == 0), stop=(kc == KC - 1))
                                for kc in range(KC)], reads=[wdep, hdep], writes=[miscd])
                    s.op("act", lambda e: e.activation(out=raw[:, 0:HALO], in_=misc[:, 0:HALO], func=AF.Copy), reads=[miscd], writes=[rawd])
                else:
                    s.op("act", lambda e: e.activation(out=raw[:, 0:HALO], in_=tail[:, cc, :], func=AF.Copy), reads=[taild[cc]], writes=[rawd])
                ps, psd = psring.next()
                s.op("pe", [lambda e, kc=kc: e.matmul(ps[:, :], lhsT=w[:, kc, :], rhs=hT[:, kc, c0:c0 + 512], start=(kc == 0), stop=(kc == KC - 1))
                            for kc in range(KC)], reads=[wdep, hdep], writes=[psd])
                s.op("act", lambda e: e.activation(out=raw[:, HALO:HALO + 512], in_=ps[:, :], func=AF.Copy), reads=[psd], writes=[rawd])
                if tb < 3:
                    s.op("act", lambda e: e.activation(out=tail[:, cc, :], in_=raw[:, 512:515], func=AF.Copy), reads=[rawd], writes=[taild[cc]])
                pst[cc] = {"raw": raw, "rawd": rawd}

            def s2(cc):
                raw, rawd = pst[cc]["raw"], pst[cc]["rawd"]
                cv, cvd = cvring.next()
                eng = "dve" if cc % 2 == 0 else CONV_ENG2
                s.op(eng, lambda e: e.tensor_scalar(out=cv[:, :], in0=raw[:, 3:515], scalar1=cw[:, cc, 3:4], scalar2=cw[:, cc, 4:5],
                                                    op0=ALU.mult, op1=ALU.add), reads=[rawd, cwdep], writes=[cvd])
                for k in range(3):
                    s.op(eng, lambda e, k=k: e.scalar_tensor_tensor(out=cv[:, :], in0=raw[:, k:k + 512], scalar=cw[:, cc, k:k + 1], in1=cv[:, :],
                                                                    op0=ALU.mult, op1=ALU.add), reads=[rawd, cwdep, cvd], writes=[cvd])
                pst[cc].update({"cv": cv, "cvd": cvd})

            def s3a(cc):
                cv, cvd = pst[cc]["cv"], pst[cc]["cvd"]
                if cc < 16:
                    xc, xcd = xcring.next()
                    s.op("act", lambda e: e.activation(out=xc[:, :], in_=cv[:, :], func=AF.Silu), reads=[cvd], writes=[xcd])
                    pst[cc].update({"src": xc, "srcd": xcd})
                elif cc < 24:
                    gi = cc - 16
                    s.op("act", lambda e: e.activation(out=BT[:, gi, :], in_=cv[:, :], func=AF.Silu), reads=[cvd], writes=[BTd])
                    pst[cc].update({"src": BT[:, gi, :], "srcd": BTd})
                else:
                    gi = cc - 24
                    s.op("act", lambda e: e.activation(out=CT[:, gi, :], in_=cv[:, :], func=AF.Silu), reads=[cvd], writes=[CTd])

            def s3b(cc):
                if cc >= 24:
                    return
                src, srcd = pst[cc]["src"], pst[cc]["srcd"]
                s.op("pe", [lambda e, q=q: e.transpose(out=ptr[:, q, :], in_=src[:, q * 128:(q + 1) * 128], identity=idb[:, :]) for q in range(4)],
                     reads=[srcd, iddep], writes=[ptrd])

            def s3c(cc):
                if cc >= 24:
                    return
                if cc < 16:
                    s.op("act", lambda e: e.activation(out=xtok[:, :, cc * 128:(cc + 1) * 128], in_=ptr[:, 0:4, :], func=AF.Copy), reads=[ptrd], writes=[xtokd])
                else:
                    gi = cc - 16
                    s.op("act", lambda e: e.activation(out=btok[:, :, gi * 128:(gi + 1) * 128], in_=ptr[:, 0:4, :], func=AF.Copy), reads=[ptrd], writes=[btokd])

            clist = list(range(24, 32)) if B_ else list(range(24))
            ncl = len(clist)
            if B_:
                s.op("sp", lambda e, tb=tb: e.dma_start(out=xtok[:, :, :], in_=P.xs_save[tb].rearrange("p (c f) -> p c f", c=4)), writes=[xtokd], dma=True)
                s.op("sp", lambda e, tb=tb: e.dma_start(out=btok[:, :, :], in_=P.bs_save[tb].rearrange("p (c f) -> p c f", c=4)), writes=[btokd], dma=True)
                s.op("sp", lambda e, tb=tb: e.dma_start(out=BT[:, :, :], in_=P.bt_save[tb].rearrange("p (c f) -> p c f", c=8)), writes=[BTd], dma=True)
                s.op("sp", lambda e, tb=tb: e.dma_start(out=dt[:, :, :], in_=P.dt_save[tb].rearrange("p (c f) -> p c f", c=4)), writes=[dtd], dma=True)
            for it in range(ncl + 3):
                if 0 <= it - 3 < ncl:
                    s3c(clist[it - 3])
                if it < ncl:
                    s1(clist[it])
                if 0 <= it - 2 < ncl:
                    s3b(clist[it - 2])
                if it < ncl:
                    s2(clist[it])
                if 0 <= it - 1 < ncl:
                    s3a(clist[it - 1])
            if not B_:
                for ck in range(4):
                    t0 = c0 + ck * 128
                    s.op("pe", [lambda e, kc=kc, t0=t0: e.matmul(misc[:, 0:32], lhsT=hT[:, kc, t0:t0 + 128], rhs=wdt[:, kc, :], start=(kc == 0), stop=(kc == KC - 1))
                                for kc in range(KC)], reads=[hdep, wdtdep], writes=[miscd])
                    s.op("dve", lambda e, ck=ck: e.tensor_tensor(out=dt[:, ck, :], in0=misc[:, 0:32], in1=vecs[:, 0, :], op=ALU.add), reads=[miscd, vecdep], writes=[dtd])
                s.op("act", lambda e: e.activation(out=dt[:, :, :], in_=dt[:, :, :], func=AF.Exp), reads=[dtd], writes=[dtd])
                s.op("act", lambda e: e.activation(out=dt[:, :, :], in_=dt[:, :, :], func=AF.Ln, bias=1.0), reads=[dtd], writes=[dtd])
                s.op("sp", lambda e, tb=tb: e.dma_start(out=P.xs_save[tb].rearrange("p (c f) -> p c f", c=4), in_=xtok[:, :, :]), reads=[xtokd], dma=True)
                s.op("sp", lambda e, tb=tb: e.dma_start(out=P.bs_save[tb].rearrange("p (c f) -> p c f", c=4), in_=btok[:, :, :]), reads=[btokd], dma=True)
                s.op("sp", lambda e, tb=tb: e.dma_start(out=P.bt_save[tb].rearrange("p (c f) -> p c f", c=8), in_=BT[:, :, :]), reads=[BTd], dma=True)
                s.op("sp", lambda e, tb=tb: e.dma_start(out=P.dt_save[tb].rearrange("p (c f) -> p c f", c=4), in_=dt[:, :, :]), reads=[dtd], dma=True)
            s.op("dve", lambda e: e.tensor_tensor(out=adt[:, :, :], in0=dt[:, :, :], in1=vecs[:, 1:2, :].to_broadcast([128, 4, 32]), op=ALU.mult),
                 reads=[dtd, vecdep], writes=[adtd])
            if B_:
                for cb in range(4):
                    wz, wzdep = wzring.next()
                    s.op("pool", lambda e, wz=wz, cb=cb: e.dma_start(out=wz[:, :, :], in_=wz_d[cb, :, :, :]), writes=[wzdep], dma=True)
                    for ck in range(4):
                        t0 = c0 + ck * 128
                        ps, psd = psring.next()
                        s.op("pe", [lambda e, ps=ps, wz=wz, kc=kc, t0=t0: e.matmul(ps[:, :], lhsT=hT[:, kc, t0:t0 + 128], rhs=wz[:, kc, :], start=(kc == 0), stop=(kc == KC - 1))
                                    for kc in range(KC)], reads=[hdep, wzdep], writes=[psd])
                        s.op("act", lambda e, ps=ps, ck=ck, cb=cb: e.activation(out=sz[:, ck, cb * 512:(cb + 1) * 512], in_=ps[:, :], func=AF.Silu), reads=[psd], writes=[szd])
            if B_ and tb == 0:
                s_init()
            mv = misc[:, 0:256].rearrange("p (c w h) -> p c w h", c=4, w=2)
            fns = []
            for ck in range(4):
                fns.append(lambda e, ck=ck: e.matmul(misc[:, ck * 64:ck * 64 + 32], lhsT=U[:, :], rhs=adt[:, ck, :], start=True, stop=True))
                fns.append(lambda e, ck=ck: e.matmul(misc[:, ck * 64 + 32:ck * 64 + 64], lhsT=ones[:, 0:128], rhs=adt[:, ck, :], start=True, stop=True))
            s.op("pe", fns, reads=[Udep, onesdep, adtd], writes=[miscd])
            s.op("act", lambda e: e.activation(out=sm[:, 0, :, :], in_=mv[:, :, 0, :], func=AF.Copy), reads=[miscd], writes=[smd[0]])
            s.op("act", lambda e: e.activation(out=sm[:, 7, :, :], in_=mv[:, :, 1, :], func=AF.Copy), reads=[miscd], writes=[smd[7]])
            s.op("dve", lambda e: e.tensor_tensor(out=sm[:, 1, :, :], in0=sm[:, 7, :, :], in1=sm[:, 0, :, :], op=ALU.subtract), reads=[smd[7], smd[0]], writes=[smd[1]])
            s.op("act", lambda e: e.activation(out=sm[:, 2, :, :], in_=sm[:, 1, :, :], func=AF.Exp), reads=[smd[1]], writes=[smd[2]])
            s.op("act", lambda e: e.activation(out=sm[:, 4, :, :], in_=sm[:, 7, :, :], func=AF.Exp), reads=[smd[7]], writes=[smd[4]])
            for ck in range(4):
                s.op("dve", lambda e, ck=ck: e.tensor_tensor(out=tsum[:, :], in0=sm[:, 7, ck, :], in1=tsum[:, :], op=ALU.add), reads=[smd[7], tsumd], writes=[tsumd])
            s.op("dve", lambda e: e.tensor_tensor(out=sm[:, 6, :, :], in0=dt[:, :, :], in1=sm[:, 2, :, :], op=ALU.mult), reads=[dtd, smd[2]], writes=[smd[6]])
            if B_:
                s.op("act", lambda e: e.activation(out=sm[:, 3, :, :], in_=sm[:, 0, :, :], func=AF.Exp), reads=[smd[0]], writes=[smd[3]])
                s.op("dve", lambda e: e.tensor_scalar(out=sm[:, 5, :, :], in0=sm[:, 0, :, :], scalar1=-1.0, scalar2=0.0, op0=ALU.mult, op1=ALU.add), reads=[smd[0]], writes=[smd[5]])
            for ck in range(4):
                k0 = ck * 128
                s.op(OFF_ENG, lambda e, ck=ck: e.tensor_tensor(out=h64(xdtd[:, :]), in0=h64(xtok[:, ck, :]), in1=bc64(sm[:, 6, ck, :]), op=ALU.mult),
                     reads=[xtokd, smd[6]], writes=[xdtdd])
                if B_:
                    s.op(OFF_ENG, lambda e, ck=ck: e.tensor_tensor(out=h64(xdt[:, :]), in0=h64(xtok[:, ck, :]), in1=bc64(dt[:, ck, :]), op=ALU.mult),
                         reads=[xtokd, dtd], writes=[xdt_d])
                    s.op(OFF_ENG, lambda e, ck=ck: e.tensor_tensor(out=h64(xsD[:, :]), in0=h64(xtok[:, ck, :]), in1=bc64(vecs[:, 2, :]), op=ALU.mult),
                         reads=[xtokd, vecdep], writes=[xsD_d])
                    for q in range(2):
                        s.op("pe", [lambda e, q=q, gg=gg, k0=k0: e.matmul(cbk[:, gg * 128:(gg + 1) * 128], lhsT=BT[:, 4 * q + gg, k0:k0 + 128], rhs=CT[:, 4 * q + gg, k0:k0 + 128],
                                                                       start=True, stop=True) for gg in range(4)], reads=[BTd, CTd], writes=[cbkd])
                        s.op("dve", lambda e, q=q: e.tensor_tensor(out=cbmq[q][:, :, :], in0=cbk[:, :].rearrange("p (a b) -> p a b", a=4),
                                                                    in1=U[:, :].unsqueeze(1).to_broadcast([128, 4, 128]), op=ALU.mult), reads=[cbkd, Udep], writes=[cbmqd[q]])

                    def stA(g, ck=ck):
                        ps, psd = psring.next()
                        s.op("pe", [lambda e, ps=ps, r=r, hh=4 * g + r: e.matmul(ps[:, r * 128:(r + 1) * 128], lhsT=adt[:, ck, hh:hh + 1].to_broadcast([128, 128]), rhs=U[:, :],
                                                                              start=True, stop=True) for r in range(4)], reads=[adtd, Udep], writes=[psd])
                        return ps, psd

                    def stB1(g, ps, psd, ck=ck):
                        tt, ttd = ttring.next()
                        s.op("dve", lambda e: e.tensor_tensor(out=tt[:, :].rearrange("p (a b) -> p a b", a=4), in0=ps[:, :].rearrange("p (a b) -> p a b", a=4),
                                                              in1=sm[:, 5, ck, 4 * g:4 * g + 4].unsqueeze(2).to_broadcast([128, 4, 128]), op=ALU.add), reads=[psd, smd[5]], writes=[ttd])
                        lx, lxd = lxring.next()
                        s.op("act", lambda e: e.activation(out=lx[:, :], in_=tt[:, :], func=AF.Exp), reads=[ttd], writes=[lxd])
                        return lx, lxd

                    def stB2(g, lx, lxd):
                        mt, mtd = mtring.next()
                        q, gg = divmod(g, 4)
                        s.op("dve", lambda e: e.scalar_tensor_tensor(out=mt[:, :].rearrange("p (a b) -> p a b", a=4), in0=lx[:, :].rearrange("p (a b) -> p a b", a=4), scalar=1.0,
                                                                     in1=cbmq[q][:, gg:gg + 1, :].to_broadcast([128, 4, 128]), op0=ALU.min, op1=ALU.mult),
                             reads=[lxd, cbmqd[q]], writes=[mtd])
                        return mt, mtd

                    def stC1(g, mt, mtd, k0=k0):
                        yb, ybd = ybring.next()
                        fns = [lambda e: e.matmul(yb[:, 0:256], lhsT=idb[:, :], rhs=xsD[:, g * 256:(g + 1) * 256], start=True, stop=False)]
                        for r in range(4):
                            hh = 4 * g + r
                            fns.append(lambda e, r=r, hh=hh: e.matmul(yb[:, r * 64:(r + 1) * 64], lhsT=mt[:, r * 128:(r + 1) * 128], rhs=xdt[:, hh * 64:(hh + 1) * 64], start=False, stop=(r == 3)))
                        fns.append(lambda e: e.matmul(yb[:, 256:512], lhsT=CT[:, g, k0:k0 + 128], rhs=Sb[:, g * 256:(g + 1) * 256], start=True, stop=True))
                        s.op("pe", fns, reads=[iddep, xsD_d, mtd, xdt_d, CTd, Sbd], writes=[ybd])
                        return yb, ybd

                    def stC2(g, yb, ybd, ck=ck):
                        yt, ytd = ytring.next()
                        s.op("dve", lambda e: e.tensor_tensor(out=h64(yt[:, :]), in0=h64(yb[:, 256:512]), in1=bc64(sm[:, 3, ck, 4 * g:4 * g + 4]), op=ALU.mult),
                             reads=[ybd, smd[3]], writes=[ytd])
                        s.op("dve", lambda e: e.tensor_tensor(out=y[:, g * 256:(g + 1) * 256], in0=yb[:, 0:256], in1=yt[:, :], op=ALU.add),
                             reads=[ybd, ytd], writes=[yd])

                    As = {0: stA(0), 1: stA(1)}
                    Ls = {0: stB1(0, *As[0])}
                    Ys = {}
                    for gi_ in range(8):
                        if gi_ + 2 < 8:
                            As[gi_ + 2] = stA(gi_ + 2)
                        if gi_ + 1 < 8:
                            Ls[gi_ + 1] = stB1(gi_ + 1, *As[gi_ + 1])
                        mt, mtd = stB2(gi_, *Ls[gi_])
                        Ys[gi_] = stC1(gi_, mt, mtd)
                        if gi_ >= 2 and deferred:
                            deferred.pop(0)()
                        if gi_ >= 1:
                            stC2(gi_ - 1, *Ys[gi_ - 1])
                    stC2(7, *Ys[7])
                if debug and tb == 0 and ck == 0:
                    outtoks.append(s.op("sp", lambda e: e.dma_start(out=dbg["g_dt"][:, :, :], in_=dt[:, :, :]), reads=[dtd], dma=True))
                    outtoks.append(s.op("sp", lambda e: e.dma_start(out=dbg["g_sm"][:, :, :], in_=sm[:, :, :]), reads=smd, dma=True))
                    outtoks.append(s.op("sp", lambda e: e.dma_start(out=dbg["g_xtok"][:, :, :], in_=xtok[:, :, :]), reads=[xtokd], dma=True))
                    outtoks.append(s.op("sp", lambda e: e.dma_start(out=dbg["g_btok"][:, :, :], in_=btok[:, :, :]), reads=[btokd], dma=True))
                    outtoks.append(s.op("sp", lambda e: e.dma_start(out=dbg["g_hT"][:, :, :], in_=hT[:, :, :]), reads=[hdep], dma=True))
                    if B_:
                        outtoks.append(s.op("sp", lambda e: e.dma_start(out=dbg["g_y"][:, :], in_=y[:, :]), reads=[yd], dma=True))
                        outtoks.append(s.op("sp", lambda e: e.dma_start(out=dbg["g_sz"][:, :, :], in_=sz[:, :, :]), reads=[szd], dma=True))
                s.op("dve", lambda e, ck=ck: e.tensor_tensor(out=h64(S[:, :]), in0=h64(S[:, :]), in1=bc64(sm[:, 4, ck, :]), op=ALU.mult), reads=[Sd, smd[4]] + ([Sbd] if B_ else []), writes=[Sd])
                for gp in range(4):
                    stp, stpd = psring.next()
                    s.op("pe", [lambda e, g=g, ck=ck, stp=stp: e.matmul(stp[:, (g % 2) * 256:(g % 2) * 256 + 256], lhsT=btok[:, ck, g * 128:(g + 1) * 128], rhs=xdtd[:, g * 256:(g + 1) * 256],
                                                                      start=True, stop=True) for g in (2 * gp, 2 * gp + 1)], reads=[btokd, xdtdd], writes=[stpd])
                    s.op("dve", lambda e, gp=gp, stp=stp: e.tensor_tensor(out=S[:, gp * 512:(gp + 1) * 512], in0=stp[:, :], in1=S[:, gp * 512:(gp + 1) * 512], op=ALU.add),
                         reads=[stpd, Sd], writes=[Sd])
                if B_:
                    s.op("act", lambda e: e.activation(out=Sb[:, :], in_=S[:, :], func=AF.Copy), reads=[Sd], writes=[Sbd])
                    s.op("dve", lambda e, ck=ck: e.tensor_tensor(out=y[:, :], in0=y[:, :], in1=sz[:, ck, :], op=ALU.mult), reads=[yd, szd], writes=[yd])
                    s.op("dve", lambda e: e.memset(ssq[:, :], 0.0), writes=[ssqd])
                    for g in range(8):
                        s.op("act", lambda e, g=g: e.activation(out=junk[:, :], in_=y[:, g * 256:(g + 1) * 256], func=AF.Square, accum_out=ssq[:, g:g + 1]),
                             reads=[yd], writes=[junkd, ssqd])
                    s.op("act", lambda e: e.activation(out=ssq[:, 8:16], in_=ssq[:, 0:8], func=AF.Sqrt, bias=ones[:, 128:129], scale=1.0 / 256), reads=[ssqd, onesdep], writes=[ssqd])
                    s.op("dve", lambda e: e.reciprocal(out=ssq[:, 8:16], in_=ssq[:, 8:16]), reads=[ssqd], writes=[ssqd])
                    s.op("dve", lambda e: e.tensor_tensor(out=gn[:, :].rearrange("p (g q) -> p g q", g=8), in0=y[:, :].rearrange("p (g q) -> p g q", g=8),
                                                          in1=ssq[:, 8:16].unsqueeze(2).to_broadcast([128, 8, 256]), op=ALU.mult), reads=[yd, ssqd], writes=[gnd])
                    if debug and tb == 0 and ck == 0:
                        outtoks.append(s.op("sp", lambda e: e.dma_start(out=dbg["g_yg"][:, :], in_=y[:, :]), reads=[yd], dma=True))
                        outtoks.append(s.op("sp", lambda e: e.dma_start(out=dbg["g_gn"][:, :], in_=gn[:, :]), reads=[gnd], dma=True))
                        outtoks.append(s.op("sp", lambda e: e.dma_start(out=dbg["g_S"][:, :], in_=S[:, :]), reads=[Sd], dma=True))
                    def p4b(c4, k0=k0):
                        s.op("pe", [lambda e, q=q: e.transpose(out=ptr[:, q, :], in_=gn[:, (c4 * 4 + q) * 128:(c4 * 4 + q + 1) * 128], identity=idb[:, :]) for q in range(4)],
                             reads=[gnd, iddep], writes=[ptrd])
                        for q in range(4):
                            ccx = c4 * 4 + q
                            s.op("act", lambda e, q=q, ccx=ccx: e.activation(out=gnT[:, ccx, k0:k0 + 128], in_=ptr[:, q, :], func=AF.Copy, scale=gw[:, ccx:ccx + 1]),
                                 reads=[ptrd, gwd], writes=[gnTd])
                    deferred.extend([(lambda c4=c4, f=p4b: f(c4)) for c4 in range(4)])
            while deferred:
                deferred.pop(0)()
            if debug and B_ and tb == 0:
                outtoks.append(s.op("sp", lambda e: e.dma_start(out=dbg["g_gnT"][:, :, :], in_=gnT[:, :, :]), reads=[gnTd], dma=True))
            if B_:
                for dc in range(KC):
                    wo, wodep = woring.next()
                    s.op("pool", lambda e, wo=wo, dc=dc: e.dma_start(out=wo[:, :, :], in_=wout_d[dc, :, :, :]), writes=[wodep], dma=True)
                    ps, psd = psring.next()
                    s.op("pe", [lambda e, ps=ps, wo=wo, kc=kc: e.matmul(ps[:, :], lhsT=wo[:, kc, :], rhs=gnT[:, kc, :], start=(kc == 0), stop=(kc == 15)) for kc in range(16)],
                         reads=[wodep, gnTd], writes=[psd])
                    ob, obd = oring.next()
                    s.op("sp", lambda e, ob=ob, dc=dc, tb=tb: e.dma_start(out=ob[:, :], in_=x_own[dc * 128:(dc + 1) * 128, tb * 512:(tb + 1) * 512]), writes=[obd], dma=True)
                    s.op("dve", lambda e, ob=ob, ps=ps: e.tensor_tensor(out=ob[:, :], in0=ps[:, :], in1=ob[:, :], op=ALU.add), reads=[psd, obd], writes=[obd])
                    outtoks.append(s.op("sp", lambda e, ob=ob, dc=dc, tb=tb: e.dma_start(out=x_out[dc * 128:(dc + 1) * 128, tb * 512:(tb + 1) * 512], in_=ob[:, :]),
                                        reads=[obd], dma=True))
                    if tb == 3 and io.get("hmsg") is not None:
                        s.op("sp", lambda e, ob=ob, dc=dc: e.dma_start(out=io["hmsg"][:, dc, :], in_=ob[:, 512 - HALO:512]), reads=[obd], dma=True)
        if not B_:
            HS = SMSG // 2
            outtoks.append(s.op("sp", lambda e: e.dma_start(out=smsg[0][:, :], in_=S[:, 0:HS]), reads=[Sd], dma=True))
            outtoks.append(s.op("sp", lambda e: e.dma_start(out=smsg[1][:, 0:DIN - HS], in_=S[:, HS:DIN]), reads=[Sd], dma=True))
            outtoks.append(s.op("sp", lambda e: e.dma_start(out=smsg[1][:, DIN - HS:DIN - HS + NHS], in_=tsum[:, :]), reads=[tsumd], dma=True))
        s.barrier()
        s.emit()


def _ssm_common_maps(xTs, nw, w_in, conv_w, conv_b, dt_bias, a_log, d_skip):
    wr = w_in[:, DIN:DIN + 4096].reshape(KC, 128, 32, 128)
    win = np.ascontiguousarray(wr.transpose(2, 1, 0, 3))
    wdt = np.ascontiguousarray(w_in[:, DIN + 4096:].reshape(KC, 128, 32).transpose(1, 0, 2))
    cw = np.empty((128, 32, 5), np.float32)
    cw[:, :, 0:4] = conv_w.reshape(4, 32, 128).transpose(2, 1, 0)
    cw[:, :, 4] = conv_b.reshape(32, 128).T
    vecs = np.ascontiguousarray(np.stack([dt_bias, a_log, d_skip]).astype(np.float32))
    halos = _halo_cols(xTs)
    return [{"x_own": xTs[c], "x_halo": halos[c], "nw": _cols128(nw), "win": win, "wdt": wdt, "cw": cw, "vecs": vecs}
            for c in range(NCORES)]


def run_ssm(xTs, nw, w_in, conv_w, conv_b, dt_bias, a_log, d_skip, norm_w, w_out):
    maps = _ssm_common_maps(xTs, nw, w_in, conv_w, conv_b, dt_bias, a_log, d_skip)
    ncA = _prog("ssmA", lambda: build_ssm("A"))
    resA = run_bass_kernel_spmd(ncA, maps, core_ids=list(range(NCORES)))
    sl = [np.asarray(r["s_out"]) for r in resA.results]
    dl = [np.asarray(r["d_out"]) for r in resA.results]
    ncB = _prog("ssmB", lambda: build_ssm("B"))
    wz = np.ascontiguousarray(w_in[:, 0:DIN].reshape(KC, 128, 4, 512).transpose(2, 1, 0, 3))
    gw = np.ascontiguousarray(norm_w.reshape(16, 128).T)
    wout = np.ascontiguousarray(w_out.reshape(16, 128, KC, 128).transpose(2, 1, 0, 3))
    for c in range(NCORES):
        q = c % 4
        sp = np.zeros((3, 128, DIN), np.float32)
        dp = np.zeros((3, 128, NHS), np.float32)
        for i, src in enumerate((c - 3, c - 2, c - 1)):
            if src >= c - q:
                sp[i] = sl[src]
                dp[i] = dl[src]
        maps[c].update({"wz": wz, "gw": gw, "wout": wout, "sprev": sp, "dprev": dp})
    resB = run_bass_kernel_spmd(ncB, maps, core_ids=list(range(NCORES)))
    return [np.asarray(r["x_out"]) for r in resB.results]


I32 = mybir.dt.int32
GROUPS = [[0, 1, 2, 3], [4, 5, 6, 7]]
SMSG = DIN + NHS
VMSG = 21 * 128


class Prog:
    pass


def build_fused(stop=None):
    nc = bass.Bass("TRN2", target_bir_lowering=False)
    P = Prog()
    P.nc = nc
    P.st = {}

    def din(name, shape, dt=F32):
        return nc.dram_tensor(name, list(shape), dt, kind="ExternalInput").ap()

    SMSG_ = SMSG
    x_d = din("x", [D, T])
    out_d = nc.dram_tensor("out", [D, T], F32, kind="ExternalOutput").ap()
    pidx_d = din("pidx", [1, 4], I32)
    cos_d, sin_d, pm_d, mk_d = din("cosT", [128, T]), din("sinT", [128, T]), din("pm", [128, 128]), din("masks", [128, 3, 512])
    fw_d = din("fw", [128, KC])
    L = []
    for i in range(4):
        d = {"mnw": din("mnw%d" % i, [128, KC]), "fnw": din("fnw%d" % i, [128, KC]), "wup": din("wup%d" % i, [NJ, 128, KC, 256]),
             "fcw": din("fcw%d" % i, [128, NJ, 2, 4]), "wdn": din("wdn%d" % i, [DFF, D])}
        if i % 2 == 0:
            d.update({"wq": din("wq%d" % i, [NG, NH, 128, KC, 128]), "wk": din("wk%d" % i, [NG, NH, 128, KC, 128]),
                      "wv": din("wv%d" % i, [NG, 128, KC, 1024]), "wo": din("wo%d" % i, [D, D])})
        else:
            d.update({"win": din("win%d" % i, [32, 128, KC, 128]), "wdt": din("wdt%d" % i, [128, KC, 32]), "scw": din("scw%d" % i, [128, 32, 5]),
                      "vecs": din("vecs%d" % i, [3, 32]), "wz": din("wz%d" % i, [4, 128, KC, 512]), "gw": din("gw%d" % i, [128, 16]),
                      "wout": din("wout%d" % i, [KC, 128, 16, 128])})
        L.append(d)
    xb = [nc.dram_tensor("xb%d" % i, [D, T], F32).ap() for i in range(2)]
    k_own = nc.dram_tensor("k_own", [NG, NH, 128, T], BF16).ap()
    v_own = nc.dram_tensor("v_own", [NG, NH, 128, 16, 128], BF16).ap()
    kmsg = nc.dram_tensor("kmsg", [NH, 128, KMSG], BF16).ap()
    kall = nc.dram_tensor("kall", [NH, 5 * 128, KMSG], BF16).ap()
    vmsg = nc.dram_tensor("vmsg", [NH, 128, VMSG], BF16).ap()
    vall = nc.dram_tensor("vall", [NH, 5 * 128, VMSG], BF16).ap()
    hmsg = nc.dram_tensor("hmsg", [128, KC * HALO], F32).ap()
    hall = nc.dram_tensor("hall", [4 * 128, KC * HALO], F32).ap()
    kloc = nc.dram_tensor("kloc", [NH, 128, KMSG], BF16).ap()
    vloc = nc.dram_tensor("vloc", [NH, 128, VMSG], BF16).ap()
    flags_d = din("flags", [128, 8])
    P.hall = hall
    P.hsave = nc.dram_tensor("hsave", [128, KC, HALO + T], BF16).ap()
    P.xs_save = [nc.dram_tensor("xs_save%d" % i, [128, 4 * DIN], BF16).ap() for i in range(4)]
    P.bs_save = [nc.dram_tensor("bs_save%d" % i, [128, 4 * 1024], BF16).ap() for i in range(4)]
    P.bt_save = [nc.dram_tensor("bt_save%d" % i, [128, 8 * 512], BF16).ap() for i in range(4)]
    P.dt_save = [nc.dram_tensor("dt_save%d" % i, [128, 4 * 32], F32).ap() for i in range(4)]
    P.flags = flags_d
    smsg = [nc.dram_tensor("smsg%d" % i, [128, SMSG // 2], F32).ap() for i in range(2)]
    sall = [nc.dram_tensor("sall%d" % i, [4 * 128, SMSG // 2], F32).ap() for i in range(2)]
    P.sall = sall

    with contextlib.ExitStack() as ges:
        S = Sched(nc, ges)
        P.S = S

        def setup(e):
            ins = None
            P.st["regs"] = []
            for k in range(1):
                reg = e.alloc_register("pidx%d" % k)
                ins = e.reg_load(reg, pidx_d[0:1, k:k + 1])
                P.st["regs"].append(reg)
                P.st["c%d" % (k + 1)] = e.snap(reg, min_val=0, max_val=4)
            return ins
        S.op("sp", setup)

        def pre_sp(e):
            for k, reg in enumerate(P.st.get("regs", [])):
                P.st["c%d" % (k + 1)] = e.snap(reg, min_val=0, max_val=4)
        S.pre_sp = pre_sp
        with contextlib.ExitStack() as es:
            cx = Ctx(nc, es, S)
            zb = cx.sb([128, KMSG], BF16, "zb")
            zf = cx.sb([128, SMSG], F32, "zf")
            zd = Dep()
            S.op("dve", lambda e: e.memset(zb[:, :], 0.0), writes=[zd])
            S.op("dve", lambda e: e.memset(zf[:, :], 0.0), writes=[zd])
            for h in range(NH):
                S.op("sp", lambda e, h=h: e.dma_start(out=kall[h, 512:640, :], in_=zb[:, :]), reads=[zd], dma=True)
                S.op("sp", lambda e, h=h: e.dma_start(out=vall[h, 512:640, :], in_=zb[:, 0:VMSG]), reads=[zd], dma=True)
            S.barrier()
            S.emit()

        def coll(msg2d, all2d, nrows):
            S.op("pool", lambda e: e.collective_compute("AllGather", ALU.bypass, replica_groups=GROUPS,
                                                        ins=[msg2d.opt()], outs=[all2d[0:4 * nrows, :].opt()]), cc=True)
            S.barrier()

        kalld = [Dep() for _ in range(NH)]
        valld = [Dep() for _ in range(NH)]
        klocd = [Dep() for _ in range(NH)]
        vlocd = [Dep() for _ in range(NH)]

        def coll_k(h, dep):
            S.op("pool", lambda e: e.collective_compute("AllGather", ALU.bypass, replica_groups=GROUPS,
                                                        ins=[kmsg[h, :, :].opt()], outs=[kall[h, 0:512, :].opt()]), reads=[dep], writes=[kalld[h]], cc=True)

        def coll_v(h, dep):
            S.op("pool", lambda e: e.collective_compute("AllGather", ALU.bypass, replica_groups=GROUPS,
                                                        ins=[vmsg[h, :, :].opt()], outs=[vall[h, 0:512, :].opt()]), reads=[dep], writes=[valld[h]], cc=True)

        def coll_kv():
            kv = kall.rearrange("h (r p) c -> h r p c", p=128)
            vv = vall.rearrange("h (r p) c -> h r p c", p=128)
            for h in range(NH):
                S.op("sp", lambda e, h=h: e.dma_start(out=vloc[h, :, :], in_=vv[h][P.st["c1"]]), reads=[valld[h]], writes=[vlocd[h]], dma=True)
                S.op("sp", lambda e, h=h: e.dma_start(out=kloc[h, :, :], in_=kv[h][P.st["c1"]]), reads=[kalld[h]], writes=[klocd[h]], dma=True)

        hmsg3 = hmsg.rearrange("p (kc t) -> p kc t", t=HALO)
        kmsg3 = kmsg
        vmsg4 = vmsg.rearrange("h p (b e) -> h p b e", e=128)
        vloc4 = vloc.rearrange("h p (b e) -> h p b e", e=128)
        khalo = lambda g, h: kloc[h, :, KOFF[g]:KOFF[g] + DIL[g] * 128].rearrange("p (q l) -> p q l", q=DIL[g])
        vhalo = lambda g, h: vloc4[h, :, BOFF[g]:BOFF[g] + DIL[g], :]

        step = [0]

        def go():
            step[0] += 1
            return stop is None or step[0] <= stop

        _coll = coll

        def coll(a, b, n):
            if go():
                _coll(a, b, n)

        cur = x_d
        nxt = 0
        for i in range(4):
            d = L[i]
            last = i == 3
            if i % 2 == 0:
                if go():
                  emit_attn_kv(P, {"x_in": cur, "nw": d["mnw"], "wk": d["wk"], "wv": d["wv"], "cosT": cos_d, "sinT": sin_d, "pm": pm_d,
                                 "k_own": k_own, "v_own": v_own, "kmsg": kmsg3, "vmsg": vmsg4, "coll_k": coll_k, "coll_v": coll_v})
                if go():
                    coll_kv()
                if go():
                  emit_attn_main(P, {"x_in": cur, "nw": d["mnw"], "wq": d["wq"], "wo": d["wo"], "cosT": cos_d, "sinT": sin_d, "pm": pm_d,
                                   "masks": mk_d, "k_own": k_own, "v_own": v_own, "khalo": khalo, "vhalo": vhalo, "klocd": klocd, "vlocd": vlocd,
                                   "x_out": xb[nxt], "hmsg": hmsg3})
            else:
                common = {"x_in": cur, "nw": d["mnw"], "win": d["win"], "wdt": d["wdt"], "cw": d["scw"], "vecs": d["vecs"]}
                if go():
                    emit_ssm(P, dict(common, smsg=smsg), "A")
                coll(smsg[0], sall[0], 128)
                coll(smsg[1], sall[1], 128)
                if go():
                  emit_ssm(P, dict(common, wz=d["wz"], gw=d["gw"], wout=d["wout"], x_out=xb[nxt], hmsg=hmsg3), "B")
            cur = xb[nxt]
            nxt = 1 - nxt
            coll(hmsg, hall, 128)
            io = {"x_in": cur, "nw": d["fnw"], "wup": d["wup"], "cw": d["fcw"], "wdn": d["wdn"],
                  "x_out": out_d if last else xb[nxt], "hmsg": None if (last or i % 2 == 1) else hmsg3}
            if last:
                io["fw"] = fw_d
            if go():
                emit_ffn(P, io, final_norm=last)
            if not last:
                cur = xb[nxt]
                nxt = 1 - nxt
                if i % 2 == 0:
                    coll(hmsg, hall, 128)
        if stop is not None:
            S.emit()
    return nc


def _prep_maps(inp):
    f = lambda a: np.ascontiguousarray(np.asarray(a, dtype=np.float32))
    x = f(inp["x"])
    xTs = _shards_T(x)
    shared = {"pm": perm_matrix(), "fw": _cols128(f(inp["final_norm_w"]))}
    for i in range(4):
        j = i // 2
        shared["mnw%d" % i] = _cols128(f(inp["mix_norm_w"])[i])
        shared["fnw%d" % i] = _cols128(f(inp["ffn_norm_w"])[i])
        w_up = f(inp["ffn_w_up"])[i]
        shared["wup%d" % i] = np.ascontiguousarray(w_up.reshape(KC, 128, 2, NJ, 128).transpose(3, 1, 0, 2, 4).reshape(NJ, 128, KC, 256))
        cw = np.empty((128, NJ, 2, 4), np.float32)
        cw[:, :, :, 0:3] = f(inp["ffn_conv_w"])[i].reshape(3, 2, NJ, 128).transpose(3, 2, 1, 0)
        cw[:, :, :, 3] = f(inp["ffn_conv_b"])[i].reshape(2, NJ, 128).transpose(2, 1, 0)
        shared["fcw%d" % i] = cw
        shared["wdn%d" % i] = f(inp["ffn_w_down"])[i]
        if i % 2 == 0:
            wr = f(inp["attn_w_qkv"])[j].reshape(KC, 128, NG, 3, NH, 128)
            shared["wq%d" % i] = np.ascontiguousarray(wr[:, :, :, 0].transpose(2, 3, 1, 0, 4))
            shared["wk%d" % i] = np.ascontiguousarray(wr[:, :, :, 1].transpose(2, 3, 1, 0, 4))
            shared["wv%d" % i] = np.ascontiguousarray(wr[:, :, :, 2].transpose(2, 1, 0, 3, 4).reshape(NG, 128, KC, 1024))
            shared["wo%d" % i] = f(inp["attn_w_o"])[j]
        else:
            w_in = f(inp["ssm_w_in"])[j]
            shared["win%d" % i] = np.ascontiguousarray(w_in[:, DIN:DIN + 4096].reshape(KC, 128, 32, 128).transpose(2, 1, 0, 3))
            shared["wdt%d" % i] = np.ascontiguousarray(w_in[:, DIN + 4096:].reshape(KC, 128, 32).transpose(1, 0, 2))
            scw = np.empty((128, 32, 5), np.float32)
            scw[:, :, 0:4] = f(inp["ssm_conv_w"])[j].reshape(4, 32, 128).transpose(2, 1, 0)
            scw[:, :, 4] = f(inp["ssm_conv_b"])[j].reshape(32, 128).T
            shared["scw%d" % i] = scw
            shared["vecs%d" % i] = np.ascontiguousarray(np.stack([f(inp["ssm_dt_bias"])[j], f(inp["ssm_a_log"])[j], f(inp["ssm_d"])[j]]))
            shared["wz%d" % i] = np.ascontiguousarray(w_in[:, 0:DIN].reshape(KC, 128, 4, 512).transpose(2, 1, 0, 3))
            shared["gw%d" % i] = np.ascontiguousarray(f(inp["ssm_norm_w"])[j].reshape(16, 128).T)
            shared["wout%d" % i] = np.ascontiguousarray(f(inp["ssm_w_out"])[j].reshape(16, 128, KC, 128).transpose(2, 1, 0, 3))
    maps = []
    for c in range(NCORES):
        q = c % 4
        cs, sn = rope_tables_np(q * T)
        m = dict(shared)
        fl = np.zeros((128, 8), np.float32)
        if q >= 1:
            fl[:, q - 1] = 1.0
        for r in range(4):
            if r < q:
                fl[:, 4 + r] = 1.0
        m.update({"x": xTs[c], "cosT": cs, "sinT": sn, "masks": attn_masks(q != 0), "flags": fl,
                  "pidx": np.array([[q - 1 if q >= 1 else 4, q - 2 if q >= 2 else 4, q - 3 if q >= 3 else 4, 0]], np.int32)})
        maps.append(m)
    return maps


def kernel(**inputs):
    maps = _prep_maps(inputs)
    nc = _prog("fused", build_fused)
    res = run_bass_kernel_spmd(nc, maps, core_ids=list(range(NCORES)))
    out = np.empty((2, 4 * T, D), np.float32)
    for c in range(NCORES):
        b, q = divmod(c, 4)
        out[b, q * T:(q + 1) * T, :] = np.asarray(res.results[c]["out"]).T
    return out
```
